# Optimizing a Trainium2 kernel written in Bass

```python
import functools
import jax, jax.numpy as jnp
from jax import lax
import numpy as np

D_MODEL = 1024
BATCH = 8
SEQ = 4096
DEPTH = 1
DEC_BATCH = 32
DEC_SEQ = 1
PAST_LEN = 16384
PAGE_SIZE = 128

D_RNN = D_MODEL
RNN_BLOCKS = 4
RNN_BLOCK_W = D_RNN // RNN_BLOCKS
CONV_W = 4
LRU_C = 8.0
N_HEADS = 8
HEAD_DIM = 128
N_KV_HEADS = 2
D_ATT = N_HEADS * HEAD_DIM
N_IDX_HEADS = 8
IDX_DIM = 64
IDX_W_SCALE = (N_IDX_HEADS * IDX_DIM) ** -0.5
TOPK_MAX = 256
ROPE_THETA = 10000.0
Q_BLOCK = 128
EPS = 1e-6
SPLITS = (D_RNN, D_RNN, D_ATT, N_KV_HEADS * HEAD_DIM, N_KV_HEADS * HEAD_DIM, D_ATT,
          N_IDX_HEADS * IDX_DIM, IDX_DIM, N_IDX_HEADS, D_MODEL, D_MODEL)
D_IN = sum(SPLITS)

kernel_name = 'hybrid_rglru_dsa_decoder_step'


def rms_norm(x, g):
    xf = x.astype(jnp.float32)
    y = xf * lax.rsqrt(jnp.mean(xf * xf, axis=-1, keepdims=True) + EPS)
    return (y * g.astype(jnp.float32)).astype(x.dtype)


def layer_norm(x, g, b):
    xf = x.astype(jnp.float32)
    mu = jnp.mean(xf, axis=-1, keepdims=True)
    var = jnp.mean(jnp.square(xf - mu), axis=-1, keepdims=True)
    y = (xf - mu) * lax.rsqrt(var + EPS) * g.astype(jnp.float32) + b.astype(jnp.float32)
    return y.astype(x.dtype)


def rotary(x, pos):
    half = x.shape[-1] // 2
    inv_freq = ROPE_THETA ** (-jnp.arange(half, dtype=jnp.float32) / half)
    ang = pos.astype(jnp.float32)[:, None] * inv_freq[None, :]
    cos = jnp.cos(ang)[None, :, None, :]
    sin = jnp.sin(ang)[None, :, None, :]
    xf = x.astype(jnp.float32)
    x1, x2 = xf[..., :half], xf[..., half:]
    return jnp.concatenate([x1 * cos - x2 * sin, x2 * cos + x1 * sin], axis=-1).astype(x.dtype)


def adaln_modulation(c, w, b):
    m = jax.nn.silu(c) @ w + b
    shift, scale, gate = jnp.split(m, 3, axis=-1)
    return shift[:, None, :], scale[:, None, :], gate[:, None, :]


def split_cols(z):
    offsets = np.cumsum(SPLITS)[:-1].tolist()
    return jnp.split(z, offsets, axis=-1)


def causal_depthwise_conv(x, buf, w, b):
    T = x.shape[1]
    xp = jnp.concatenate([buf.astype(x.dtype), x], axis=1)
    y = b
    for j in range(CONV_W):
        y = y + xp[:, j:j + T] * w[j]
    return y, xp[:, T:]


def block_diag(x, w):
    B, T, _ = x.shape
    xb = x.reshape(B, T, RNN_BLOCKS, RNN_BLOCK_W)
    return jnp.einsum('btnc,ncd->btnd', xb, w).reshape(B, T, D_RNN)


def rg_lru(x, h0, w_ra, b_ra, w_rx, b_rx, lam):
    r = jax.nn.sigmoid(block_diag(x, w_ra) + b_ra).astype(jnp.float32)
    i = jax.nn.sigmoid(block_diag(x, w_rx) + b_rx)
    log_a = -LRU_C * r * jax.nn.softplus(-lam.astype(jnp.float32))
    a = jnp.exp(log_a)
    u = jnp.sqrt(-jnp.expm1(2.0 * log_a)) * (i * x).astype(jnp.float32)

    def step(h, au):
        a_t, u_t = au
        h = a_t * h + u_t
        return h, h

    h_last, hs = lax.scan(step, h0.astype(jnp.float32), (jnp.swapaxes(a, 0, 1), jnp.swapaxes(u, 0, 1)))
    return jnp.swapaxes(hs, 0, 1).astype(x.dtype), h_last.astype(h0.dtype)


def take_rows(rows, idx):
    return jax.vmap(lambda r, i: r[i])(rows, idx)


def indexer_topk(qi, wi, ki, q_pos, k_sel):
    s = jnp.einsum('bthd,bsd->bths', qi, ki).astype(jnp.float32)
    score = jnp.einsum('bths,bth->bts', jax.nn.relu(s), wi.astype(jnp.float32))
    key_pos = jnp.arange(ki.shape[1], dtype=jnp.int32)
    admissible = key_pos[None, None, :] <= q_pos[None, :, None]
    score = jnp.where(admissible, score, -jnp.inf)
    _, sel = lax.top_k(score, k_sel)
    valid = sel <= q_pos[None, :, None]
    return sel, valid


def gathered_attention(q, k_sel, v_sel, valid):
    B, T, H, Dh = q.shape
    qg = q.reshape(B, T, N_KV_HEADS, H // N_KV_HEADS, Dh)
    s = jnp.einsum('btgrd,btkgd->btgrk', qg, k_sel).astype(jnp.float32) * (HEAD_DIM ** -0.5)
    s = jnp.where(valid[:, :, None, None, :], s, -jnp.inf)
    p = jax.nn.softmax(s, axis=-1).astype(v_sel.dtype)
    o = jnp.einsum('btgrk,btkgd->btgrd', p, v_sel)
    return o.reshape(B, T, H * Dh)


def prompt_sparse_attention(q, k, v, qi, ki, wi, pos, k_sel):
    B, S = q.shape[0], q.shape[1]

    def one_block(t0):
        sl = lambda arr: lax.dynamic_slice_in_dim(arr, t0, Q_BLOCK, axis=1)
        pos_b = lax.dynamic_slice_in_dim(pos, t0, Q_BLOCK, axis=0)
        sel, valid = indexer_topk(sl(qi), sl(wi), ki, pos_b, k_sel)
        return gathered_attention(sl(q), take_rows(k, sel), take_rows(v, sel), valid)

    o = lax.map(one_block, jnp.arange(0, S, Q_BLOCK))
    return jnp.swapaxes(o, 0, 1).reshape(B, S, -1)


def sample_sparse_attention(q, k, v, qi, ki, wi, pos, k_sel, cache_k, cache_v, cache_idx_k, page_table, layer):
    B, T = q.shape[0], q.shape[1]
    past_ki = cache_idx_k[layer, page_table].reshape(B, -1, IDX_DIM)
    past_len = past_ki.shape[1]
    all_ki = jnp.concatenate([past_ki.astype(ki.dtype), ki], axis=1)
    sel, valid = indexer_topk(qi, wi, all_ki, pos, k_sel)
    in_past = (sel < past_len)[..., None, None]
    sp = jnp.minimum(sel, past_len - 1)
    phys = jnp.take_along_axis(page_table, (sp // PAGE_SIZE).reshape(B, -1), axis=1).reshape(sel.shape)
    off = sp % PAGE_SIZE
    sn = jnp.clip(sel - past_len, 0, T - 1)
    k_rows = jnp.where(in_past, cache_k[layer, phys, off].astype(k.dtype), take_rows(k, sn))
    v_rows = jnp.where(in_past, cache_v[layer, phys, off].astype(v.dtype), take_rows(v, sn))
    return gathered_attention(q, k_rows, v_rows, valid)


def trunk_layer(x, c, pos, conv_buf, h0, attend, p):
    B, T, _ = x.shape
    shift, scale, gate = adaln_modulation(c, p['w_ada'], p['b_ada'])
    xn = rms_norm(x, p['g_norm']) * (1.0 + scale) + shift
    z = xn @ p['w_in']
    xa, ga, q, k, v, gb, qi, ki, wi, ma, mb = split_cols(z)
    xa_conv, new_buf = causal_depthwise_conv(xa, conv_buf, p['w_conv'], p['b_conv'])
    ha, h_last = rg_lru(xa_conv, h0, p['w_ra'], p['b_ra'], p['w_rx'], p['b_rx'], p['lru_lambda'])
    ya = (ha * jax.nn.silu(ga)) @ p['w_pa']
    q = rotary(q.reshape(B, T, N_HEADS, HEAD_DIM), pos)
    k = rotary(k.reshape(B, T, N_KV_HEADS, HEAD_DIM), pos)
    v = v.reshape(B, T, N_KV_HEADS, HEAD_DIM)
    qi = rotary(qi.reshape(B, T, N_IDX_HEADS, IDX_DIM), pos)
    ki = rotary(layer_norm(ki, p['idx_k_norm_g'], p['idx_k_norm_b'])[:, :, None, :], pos)[:, :, 0, :]
    wi = wi * IDX_W_SCALE
    o = attend(q, k, v, qi, ki, wi, pos)
    yb = (o * jax.nn.silu(gb)) @ p['w_pb']
    m = jax.nn.sigmoid(ma) * ya + jax.nn.sigmoid(mb) * yb
    x = x + gate * (m @ p['w_o'])
    return x, (k, v, ki, new_buf, h_last)


def setup_inputs(seed: int = 0) -> dict:
    key = jax.random.key(seed)
    ks = jax.random.split(key, 32)
    f32 = jnp.float32
    n_pages = PAST_LEN // PAGE_SIZE
    n_used = DEC_BATCH * n_pages
    n_pool = n_used + max(1, n_used // 4)
    nrm = lambda k, shape, s: s * jax.random.normal(k, shape, f32)
    page_table = jax.random.permutation(ks[0], n_pool)[:n_used].reshape(DEC_BATCH, n_pages).astype(jnp.int32)
    a_base = jax.random.uniform(ks[1], (DEPTH, D_RNN), f32, 0.9, 0.999)
    s_base = a_base ** (1.0 / LRU_C)
    lru_lambda = jnp.log(s_base) - jnp.log1p(-s_base)
    return {
        'x_prompt': nrm(ks[2], (BATCH, SEQ, D_MODEL), 1.0),
        'x_sample': nrm(ks[3], (DEC_BATCH, DEC_SEQ, D_MODEL), 1.0),
        'cache_k': nrm(ks[4], (DEPTH, n_pool, PAGE_SIZE, N_KV_HEADS, HEAD_DIM), 1.0),
        'cache_v': nrm(ks[5], (DEPTH, n_pool, PAGE_SIZE, N_KV_HEADS, HEAD_DIM), 1.0),
        'cache_idx_k': nrm(ks[6], (DEPTH, n_pool, PAGE_SIZE, IDX_DIM), 1.0),
        'state_conv': nrm(ks[7], (DEPTH, DEC_BATCH, CONV_W - 1, D_RNN), 1.0),
        'state_rglru': nrm(ks[8], (DEPTH, DEC_BATCH, D_RNN), 0.5),
        'page_table': page_table,
        'c_prompt': nrm(ks[9], (BATCH, D_MODEL), 1.0),
        'c_sample': nrm(ks[10], (DEC_BATCH, D_MODEL), 1.0),
        'w_ada': nrm(ks[11], (DEPTH, D_MODEL, 3 * D_MODEL), 0.5 * D_MODEL ** -0.5),
        'b_ada': nrm(ks[12], (DEPTH, 3 * D_MODEL), 0.02),
        'g_norm': 1.0 + nrm(ks[13], (DEPTH, D_MODEL), 0.02),
        'w_in': nrm(ks[14], (DEPTH, D_MODEL, D_IN), D_MODEL ** -0.5),
        'w_conv': nrm(ks[15], (DEPTH, CONV_W, D_RNN), CONV_W ** -0.5),
        'b_conv': nrm(ks[16], (DEPTH, D_RNN), 0.02),
        'w_ra': nrm(ks[17], (DEPTH, RNN_BLOCKS, RNN_BLOCK_W, RNN_BLOCK_W), RNN_BLOCK_W ** -0.5),
        'b_ra': nrm(ks[18], (DEPTH, D_RNN), 0.02),
        'w_rx': nrm(ks[19], (DEPTH, RNN_BLOCKS, RNN_BLOCK_W, RNN_BLOCK_W), RNN_BLOCK_W ** -0.5),
        'b_rx': nrm(ks[20], (DEPTH, D_RNN), 0.02),
        'lru_lambda': lru_lambda,
        'idx_k_norm_g': 1.0 + nrm(ks[21], (DEPTH, IDX_DIM), 0.02),
        'idx_k_norm_b': nrm(ks[22], (DEPTH, IDX_DIM), 0.02),
        'w_pa': nrm(ks[23], (DEPTH, D_RNN, D_MODEL), D_RNN ** -0.5),
        'w_pb': nrm(ks[24], (DEPTH, D_ATT, D_MODEL), D_ATT ** -0.5),
        'w_o': nrm(ks[25], (DEPTH, D_MODEL, D_MODEL), D_MODEL ** -0.5),
        'g_final': 1.0 + nrm(ks[26], (D_MODEL,), 0.02),
    }


def reference(x_prompt, x_sample, cache_k, cache_v, cache_idx_k, state_conv, state_rglru, page_table,
              c_prompt, c_sample, w_ada, b_ada, g_norm, w_in, w_conv, b_conv, w_ra, b_ra, w_rx, b_rx,
              lru_lambda, idx_k_norm_g, idx_k_norm_b, w_pa, w_pb, w_o, g_final):
    pos_prompt = jnp.arange(SEQ, dtype=jnp.int32)
    pos_sample = PAST_LEN + jnp.arange(DEC_SEQ, dtype=jnp.int32)
    k_prompt = min(TOPK_MAX, SEQ // 4)
    k_sample = min(TOPK_MAX, (PAST_LEN + DEC_SEQ) // 4)
    hp, hs = x_prompt, x_sample
    kp, vp, kip, cp, lp = [], [], [], [], []
    ksm, vsm, kism, csm, lsm = [], [], [], [], []
    for l in range(DEPTH):
        p = {
            'w_ada': w_ada[l], 'b_ada': b_ada[l], 'g_norm': g_norm[l], 'w_in': w_in[l],
            'w_conv': w_conv[l], 'b_conv': b_conv[l], 'w_ra': w_ra[l], 'b_ra': b_ra[l],
            'w_rx': w_rx[l], 'b_rx': b_rx[l], 'lru_lambda': lru_lambda[l],
            'idx_k_norm_g': idx_k_norm_g[l], 'idx_k_norm_b': idx_k_norm_b[l],
            'w_pa': w_pa[l], 'w_pb': w_pb[l], 'w_o': w_o[l],
        }
        prompt_attend = functools.partial(prompt_sparse_attention, k_sel=k_prompt)
        sample_attend = functools.partial(sample_sparse_attention, k_sel=k_sample, cache_k=cache_k,
                                          cache_v=cache_v, cache_idx_k=cache_idx_k,
                                          page_table=page_table, layer=l)
        conv0 = jnp.zeros((BATCH, CONV_W - 1, D_RNN), x_prompt.dtype)
        h0 = jnp.zeros((BATCH, D_RNN), x_prompt.dtype)
        hp, st_p = trunk_layer(hp, c_prompt, pos_prompt, conv0, h0, prompt_attend, p)
        hs, st_s = trunk_layer(hs, c_sample, pos_sample, state_conv[l], state_rglru[l], sample_attend, p)
        kp.append(st_p[0]); vp.append(st_p[1]); kip.append(st_p[2]); cp.append(st_p[3]); lp.append(st_p[4])
        ksm.append(st_s[0]); vsm.append(st_s[1]); kism.append(st_s[2]); csm.append(st_s[3]); lsm.append(st_s[4])
    y_prompt = rms_norm(hp, g_final)
    y_sample = rms_norm(hs, g_final)
    new_k_prompt = jnp.stack(kp)
    new_v_prompt = jnp.stack(vp)
    new_idx_k_prompt = jnp.stack(kip)
    new_conv_prompt = jnp.stack(cp)
    new_lru_prompt = jnp.stack(lp)
    new_k_sample = jnp.stack(ksm)
    new_v_sample = jnp.stack(vsm)
    new_idx_k_sample = jnp.stack(kism)
    new_conv_sample = jnp.stack(csm)
    new_lru_sample = jnp.stack(lsm)
    return (y_prompt, y_sample, new_k_prompt, new_v_prompt, new_idx_k_prompt, new_conv_prompt, new_lru_prompt,
            new_k_sample, new_v_sample, new_idx_k_sample, new_conv_sample, new_lru_sample)
```

```python
import numpy as np
import concourse.bass as bass
import concourse.mybir as mybir
from concourse.bass_utils import run_bass_kernel_spmd
from contextlib import ExitStack

F32 = mybir.dt.float32
BF16 = mybir.dt.bfloat16
I32 = mybir.dt.int32
AF = mybir.ActivationFunctionType
ALU = mybir.AluOpType
AX = mybir.AxisListType

D = 1024
SEQ = 4096
NCORE = 8
NS = 4
PAST = 16384
NPAGE = 128
NPOOL = 5120
D_IN = 7240
EPS = 1e-6
IDX_W_SCALE = 512.0 ** -0.5
ATT_SCALE = 128.0 ** -0.5
TOPK = 256
NEG = -30000.0
CH = 512
NCHUNK = SEQ // CH
N_BISECT = 18
N_BISECT_S = 26

FM_XA, FM_GA, FM_GB, FM_MA, FM_MB, FM_PA, FM_PB = 0, 8, 16, 24, 32, 40, 48
NFM = 56
FM_COLS = [0, 1024, 3584, 5192, 6216]
TM_Q0, TM_Q1, TM_KV, TM_QI, TM_KW, TM_O0, TM_O1 = range(7)
NTM = 7
TM_COLS = [2048, 2560, 3072, 4608, 5120]

CP_GN, CP_BADA, CP_WCONV, CP_BCONV, CP_BRA, CP_BRX, CP_LAM = 0, 8, 32, 64, 72, 80, 88
NCP = 96
CS_ID, CS_TS, CS_JF, CS_IR, CS_CAUS, CS_J1, CS_P2, CS_END = 0, 128, 256, 512, 640, 768, 770, 802

DO_SAMPLE = True
DO_ATTN = True
STOP = 'all'
NCHUNK_RUN = NCHUNK


class Buf:
    __slots__ = ("name", "w", "r")

    def __init__(self, name=""):
        self.name = name
        self.w = None
        self.r = []


class Prog:
    EPOCH = 30000

    def __init__(self, nc, es, n_dma_sems=12):
        self.nc = nc
        self.es = es
        self.engs = {"pe": nc.tensor, "act": nc.scalar, "dve": nc.vector, "pool": nc.gpsimd, "sp": nc.sync}
        self.cnt = {k: 0 for k in self.engs}
        self.sems = {k: [es.enter_context(nc.semaphore("s_" + k))] for k in self.engs}
        self.waited = {k: {} for k in self.engs}
        self.dsems = {}
        self.dcnt = {}
        self.dnext = {}
        for q in ("sp", "act", "pool"):
            self.dsems[q] = [es.enter_context(nc.semaphore("d%s%d" % (q, i))) for i in range(n_dma_sems)]
            self.dcnt[q] = [0] * n_dma_sems
            self.dnext[q] = 0
        self.ninst = 0
        self.out_toks = []

    def _wait(self, ek, tok):
        if tok is None:
            return
        sem, val = tok
        key = id(sem)
        if self.waited[ek].get(key, 0) >= val:
            return
        self.engs[ek].wait_ge(sem, val)
        self.waited[ek][key] = val

    def _same(self, ek, tok):
        if tok is None:
            return False
        sem = tok[0]
        for s_ in self.sems[ek]:
            if sem is s_:
                return True
        return False

    def _deps(self, ek, reads, writes):
        pe = (ek == "pe")
        for b in reads:
            if pe and self._same(ek, b.w):
                continue
            self._wait(ek, b.w)
        for b in writes:
            if not (pe and self._same(ek, b.w)):
                self._wait(ek, b.w)
            for t in b.r:
                if not (pe and self._same(ek, t)):
                    self._wait(ek, t)

    def _mark(self, tok, reads, writes):
        for b in reads:
            b.r.append(tok)
            if len(b.r) > 64:
                b.r = b.r[-48:]
        for b in writes:
            b.w = tok
            b.r = []

    def op(self, ek, fn, reads=(), writes=()):
        self._deps(ek, reads, writes)
        ins = fn(self.engs[ek])
        if self.cnt[ek] >= self.EPOCH:
            self.sems[ek].append(self.es.enter_context(self.nc.semaphore("s_%s_%d" % (ek, len(self.sems[ek])))))
            self.cnt[ek] = 0
        sem = self.sems[ek][-1]
        self.cnt[ek] += 1
        ins.then_inc(sem, 1)
        tok = (sem, self.cnt[ek])
        self._mark(tok, reads, writes)
        self.ninst += 1
        return tok

    def dma(self, ek, fn, reads=(), writes=(), is_out=False):
        q = ek
        i = self.dnext[q]
        self.dnext[q] = (i + 1) % len(self.dsems[q])
        sem = self.dsems[q][i]
        if self.dcnt[q][i] > 0:
            self._wait(ek, (sem, self.dcnt[q][i]))
        self._deps(ek, reads, writes)
        ins = fn(self.engs[ek])
        self.dcnt[q][i] += 16
        ins.then_inc(sem, 16)
        tok = (sem, self.dcnt[q][i])
        self._mark(tok, reads, writes)
        self.ninst += 1
        if is_out:
            self.out_toks.append(tok)
        return tok

    def all_tokens(self):
        toks = []
        for k in self.engs:
            if self.cnt[k] > 0:
                toks.append((self.sems[k][-1], self.cnt[k]))
        for q in self.dsems:
            for s, c in zip(self.dsems[q], self.dcnt[q]):
                if c > 0:
                    toks.append((s, c))
        return toks

    def barrier(self):
        toks = self.all_tokens()
        for ek in self.engs:
            for t in toks:
                self._wait(ek, t)

    def finish(self):
        toks = self.all_tokens()
        for t in toks:
            self._wait("sp", t)


class Ring:
    def __init__(self, items):
        self.items = items
        self.i = 0

    def next(self):
        it = self.items[self.i]
        self.i = (self.i + 1) % len(self.items)
        return it


def build_program():
    nc = bass.Bass("TRN2", target_bir_lowering=False)
    dt_in = lambda n, s, d=F32: nc.dram_tensor(n, s, d, kind="ExternalInput").ap()
    dt_out = lambda n, s, d=F32: nc.dram_tensor(n, s, d, kind="ExternalOutput").ap()

    x_p = dt_in("x_p", [SEQ, D])
    x_s = dt_in("x_s", [NS, D])
    c5 = dt_in("c5", [1 + NS, D])
    cache_k = dt_in("cache_k", [NPOOL * 128, 256])
    cache_v = dt_in("cache_v", [NPOOL * 128, 256])
    cache_i = dt_in("cache_i", [NPOOL, 128 * 64])
    st_conv = dt_in("st_conv", [NS * 3, D])
    st_lru = dt_in("st_lru", [NS, D])
    ptab = dt_in("ptab", [128, NS], I32)
    w_ada = dt_in("w_ada", [6, 128, 8 * 512])
    b_ada = dt_in("b_ada", [1, 3 * D])
    w_fm = dt_in("w_fm", [NFM, 128, 1024])
    w_tm = dt_in("w_tm", [NTM, 128, 4096])
    w_rr = dt_in("w_rr", [128, 2 * 2048])
    chp = dt_in("chp", [128, NCP])
    cst = dt_in("cst", [128, CS_END])
    ropeq = dt_in("ropeq", [SEQ, 256])
    ropei = dt_in("ropei", [SEQ, 128])
    ropes = dt_in("ropes", [NS, 384])
    g_fin = dt_in("g_fin", [1, D])
    idx_gb = dt_in("idx_gb", [1, 128])

    y_p = dt_out("y_p", [SEQ, D])
    y_s = dt_out("y_s", [NS, D])
    nk_p = dt_out("nk_p", [SEQ, 256])
    nv_p = dt_out("nv_p", [SEQ, 256])
    nki_p = dt_out("nki_p", [SEQ, 64])
    ncv_p = dt_out("ncv_p", [3, D])
    nlr_p = dt_out("nlr_p", [1, D])
    nk_s = dt_out("nk_s", [NS, 256])
    nv_s = dt_out("nv_s", [NS, 256])
    nki_s = dt_out("nki_s", [NS, 64])
    ncv_s = dt_out("ncv_s", [NS, 3, D])
    nlr_s = dt_out("nlr_s", [NS, D])

    wfm = nc.dram_tensor("wfm_bf", [NFM, 128, 1024], BF16, kind="Internal").ap()
    wtm = nc.dram_tensor("wtm_bf", [NTM, 128, 4096], BF16, kind="Internal").ap()

    with ExitStack() as es:
        P = Prog(nc, es)

        def sb(name, shape, dtype=F32, scope=es):
            return scope.enter_context(nc.sbuf_tensor(name, shape, dtype))

        def ring(name, n, shape, dtype=F32, scope=es):
            return Ring([(sb("%s%d" % (name, i), shape, dtype, scope), Buf(name)) for i in range(n)])

        OPQ = ["act", "dve", "pool"]

        psA = es.enter_context(nc.psum_tensor("psA", [128, 512], F32)); bpsA = Buf()
        psB = es.enter_context(nc.psum_tensor("psB", [128, 512], F32)); bpsB = Buf()
        psS0 = es.enter_context(nc.psum_tensor("psS0", [128, 512], F32)); bpsS0 = Buf()
        psS1 = es.enter_context(nc.psum_tensor("psS1", [128, 512], F32)); bpsS1 = Buf()
        psO = es.enter_context(nc.psum_tensor("psO", [128, 512], F32)); bpsO = Buf()
        psL = es.enter_context(nc.psum_tensor("psL", [128, 512], F32)); bpsL = Buf()
        psT0 = es.enter_context(nc.psum_tensor("psT0", [128, 512], F32)); bpsT0 = Buf()
        psT1 = es.enter_context(nc.psum_tensor("psT1", [128, 512], F32)); bpsT1 = Buf()
        rPS = Ring([(psA, bpsA), (psB, bpsB)])
        rPSS = Ring([(psS0, bpsS0), (psS1, bpsS1)])
        rPST = Ring([(psT0[:].bitcast(BF16), bpsT0), (psT1[:].bitcast(BF16), bpsT1)])
        rPACC = Ring([(psT0, bpsT0), (psT1, bpsT1)])

        cstt = sb("cstt", [128, CS_END]); bcst = Buf()
        chpt = sb("chpt", [128, NCP]); bchp = Buf()
        identb = sb("identb", [128, 128], BF16); bidb = Buf()
        ident4 = sb("ident4", [128, 512], BF16); bid4 = Buf()
        onesb = sb("onesb", [128, 128], BF16); bonesb = Buf()
        onesf = sb("onesf", [128, 128]); bonesf = Buf()
        clam = sb("clam", [128, 8]); bclam = Buf()
        gate_bc = sb("gate_bc", [128, D]); bgate = Buf()
        gfin_bc = sb("gfin_bc", [128, D]); bgfin = Buf()
        idxgb_bc = sb("idxgb_bc", [128, 128]); bidxgb = Buf()
        wrr = sb("wrr", [128, 4096], BF16); bwrr = Buf()
        A_p = sb("A_p", [128, 8]); bAp = Buf()
        B_p = sb("B_p", [128, 8]); bBp = Buf()
        mT = sb("mT", [128, 24 * 5]); bmT = Buf()
        silucT = sb("silucT", [128, 8 * 5], BF16); bsil = Buf()
        identf = cstt[:, CS_ID:CS_ID + 128]
        rFM = ring("wfmr", 4, [128, 1024], BF16)
        rTM = ring("wtmr", 2, [128, 4096], BF16)
        sS = es.enter_context(ExitStack())
        gate_s = sb("gate_s", [NS, D], F32, sS); bgs = Buf()

        P.dma("sp", lambda e: e.dma_start(out=cstt[:], in_=cst[:, :]), writes=[bcst])
        P.dma("sp", lambda e: e.dma_start(out=chpt[:], in_=chp[:, :]), writes=[bchp])
        P.dma("sp", lambda e: e.dma_start(out=gfin_bc[:], in_=g_fin[0:1, :].broadcast_to([128, D])), writes=[bgfin])
        P.dma("sp", lambda e: e.dma_start(out=gate_bc[:], in_=b_ada[0:1, 2 * D:3 * D].broadcast_to([128, D])), writes=[bgate])
        P.dma("sp", lambda e: e.dma_start(out=gate_s[:], in_=b_ada[0:1, 2 * D:3 * D].broadcast_to([NS, D])), writes=[bgs])
        P.dma("sp", lambda e: e.dma_start(out=idxgb_bc[:], in_=idx_gb[0:1, :].broadcast_to([128, 128])), writes=[bidxgb])
        P.op("dve", lambda e: e.tensor_copy(out=identb[:], in_=identf), reads=[bcst], writes=[bidb])
        for r4 in range(4):
            P.op("pool", lambda e: e.tensor_copy(out=ident4[:, r4 * 128:(r4 + 1) * 128], in_=identf), reads=[bcst], writes=[bid4])
        P.op("pool", lambda e: e.memset(onesb[:], 1.0), writes=[bonesb])
        P.op("pool", lambda e: e.memset(onesf[:], 1.0), writes=[bonesf])
        P.op("act", lambda e: e.activation(out=clam[:], in_=chpt[:, CP_LAM:CP_LAM + 8], func=AF.Exp, scale=-1.0), reads=[bchp], writes=[bclam])
        P.op("act", lambda e: e.activation(out=clam[:], in_=clam[:], func=AF.Ln, bias=1.0), reads=[bclam], writes=[bclam])
        P.op("dve", lambda e: e.tensor_scalar(out=clam[:], in0=clam[:], scalar1=-8.0, scalar2=None, op0=ALU.mult), reads=[bclam], writes=[bclam])

        with ExitStack() as s0:
            rst = ring("w0st", 2, [128, 4096], F32, s0)
            rsb = ring("w0sb", 2, [128, 4096], BF16, s0)
            k = 0
            for src, dst, n in ((w_fm, wfm, NFM // 4), (w_tm, wtm, NTM)):
                for b in range(n):
                    st, bst = rst.next()
                    sbf, bsbf = rsb.next()
                    if src is w_fm:
                        sap = src[4 * b:4 * b + 4].rearrange("n p f -> p n f")
                        dap = dst[4 * b:4 * b + 4].rearrange("n p f -> p n f")
                        tap_s = st[:].rearrange("p (n f) -> p n f", n=4)
                        tap_b = sbf[:].rearrange("p (n f) -> p n f", n=4)
                    else:
                        sap, dap, tap_s, tap_b = src[b], dst[b], st[:], sbf[:]
                    P.dma("sp", lambda e: e.dma_start(out=tap_s, in_=sap), writes=[bst])
                    ek = OPQ[k % 3]; k += 1
                    if ek == "act":
                        P.op(ek, lambda e: e.copy(out=sbf[:], in_=st[:]), reads=[bst], writes=[bsbf])
                    else:
                        P.op(ek, lambda e: e.tensor_copy(out=sbf[:], in_=st[:]), reads=[bst], writes=[bsbf])
                    P.dma("act", lambda e: e.dma_start(out=dap, in_=tap_b), reads=[bsbf])
            st, bst = rst.next()
            P.dma("sp", lambda e: e.dma_start(out=st[:], in_=w_rr[:, :]), writes=[bst])
            P.op("dve", lambda e: e.tensor_copy(out=wrr[:], in_=st[:]), reads=[bst], writes=[bwrr])

            c5t = sb("c5t", [1 + NS, D], F32, s0); bc5 = Buf()
            P.dma("sp", lambda e: e.dma_start(out=c5t[:], in_=c5[:, :]), writes=[bc5])
            P.op("act", lambda e: e.activation(out=c5t[:], in_=c5t[:], func=AF.Silu), reads=[bc5], writes=[bc5])
            for kc in range(8):
                P.op("pe", lambda e: e.transpose(out=psA[:, kc * 5:kc * 5 + 5], in_=c5t[:, kc * 128:(kc + 1) * 128], identity=identf[0:5, 0:5]),
                     reads=[bc5, bcst], writes=[bpsA])
            P.op("dve", lambda e: e.tensor_copy(out=silucT[:], in_=psA[:, 0:40]), reads=[bpsA], writes=[bsil])
            silrep = sb("silrep", [128, 8 * 128], BF16, s0); bsilrep = Buf()
            sil3 = silucT[:].rearrange("p (k t) -> p k t", t=5)
            P.op("dve", lambda e: e.tensor_copy(out=silrep[:].rearrange("p (k m) -> p k m", m=128),
                                                in_=sil3[:, :, 0:1].to_broadcast([128, 8, 128])), reads=[bsil], writes=[bsilrep])
            for blk in range(6):
                st, bst = rst.next()
                sbf, bsbf = rsb.next()
                P.dma("sp", lambda e: e.dma_start(out=st[:], in_=w_ada[blk]), writes=[bst])
                P.op(OPQ[blk % 3], (lambda e: e.copy(out=sbf[:], in_=st[:])) if blk % 3 == 0 else (lambda e: e.tensor_copy(out=sbf[:], in_=st[:])),
                     reads=[bst], writes=[bsbf])
                w3 = sbf[:].rearrange("p (k c) -> p k c", c=512)
                for q in range(4):
                    cc = blk * 4 + q
                    for kc in range(8):
                        P.op("pe", lambda e: e.matmul(psB[:, cc * 5:cc * 5 + 5], lhsT=w3[:, kc, q * 128:(q + 1) * 128], rhs=sil3[:, kc, :],
                                                      start=(kc == 0), stop=(kc == 7)), reads=[bsbf, bsil], writes=[bpsB])
                if blk >= 4:
                    hb = blk - 4
                    ps, bps = rPSS.next()
                    for kc in range(8):
                        P.op("pe", lambda e: e.matmul(ps[:, :], lhsT=silrep[:, kc * 128:(kc + 1) * 128], rhs=w3[:, kc, :],
                                                      start=(kc == 0), stop=(kc == 7)), reads=[bsbf, bsilrep], writes=[bps])
                    P.op("dve", lambda e: e.tensor_tensor(out=gate_bc[:, hb * 512:(hb + 1) * 512], in0=ps[:, :], in1=gate_bc[:, hb * 512:(hb + 1) * 512], op=ALU.add),
                         reads=[bps, bgate], writes=[bgate])
                    ps, bps = rPSS.next()
                    for kc in range(8):
                        P.op("pe", lambda e: e.matmul(ps[0:NS, :], lhsT=sil3[:, kc, 1:5], rhs=w3[:, kc, :],
                                                      start=(kc == 0), stop=(kc == 7)), reads=[bsbf, bsil], writes=[bps])
                    P.op("dve", lambda e: e.tensor_tensor(out=gate_s[:, hb * 512:(hb + 1) * 512], in0=ps[0:NS, :], in1=gate_s[:, hb * 512:(hb + 1) * 512], op=ALU.add),
                         reads=[bps, bgs], writes=[bgs])
            mT3 = mT[:].rearrange("p (c t) -> p c t", t=5)
            P.op("dve", lambda e: e.tensor_tensor(out=mT3, in0=psB[:, 0:120].rearrange("p (c t) -> p c t", t=5),
                                                  in1=chpt[:, CP_BADA:CP_BADA + 24].unsqueeze(2).to_broadcast([128, 24, 5]), op=ALU.add),
                 reads=[bpsB, bchp], writes=[bmT])
            P.op("dve", lambda e: e.scalar_tensor_tensor(out=A_p[:], in0=mT3[:, 8:16, 0], scalar=1.0, in1=chpt[:, CP_GN:CP_GN + 8], op0=ALU.add, op1=ALU.mult),
                 reads=[bmT, bchp], writes=[bAp])
            P.op("dve", lambda e: e.tensor_copy(out=B_p[:], in_=mT3[:, 0:8, 0]), reads=[bmT], writes=[bBp])
        P.barrier()


        def load_fm(idx):
            t, b = rFM.next()
            P.dma("sp", lambda e: e.dma_start(out=t[:], in_=wfm[idx]), writes=[b])
            return t[:].rearrange("p (k c) -> p k c", c=128), b

        def load_tm(idx):
            t, b = rTM.next()
            P.dma("sp", lambda e: e.dma_start(out=t[:], in_=wtm[idx]), writes=[b])
            return t[:].rearrange("p (k c) -> p k c", c=512), b

        def rope(ek2, out_ap, x_ap, cosf, sinf, H, Dh, t1, t2, reads, writes, bt1, bt2):
            hf = Dh // 2
            p = x_ap.shape[0]
            cb = cosf.unsqueeze(1).to_broadcast([p, H, Dh])
            s1 = sinf[:, 0:hf].unsqueeze(1).to_broadcast([p, H, hf])
            s2 = sinf[:, hf:Dh].unsqueeze(1).to_broadcast([p, H, hf])
            P.op("dve", lambda e: e.tensor_tensor(out=t1, in0=x_ap, in1=cb, op=ALU.mult), reads=reads, writes=[bt1])
            P.op("dve", lambda e: e.tensor_tensor(out=t2[:, :, 0:hf], in0=x_ap[:, :, hf:Dh], in1=s1, op=ALU.mult), reads=reads, writes=[bt2])
            P.op("dve", lambda e: e.tensor_tensor(out=t2[:, :, hf:Dh], in0=x_ap[:, :, 0:hf], in1=s2, op=ALU.mult), reads=reads, writes=[bt2])
            return P.op(ek2, lambda e: e.tensor_tensor(out=out_ap, in0=t1, in1=t2, op=ALU.add), reads=[bt1, bt2], writes=writes)

        if DO_SAMPLE:
            sample_phase(nc, P, locals())
            P.barrier()
        sS.close()

        if STOP != 'p0':
            prompt_phase(nc, P, locals())
        P.finish()
        print("ninst", P.ninst, {k: (len(P.sems[k]) - 1) * P.EPOCH + P.cnt[k] for k in P.cnt})
    return nc


def prompt_phase(nc, P, G):
    es = G["es"]; sb = G["sb"]; ring = G["ring"]
    x_p = G["x_p"]; y_p = G["y_p"]; nk_p = G["nk_p"]; nv_p = G["nv_p"]; nki_p = G["nki_p"]; ncv_p = G["ncv_p"]; nlr_p = G["nlr_p"]
    ropeq = G["ropeq"]; ropei = G["ropei"]
    rPS = G["rPS"]; rPSS = G["rPSS"]; rPST = G["rPST"]; rPACC = G["rPACC"]
    psO = G["psO"]; bpsO = G["bpsO"]; psL = G["psL"]; bpsL = G["bpsL"]
    cstt = G["cstt"]; bcst = G["bcst"]; chpt = G["chpt"]; bchp = G["bchp"]
    identb = G["identb"]; bidb = G["bidb"]; ident4 = G["ident4"]; bid4 = G["bid4"]
    onesb = G["onesb"]; bonesb = G["bonesb"]; identf = G["identf"]
    clam = G["clam"]; bclam = G["bclam"]
    gate_bc = G["gate_bc"]; bgate = G["bgate"]; gfin_bc = G["gfin_bc"]; bgfin = G["bgfin"]
    idxgb_bc = G["idxgb_bc"]; bidxgb = G["bidxgb"]
    wrr = G["wrr"]; bwrr = G["bwrr"]; A_p = G["A_p"]; bAp = G["bAp"]; B_p = G["B_p"]; bBp = G["bBp"]
    load_fm = G["load_fm"]; load_tm = G["load_tm"]; rope = G["rope"]
    caus = cstt[:, CS_CAUS:CS_CAUS + 128]
    wrr5 = wrr[:].rearrange("p (a n k c) -> p a n k c", a=2, n=4, k=2)

    KT = sb("KT", [128, 2 * SEQ], BF16); KT3 = KT[:].rearrange("p (g t) -> p g t", g=2)
    Vres = sb("Vres", [128, 32 * 256], BF16); V4 = Vres[:].rearrange("p (i g d) -> p i g d", i=32, g=2)
    kiT = sb("kiT", [128, SEQ], BF16)
    bKV = [Buf() for _ in range(32)]
    hist = sb("hist", [128, 8 * 3]); bhist = Buf(); hist3 = hist[:].rearrange("p (c j) -> p c j", j=3)
    hprev = sb("hprev", [128, 8]); bhprev = Buf()
    P.op("pool", lambda e: e.memset(hist[:], 0.0), writes=[bhist])
    P.op("pool", lambda e: e.memset(hprev[:], 0.0), writes=[bhprev])

    rX = ring("xt", 1, [128, D])
    rXh = ring("xh", 1, [128, D], BF16)
    xnT = sb("xnT", [128, 8 * CH], BF16); bxn = Buf(); xn3 = xnT[:].rearrange("p (k t) -> p k t", k=8)
    rq = sb("rq", [128, 4 * 256]); brq = Buf(); rq3 = rq[:].rearrange("p (t c) -> p t c", t=4)
    ri = sb("ri", [128, 4 * 128]); bri = Buf(); ri3 = ri[:].rearrange("p (t c) -> p t c", t=4)
    rQr = ring("qrot", 1, [128, 512], BF16)
    rKr = ring("krot", 1, [128, 256]); rVf = ring("vf", 1, [128, 256]); rKi = ring("kio", 1, [128, 64])
    rKb = ring("kb16", 1, [128, 256], BF16); rKi2 = ring("ki2", 1, [128, 128], BF16)
    qT = sb("qT", [128, 4 * 1024], BF16); bqT = [Buf() for _ in range(4)]; qT4 = qT[:].rearrange("p (t h q) -> p t h q", t=4, h=8)
    qiT = sb("qiT", [128, 4 * 512], BF16); bqiT = [Buf() for _ in range(4)]; qiT4 = qiT[:].rearrange("p (t h q) -> p t h q", t=4, h=4)
    wS = sb("wS", [128, 4 * 8]); bwS = [Buf() for _ in range(4)]; wS3 = wS[:].rearrange("p (t h) -> p t h", t=4)
    awS = sb("awS", [128, 4 * 8]); awS3 = awS[:].rearrange("p (t h) -> p t h", t=4)
    sgS = sb("sgS", [128, 4 * 8]); sgS3 = sgS[:].rearrange("p (t h) -> p t h", t=4)
    rDg = ring("diagS", 2, [128, 8 * 128], BF16)
    Wt = sb("bs_W", [128, 32]); bWt = Buf()
    W2t = sb("bs_W2", [128, 32]); bW2t = Buf()
    pow2 = cstt[:, CS_P2:CS_P2 + N_BISECT]
    sm = sb("smallst", [128, 16]); bsm = Buf()
    xa = sb("xa", [128, 2 * 515]); bxa = Buf(); xa3 = xa[:].rearrange("p (c t) -> p c t", c=2)
    xc = sb("xc", [128, 2 * CH]); bxc = Buf(); xc3 = xc[:].rearrange("p (c t) -> p c t", c=2)
    xcb = sb("xcb", [128, 2 * CH], BF16); bxcb = Buf(); xcb3 = xcb[:].rearrange("p (c t) -> p c t", c=2)
    junkb = xcb; bjunk = bxcb
    g_r = sb("g_r", [128, CH]); b_r = Buf()
    g_i = sb("g_i", [128, CH]); b_i = Buf()
    g_a = sb("g_a", [128, CH]); b_a = Buf()
    g_t = sb("g_t", [128, CH]); b_t = Buf()
    g_u = g_i; b_u = b_i
    g_h = sb("g_h", [128, CH]); b_h = Buf()
    rT1 = Ring([(g_a, b_a)]); rT2 = Ring([(g_t, b_t)])
    g_sg = g_r; b_sg = b_r
    actT = sb("actT", [128, 8 * CH], BF16); bact = [Buf() for _ in range(8)]; act3 = actT[:].rearrange("p (k t) -> p k t", k=8)
    mTt = sb("mTt", [128, 8 * CH], BF16); bmm = [Buf() for _ in range(8)]; m3 = mTt[:].rearrange("p (k t) -> p k t", k=8)
    rSg = Ring([(g_r, b_r), (g_i, b_i)])
    sc = sb("sc", [128, SEQ]); bsc = Buf()
    rMB = ring("MB", 2, [128, SEQ], BF16)
    junk8 = sb("junk8", [128, SEQ], mybir.dt.float8e4); bj8 = Buf()
    rR = ring("Rr", 2, [128, 512], BF16)
    rPT = ring("PTr", 2, [128, 512], BF16)
    rl = g_h; brl = b_h
    hres = sb("hres", [128, D]); bhres = Buf()
    yo = hres; byo = bhres
    lo = sb("bs_lo", [128, 1]); blo = Buf()
    wd = sb("bs_w", [128, 1]); bwd = Buf()
    mid = sb("bs_mid", [128, 1]); bmid = Buf()
    cnt = sb("bs_cnt", [128, 1]); bcnt = Buf()
    cond = sb("bs_cond", [128, 1]); bcond = Buf()
    otr = hres; botr = bhres

    for c in range(NCHUNK_RUN):
        t0 = c * CH
        P.dma("sp", lambda e: e.dma_start(out=rq3, in_=ropeq[t0:t0 + CH, :].rearrange("(t p) c -> p t c", p=128)), writes=[brq])
        P.dma("sp", lambda e: e.dma_start(out=ri3, in_=ropei[t0:t0 + CH, :].rearrange("(t p) c -> p t c", p=128)), writes=[bri])
        for tt in range(4):
            xt, bxt = rX.next()
            xh, bxh = rXh.next()
            P.dma("sp", lambda e: e.dma_start(out=xt[:], in_=x_p[t0 + tt * 128:t0 + (tt + 1) * 128, :]), writes=[bxt])
            P.op("act", lambda e: e.activation(out=junkb[:], in_=xt[:], func=AF.Square, accum_out=sm[:, 0:1]), reads=[bxt], writes=[bjunk, bsm])
            P.op("act", lambda e: e.activation(out=sm[:, 1:2], in_=sm[:, 0:1], func=AF.Sqrt, scale=1.0 / D, bias=EPS), reads=[bsm], writes=[bsm])
            P.op("dve", lambda e: e.reciprocal(out=sm[:, 2:3], in_=sm[:, 1:2]), reads=[bsm], writes=[bsm])
            P.op("dve", lambda e: e.tensor_scalar(out=xh[:], in0=xt[:], scalar1=sm[:, 2:3], scalar2=None, op0=ALU.mult), reads=[bxt, bsm], writes=[bxh])
            pt, bpt = rPST.next()
            for kc in range(8):
                P.op("pe", lambda e: e.transpose(out=pt[:, kc * 128:(kc + 1) * 128], in_=xh[:, kc * 128:(kc + 1) * 128], identity=identb[:]),
                     reads=[bxh, bidb], writes=[bpt])
            for kc in range(8):
                P.op("dve", lambda e: e.tensor_scalar(out=xn3[:, kc, tt * 128:(tt + 1) * 128], in0=pt[:, kc * 128:(kc + 1) * 128],
                                                      scalar1=A_p[:, kc:kc + 1], scalar2=B_p[:, kc:kc + 1], op0=ALU.mult, op1=ALU.add),
                     reads=[bpt, bAp, bBp], writes=[bxn])

        if STOP == 'p1':
            continue
        for blk in (TM_Q0, TM_Q1, TM_KV, TM_QI, TM_KW):
            w3, bw = load_tm(blk)
            ncol = 72 if blk == TM_KW else 512
            for tt in range(4):
                i = c * 4 + tt
                r0 = t0 + tt * 128
                ps, bps = rPS.next()
                for kc in range(8):
                    P.op("pe", lambda e: e.matmul(ps[:, 0:ncol], lhsT=xn3[:, kc, tt * 128:(tt + 1) * 128], rhs=w3[:, kc, 0:ncol],
                                                  start=(kc == 0), stop=(kc == 7)), reads=[bxn, bw], writes=[bps])
                t1, bt1 = rT1.next(); t2, bt2 = rT2.next()
                if blk in (TM_Q0, TM_Q1):
                    qr, bqr = rQr.next()
                    rope("pool", qr[:].rearrange("p (h d) -> p h d", h=4), ps[:, :].rearrange("p (h d) -> p h d", h=4),
                         rq3[:, tt, 0:128], rq3[:, tt, 128:256], 4, 128,
                         t1[:].rearrange("p (h d) -> p h d", h=4), t2[:].rearrange("p (h d) -> p h d", h=4),
                         [bps, brq], [bqr], bt1, bt2)
                    pt, bpt = rPST.next()
                    for h in range(4):
                        P.op("pe", lambda e: e.transpose(out=pt[:, h * 128:(h + 1) * 128], in_=qr[:, h * 128:(h + 1) * 128], identity=identb[:]),
                             reads=[bqr, bidb], writes=[bpt])
                    h0 = 4 * (blk - TM_Q0)
                    P.op("act", lambda e: e.copy(out=qT4[:, tt, h0:h0 + 4, :], in_=pt[:, 0:512].rearrange("p (h q) -> p h q", h=4)),
                         reads=[bpt], writes=[bqT[tt]])
                elif blk == TM_KV:
                    kr, bkr = rKr.next(); vf, bvf = rVf.next(); kb, bkb = rKb.next()
                    rope("pool", kr[:].rearrange("p (h d) -> p h d", h=2), ps[:, 0:256].rearrange("p (h d) -> p h d", h=2),
                         rq3[:, tt, 0:128], rq3[:, tt, 128:256], 2, 128,
                         t1[:, 0:256].rearrange("p (h d) -> p h d", h=2), t2[:, 0:256].rearrange("p (h d) -> p h d", h=2),
                         [bps, brq], [bkr], bt1, bt2)
                    P.dma("act", lambda e: e.dma_start(out=nk_p[r0:r0 + 128, :], in_=kr[:]), reads=[bkr], is_out=True)
                    P.op("act", lambda e: e.copy(out=vf[:], in_=ps[:, 256:512]), reads=[bps], writes=[bvf])
                    P.dma("act", lambda e: e.dma_start(out=nv_p[r0:r0 + 128, :], in_=vf[:]), reads=[bvf], is_out=True)
                    P.op("act", lambda e: e.copy(out=V4[:, i, :, :], in_=ps[:, 256:512].rearrange("p (g d) -> p g d", g=2)), reads=[bps], writes=[bKV[i]])
                    P.op("pool", lambda e: e.tensor_copy(out=kb[:], in_=kr[:]), reads=[bkr], writes=[bkb])
                    pt, bpt = rPST.next()
                    for g in range(2):
                        P.op("pe", lambda e: e.transpose(out=pt[:, g * 128:(g + 1) * 128], in_=kb[:, g * 128:(g + 1) * 128], identity=identb[:]),
                             reads=[bkb, bidb], writes=[bpt])
                    P.op("act", lambda e: e.copy(out=KT3[:, :, r0:r0 + 128], in_=pt[:, 0:256].rearrange("p (g q) -> p g q", g=2)),
                         reads=[bpt], writes=[bKV[i]])
                elif blk == TM_QI:
                    qr, bqr = rQr.next()
                    rope("pool", qr[:].rearrange("p (h d) -> p h d", h=8), ps[:, :].rearrange("p (h d) -> p h d", h=8),
                         ri3[:, tt, 0:64], ri3[:, tt, 64:128], 8, 64,
                         t1[:].rearrange("p (h d) -> p h d", h=8), t2[:].rearrange("p (h d) -> p h d", h=8),
                         [bps, bri], [bqr], bt1, bt2)
                    pt, bpt = rPST.next()
                    for hp in range(4):
                        P.op("pe", lambda e: e.transpose(out=pt[:, hp * 128:(hp + 1) * 128], in_=qr[:, hp * 128:(hp + 1) * 128], identity=identb[:]),
                             reads=[bqr, bidb], writes=[bpt])
                    P.op("act", lambda e: e.copy(out=qiT4[:, tt, :, :], in_=pt[:, 0:512].rearrange("p (h q) -> p h q", h=4)),
                         reads=[bpt], writes=[bqiT[tt]])
                else:
                    kio, bkio = rKi.next(); ki2, bki2 = rKi2.next()
                    P.op("dve", lambda e: e.tensor_scalar(out=wS3[:, tt, :], in0=ps[:, 64:72], scalar1=IDX_W_SCALE, scalar2=None, op0=ALU.mult),
                         reads=[bps], writes=[bwS[tt]])
                    P.op("dve", lambda e: e.tensor_scalar(out=sgS3[:, tt, :], in0=wS3[:, tt, :], scalar1=0.0, scalar2=0.5, op0=ALU.is_ge, op1=ALU.subtract),
                         reads=[bwS[tt]], writes=[bwS[tt]])
                    P.op("dve", lambda e: e.scalar_tensor_tensor(out=awS3[:, tt, :], in0=wS3[:, tt, :], scalar=4.0, in1=sgS3[:, tt, :], op0=ALU.mult, op1=ALU.mult),
                         reads=[bwS[tt]], writes=[bwS[tt]])
                    P.op("dve", lambda e: e.tensor_reduce(out=sm[:, 4:5], in_=ps[:, 0:64], axis=AX.X, op=ALU.add), reads=[bps], writes=[bsm])
                    P.op("dve", lambda e: e.tensor_scalar(out=sm[:, 5:6], in0=sm[:, 4:5], scalar1=-1.0 / 64, scalar2=None, op0=ALU.mult), reads=[bsm], writes=[bsm])
                    P.op("dve", lambda e: e.tensor_scalar(out=t1[:, 0:64], in0=ps[:, 0:64], scalar1=sm[:, 5:6], scalar2=None, op0=ALU.add),
                         reads=[bps, bsm], writes=[bt1])
                    P.op("act", lambda e: e.activation(out=t2[:, 0:64], in_=t1[:, 0:64], func=AF.Square, accum_out=sm[:, 6:7]), reads=[bt1], writes=[bt2, bsm])
                    P.op("act", lambda e: e.activation(out=sm[:, 7:8], in_=sm[:, 6:7], func=AF.Sqrt, scale=1.0 / 64, bias=EPS), reads=[bsm], writes=[bsm])
                    P.op("dve", lambda e: e.reciprocal(out=sm[:, 8:9], in_=sm[:, 7:8]), reads=[bsm], writes=[bsm])
                    P.op("dve", lambda e: e.scalar_tensor_tensor(out=t1[:, 64:128], in0=t1[:, 0:64], scalar=sm[:, 8:9], in1=idxgb_bc[:, 0:64], op0=ALU.mult, op1=ALU.mult),
                         reads=[bt1, bsm, bidxgb], writes=[bt1])
                    P.op("dve", lambda e: e.tensor_tensor(out=t1[:, 128:192], in0=t1[:, 64:128], in1=idxgb_bc[:, 64:128], op=ALU.add),
                         reads=[bt1, bidxgb], writes=[bt1])
                    rope("pool", kio[:].unsqueeze(1), t1[:, 128:192].unsqueeze(1), ri3[:, tt, 0:64], ri3[:, tt, 64:128], 1, 64,
                         t2[:, 64:128].unsqueeze(1), t2[:, 128:192].unsqueeze(1), [bt1, bri], [bkio], bt2, bt2)
                    P.dma("act", lambda e: e.dma_start(out=nki_p[r0:r0 + 128, :], in_=kio[:]), reads=[bkio], is_out=True)
                    P.op("pool", lambda e: e.tensor_copy(out=ki2[:].rearrange("p (a d) -> p a d", a=2), in_=kio[:].unsqueeze(1).to_broadcast([128, 2, 64])),
                         reads=[bkio], writes=[bki2])
                    pt, bpt = rPST.next()
                    P.op("pe", lambda e: e.transpose(out=pt[:, 0:128], in_=ki2[:], identity=identb[:]), reads=[bki2, bidb], writes=[bpt])
                    P.op("act", lambda e: e.copy(out=kiT[:, r0:r0 + 128], in_=pt[:, 0:128]), reads=[bpt], writes=[bKV[i]])

        if STOP == 'p2':
            continue
        for n in range(4):
            for c2 in range(2):
                cc = 2 * n + c2
                w3, bw = load_fm(FM_XA + cc)
                ps, bps = rPS.next()
                for kc in range(8):
                    P.op("pe", lambda e: e.matmul(ps[:, :], lhsT=w3[:, kc, :], rhs=xn3[:, kc, :], start=(kc == 0), stop=(kc == 7)),
                         reads=[bw, bxn], writes=[bps])
                P.op("pool", lambda e: e.tensor_copy(out=xa3[:, c2, 0:3], in_=hist3[:, cc, :]), reads=[bhist], writes=[bxa])
                P.op("act", lambda e: e.copy(out=xa3[:, c2, 3:515], in_=ps[:, :]), reads=[bps], writes=[bxa])
                P.op("pool", lambda e: e.tensor_copy(out=hist3[:, cc, :], in_=xa3[:, c2, 512:515]), reads=[bxa], writes=[bhist])
                wc = lambda j: chpt[:, CP_WCONV + j * 8 + cc:CP_WCONV + j * 8 + cc + 1]
                P.op("dve", lambda e: e.tensor_scalar(out=xc3[:, c2, :], in0=xa3[:, c2, 3:515], scalar1=wc(3), scalar2=chpt[:, CP_BCONV + cc:CP_BCONV + cc + 1],
                                                      op0=ALU.mult, op1=ALU.add), reads=[bxa, bchp], writes=[bxc])
                for j in range(3):
                    P.op("dve", lambda e: e.scalar_tensor_tensor(out=xc3[:, c2, :], in0=xa3[:, c2, j:j + 512], scalar=wc(j), in1=xc3[:, c2, :],
                                                                 op0=ALU.mult, op1=ALU.add), reads=[bxa, bxc, bchp], writes=[bxc])
                P.op("pool", lambda e: e.tensor_copy(out=xcb3[:, c2, :], in_=xc3[:, c2, :]), reads=[bxc], writes=[bxcb])
            for c2 in range(2):
                cc = 2 * n + c2
                for which, dst, bdst, bias0 in ((0, g_r, b_r, CP_BRA), (1, g_i, b_i, CP_BRX)):
                    ps, bps = rPS.next()
                    for k2 in range(2):
                        P.op("pe", lambda e: e.matmul(ps[:, :], lhsT=wrr5[:, which, n, k2, c2 * 128:(c2 + 1) * 128], rhs=xcb3[:, k2, :],
                                                      start=(k2 == 0), stop=(k2 == 1)), reads=[bwrr, bxcb], writes=[bps])
                    P.op("act", lambda e: e.activation(out=dst[:], in_=ps[:, :], func=AF.Sigmoid, bias=chpt[:, bias0 + cc:bias0 + cc + 1]),
                         reads=[bps, bchp], writes=[bdst])
                P.op("act", lambda e: e.activation(out=g_a[:], in_=g_r[:], func=AF.Exp, scale=clam[:, cc:cc + 1]), reads=[b_r, bclam], writes=[b_a])
                P.op("pool", lambda e: e.tensor_tensor(out=g_t[:], in0=g_a[:], in1=g_a[:], op=ALU.mult), reads=[b_a], writes=[b_t])
                P.op("pool", lambda e: e.tensor_scalar(out=g_t[:], in0=g_t[:], scalar1=-1.0, scalar2=1.0, op0=ALU.mult, op1=ALU.add), reads=[b_t], writes=[b_t])
                P.op("act", lambda e: e.activation(out=g_t[:], in_=g_t[:], func=AF.Sqrt), reads=[b_t], writes=[b_t])
                P.op("pool", lambda e: e.tensor_tensor(out=g_u[:], in0=g_i[:], in1=xc3[:, c2, :], op=ALU.mult), reads=[b_i, bxc], writes=[b_u])
                P.op("pool", lambda e: e.tensor_tensor(out=g_u[:], in0=g_u[:], in1=g_t[:], op=ALU.mult), reads=[b_u, b_t], writes=[b_u])
                P.op("dve", lambda e: e.tensor_tensor_scan(out=g_h[:], data0=g_a[:], data1=g_u[:], initial=hprev[:, cc:cc + 1], op0=ALU.mult, op1=ALU.add),
                     reads=[b_a, b_u, bhprev], writes=[b_h])
                P.op("pool", lambda e: e.tensor_copy(out=hprev[:, cc:cc + 1], in_=g_h[:, CH - 1:CH]), reads=[b_h], writes=[bhprev])
                w3, bw = load_fm(FM_GA + cc)
                ps, bps = rPS.next()
                for kc in range(8):
                    P.op("pe", lambda e: e.matmul(ps[:, :], lhsT=w3[:, kc, :], rhs=xn3[:, kc, :], start=(kc == 0), stop=(kc == 7)),
                         reads=[bw, bxn], writes=[bps])
                P.op("act", lambda e: e.activation(out=g_sg[:], in_=ps[:, :], func=AF.Silu), reads=[bps], writes=[b_sg])
                P.op("pool", lambda e: e.tensor_tensor(out=act3[:, cc, :], in0=g_h[:], in1=g_sg[:], op=ALU.mult), reads=[b_h, b_sg], writes=[bact[cc]])
        if c == NCHUNK_RUN - 1:
            for src3, nrow, dst in ((hist3, 3, ncv_p), (hprev[:].unsqueeze(2), 1, nlr_p)):
                for hb in range(2):
                    ps, bps = rPS.next()
                    for c4 in range(4):
                        cc = hb * 4 + c4
                        P.op("pe", lambda e: e.transpose(out=ps[0:nrow, c4 * 128:(c4 + 1) * 128], in_=src3[:, cc, :], identity=identf),
                             reads=[bhist, bhprev, bcst], writes=[bps])
                    P.op("act", lambda e: e.copy(out=otr[0:nrow, hb * 512:(hb + 1) * 512], in_=ps[0:nrow, :]), reads=[bps], writes=[botr])
                P.dma("act", lambda e: e.dma_start(out=dst[:, :], in_=otr[0:nrow, :]), reads=[botr], is_out=True)
        if STOP == 'p3':
            continue
        for cc in range(8):
            w3, bw = load_fm(FM_MA + cc)
            ps, bps = rPS.next()
            for kc in range(8):
                P.op("pe", lambda e: e.matmul(ps[:, :], lhsT=w3[:, kc, :], rhs=xn3[:, kc, :], start=(kc == 0), stop=(kc == 7)),
                     reads=[bw, bxn], writes=[bps])
            sg, bsg = rSg.next()
            P.op("act", lambda e: e.activation(out=sg[:], in_=ps[:, :], func=AF.Sigmoid), reads=[bps], writes=[bsg])
            w3, bw = load_fm(FM_PA + cc)
            ps, bps = rPS.next()
            for kc in range(8):
                P.op("pe", lambda e: e.matmul(ps[:, :], lhsT=w3[:, kc, :], rhs=act3[:, kc, :], start=(kc == 0), stop=(kc == 7)),
                     reads=[bw, bact[kc]], writes=[bps])
            P.op("dve", lambda e: e.tensor_tensor(out=m3[:, cc, :], in0=ps[:, :], in1=sg[:], op=ALU.mult), reads=[bps, bsg], writes=[bmm[cc]])

        if STOP == 'p4':
            continue
        def stage_A(tt):
            i = c * 4 + tt
            nk = (i + 1) * 128
            dg, bdg = rDg.next()
            dg3 = dg[:].rearrange("p (h q) -> p h q", h=8)
            P.op("pool", lambda e: e.tensor_tensor(out=dg3, in0=identb[:].unsqueeze(1).to_broadcast([128, 8, 128]),
                                                   in1=sgS3[:, tt, :].unsqueeze(2).to_broadcast([128, 8, 128]), op=ALU.mult),
                 reads=[bidb, bwS[tt]], writes=[bdg])
            for kb in range((nk + 511) // 512):
                k0 = kb * 512
                cols = min(512, nk - k0)
                pacc, bpacc = rPACC.next()
                for h in range(8):
                    hp, h2 = h // 2, h % 2
                    ps, bps = rPS.next()
                    P.op("pe", lambda e: e.matmul(ps[:, 0:cols], lhsT=qiT4[64 * h2:64 * h2 + 64, tt, hp, :], rhs=kiT[64 * h2:64 * h2 + 64, k0:k0 + cols],
                                                  start=True, stop=True), reads=[bqiT[tt]] + bKV[k0 // 128:(k0 + cols) // 128], writes=[bps])
                    R, bR = rR.next()
                    P.op("act", lambda e: e.activation(out=R[:, 0:cols], in_=ps[:, 0:cols], func=AF.Relu, scale=awS3[:, tt, h:h + 1]),
                         reads=[bps, bwS[tt]], writes=[bR])
                    P.op("pe", lambda e: e.matmul(pacc[:, 0:cols], lhsT=dg3[:, h, :], rhs=R[:, 0:cols], start=(h == 0), stop=(h == 7)),
                         reads=[bdg, bR], writes=[bpacc])
                P.op("act", lambda e: e.copy(out=sc[:, k0:k0 + cols], in_=pacc[:, 0:cols]), reads=[bpacc], writes=[bsc])
            P.op("dve", lambda e: e.tensor_tensor(out=sc[:, i * 128:nk], in0=sc[:, i * 128:nk], in1=caus, op=ALU.add), reads=[bsc, bcst], writes=[bsc])

        def stage_B(tt):
            i = c * 4 + tt
            nk = (i + 1) * 128
            MB, bMB = rMB.next()
            NB = N_BISECT
            if i >= 2:
                P.op("dve", lambda e: e.tensor_reduce(out=mid[:], in_=sc[:, 0:nk], axis=AX.X, op=ALU.max), reads=[bsc], writes=[bmid])
                P.op("dve", lambda e: e.tensor_reduce(out=lo[:], in_=sc[:, 0:i * 128], axis=AX.X, op=ALU.min), reads=[bsc], writes=[blo])
                P.op("dve", lambda e: e.tensor_tensor(out=wd[:], in0=mid[:], in1=lo[:], op=ALU.subtract), reads=[bmid, blo], writes=[bwd])
                P.op("dve", lambda e: e.tensor_scalar(out=Wt[:, 0:NB], in0=pow2, scalar1=wd[:], scalar2=None, op0=ALU.mult), reads=[bcst, bwd], writes=[bWt])
                P.op("dve", lambda e: e.tensor_scalar(out=W2t[:, 0:NB], in0=pow2, scalar1=wd[:], scalar2=2.0, op0=ALU.mult, op1=ALU.mult), reads=[bcst, bwd], writes=[bW2t])
                P.op("dve", lambda e: e.tensor_tensor(out=mid[:], in0=lo[:], in1=Wt[:, 0:1], op=ALU.add), reads=[blo, bWt], writes=[bmid])
                for k in range(NB):
                    P.op("dve", lambda e: e.tensor_scalar(out=junk8[:, 0:nk], in0=sc[:, 0:nk], scalar1=mid[:], scalar2=None, op0=ALU.is_ge, op1=ALU.add,
                                                          accum_out=cnt[:], saturate=False), reads=[bsc, bmid], writes=[bj8, bcnt])
                    if k < NB - 1:
                        P.op("dve", lambda e: e.scalar_tensor_tensor(out=cond[:], in0=cnt[:], scalar=float(TOPK) - 0.5, in1=W2t[:, k + 1:k + 2], op0=ALU.is_ge, op1=ALU.mult),
                             reads=[bcnt, bW2t], writes=[bcond])
                        P.op("dve", lambda e: e.scalar_tensor_tensor(out=mid[:], in0=cond[:], scalar=Wt[:, k + 1:k + 2], in1=mid[:], op0=ALU.subtract, op1=ALU.add),
                             reads=[bcond, bWt, bmid], writes=[bmid])
                    else:
                        P.op("dve", lambda e: e.scalar_tensor_tensor(out=cond[:], in0=cnt[:], scalar=float(TOPK) - 0.5, in1=Wt[:, k:k + 1], op0=ALU.is_ge, op1=ALU.mult),
                             reads=[bcnt, bWt], writes=[bcond])
                        P.op("dve", lambda e: e.scalar_tensor_tensor(out=lo[:], in0=cond[:], scalar=Wt[:, k:k + 1], in1=mid[:], op0=ALU.subtract, op1=ALU.add),
                             reads=[bcond, bWt, bmid], writes=[blo])
                P.op("dve", lambda e: e.tensor_scalar(out=MB[:, 0:nk], in0=sc[:, 0:nk], scalar1=lo[:], scalar2=NEG, op0=ALU.is_lt, op1=ALU.mult),
                     reads=[bsc, blo], writes=[bMB])
            else:
                P.op("dve", lambda e: e.tensor_scalar(out=MB[:, 0:nk], in0=sc[:, 0:nk], scalar1=-1e29, scalar2=NEG, op0=ALU.is_lt, op1=ALU.mult),
                     reads=[bsc], writes=[bMB])
            return MB, bMB

        def stage_C(tt, MB, bMB):
            i = c * 4 + tt
            for g in range(2):
                for j in range(i + 1):
                    ps, bps = rPSS.next()
                    P.op("pe", lambda e: e.matmul(ps[:, :], lhsT=KT3[:, g, j * 128:(j + 1) * 128], rhs=qT4[:, tt, 4 * g:4 * g + 4, :],
                                                  start=True, stop=False), reads=[bKV[j], bqT[tt]], writes=[bps])
                    P.op("pe", lambda e: e.matmul(ps[:, :], lhsT=MB[:, j * 128:(j + 1) * 128], rhs=ident4[:], start=False, stop=True),
                         reads=[bMB, bid4], writes=[bps])
                    PT, bPT = rPT.next()
                    P.op("act", lambda e: e.activation(out=PT[:], in_=ps[:, :], func=AF.Exp, scale=ATT_SCALE), reads=[bps], writes=[bPT])
                    P.op("pe", lambda e: e.matmul(psO[:, :], lhsT=V4[:, j, g, :], rhs=PT[:], start=(j == 0), stop=(j == i)), reads=[bKV[j], bPT], writes=[bpsO])
                    P.op("pe", lambda e: e.matmul(psL[:, :], lhsT=onesb[:], rhs=PT[:], start=(j == 0), stop=(j == i)), reads=[bonesb, bPT], writes=[bpsL])
                P.op("dve", lambda e: e.reciprocal(out=rl[:], in_=psL[:, :]), reads=[bpsL], writes=[brl])
                P.op("dve", lambda e: e.tensor_tensor(out=act3[:, 4 * g:4 * g + 4, tt * 128:(tt + 1) * 128], in0=psO[:, :].rearrange("p (h q) -> p h q", h=4),
                                                      in1=rl[:].rearrange("p (h q) -> p h q", h=4), op=ALU.mult),
                     reads=[bpsO, brl], writes=[bact[4 * g + hh] for hh in range(4)])

        pend = None
        for tt in range(4):
            stage_A(tt)
            mb = stage_B(tt)
            if pend is not None:
                stage_C(*pend)
            pend = (tt, mb[0], mb[1])
        stage_C(*pend)
        for cc in range(8):
            w3, bw = load_fm(FM_GB + cc)
            ps, bps = rPS.next()
            for kc in range(8):
                P.op("pe", lambda e: e.matmul(ps[:, :], lhsT=w3[:, kc, :], rhs=xn3[:, kc, :], start=(kc == 0), stop=(kc == 7)),
                     reads=[bw, bxn], writes=[bps])
            sg, bsg = rSg.next()
            P.op("act", lambda e: e.activation(out=sg[:], in_=ps[:, :], func=AF.Silu), reads=[bps], writes=[bsg])
            P.op("pool", lambda e: e.tensor_tensor(out=act3[:, cc, :], in0=act3[:, cc, :], in1=sg[:], op=ALU.mult), reads=[bact[cc], bsg], writes=[bact[cc]])
        if STOP == 'p5':
            continue
        for cc in range(8):
            w3, bw = load_fm(FM_MB + cc)
            ps, bps = rPS.next()
            for kc in range(8):
                P.op("pe", lambda e: e.matmul(ps[:, :], lhsT=w3[:, kc, :], rhs=xn3[:, kc, :], start=(kc == 0), stop=(kc == 7)),
                     reads=[bw, bxn], writes=[bps])
            sg, bsg = rSg.next()
            P.op("act", lambda e: e.activation(out=sg[:], in_=ps[:, :], func=AF.Sigmoid), reads=[bps], writes=[bsg])
            w3, bw = load_fm(FM_PB + cc)
            ps, bps = rPS.next()
            for kc in range(8):
                P.op("pe", lambda e: e.matmul(ps[:, :], lhsT=w3[:, kc, :], rhs=act3[:, kc, :], start=(kc == 0), stop=(kc == 7)),
                     reads=[bw, bact[kc]], writes=[bps])
            P.op("dve", lambda e: e.tensor_tensor(out=sg[:], in0=ps[:, :], in1=sg[:], op=ALU.mult), reads=[bps, bsg], writes=[bsg])
            P.op("pool", lambda e: e.tensor_tensor(out=m3[:, cc, :], in0=m3[:, cc, :], in1=sg[:], op=ALU.add), reads=[bmm[cc], bsg], writes=[bmm[cc]])
        if STOP == 'p6':
            continue
        wo0, bwo0 = load_tm(TM_O0)
        wo1, bwo1 = load_tm(TM_O1)
        for tt in range(4):
            r0 = t0 + tt * 128
            xt, bxt = rX.next()
            P.dma("sp", lambda e: e.dma_start(out=xt[:], in_=x_p[r0:r0 + 128, :]), writes=[bxt])
            for hb, (wo, bwo) in enumerate(((wo0, bwo0), (wo1, bwo1))):
                ps, bps = rPS.next()
                for kc in range(8):
                    P.op("pe", lambda e: e.matmul(ps[:, :], lhsT=m3[:, kc, tt * 128:(tt + 1) * 128], rhs=wo[:, kc, :], start=(kc == 0), stop=(kc == 7)),
                         reads=[bmm[kc], bwo], writes=[bps])
                P.op("dve", lambda e: e.tensor_tensor(out=hres[:, hb * 512:(hb + 1) * 512], in0=ps[:, :], in1=gate_bc[:, hb * 512:(hb + 1) * 512], op=ALU.mult),
                     reads=[bps, bgate], writes=[bhres])
            P.op("pool", lambda e: e.tensor_tensor(out=hres[:], in0=hres[:], in1=xt[:], op=ALU.add), reads=[bhres, bxt], writes=[bhres])
            P.op("act", lambda e: e.activation(out=junkb[:], in_=hres[:], func=AF.Square, accum_out=sm[:, 10:11]), reads=[bhres], writes=[bjunk, bsm])
            P.op("act", lambda e: e.activation(out=sm[:, 11:12], in_=sm[:, 10:11], func=AF.Sqrt, scale=1.0 / D, bias=EPS), reads=[bsm], writes=[bsm])
            P.op("dve", lambda e: e.reciprocal(out=sm[:, 12:13], in_=sm[:, 11:12]), reads=[bsm], writes=[bsm])
            P.op("dve", lambda e: e.scalar_tensor_tensor(out=yo[:], in0=hres[:], scalar=sm[:, 12:13], in1=gfin_bc[:], op0=ALU.mult, op1=ALU.mult),
                 reads=[bhres, bsm, bgfin], writes=[byo])
            P.dma("act", lambda e: e.dma_start(out=y_p[r0:r0 + 128, :], in_=yo[:]), reads=[byo], is_out=True)


def sample_phase(nc, P, G):
    sS = G["sS"]
    sb = lambda n, shp, d=F32: G["sb"](n, shp, d, sS)
    x_s = G["x_s"]; st_conv = G["st_conv"]; st_lru = G["st_lru"]; ptab = G["ptab"]
    cache_k = G["cache_k"]; cache_v = G["cache_v"]; cache_i = G["cache_i"]; ropes = G["ropes"]
    y_s = G["y_s"]; nk_s = G["nk_s"]; nv_s = G["nv_s"]; nki_s = G["nki_s"]; ncv_s = G["ncv_s"]; nlr_s = G["nlr_s"]
    rPS = G["rPS"]; rPST = G["rPST"]
    psA = G["psA"]; bpsA = G["bpsA"]; psB = G["psB"]; bpsB = G["bpsB"]
    psS0 = G["psS0"]; bpsS0 = G["bpsS0"]; psS1 = G["psS1"]; bpsS1 = G["bpsS1"]
    psO = G["psO"]; bpsO = G["bpsO"]; psL = G["psL"]; bpsL = G["bpsL"]
    cstt = G["cstt"]; bcst = G["bcst"]; chpt = G["chpt"]; bchp = G["bchp"]
    identb = G["identb"]; bidb = G["bidb"]; identf = G["identf"]
    onesb = G["onesb"]; bonesb = G["bonesb"]; onesf = G["onesf"]; bonesf = G["bonesf"]
    clam = G["clam"]; bclam = G["bclam"]; gfin_bc = G["gfin_bc"]; bgfin = G["bgfin"]
    idxgb_bc = G["idxgb_bc"]; bidxgb = G["bidxgb"]; wrr = G["wrr"]; bwrr = G["bwrr"]
    mT = G["mT"]; bmT = G["bmT"]; gate_s = G["gate_s"]; bgs = G["bgs"]
    load_fm = G["load_fm"]; load_tm = G["load_tm"]; rope = G["rope"]
    T = NS
    I4 = identf[0:T, 0:T]
    mT3 = mT[:].rearrange("p (c t) -> p c t", t=5)
    wrr5 = wrr[:].rearrange("p (a n k c) -> p a n k c", a=2, n=4, k=2)
    Jf = cstt[:, CS_JF:CS_JF + 256]; iota_r = cstt[:, CS_IR:CS_IR + 128]; Tstrict = cstt[:, CS_TS:CS_TS + 128]

    def bc84(col0):
        return chpt[:, col0:col0 + 8].unsqueeze(2).to_broadcast([128, 8, T])

    def tt(ek, out, a, b, op, reads, writes):
        return P.op(ek, lambda e: e.tensor_tensor(out=out, in0=a, in1=b, op=op), reads=reads, writes=writes)

    def fm_to_tm(src3, ncc, nrow_in_free, dst_tile, bdst, bsrc):
        for hb in range(2):
            ps, bps = rPS.next()
            for c4 in range(4):
                cc = hb * 4 + c4
                P.op("pe", lambda e: e.transpose(out=ps[0:nrow_in_free, c4 * 128:(c4 + 1) * 128], in_=src3[:, cc, :], identity=identf),
                     reads=[bsrc, bcst], writes=[bps])
            P.op("act", lambda e: e.copy(out=dst_tile[0:nrow_in_free, hb * 512:(hb + 1) * 512], in_=ps[0:nrow_in_free, :]), reads=[bps], writes=[bdst])

    def tm_to_fm(src_tile, nrow, dst_ps, bps, bsrc, col_of):
        for kc in range(8):
            c0 = col_of(kc)
            P.op("pe", lambda e: e.transpose(out=dst_ps[:, c0:c0 + nrow], in_=src_tile[0:nrow, kc * 128:(kc + 1) * 128], identity=identf[0:nrow, 0:nrow]),
                 reads=[bsrc, bcst], writes=[bps])

    xs = sb("xs", [T, D]); bxs = Buf()
    P.dma("sp", lambda e: e.dma_start(out=xs[:], in_=x_s[:, :]), writes=[bxs])
    tm_to_fm(xs, T, psA, bpsA, bxs, lambda kc: kc * T)
    xsT = sb("xsT", [128, 8 * T]); bxsT = Buf(); xsT3 = xsT[:].rearrange("p (k t) -> p k t", t=T)
    P.op("dve", lambda e: e.tensor_copy(out=xsT[:], in_=psA[:, 0:8 * T]), reads=[bpsA], writes=[bxsT])
    tmpA = sb("tmpA", [128, 8 * T]); btA = Buf(); tmpA3 = tmpA[:].rearrange("p (k t) -> p k t", t=T)
    tmpB = sb("tmpB", [128, 8 * T]); btB = Buf(); tmpB3 = tmpB[:].rearrange("p (k t) -> p k t", t=T)
    tt("dve", tmpA[:], xsT[:], xsT[:], ALU.mult, [bxsT], [btA])
    for kc in range(8):
        P.op("pe", lambda e: e.matmul(psB[:, 0:T], lhsT=onesf[:], rhs=tmpA[:, kc * T:(kc + 1) * T], start=(kc == 0), stop=(kc == 7)),
             reads=[bonesf, btA], writes=[bpsB])
    rstd = sb("rstd_s", [128, T]); brstd = Buf()
    P.op("act", lambda e: e.activation(out=rstd[:], in_=psB[:, 0:T], func=AF.Sqrt, scale=1.0 / D, bias=EPS), reads=[bpsB], writes=[brstd])
    P.op("dve", lambda e: e.reciprocal(out=rstd[:], in_=rstd[:]), reads=[brstd], writes=[brstd])
    P.op("dve", lambda e: e.scalar_tensor_tensor(out=tmpB3, in0=mT3[:, 8:16, 1:5], scalar=1.0, in1=bc84(CP_GN), op0=ALU.add, op1=ALU.mult),
         reads=[bmT, bchp], writes=[btB])
    tt("dve", tmpA3, xsT3, rstd[:].unsqueeze(1).to_broadcast([128, 8, T]), ALU.mult, [bxsT, brstd], [btA])
    tt("dve", tmpA3, tmpA3, tmpB3, ALU.mult, [btA, btB], [btA])
    xnsT = sb("xnsT", [128, 8 * T], BF16); bxns = Buf(); xns3 = xnsT[:].rearrange("p (k t) -> p k t", t=T)
    tt("dve", xns3, tmpA3, mT3[:, 0:8, 1:5], ALU.add, [btA, bmT], [bxns])

    ztm = sb("ztm", [T, 2120]); bztm = Buf()
    off = 0
    for blk in (TM_Q0, TM_Q1, TM_KV, TM_QI, TM_KW):
        w3, bw = load_tm(blk)
        ncol = 72 if blk == TM_KW else 512
        ps, bps = rPS.next()
        for kc in range(8):
            P.op("pe", lambda e: e.matmul(ps[0:T, 0:ncol], lhsT=xns3[:, kc, :], rhs=w3[:, kc, 0:ncol], start=(kc == 0), stop=(kc == 7)),
                 reads=[bxns, bw], writes=[bps])
        P.op("act", lambda e: e.copy(out=ztm[:, off:off + ncol], in_=ps[0:T, 0:ncol]), reads=[bps], writes=[bztm])
        off += ncol
    Q0, K0, V0, QI0, KI0, WI0 = 0, 1024, 1280, 1536, 2048, 2112
    for idx in range(40):
        w3, bw = load_fm(idx)
        for kc in range(8):
            P.op("pe", lambda e: e.matmul(psO[:, idx * T:(idx + 1) * T], lhsT=w3[:, kc, :], rhs=xns3[:, kc, :], start=(kc == 0), stop=(kc == 7)),
                 reads=[bw, bxns], writes=[bpsO])
    zfm = sb("zfm", [128, 40 * T]); bzfm = Buf(); zfm3 = zfm[:].rearrange("p (c t) -> p c t", t=T)
    P.op("dve", lambda e: e.tensor_copy(out=zfm[:], in_=psO[:, 0:40 * T]), reads=[bpsO], writes=[bzfm])

    stc = sb("stc", [T * 3, D]); bstc = Buf()
    P.dma("sp", lambda e: e.dma_start(out=stc[:], in_=st_conv[:, :]), writes=[bstc])
    for t in range(T):
        P.dma("act", lambda e: e.dma_start(out=ncv_s[t, 0:2, :], in_=stc[t * 3 + 1:t * 3 + 3, :]), reads=[bstc], is_out=True)
    tm_to_fm(stc, T * 3, psL, bpsL, bstc, lambda kc: kc * 12)
    stT = sb("stT", [128, 96]); bstT = Buf(); stT4 = stT[:].rearrange("p (c t j) -> p c t j", c=8, t=T)
    P.op("dve", lambda e: e.tensor_copy(out=stT[:], in_=psL[:, 0:96]), reads=[bpsL], writes=[bstT])
    rowt = sb("rowt", [T, D]); browt = Buf()
    fm_to_tm(zfm3[:, 0:8, :], 8, T, rowt, browt, bzfm)
    P.dma("act", lambda e: e.dma_start(out=ncv_s[:, 2, :], in_=rowt[:]), reads=[browt], is_out=True)
    xcs = sb("xcs", [128, 8 * T]); bxcs = Buf(); xcs3 = xcs[:].rearrange("p (k t) -> p k t", t=T)
    tt("dve", xcs3, zfm3[:, 0:8, :], bc84(CP_WCONV + 24), ALU.mult, [bzfm, bchp], [bxcs])
    tt("dve", xcs3, xcs3, bc84(CP_BCONV), ALU.add, [bxcs, bchp], [bxcs])
    for j in range(3):
        tt("dve", tmpA3, stT4[:, :, :, j], bc84(CP_WCONV + 8 * j), ALU.mult, [bstT, bchp], [btA])
        tt("dve", xcs3, xcs3, tmpA3, ALU.add, [bxcs, btA], [bxcs])
    xcsb = sb("xcsb", [128, 8 * T], BF16); bxcsb = Buf(); xcsb3 = xcsb[:].rearrange("p (k t) -> p k t", t=T)
    P.op("dve", lambda e: e.tensor_copy(out=xcsb[:], in_=xcs[:]), reads=[bxcs], writes=[bxcsb])
    for which in range(2):
        for cc in range(8):
            n, c2 = cc // 2, cc % 2
            c0 = (which * 8 + cc) * T
            for k2 in range(2):
                P.op("pe", lambda e: e.matmul(psB[:, c0:c0 + T], lhsT=wrr5[:, which, n, k2, c2 * 128:(c2 + 1) * 128], rhs=xcsb3[:, 2 * n + k2, :],
                                              start=(k2 == 0), stop=(k2 == 1)), reads=[bwrr, bxcsb], writes=[bpsB])
    r_s = sb("r_s", [128, 8 * T]); br_s = Buf(); r_s3 = r_s[:].rearrange("p (k t) -> p k t", t=T)
    i_s = sb("i_s", [128, 8 * T]); bi_s = Buf(); i_s3 = i_s[:].rearrange("p (k t) -> p k t", t=T)
    a_s = sb("a_s", [128, 8 * T]); ba_s = Buf(); a_s3 = a_s[:].rearrange("p (k t) -> p k t", t=T)
    tt("dve", r_s3, psB[:, 0:8 * T].rearrange("p (k t) -> p k t", t=T), bc84(CP_BRA), ALU.add, [bpsB, bchp], [br_s])
    tt("dve", i_s3, psB[:, 8 * T:16 * T].rearrange("p (k t) -> p k t", t=T), bc84(CP_BRX), ALU.add, [bpsB, bchp], [bi_s])
    P.op("act", lambda e: e.activation(out=r_s[:], in_=r_s[:], func=AF.Sigmoid), reads=[br_s], writes=[br_s])
    P.op("act", lambda e: e.activation(out=i_s[:], in_=i_s[:], func=AF.Sigmoid), reads=[bi_s], writes=[bi_s])
    tt("dve", a_s3, r_s3, clam[:].unsqueeze(2).to_broadcast([128, 8, T]), ALU.mult, [br_s, bclam], [ba_s])
    P.op("act", lambda e: e.activation(out=a_s[:], in_=a_s[:], func=AF.Exp), reads=[ba_s], writes=[ba_s])
    tt("dve", tmpA[:], a_s[:], a_s[:], ALU.mult, [ba_s], [btA])
    P.op("dve", lambda e: e.tensor_scalar(out=tmpA[:], in0=tmpA[:], scalar1=-1.0, scalar2=1.0, op0=ALU.mult, op1=ALU.add), reads=[btA], writes=[btA])
    P.op("act", lambda e: e.activation(out=tmpA[:], in_=tmpA[:], func=AF.Sqrt), reads=[btA], writes=[btA])
    tt("dve", tmpB[:], i_s[:], xcs[:], ALU.mult, [bi_s, bxcs], [btB])
    tt("dve", tmpB[:], tmpB[:], tmpA[:], ALU.mult, [btB, btA], [btB])
    hst = sb("hst", [T, D]); bhst = Buf()
    P.dma("sp", lambda e: e.dma_start(out=hst[:], in_=st_lru[:, :]), writes=[bhst])
    tm_to_fm(hst, T, psA, bpsA, bhst, lambda kc: kc * T)
    h_s = sb("h_s", [128, 8 * T]); bh_s = Buf(); h_s3 = h_s[:].rearrange("p (k t) -> p k t", t=T)
    tt("dve", h_s[:], psA[:, 0:8 * T], a_s[:], ALU.mult, [bpsA, ba_s], [bh_s])
    tt("dve", h_s[:], h_s[:], tmpB[:], ALU.add, [bh_s, btB], [bh_s])
    fm_to_tm(h_s3, 8, T, rowt, browt, bh_s)
    P.dma("act", lambda e: e.dma_start(out=nlr_s[:, :], in_=rowt[:]), reads=[browt], is_out=True)
    acta = sb("acta_s", [128, 8 * T], BF16); bacta = Buf(); acta3 = acta[:].rearrange("p (k t) -> p k t", t=T)
    P.op("act", lambda e: e.activation(out=tmpA3, in_=zfm3[:, 8:16, :], func=AF.Silu), reads=[bzfm], writes=[btA])
    tt("dve", acta[:], h_s[:], tmpA[:], ALU.mult, [bh_s, btA], [bacta])
    for cc in range(8):
        w3, bw = load_fm(FM_PA + cc)
        for kc in range(8):
            P.op("pe", lambda e: e.matmul(psB[:, cc * T:(cc + 1) * T], lhsT=w3[:, kc, :], rhs=acta3[:, kc, :], start=(kc == 0), stop=(kc == 7)),
                 reads=[bw, bacta], writes=[bpsB])
    m_s = sb("m_s", [128, 8 * T]); bm_s = Buf(); m_s3 = m_s[:].rearrange("p (k t) -> p k t", t=T)
    P.op("act", lambda e: e.activation(out=tmpA3, in_=zfm3[:, 24:32, :], func=AF.Sigmoid), reads=[bzfm], writes=[btA])
    tt("dve", m_s[:], psB[:, 0:8 * T], tmpA[:], ALU.mult, [bpsB, btA], [bm_s])

    rs = sb("ropes_t", [T, 384]); brs = Buf()
    P.dma("sp", lambda e: e.dma_start(out=rs[:], in_=ropes[:, :]), writes=[brs])
    t1 = sb("s_t1", [T, 1024]); bt1 = Buf()
    t2 = sb("s_t2", [T, 1024]); bt2 = Buf()
    q_r = sb("q_r", [T, 1024]); bq_r = Buf()
    k_r = sb("k_r", [T, 256]); bk_r = Buf()
    qi_r = sb("qi_r", [T, 512]); bqi_r = Buf()
    ki_r = sb("ki_r", [T, 64]); bki_r = Buf()
    sm = sb("s_sm", [T, 16]); bsm = Buf()
    v3 = lambda ap, h: ap.rearrange("p (h d) -> p h d", h=h)
    rope("dve", v3(q_r[:], 8), v3(ztm[:, Q0:Q0 + 1024], 8), rs[:, 0:128], rs[:, 128:256], 8, 128, v3(t1[:], 8), v3(t2[:], 8), [bztm, brs], [bq_r], bt1, bt2)
    rope("dve", v3(k_r[:], 2), v3(ztm[:, K0:K0 + 256], 2), rs[:, 0:128], rs[:, 128:256], 2, 128, v3(t1[:, 0:256], 2), v3(t2[:, 0:256], 2), [bztm, brs], [bk_r], bt1, bt2)
    P.dma("act", lambda e: e.dma_start(out=nk_s[:, :], in_=k_r[:]), reads=[bk_r], is_out=True)
    P.dma("act", lambda e: e.dma_start(out=nv_s[:, :], in_=ztm[:, V0:V0 + 256]), reads=[bztm], is_out=True)
    rope("dve", v3(qi_r[:], 8), v3(ztm[:, QI0:QI0 + 512], 8), rs[:, 256:320], rs[:, 320:384], 8, 64, v3(t1[:, 0:512], 8), v3(t2[:, 0:512], 8), [bztm, brs], [bqi_r], bt1, bt2)
    P.op("dve", lambda e: e.tensor_reduce(out=sm[:, 0:1], in_=ztm[:, KI0:KI0 + 64], axis=AX.X, op=ALU.add), reads=[bztm], writes=[bsm])
    P.op("dve", lambda e: e.tensor_scalar(out=sm[:, 1:2], in0=sm[:, 0:1], scalar1=-1.0 / 64, scalar2=None, op0=ALU.mult), reads=[bsm], writes=[bsm])
    P.op("dve", lambda e: e.tensor_scalar(out=t1[:, 0:64], in0=ztm[:, KI0:KI0 + 64], scalar1=sm[:, 1:2], scalar2=None, op0=ALU.add), reads=[bztm, bsm], writes=[bt1])
    P.op("act", lambda e: e.activation(out=t2[:, 0:64], in_=t1[:, 0:64], func=AF.Square, accum_out=sm[:, 2:3]), reads=[bt1], writes=[bt2, bsm])
    P.op("act", lambda e: e.activation(out=sm[:, 3:4], in_=sm[:, 2:3], func=AF.Sqrt, scale=1.0 / 64, bias=EPS), reads=[bsm], writes=[bsm])
    P.op("dve", lambda e: e.reciprocal(out=sm[:, 4:5], in_=sm[:, 3:4]), reads=[bsm], writes=[bsm])
    P.op("dve", lambda e: e.scalar_tensor_tensor(out=t1[:, 64:128], in0=t1[:, 0:64], scalar=sm[:, 4:5], in1=idxgb_bc[0:T, 0:64], op0=ALU.mult, op1=ALU.mult),
         reads=[bt1, bsm, bidxgb], writes=[bt1])
    tt("dve", t1[:, 128:192], t1[:, 64:128], idxgb_bc[0:T, 64:128], ALU.add, [bt1, bidxgb], [bt1])
    rope("dve", ki_r[:].unsqueeze(1), t1[:, 128:192].unsqueeze(1), rs[:, 256:320], rs[:, 320:384], 1, 64,
         t2[:, 64:128].unsqueeze(1), t2[:, 128:192].unsqueeze(1), [bt1, brs], [bki_r], bt2, bt2)
    P.dma("act", lambda e: e.dma_start(out=nki_s[:, :], in_=ki_r[:]), reads=[bki_r], is_out=True)
    w_s = sb("w_s", [T, 8]); bw_s = Buf()
    P.op("dve", lambda e: e.tensor_scalar(out=w_s[:], in0=ztm[:, WI0:WI0 + 8], scalar1=IDX_W_SCALE, scalar2=None, op0=ALU.mult), reads=[bztm], writes=[bw_s])
    s8 = sb("s8", [T, 8]); bs8 = Buf()
    selfsc = sb("selfsc", [T, 1]); bselfsc = Buf()
    sl = sb("sl", [T, 8]); bsl = Buf()
    tt("dve", v3(t1[:, 0:512], 8), v3(qi_r[:], 8), ki_r[:].unsqueeze(1).to_broadcast([T, 8, 64]), ALU.mult, [bqi_r, bki_r], [bt1])
    P.op("dve", lambda e: e.tensor_reduce(out=s8[:], in_=v3(t1[:, 0:512], 8), axis=AX.X, op=ALU.add), reads=[bt1], writes=[bs8])
    P.op("dve", lambda e: e.scalar_tensor_tensor(out=s8[:], in0=s8[:], scalar=0.0, in1=w_s[:], op0=ALU.max, op1=ALU.mult), reads=[bs8, bw_s], writes=[bs8])
    P.op("dve", lambda e: e.tensor_reduce(out=selfsc[:], in_=s8[:], axis=AX.X, op=ALU.add), reads=[bs8], writes=[bselfsc])
    tt("dve", t1[:].rearrange("p (g l d) -> p g l d", g=2, l=4), q_r[:].rearrange("p (g l d) -> p g l d", g=2, l=4),
       k_r[:].rearrange("p (g d) -> p g d", g=2).unsqueeze(2).to_broadcast([T, 2, 4, 128]), ALU.mult, [bq_r, bk_r], [bt1])
    P.op("dve", lambda e: e.tensor_reduce(out=sl[:], in_=v3(t1[:], 8), axis=AX.X, op=ALU.add), reads=[bt1], writes=[bsl])
    q_b = sb("q_b", [T, 1024], BF16); bq_b = Buf()
    P.op("dve", lambda e: e.tensor_copy(out=q_b[:], in_=q_r[:]), reads=[bq_r], writes=[bq_b])
    pt, bpt = rPST.next()
    for h in range(8):
        P.op("pe", lambda e: e.transpose(out=pt[:, h * T:(h + 1) * T], in_=q_b[:, h * 128:(h + 1) * 128], identity=identb[0:T, 0:T]),
             reads=[bq_b, bidb], writes=[bpt])
    qsT = sb("qsT", [128, 8 * T], BF16); bqsT = Buf(); qsT3 = qsT[:].rearrange("p (h t) -> p h t", t=T)
    P.op("dve", lambda e: e.tensor_copy(out=qsT[:], in_=pt[:, 0:8 * T]), reads=[bpt], writes=[bqsT])
    qpad = sb("qpad", [T, 2 * 8 * 128], BF16); bqpad = Buf(); qpad4 = qpad[:].rearrange("p (a h d) -> p a h d", a=2, h=8)
    P.op("pool", lambda e: e.memset(qpad[:], 0.0), writes=[bqpad])
    P.op("dve", lambda e: e.tensor_copy(out=qpad4[:, 0, :, 0:64], in_=v3(qi_r[:], 8)), reads=[bqi_r], writes=[bqpad])
    P.op("dve", lambda e: e.tensor_copy(out=qpad4[:, 1, :, 64:128], in_=v3(qi_r[:], 8)), reads=[bqi_r], writes=[bqpad])
    pt, bpt = rPST.next()
    for a in range(2):
        for h in range(8):
            c0 = (a * 8 + h) * T
            P.op("pe", lambda e: e.transpose(out=pt[:, c0:c0 + T], in_=qpad4[:, a, h, :], identity=identb[0:T, 0:T]), reads=[bqpad, bidb], writes=[bpt])
    qiz = sb("qiz", [128, T * 16], BF16); bqiz = Buf(); qiz3 = qiz[:].rearrange("p (t a) -> p t a", t=T)
    P.op("dve", lambda e: e.tensor_copy(out=qiz3, in_=pt[:, 0:16 * T].rearrange("p (a t) -> p t a", t=T)), reads=[bpt], writes=[bqiz])
    W4 = sb("W4", [T, T * 8 + T]); bW4 = Buf()
    tt("dve", W4[:, 0:T * 8].rearrange("p (t h) -> p t h", t=T), w_s[:].unsqueeze(1).to_broadcast([T, T, 8]), I4.unsqueeze(2).to_broadcast([T, T, 8]), ALU.mult,
       [bw_s, bcst], [bW4])
    P.op("dve", lambda e: e.tensor_scalar(out=W4[:, T * 8:T * 9], in0=I4, scalar1=selfsc[:], scalar2=None, op0=ALU.mult), reads=[bcst, bselfsc], writes=[bW4])
    P.op("pe", lambda e: e.matmul(psB[:, 0:T * 9], lhsT=onesf[0:T, :], rhs=W4[:], start=True, stop=True), reads=[bonesf, bW4], writes=[bpsB])
    wbcS = sb("wbcS", [128, T * 9]); bwbcS = Buf()
    P.op("dve", lambda e: e.tensor_copy(out=wbcS[:], in_=psB[:, 0:T * 9]), reads=[bpsB], writes=[bwbcS])
    selfb = wbcS[:, T * 8:T * 9]
    pti = sb("pti", [128, T], I32); bpti = Buf()
    ptf = sb("ptf", [128, T]); bptf = Buf()
    P.dma("sp", lambda e: e.dma_start(out=pti[:], in_=ptab[:, :]), writes=[bpti])
    P.op("dve", lambda e: e.tensor_copy(out=ptf[:], in_=pti[:]), reads=[bpti], writes=[bptf])

    Gt = sb("Gt", [128, 8192]); bGt = Buf()
    kTs = sb("kTs", [128, 64 * 128], BF16); bkTs = Buf(); kTs3 = kTs[:].rearrange("p (r q) -> p r q", r=64)
    scs = sb("scs", [128, T * 128]); bscs = Buf(); scs3 = scs[:].rearrange("p (t r) -> p t r", t=T)
    tmpS = sb("tmpS", [128, 512]); btS = Buf()
    for t in range(T):
        P.dma("pool", lambda e: e.indirect_dma_start(out=Gt[:], out_offset=None, in_=cache_i[:, :],
                                                     in_offset=bass.IndirectOffsetOnAxis(ap=pti[:, t:t + 1], axis=0)), reads=[bpti], writes=[bGt])
        for r4 in range(16):
            ps, bps = rPS.next()
            for q in range(4):
                rp = r4 * 4 + q
                P.op("pe", lambda e: e.transpose(out=ps[:, q * 128:(q + 1) * 128], in_=Gt[:, rp * 128:(rp + 1) * 128], identity=identf), reads=[bGt, bcst], writes=[bps])
            if r4 % 2 == 0:
                P.op("act", lambda e: e.copy(out=kTs[:, r4 * 512:(r4 + 1) * 512], in_=ps[:, :]), reads=[bps], writes=[bkTs])
            else:
                P.op("dve", lambda e: e.tensor_copy(out=kTs[:, r4 * 512:(r4 + 1) * 512], in_=ps[:, :]), reads=[bps], writes=[bkTs])
        for half, (psX, bpsX) in enumerate(((psS0, bpsS0), (psS1, bpsS1))):
            for rr in range(64):
                r = half * 64 + rr
                rp, r2 = r // 2, r % 2
                P.op("pe", lambda e: e.matmul(psX[:, rr * 8:(rr + 1) * 8], lhsT=kTs3[:, rp, :], rhs=qiz3[:, t, r2 * 8:(r2 + 1) * 8], start=True, stop=True),
                     reads=[bkTs, bqiz], writes=[bpsX])
            P.op("dve", lambda e: e.scalar_tensor_tensor(out=tmpS[:].rearrange("p (r h) -> p r h", h=8), in0=psX[:, :].rearrange("p (r h) -> p r h", h=8), scalar=0.0,
                                                         in1=wbcS[:, t * 8:(t + 1) * 8].unsqueeze(1).to_broadcast([128, 64, 8]), op0=ALU.max, op1=ALU.mult),
                 reads=[bpsX, bwbcS], writes=[btS])
            P.op("dve", lambda e: e.tensor_reduce(out=scs3[:, t, half * 64:(half + 1) * 64], in_=tmpS[:].rearrange("p (r h) -> p r h", h=8), axis=AX.X, op=ALU.add),
                 reads=[btS], writes=[bscs])

    mx = sb("b_mx", [128, 2 * T]); bmx = Buf()
    P.op("dve", lambda e: e.tensor_reduce(out=mx[:, 0:T], in_=scs3, axis=AX.X, op=ALU.max), reads=[bscs], writes=[bmx])
    P.op("dve", lambda e: e.tensor_reduce(out=mx[:, T:2 * T], in_=scs3, axis=AX.X, op=ALU.min), reads=[bscs], writes=[bmx])
    ps, bps = rPS.next()
    P.op("pe", lambda e: e.transpose(out=ps[0:2 * T, 0:128], in_=mx[:], identity=identf), reads=[bmx, bcst], writes=[bps])
    hl = sb("b_hl", [2 * T, 2]); bhl = Buf()
    P.op("dve", lambda e: e.tensor_reduce(out=hl[:, 0:1], in_=ps[0:2 * T, 0:128], axis=AX.X, op=ALU.max), reads=[bps], writes=[bhl])
    P.op("dve", lambda e: e.tensor_reduce(out=hl[:, 1:2], in_=ps[0:2 * T, 0:128], axis=AX.X, op=ALU.min), reads=[bps], writes=[bhl])
    HL = sb("b_HL", [2 * T, 2 * T]); bHL = Buf()
    P.op("pool", lambda e: e.memset(HL[:], 0.0), writes=[bHL])
    P.op("dve", lambda e: e.tensor_scalar(out=HL[0:T, 0:T], in0=I4, scalar1=hl[0:T, 0:1], scalar2=None, op0=ALU.mult), reads=[bcst, bhl], writes=[bHL])
    ps2, bps2 = rPS.next()
    P.op("pe", lambda e: e.matmul(ps2[:, 0:T], lhsT=onesf[0:T, :], rhs=HL[0:T, 0:T], start=True, stop=True), reads=[bonesf, bHL], writes=[bps2])
    hib = sb("b_hib", [128, T]); bhib = Buf()
    tt("dve", hib[:], ps2[:, 0:T], selfb, ALU.max, [bps2, bwbcS], [bhib])
    ps, bps = rPS.next()
    P.op("pe", lambda e: e.transpose(out=ps[0:T, 0:128], in_=mx[:, T:2 * T], identity=identf), reads=[bmx, bcst], writes=[bps])
    P.op("dve", lambda e: e.tensor_reduce(out=hl[0:T, 1:2], in_=ps[0:T, 0:128], axis=AX.X, op=ALU.min), reads=[bps], writes=[bhl])
    P.op("dve", lambda e: e.tensor_scalar(out=HL[0:T, T:2 * T], in0=I4, scalar1=hl[0:T, 1:2], scalar2=None, op0=ALU.mult), reads=[bcst, bhl], writes=[bHL])
    ps2, bps2 = rPS.next()
    P.op("pe", lambda e: e.matmul(ps2[:, 0:T], lhsT=onesf[0:T, :], rhs=HL[0:T, T:2 * T], start=True, stop=True), reads=[bonesf, bHL], writes=[bps2])
    lob = sb("b_lob", [128, T]); blob = Buf()
    tt("dve", lob[:], ps2[:, 0:T], selfb, ALU.min, [bps2, bwbcS], [blob])
    wb = sb("b_wb", [128, T]); bwb = Buf()
    tt("dve", wb[:], hib[:], lob[:], ALU.subtract, [bhib, blob], [bwb])
    midb = sb("b_mid", [128, T]); bmidb = Buf()
    cmpj = sb("b_cmpj", [128, T * 128]); bcmpj = Buf(); cmpj3 = cmpj[:].rearrange("p (t r) -> p t r", t=T)
    cntp = sb("b_cntp", [128, T]); bcntp = Buf()
    tot = sb("b_tot", [128, T]); btot = Buf()
    sge = sb("b_sge", [128, T]); bsge = Buf()
    for it in range(N_BISECT_S):
        P.op("dve", lambda e: e.tensor_scalar(out=wb[:], in0=wb[:], scalar1=0.5, scalar2=None, op0=ALU.mult), reads=[bwb], writes=[bwb])
        tt("dve", midb[:], lob[:], wb[:], ALU.add, [blob, bwb], [bmidb])
        tt("dve", cmpj3, scs3, midb[:].unsqueeze(2).to_broadcast([128, T, 128]), ALU.is_ge, [bscs, bmidb], [bcmpj])
        P.op("dve", lambda e: e.tensor_reduce(out=cntp[:], in_=cmpj3, axis=AX.X, op=ALU.add), reads=[bcmpj], writes=[bcntp])
        ps, bps = rPS.next()
        P.op("pe", lambda e: e.matmul(ps[:, 0:T], lhsT=onesf[:], rhs=cntp[:], start=True, stop=True), reads=[bonesf, bcntp], writes=[bps])
        tt("dve", sge[:], selfb, midb[:], ALU.is_ge, [bwbcS, bmidb], [bsge])
        tt("dve", tot[:], ps[:, 0:T], sge[:], ALU.add, [bps, bsge], [btot])
        P.op("dve", lambda e: e.tensor_scalar(out=tot[:], in0=tot[:], scalar1=float(TOPK) - 0.5, scalar2=None, op0=ALU.is_ge), reads=[btot], writes=[btot])
        tt("dve", tot[:], tot[:], wb[:], ALU.mult, [btot, bwb], [btot])
        tt("dve", lob[:], lob[:], tot[:], ALU.add, [blob, btot], [blob])
    Msel = sb("Msel", [128, T * 128]); bMsel = Buf(); Msel3 = Msel[:].rearrange("p (t r) -> p t r", t=T)
    tt("dve", Msel3, scs3, lob[:].unsqueeze(2).to_broadcast([128, T, 128]), ALU.is_ge, [bscs, blob], [bMsel])
    thr_tm = sb("thr_tm", [T, 4]); bthr = Buf()
    tt("dve", t1[:, 0:T], lob[0:T, :], I4, ALU.mult, [blob, bcst], [bt1])
    P.op("dve", lambda e: e.tensor_reduce(out=thr_tm[:, 0:1], in_=t1[:, 0:T], axis=AX.X, op=ALU.add), reads=[bt1], writes=[bthr])
    tt("dve", thr_tm[:, 1:2], selfsc[:], thr_tm[:, 0:1], ALU.is_ge, [bselfsc, bthr], [bthr])
    P.op("dve", lambda e: e.tensor_scalar(out=thr_tm[:, 2:3], in0=thr_tm[:, 1:2], scalar1=-NEG, scalar2=NEG, op0=ALU.mult, op1=ALU.add), reads=[bthr], writes=[bthr])
    pself = sb("pself", [T, 8]); bpself = Buf()
    P.op("act", lambda e: e.activation(out=pself[:], in_=sl[:], func=AF.Exp, scale=ATT_SCALE, bias=thr_tm[:, 2:3]), reads=[bsl, bthr], writes=[bpself])

    csel = sb("csel", [128, T]); bcsel = Buf()
    P.op("dve", lambda e: e.tensor_reduce(out=csel[:], in_=Msel3, axis=AX.X, op=ALU.add), reads=[bMsel], writes=[bcsel])
    ps, bps = rPS.next()
    P.op("pe", lambda e: e.matmul(ps[:, 0:T], lhsT=Tstrict, rhs=csel[:], start=True, stop=True), reads=[bcst, bcsel], writes=[bps])
    osel = sb("osel", [128, T]); bosel = Buf()
    esel = sb("esel", [128, T]); besel = Buf()
    P.op("dve", lambda e: e.tensor_copy(out=osel[:], in_=ps[:, 0:T]), reads=[bps], writes=[bosel])
    tt("dve", esel[:], osel[:], csel[:], ALU.add, [bosel, bcsel], [besel])
    rhsT = sb("rhsT", [128, T * 130]); brhsT = Buf(); rhsT3 = rhsT[:].rearrange("p (t c) -> p t c", t=T)
    for t in range(T):
        P.op("dve", lambda e: e.tensor_tensor_scan(out=rhsT3[:, t, 0:128], data0=onesf[:], data1=Msel3[:, t, :], initial=0.0, op0=ALU.mult, op1=ALU.add),
             reads=[bonesf, bMsel], writes=[brhsT])
    tt("dve", rhsT3[:, :, 0:128], rhsT3[:, :, 0:128], Msel3, ALU.mult, [brhsT, bMsel], [brhsT])
    P.op("dve", lambda e: e.tensor_copy(out=rhsT3[:, :, 128], in_=osel[:]), reads=[bosel], writes=[brhsT])
    P.op("dve", lambda e: e.tensor_copy(out=rhsT3[:, :, 129], in_=ptf[:]), reads=[bptf], writes=[brhsT])
    Asel = sb("Asel", [128, 256]); bAsel = Buf()
    A2 = sb("A2", [128, 256]); bA2 = Buf()
    idxT = sb("idxT", [128, 2 * T], I32); bidxT = Buf()
    vbias = sb("vbias", [128, 2 * T]); bvbias = Buf()
    c4 = sb("c4", [128, 8]); bc4 = Buf()
    eqt = sb("eqt", [128, 128]); beqt = Buf()
    for t in range(T):
        P.op("dve", lambda e: e.tensor_scalar(out=Asel[:], in0=Jf, scalar1=osel[:, t:t + 1], scalar2=None, op0=ALU.is_ge), reads=[bcst, bosel], writes=[bAsel])
        P.op("dve", lambda e: e.tensor_scalar(out=A2[:], in0=Jf, scalar1=esel[:, t:t + 1], scalar2=None, op0=ALU.is_lt), reads=[bcst, besel], writes=[bA2])
        tt("dve", Asel[:], Asel[:], A2[:], ALU.mult, [bAsel, bA2], [bAsel])
        for jc in range(2):
            col = t * 2 + jc
            ps, bps = rPS.next()
            P.op("pe", lambda e: e.matmul(ps[:, 0:130], lhsT=Asel[:, jc * 128:(jc + 1) * 128], rhs=rhsT3[:, t, :], start=True, stop=True),
                 reads=[bAsel, brhsT], writes=[bps])
            tt("dve", c4[:, 0:1], cstt[:, CS_J1 + jc:CS_J1 + jc + 1], ps[:, 128:129], ALU.subtract, [bcst, bps], [bc4])
            P.op("dve", lambda e: e.tensor_scalar(out=eqt[:], in0=ps[:, 0:128], scalar1=c4[:, 0:1], scalar2=None, op0=ALU.is_equal), reads=[bps, bc4], writes=[beqt])
            P.op("dve", lambda e: e.tensor_reduce(out=c4[:, 1:2], in_=eqt[:], axis=AX.X, op=ALU.add), reads=[beqt], writes=[bc4])
            tt("dve", eqt[:], eqt[:], iota_r, ALU.mult, [beqt, bcst], [beqt])
            P.op("dve", lambda e: e.tensor_reduce(out=c4[:, 2:3], in_=eqt[:], axis=AX.X, op=ALU.add), reads=[beqt], writes=[bc4])
            P.op("dve", lambda e: e.scalar_tensor_tensor(out=c4[:, 3:4], in0=ps[:, 129:130], scalar=128.0, in1=c4[:, 2:3], op0=ALU.mult, op1=ALU.add),
                 reads=[bps, bc4], writes=[bc4])
            P.op("dve", lambda e: e.tensor_copy(out=idxT[:, col:col + 1], in_=c4[:, 3:4]), reads=[bc4], writes=[bidxT])
            P.op("dve", lambda e: e.tensor_scalar(out=vbias[:, col:col + 1], in0=c4[:, 1:2], scalar1=-NEG, scalar2=NEG, op0=ALU.mult, op1=ALU.add),
                 reads=[bc4], writes=[bvbias])

    Ksel = sb("Ksel", [128, 512]); bKsel = Buf()
    Vsel = sb("Vsel", [128, 512]); bVsel = Buf()
    Kb = sb("Kb_s", [128, 512], BF16); bKb = Buf()
    Vx = sb("Vx", [128, 4 * 129], BF16); bVx = Buf(); Vx4 = Vx[:].rearrange("p (j g d) -> p j g d", j=2, g=2)
    KselT = sb("KselT", [128, 512], BF16); bKselT = Buf(); KselT3 = KselT[:].rearrange("p (g j) -> p g j", g=2)
    PTs = sb("PTs", [128, 16], BF16); bPTs = Buf()
    vself = sb("vself", [T, 2 * 129], BF16); bvself = Buf(); vself3 = vself[:].rearrange("p (g d) -> p g d", g=2)
    pselfm = sb("pselfm", [T, 8], BF16); bpselfm = Buf()
    osb = sb("osb", [4, T * 2 * 128]); bosb = Buf(); osb4 = osb[:].rearrange("p (t g d) -> p t g d", t=T, g=2)
    rcp = sb("rcp", [4, 2]); brcp = Buf()
    P.op("pool", lambda e: e.memset(Vx[:], 1.0), writes=[bVx])
    P.op("pool", lambda e: e.memset(vself[:], 1.0), writes=[bvself])
    P.op("dve", lambda e: e.tensor_copy(out=vself3[:, :, 0:128], in_=ztm[:, V0:V0 + 256].rearrange("p (g d) -> p g d", g=2)), reads=[bztm], writes=[bvself])
    for t in range(T):
        for jc in range(2):
            col = t * 2 + jc
            P.dma("pool", lambda e: e.indirect_dma_start(out=Ksel[:, jc * 256:(jc + 1) * 256], out_offset=None, in_=cache_k[:, :],
                                                         in_offset=bass.IndirectOffsetOnAxis(ap=idxT[:, col:col + 1], axis=0)), reads=[bidxT], writes=[bKsel])
            P.dma("pool", lambda e: e.indirect_dma_start(out=Vsel[:, jc * 256:(jc + 1) * 256], out_offset=None, in_=cache_v[:, :],
                                                         in_offset=bass.IndirectOffsetOnAxis(ap=idxT[:, col:col + 1], axis=0)), reads=[bidxT], writes=[bVsel])
        P.op("dve", lambda e: e.tensor_copy(out=Kb[:], in_=Ksel[:]), reads=[bKsel], writes=[bKb])
        P.op("dve", lambda e: e.tensor_copy(out=Vx4[:, :, :, 0:128], in_=Vsel[:].rearrange("p (j g d) -> p j g d", j=2, g=2)), reads=[bVsel], writes=[bVx])
        pt, bpt = rPST.next()
        for g in range(2):
            for jc in range(2):
                c0 = (g * 2 + jc) * 128
                P.op("pe", lambda e: e.transpose(out=pt[:, c0:c0 + 128], in_=Kb[:, jc * 256 + g * 128:jc * 256 + (g + 1) * 128], identity=identb[:]),
                     reads=[bKb, bidb], writes=[bpt])
        P.op("act", lambda e: e.copy(out=KselT[:], in_=pt[:, 0:512]), reads=[bpt], writes=[bKselT])
        ps, bps = rPS.next()
        for jc in range(2):
            for g in range(2):
                c0 = (jc * 2 + g) * 4
                P.op("pe", lambda e: e.matmul(ps[:, c0:c0 + 4], lhsT=KselT3[:, g, jc * 128:(jc + 1) * 128], rhs=qsT3[:, 4 * g:4 * g + 4, t], start=True, stop=True),
                     reads=[bKselT, bqsT], writes=[bps])
        for jc in range(2):
            col = t * 2 + jc
            P.op("act", lambda e: e.activation(out=PTs[:, jc * 8:(jc + 1) * 8], in_=ps[:, jc * 8:(jc + 1) * 8], func=AF.Exp, scale=ATT_SCALE, bias=vbias[:, col:col + 1]),
                 reads=[bps, bvbias], writes=[bPTs])
        P.op("dve", lambda e: e.tensor_scalar(out=pselfm[:], in0=pself[:], scalar1=I4[:, t:t + 1], scalar2=None, op0=ALU.mult), reads=[bpself, bcst], writes=[bpselfm])
        for g in range(2):
            c0 = g * 129
            for jc in range(2):
                P.op("pe", lambda e: e.matmul(psO[0:4, c0:c0 + 129], lhsT=PTs[:, jc * 8 + 4 * g:jc * 8 + 4 * g + 4], rhs=Vx4[:, jc, g, :], start=(jc == 0), stop=False),
                     reads=[bPTs, bVx], writes=[bpsO])
            P.op("pe", lambda e: e.matmul(psO[0:4, c0:c0 + 129], lhsT=pselfm[:, 4 * g:4 * g + 4], rhs=vself3[:, g, :], start=False, stop=True),
                 reads=[bpselfm, bvself], writes=[bpsO])
        for g in range(2):
            c0 = g * 129
            P.op("dve", lambda e: e.reciprocal(out=rcp[:, g:g + 1], in_=psO[0:4, c0 + 128:c0 + 129]), reads=[bpsO], writes=[brcp])
            P.op("dve", lambda e: e.tensor_scalar(out=osb4[:, t, g, :], in0=psO[0:4, c0:c0 + 128], scalar1=rcp[:, g:g + 1], scalar2=None, op0=ALU.mult),
                 reads=[bpsO, brcp], writes=[bosb])
    ps, bps = rPS.next()
    for t in range(T):
        for g in range(2):
            c0 = (t * 2 + g) * 4
            P.op("pe", lambda e: e.transpose(out=ps[:, c0:c0 + 4], in_=osb4[:, t, g, :], identity=identf[0:4, 0:4]), reads=[bosb, bcst], writes=[bps])
    oT = sb("oT_s", [128, 8 * T]); boT = Buf(); oT3 = oT[:].rearrange("p (h t) -> p h t", t=T)
    P.op("dve", lambda e: e.tensor_copy(out=oT3, in_=ps[:, 0:32].rearrange("p (t h) -> p h t", t=T)), reads=[bps], writes=[boT])
    actb = sb("actb_s", [128, 8 * T], BF16); bactb = Buf(); actb3 = actb[:].rearrange("p (k t) -> p k t", t=T)
    P.op("act", lambda e: e.activation(out=tmpA3, in_=zfm3[:, 16:24, :], func=AF.Silu), reads=[bzfm], writes=[btA])
    tt("dve", actb[:], oT[:], tmpA[:], ALU.mult, [boT, btA], [bactb])
    for cc in range(8):
        w3, bw = load_fm(FM_PB + cc)
        for kc in range(8):
            P.op("pe", lambda e: e.matmul(psB[:, cc * T:(cc + 1) * T], lhsT=w3[:, kc, :], rhs=actb3[:, kc, :], start=(kc == 0), stop=(kc == 7)),
                 reads=[bw, bactb], writes=[bpsB])
    P.op("act", lambda e: e.activation(out=tmpA3, in_=zfm3[:, 32:40, :], func=AF.Sigmoid), reads=[bzfm], writes=[btA])
    tt("dve", tmpA[:], psB[:, 0:8 * T], tmpA[:], ALU.mult, [bpsB, btA], [btA])
    msb = sb("msb", [128, 8 * T], BF16); bmsb = Buf(); msb3 = msb[:].rearrange("p (k t) -> p k t", t=T)
    tt("dve", msb[:], m_s[:], tmpA[:], ALU.add, [bm_s, btA], [bmsb])
    hres = sb("hres_s", [T, D]); bhres = Buf()
    for hb, blk in enumerate((TM_O0, TM_O1)):
        wo, bwo = load_tm(blk)
        ps, bps = rPS.next()
        for kc in range(8):
            P.op("pe", lambda e: e.matmul(ps[0:T, :], lhsT=msb3[:, kc, :], rhs=wo[:, kc, :], start=(kc == 0), stop=(kc == 7)), reads=[bmsb, bwo], writes=[bps])
        tt("dve", hres[:, hb * 512:(hb + 1) * 512], ps[0:T, :], gate_s[:, hb * 512:(hb + 1) * 512], ALU.mult, [bps, bgs], [bhres])
    tt("dve", hres[:], hres[:], xs[:], ALU.add, [bhres, bxs], [bhres])
    P.op("act", lambda e: e.activation(out=t1[:], in_=hres[:], func=AF.Square, accum_out=sm[:, 8:9]), reads=[bhres], writes=[bt1, bsm])
    P.op("act", lambda e: e.activation(out=sm[:, 9:10], in_=sm[:, 8:9], func=AF.Sqrt, scale=1.0 / D, bias=EPS), reads=[bsm], writes=[bsm])
    P.op("dve", lambda e: e.reciprocal(out=sm[:, 10:11], in_=sm[:, 9:10]), reads=[bsm], writes=[bsm])
    P.op("dve", lambda e: e.scalar_tensor_tensor(out=hres[:], in0=hres[:], scalar=sm[:, 10:11], in1=gfin_bc[0:T, :], op0=ALU.mult, op1=ALU.mult),
         reads=[bhres, bsm, bgfin], writes=[bhres])
    P.dma("act", lambda e: e.dma_start(out=y_s[:, :], in_=hres[:]), reads=[bhres], is_out=True)


def _fm(W, c0):
    return np.ascontiguousarray(W[:, c0:c0 + 128].reshape(8, 128, 128).transpose(1, 0, 2).reshape(128, 1024))


def _tm(W, c0, n=512):
    blk = np.zeros((1024, 512), np.float32)
    blk[:, :n] = W[:, c0:c0 + n]
    return np.ascontiguousarray(blk.reshape(8, 128, 512).transpose(1, 0, 2).reshape(128, 4096))


def _vec_fm(v):
    return np.ascontiguousarray(np.asarray(v, np.float32).reshape(-1, 128).T)


def _rope_tab(pos, half):
    inv = np.float32(10000.0) ** (-(np.arange(half, dtype=np.float32)) / np.float32(half))
    ang = (pos.astype(np.float32)[:, None] * inv[None, :]).astype(np.float32)
    c, s_ = np.cos(ang).astype(np.float32), np.sin(ang).astype(np.float32)
    return np.concatenate([c, c, -s_, s_], axis=1).astype(np.float32)


def _host_shared(inp):
    f32 = np.float32
    w_in = np.asarray(inp["w_in"][0], f32)
    w_pa = np.asarray(inp["w_pa"][0], f32); w_pb = np.asarray(inp["w_pb"][0], f32); w_o = np.asarray(inp["w_o"][0], f32)
    fm = []
    for base in FM_COLS:
        for cc in range(8):
            fm.append(_fm(w_in, base + cc * 128))
    for W in (w_pa, w_pb):
        for cc in range(8):
            fm.append(_fm(W, cc * 128))
    tm = [_tm(w_in, 2048), _tm(w_in, 2560), _tm(w_in, 3072), _tm(w_in, 4608), _tm(w_in, 5120, 72), _tm(w_o, 0), _tm(w_o, 512)]
    w_ada = np.asarray(inp["w_ada"][0], f32)
    wada = np.stack([_tm(w_ada, b * 512) for b in range(6)])
    wr = []
    for W in (inp["w_ra"][0], inp["w_rx"][0]):
        W = np.asarray(W, f32).reshape(4, 2, 128, 256).transpose(2, 0, 1, 3)
        wr.append(W.reshape(128, 2048))
    w_rr = np.ascontiguousarray(np.concatenate(wr, axis=1))
    chp = np.zeros((128, NCP), f32)
    chp[:, CP_GN:CP_GN + 8] = _vec_fm(inp["g_norm"][0])
    chp[:, CP_BADA:CP_BADA + 24] = _vec_fm(inp["b_ada"][0])
    chp[:, CP_WCONV:CP_WCONV + 32] = np.asarray(inp["w_conv"][0], f32).reshape(4, 8, 128).transpose(2, 0, 1).reshape(128, 32)
    chp[:, CP_BCONV:CP_BCONV + 8] = _vec_fm(inp["b_conv"][0])
    chp[:, CP_BRA:CP_BRA + 8] = _vec_fm(inp["b_ra"][0])
    chp[:, CP_BRX:CP_BRX + 8] = _vec_fm(inp["b_rx"][0])
    chp[:, CP_LAM:CP_LAM + 8] = _vec_fm(inp["lru_lambda"][0])
    cst = np.zeros((128, CS_END), f32)
    ar = np.arange(128)
    cst[:, CS_ID:CS_ID + 128] = np.eye(128, dtype=f32)
    cst[:, CS_TS:CS_TS + 128] = (ar[:, None] < ar[None, :]).astype(f32)
    cst[:, CS_JF:CS_JF + 256] = np.arange(256, dtype=f32)[None, :]
    cst[:, CS_IR:CS_IR + 128] = ar.astype(f32)[None, :]
    cst[:, CS_CAUS:CS_CAUS + 128] = np.where(ar[None, :] <= ar[:, None], 0.0, -1e30).astype(f32)
    cst[:, CS_J1] = ar + 1
    cst[:, CS_J1 + 1] = ar + 129
    cst[:, CS_P2:CS_P2 + 32] = (0.5 ** np.arange(1, 33, dtype=np.float64)).astype(f32)[None, :]
    pos = np.arange(SEQ)
    ropeq = _rope_tab(pos, 64)
    ropei = _rope_tab(pos, 32)
    ps_ = np.full((NS,), PAST)
    ropes = np.concatenate([_rope_tab(ps_, 64), _rope_tab(ps_, 32)], axis=1)
    sh = {
        "cache_k": np.asarray(inp["cache_k"], f32).reshape(NPOOL * 128, 256),
        "cache_v": np.asarray(inp["cache_v"], f32).reshape(NPOOL * 128, 256),
        "cache_i": np.asarray(inp["cache_idx_k"], f32).reshape(NPOOL, 128 * 64),
        "w_ada": wada, "b_ada": np.asarray(inp["b_ada"], f32).reshape(1, 3 * D),
        "w_fm": np.stack(fm), "w_tm": np.stack(tm), "w_rr": w_rr, "chp": chp, "cst": cst,
        "ropeq": ropeq, "ropei": ropei, "ropes": np.ascontiguousarray(ropes),
        "g_fin": np.asarray(inp["g_final"], f32).reshape(1, D),
        "idx_gb": np.concatenate([np.asarray(inp["idx_k_norm_g"][0], f32), np.asarray(inp["idx_k_norm_b"][0], f32)]).reshape(1, 128),
    }
    return sh


_NC_CACHE = {}


def kernel(**inputs):
    f32 = np.float32
    sh = _host_shared(inputs)
    in_maps = []
    for c in range(NCORE):
        m = dict(sh)
        s0, s1 = c * NS, (c + 1) * NS
        m["x_p"] = np.ascontiguousarray(np.asarray(inputs["x_prompt"][c], f32))
        m["x_s"] = np.ascontiguousarray(np.asarray(inputs["x_sample"][s0:s1, 0], f32))
        m["c5"] = np.ascontiguousarray(np.concatenate([np.asarray(inputs["c_prompt"][c:c + 1], f32), np.asarray(inputs["c_sample"][s0:s1], f32)], axis=0))
        m["st_conv"] = np.ascontiguousarray(np.asarray(inputs["state_conv"][0, s0:s1], f32).reshape(NS * 3, D))
        m["st_lru"] = np.ascontiguousarray(np.asarray(inputs["state_rglru"][0, s0:s1], f32))
        m["ptab"] = np.ascontiguousarray(np.asarray(inputs["page_table"][s0:s1], np.int32).T)
        in_maps.append(m)
    if "nc" not in _NC_CACHE:
        _NC_CACHE["nc"] = build_program()
    nc = _NC_CACHE["nc"]
    res = run_bass_kernel_spmd(nc, in_maps, core_ids=list(range(NCORE)))
    R = res.results
    cat = lambda k: np.stack([np.asarray(R[c][k], f32) for c in range(NCORE)])
    y_prompt = cat("y_p")
    y_sample = cat("y_s").reshape(NCORE * NS, 1, D)
    nk_p = cat("nk_p").reshape(1, NCORE, SEQ, 2, 128)
    nv_p = cat("nv_p").reshape(1, NCORE, SEQ, 2, 128)
    nki_p = cat("nki_p").reshape(1, NCORE, SEQ, 64)
    ncv_p = cat("ncv_p").reshape(1, NCORE, 3, D)
    nlr_p = cat("nlr_p").reshape(1, NCORE, D)
    nk_s = cat("nk_s").reshape(1, NCORE * NS, 1, 2, 128)
    nv_s = cat("nv_s").reshape(1, NCORE * NS, 1, 2, 128)
    nki_s = cat("nki_s").reshape(1, NCORE * NS, 1, 64)
    ncv_s = cat("ncv_s").reshape(1, NCORE * NS, 3, D)
    nlr_s = cat("nlr_s").reshape(1, NCORE * NS, D)
    return (y_prompt, y_sample, nk_p, nv_p, nki_p, ncv_p, nlr_p, nk_s, nv_s, nki_s, ncv_s, nlr_s)
```

```python
import numpy as np
import concourse.bass as bass
import concourse.mybir as mybir
from concourse.bass_utils import run_bass_kernel_spmd
from contextlib import ExitStack

F32 = mybir.dt.float32
BF16 = mybir.dt.bfloat16
I32 = mybir.dt.int32
AF = mybir.ActivationFunctionType
ALU = mybir.AluOpType
AX = mybir.AxisListType

D = 1024
SEQ = 4096
NCORE = 8
NS = 4
PAST = 16384
NPAGE = 128
NPOOL = 5120
D_IN = 7240
EPS = 1e-6
IDX_W_SCALE = 512.0 ** -0.5
ATT_SCALE = 128.0 ** -0.5
TOPK = 256
NEG = -30000.0
CH = 512
NCHUNK = SEQ // CH
N_BISECT = 14
N_BISECT_S = 26

FM_XA, FM_GA, FM_GB, FM_MA, FM_MB, FM_PA, FM_PB = 0, 8, 16, 24, 32, 40, 48
NFM = 56
FM_COLS = [0, 1024, 3584, 5192, 6216]
TM_Q0, TM_Q1, TM_KV, TM_QI, TM_KW, TM_O0, TM_O1 = range(7)
NTM = 7
TM_COLS = [2048, 2560, 3072, 4608, 5120]

CP_GN, CP_BADA, CP_WCONV, CP_BCONV, CP_BRA, CP_BRX, CP_LAM = 0, 8, 32, 64, 72, 80, 88
NCP = 96
CS_ID, CS_TS, CS_JF, CS_IR, CS_CAUS, CS_J1, CS_P2, CS_END = 0, 128, 256, 512, 640, 768, 770, 802

DO_SAMPLE = True
DO_ATTN = True
STOP = 'all'
NCHUNK_RUN = NCHUNK


class Buf:
    __slots__ = ("name", "w", "r")

    def __init__(self, name=""):
        self.name = name
        self.w = None
        self.r = []


class Prog:
    EPOCH = 30000

    def __init__(self, nc, es, n_dma_sems=12):
        self.nc = nc
        self.es = es
        self.engs = {"pe": nc.tensor, "act": nc.scalar, "dve": nc.vector, "pool": nc.gpsimd, "sp": nc.sync}
        self.cnt = {k: 0 for k in self.engs}
        self.sems = {k: [es.enter_context(nc.semaphore("s_" + k))] for k in self.engs}
        self.waited = {k: {} for k in self.engs}
        self.dsems = {}
        self.dcnt = {}
        self.dnext = {}
        for q in ("sp", "act", "pool"):
            self.dsems[q] = [es.enter_context(nc.semaphore("d%s%d" % (q, i))) for i in range(n_dma_sems)]
            self.dcnt[q] = [0] * n_dma_sems
            self.dnext[q] = 0
        self.ninst = 0
        self.out_toks = []

    def _wait(self, ek, tok):
        if tok is None:
            return
        sem, val = tok
        key = id(sem)
        if self.waited[ek].get(key, 0) >= val:
            return
        self.engs[ek].wait_ge(sem, val)
        self.waited[ek][key] = val

    def _same(self, ek, tok):
        if tok is None:
            return False
        sem = tok[0]
        for s_ in self.sems[ek]:
            if sem is s_:
                return True
        return False

    def _deps(self, ek, reads, writes):
        pe = (ek == "pe")
        for b in reads:
            if pe and self._same(ek, b.w):
                continue
            self._wait(ek, b.w)
        for b in writes:
            if not (pe and self._same(ek, b.w)):
                self._wait(ek, b.w)
            for t in b.r:
                if not (pe and self._same(ek, t)):
                    self._wait(ek, t)

    def _mark(self, tok, reads, writes):
        for b in reads:
            b.r.append(tok)
            if len(b.r) > 64:
                b.r = b.r[-48:]
        for b in writes:
            b.w = tok
            b.r = []

    def op(self, ek, fn, reads=(), writes=()):
        self._deps(ek, reads, writes)
        ins = fn(self.engs[ek])
        if self.cnt[ek] >= self.EPOCH:
            self.sems[ek].append(self.es.enter_context(self.nc.semaphore("s_%s_%d" % (ek, len(self.sems[ek])))))
            self.cnt[ek] = 0
        sem = self.sems[ek][-1]
        self.cnt[ek] += 1
        ins.then_inc(sem, 1)
        tok = (sem, self.cnt[ek])
        self._mark(tok, reads, writes)
        self.ninst += 1
        return tok

    def dma(self, ek, fn, reads=(), writes=(), is_out=False):
        q = ek
        i = self.dnext[q]
        self.dnext[q] = (i + 1) % len(self.dsems[q])
        sem = self.dsems[q][i]
        if self.dcnt[q][i] > 0:
            self._wait(ek, (sem, self.dcnt[q][i]))
        self._deps(ek, reads, writes)
        ins = fn(self.engs[ek])
        self.dcnt[q][i] += 16
        ins.then_inc(sem, 16)
        tok = (sem, self.dcnt[q][i])
        self._mark(tok, reads, writes)
        self.ninst += 1
        if is_out:
            self.out_toks.append(tok)
        return tok

    def all_tokens(self):
        toks = []
        for k in self.engs:
            if self.cnt[k] > 0:
                toks.append((self.sems[k][-1], self.cnt[k]))
        for q in self.dsems:
            for s, c in zip(self.dsems[q], self.dcnt[q]):
                if c > 0:
                    toks.append((s, c))
        return toks

    def barrier(self):
        toks = self.all_tokens()
        for ek in self.engs:
            for t in toks:
                self._wait(ek, t)

    def finish(self):
        toks = self.all_tokens()
        for t in toks:
            self._wait("sp", t)


class Ring:
    def __init__(self, items):
        self.items = items
        self.i = 0

    def next(self):
        it = self.items[self.i]
        self.i = (self.i + 1) % len(self.items)
        return it


def build_program():
    nc = bass.Bass("TRN2", target_bir_lowering=False)
    dt_in = lambda n, s, d=F32: nc.dram_tensor(n, s, d, kind="ExternalInput").ap()
    dt_out = lambda n, s, d=F32: nc.dram_tensor(n, s, d, kind="ExternalOutput").ap()

    x_p = dt_in("x_p", [SEQ, D])
    x_s = dt_in("x_s", [NS, D])
    c5 = dt_in("c5", [1 + NS, D])
    cache_k = dt_in("cache_k", [NPOOL * 128, 256])
    cache_v = dt_in("cache_v", [NPOOL * 128, 256])
    cache_i = dt_in("cache_i", [NPOOL, 128 * 64])
    st_conv = dt_in("st_conv", [NS * 3, D])
    st_lru = dt_in("st_lru", [NS, D])
    ptab = dt_in("ptab", [128, NS], I32)
    w_ada = dt_in("w_ada", [6, 128, 8 * 512])
    b_ada = dt_in("b_ada", [1, 3 * D])
    w_fm = dt_in("w_fm", [NFM, 128, 1024])
    w_tm = dt_in("w_tm", [NTM, 128, 4096])
    w_rr = dt_in("w_rr", [128, 2 * 2048])
    chp = dt_in("chp", [128, NCP])
    cst = dt_in("cst", [128, CS_END])
    ropeq = dt_in("ropeq", [SEQ, 256])
    ropei = dt_in("ropei", [SEQ, 128])
    ropes = dt_in("ropes", [NS, 384])
    g_fin = dt_in("g_fin", [1, D])
    idx_gb = dt_in("idx_gb", [1, 128])

    y_p = dt_out("y_p", [SEQ, D])
    y_s = dt_out("y_s", [NS, D])
    nk_p = dt_out("nk_p", [SEQ, 256])
    nv_p = dt_out("nv_p", [SEQ, 256])
    nki_p = dt_out("nki_p", [SEQ, 64])
    ncv_p = dt_out("ncv_p", [3, D])
    nlr_p = dt_out("nlr_p", [1, D])
    nk_s = dt_out("nk_s", [NS, 256])
    nv_s = dt_out("nv_s", [NS, 256])
    nki_s = dt_out("nki_s", [NS, 64])
    ncv_s = dt_out("ncv_s", [NS, 3, D])
    nlr_s = dt_out("nlr_s", [NS, D])

    wfm = nc.dram_tensor("wfm_bf", [NFM, 128, 1024], BF16, kind="Internal").ap()
    wtm = nc.dram_tensor("wtm_bf", [NTM, 128, 4096], BF16, kind="Internal").ap()

    with ExitStack() as es:
        P = Prog(nc, es)

        def sb(name, shape, dtype=F32, scope=es):
            return scope.enter_context(nc.sbuf_tensor(name, shape, dtype))

        def ring(name, n, shape, dtype=F32, scope=es):
            return Ring([(sb("%s%d" % (name, i), shape, dtype, scope), Buf(name)) for i in range(n)])

        OPQ = ["act", "dve", "pool"]

        psA = es.enter_context(nc.psum_tensor("psA", [128, 512], F32)); bpsA = Buf()
        psB = es.enter_context(nc.psum_tensor("psB", [128, 512], F32)); bpsB = Buf()
        psS0 = es.enter_context(nc.psum_tensor("psS0", [128, 512], F32)); bpsS0 = Buf()
        psS1 = es.enter_context(nc.psum_tensor("psS1", [128, 512], F32)); bpsS1 = Buf()
        psO = es.enter_context(nc.psum_tensor("psO", [128, 512], F32)); bpsO = Buf()
        psL = es.enter_context(nc.psum_tensor("psL", [128, 512], F32)); bpsL = Buf()
        psT0 = es.enter_context(nc.psum_tensor("psT0", [128, 512], F32)); bpsT0 = Buf()
        psT1 = es.enter_context(nc.psum_tensor("psT1", [128, 512], F32)); bpsT1 = Buf()
        rPS = Ring([(psA, bpsA), (psB, bpsB)])
        rPSS = Ring([(psS0, bpsS0), (psS1, bpsS1)])
        rPST = Ring([(psT0[:].bitcast(BF16), bpsT0), (psT1[:].bitcast(BF16), bpsT1)])
        rPACC = Ring([(psT0, bpsT0), (psT1, bpsT1)])

        cstt = sb("cstt", [128, CS_END]); bcst = Buf()
        chpt = sb("chpt", [128, NCP]); bchp = Buf()
        identb = sb("identb", [128, 128], BF16); bidb = Buf()
        ident4 = sb("ident4", [128, 512], BF16); bid4 = Buf()
        onesb = sb("onesb", [128, 128], BF16); bonesb = Buf()
        onesf = sb("onesf", [128, 128]); bonesf = Buf()
        clam = sb("clam", [128, 8]); bclam = Buf()
        gate_bc = sb("gate_bc", [128, D]); bgate = Buf()
        gfin_bc = sb("gfin_bc", [128, D]); bgfin = Buf()
        idxgb_bc = sb("idxgb_bc", [128, 128]); bidxgb = Buf()
        wrr = sb("wrr", [128, 4096], BF16); bwrr = Buf()
        A_p = sb("A_p", [128, 8]); bAp = Buf()
        B_p = sb("B_p", [128, 8]); bBp = Buf()
        mT = sb("mT", [128, 24 * 5]); bmT = Buf()
        silucT = sb("silucT", [128, 8 * 5], BF16); bsil = Buf()
        identf = cstt[:, CS_ID:CS_ID + 128]
        rFM = ring("wfmr", 4, [128, 1024], BF16)
        rTM = ring("wtmr", 2, [128, 4096], BF16)
        sS = es.enter_context(ExitStack())
        gate_s = sb("gate_s", [NS, D], F32, sS); bgs = Buf()

        P.dma("sp", lambda e: e.dma_start(out=cstt[:], in_=cst[:, :]), writes=[bcst])
        P.dma("sp", lambda e: e.dma_start(out=chpt[:], in_=chp[:, :]), writes=[bchp])
        P.dma("sp", lambda e: e.dma_start(out=gfin_bc[:], in_=g_fin[0:1, :].broadcast_to([128, D])), writes=[bgfin])
        P.dma("sp", lambda e: e.dma_start(out=gate_bc[:], in_=b_ada[0:1, 2 * D:3 * D].broadcast_to([128, D])), writes=[bgate])
        P.dma("sp", lambda e: e.dma_start(out=gate_s[:], in_=b_ada[0:1, 2 * D:3 * D].broadcast_to([NS, D])), writes=[bgs])
        P.dma("sp", lambda e: e.dma_start(out=idxgb_bc[:], in_=idx_gb[0:1, :].broadcast_to([128, 128])), writes=[bidxgb])
        P.op("dve", lambda e: e.tensor_copy(out=identb[:], in_=identf), reads=[bcst], writes=[bidb])
        for r4 in range(4):
            P.op("pool", lambda e: e.tensor_copy(out=ident4[:, r4 * 128:(r4 + 1) * 128], in_=identf), reads=[bcst], writes=[bid4])
        P.op("pool", lambda e: e.memset(onesb[:], 1.0), writes=[bonesb])
        P.op("pool", lambda e: e.memset(onesf[:], 1.0), writes=[bonesf])
        P.op("act", lambda e: e.activation(out=clam[:], in_=chpt[:, CP_LAM:CP_LAM + 8], func=AF.Exp, scale=-1.0), reads=[bchp], writes=[bclam])
        P.op("act", lambda e: e.activation(out=clam[:], in_=clam[:], func=AF.Ln, bias=1.0), reads=[bclam], writes=[bclam])
        P.op("dve", lambda e: e.tensor_scalar(out=clam[:], in0=clam[:], scalar1=-8.0, scalar2=None, op0=ALU.mult), reads=[bclam], writes=[bclam])

        with ExitStack() as s0:
            rst = ring("w0st", 2, [128, 4096], F32, s0)
            rsb = ring("w0sb", 2, [128, 4096], BF16, s0)
            k = 0
            for src, dst, n in ((w_fm, wfm, NFM // 4), (w_tm, wtm, NTM)):
                for b in range(n):
                    st, bst = rst.next()
                    sbf, bsbf = rsb.next()
                    if src is w_fm:
                        sap = src[4 * b:4 * b + 4].rearrange("n p f -> p n f")
                        dap = dst[4 * b:4 * b + 4].rearrange("n p f -> p n f")
                        tap_s = st[:].rearrange("p (n f) -> p n f", n=4)
                        tap_b = sbf[:].rearrange("p (n f) -> p n f", n=4)
                    else:
                        sap, dap, tap_s, tap_b = src[b], dst[b], st[:], sbf[:]
                    P.dma("sp", lambda e: e.dma_start(out=tap_s, in_=sap), writes=[bst])
                    ek = OPQ[k % 3]; k += 1
                    if ek == "act":
                        P.op(ek, lambda e: e.copy(out=sbf[:], in_=st[:]), reads=[bst], writes=[bsbf])
                    else:
                        P.op(ek, lambda e: e.tensor_copy(out=sbf[:], in_=st[:]), reads=[bst], writes=[bsbf])
                    P.dma("act", lambda e: e.dma_start(out=dap, in_=tap_b), reads=[bsbf])
            st, bst = rst.next()
            P.dma("sp", lambda e: e.dma_start(out=st[:], in_=w_rr[:, :]), writes=[bst])
            P.op("dve", lambda e: e.tensor_copy(out=wrr[:], in_=st[:]), reads=[bst], writes=[bwrr])

            c5t = sb("c5t", [1 + NS, D], F32, s0); bc5 = Buf()
            P.dma("sp", lambda e: e.dma_start(out=c5t[:], in_=c5[:, :]), writes=[bc5])
            P.op("act", lambda e: e.activation(out=c5t[:], in_=c5t[:], func=AF.Silu), reads=[bc5], writes=[bc5])
            for kc in range(8):
                P.op("pe", lambda e: e.transpose(out=psA[:, kc * 5:kc * 5 + 5], in_=c5t[:, kc * 128:(kc + 1) * 128], identity=identf[0:5, 0:5]),
                     reads=[bc5, bcst], writes=[bpsA])
            P.op("dve", lambda e: e.tensor_copy(out=silucT[:], in_=psA[:, 0:40]), reads=[bpsA], writes=[bsil])
            silrep = sb("silrep", [128, 8 * 128], BF16, s0); bsilrep = Buf()
            sil3 = silucT[:].rearrange("p (k t) -> p k t", t=5)
            P.op("dve", lambda e: e.tensor_copy(out=silrep[:].rearrange("p (k m) -> p k m", m=128),
                                                in_=sil3[:, :, 0:1].to_broadcast([128, 8, 128])), reads=[bsil], writes=[bsilrep])
            for blk in range(6):
                st, bst = rst.next()
                sbf, bsbf = rsb.next()
                P.dma("sp", lambda e: e.dma_start(out=st[:], in_=w_ada[blk]), writes=[bst])
                P.op(OPQ[blk % 3], (lambda e: e.copy(out=sbf[:], in_=st[:])) if blk % 3 == 0 else (lambda e: e.tensor_copy(out=sbf[:], in_=st[:])),
                     reads=[bst], writes=[bsbf])
                w3 = sbf[:].rearrange("p (k c) -> p k c", c=512)
                for q in range(4):
                    cc = blk * 4 + q
                    for kc in range(8):
                        P.op("pe", lambda e: e.matmul(psB[:, cc * 5:cc * 5 + 5], lhsT=w3[:, kc, q * 128:(q + 1) * 128], rhs=sil3[:, kc, :],
                                                      start=(kc == 0), stop=(kc == 7)), reads=[bsbf, bsil], writes=[bpsB])
                if blk >= 4:
                    hb = blk - 4
                    ps, bps = rPSS.next()
                    for kc in range(8):
                        P.op("pe", lambda e: e.matmul(ps[:, :], lhsT=silrep[:, kc * 128:(kc + 1) * 128], rhs=w3[:, kc, :],
                                                      start=(kc == 0), stop=(kc == 7)), reads=[bsbf, bsilrep], writes=[bps])
                    P.op("dve", lambda e: e.tensor_tensor(out=gate_bc[:, hb * 512:(hb + 1) * 512], in0=ps[:, :], in1=gate_bc[:, hb * 512:(hb + 1) * 512], op=ALU.add),
                         reads=[bps, bgate], writes=[bgate])
                    ps, bps = rPSS.next()
                    for kc in range(8):
                        P.op("pe", lambda e: e.matmul(ps[0:NS, :], lhsT=sil3[:, kc, 1:5], rhs=w3[:, kc, :],
                                                      start=(kc == 0), stop=(kc == 7)), reads=[bsbf, bsil], writes=[bps])
                    P.op("dve", lambda e: e.tensor_tensor(out=gate_s[:, hb * 512:(hb + 1) * 512], in0=ps[0:NS, :], in1=gate_s[:, hb * 512:(hb + 1) * 512], op=ALU.add),
                         reads=[bps, bgs], writes=[bgs])
            mT3 = mT[:].rearrange("p (c t) -> p c t", t=5)
            P.op("dve", lambda e: e.tensor_tensor(out=mT3, in0=psB[:, 0:120].rearrange("p (c t) -> p c t", t=5),
                                                  in1=chpt[:, CP_BADA:CP_BADA + 24].unsqueeze(2).to_broadcast([128, 24, 5]), op=ALU.add),
                 reads=[bpsB, bchp], writes=[bmT])
            P.op("dve", lambda e: e.scalar_tensor_tensor(out=A_p[:], in0=mT3[:, 8:16, 0], scalar=1.0, in1=chpt[:, CP_GN:CP_GN + 8], op0=ALU.add, op1=ALU.mult),
                 reads=[bmT, bchp], writes=[bAp])
            P.op("dve", lambda e: e.tensor_copy(out=B_p[:], in_=mT3[:, 0:8, 0]), reads=[bmT], writes=[bBp])
        P.barrier()


        def load_fm(idx):
            t, b = rFM.next()
            P.dma("sp", lambda e: e.dma_start(out=t[:], in_=wfm[idx]), writes=[b])
            return t[:].rearrange("p (k c) -> p k c", c=128), b

        def load_tm(idx):
            t, b = rTM.next()
            P.dma("sp", lambda e: e.dma_start(out=t[:], in_=wtm[idx]), writes=[b])
            return t[:].rearrange("p (k c) -> p k c", c=512), b

        def rope(ek2, out_ap, x_ap, cosf, sinf, H, Dh, t1, t2, reads, writes, bt1, bt2):
            hf = Dh // 2
            p = x_ap.shape[0]
            cb = cosf.unsqueeze(1).to_broadcast([p, H, Dh])
            s1 = sinf[:, 0:hf].unsqueeze(1).to_broadcast([p, H, hf])
            s2 = sinf[:, hf:Dh].unsqueeze(1).to_broadcast([p, H, hf])
            P.op("dve", lambda e: e.tensor_tensor(out=t1, in0=x_ap, in1=cb, op=ALU.mult), reads=reads, writes=[bt1])
            P.op("dve", lambda e: e.tensor_tensor(out=t2[:, :, 0:hf], in0=x_ap[:, :, hf:Dh], in1=s1, op=ALU.mult), reads=reads, writes=[bt2])
            P.op("dve", lambda e: e.tensor_tensor(out=t2[:, :, hf:Dh], in0=x_ap[:, :, 0:hf], in1=s2, op=ALU.mult), reads=reads, writes=[bt2])
            return P.op(ek2, lambda e: e.tensor_tensor(out=out_ap, in0=t1, in1=t2, op=ALU.add), reads=[bt1, bt2], writes=writes)

        if DO_SAMPLE:
            sample_phase(nc, P, locals())
            P.barrier()
        sS.close()

        if STOP != 'p0':
            prompt_phase(nc, P, locals())
        P.finish()
        print("ninst", P.ninst, {k: (len(P.sems[k]) - 1) * P.EPOCH + P.cnt[k] for k in P.cnt})
    return nc


def prompt_phase(nc, P, G):
    es = G["es"]; sb = G["sb"]; ring = G["ring"]
    x_p = G["x_p"]; y_p = G["y_p"]; nk_p = G["nk_p"]; nv_p = G["nv_p"]; nki_p = G["nki_p"]; ncv_p = G["ncv_p"]; nlr_p = G["nlr_p"]
    ropeq = G["ropeq"]; ropei = G["ropei"]
    rPS = G["rPS"]; rPSS = G["rPSS"]; rPST = G["rPST"]; rPACC = G["rPACC"]
    psO = G["psO"]; bpsO = G["bpsO"]; psL = G["psL"]; bpsL = G["bpsL"]
    cstt = G["cstt"]; bcst = G["bcst"]; chpt = G["chpt"]; bchp = G["bchp"]
    identb = G["identb"]; bidb = G["bidb"]; ident4 = G["ident4"]; bid4 = G["bid4"]
    onesb = G["onesb"]; bonesb = G["bonesb"]; identf = G["identf"]
    clam = G["clam"]; bclam = G["bclam"]
    gate_bc = G["gate_bc"]; bgate = G["bgate"]; gfin_bc = G["gfin_bc"]; bgfin = G["bgfin"]
    idxgb_bc = G["idxgb_bc"]; bidxgb = G["bidxgb"]
    wrr = G["wrr"]; bwrr = G["bwrr"]; A_p = G["A_p"]; bAp = G["bAp"]; B_p = G["B_p"]; bBp = G["bBp"]
    load_fm = G["load_fm"]; load_tm = G["load_tm"]; rope = G["rope"]
    caus = cstt[:, CS_CAUS:CS_CAUS + 128]
    wrr5 = wrr[:].rearrange("p (a n k c) -> p a n k c", a=2, n=4, k=2)

    KT = sb("KT", [128, 2 * SEQ], BF16); KT3 = KT[:].rearrange("p (g t) -> p g t", g=2)
    Vres = sb("Vres", [128, 32 * 256], BF16); V4 = Vres[:].rearrange("p (i g d) -> p i g d", i=32, g=2)
    kiT = sb("kiT", [128, SEQ], BF16)
    bKV = [Buf() for _ in range(32)]
    hist = sb("hist", [128, 8 * 3]); bhist = Buf(); hist3 = hist[:].rearrange("p (c j) -> p c j", j=3)
    hprev = sb("hprev", [128, 8]); bhprev = Buf()
    P.op("pool", lambda e: e.memset(hist[:], 0.0), writes=[bhist])
    P.op("pool", lambda e: e.memset(hprev[:], 0.0), writes=[bhprev])

    rX = ring("xt", 1, [128, D])
    rXh = ring("xh", 1, [128, D], BF16)
    xnT = sb("xnT", [128, 8 * CH], BF16); bxn = Buf(); xn3 = xnT[:].rearrange("p (k t) -> p k t", k=8)
    rq = sb("rq", [128, 4 * 256]); brq = Buf(); rq3 = rq[:].rearrange("p (t c) -> p t c", t=4)
    ri = sb("ri", [128, 4 * 128]); bri = Buf(); ri3 = ri[:].rearrange("p (t c) -> p t c", t=4)
    rQr = ring("qrot", 1, [128, 512], BF16)
    rKr = ring("krot", 1, [128, 256]); rVf = ring("vf", 1, [128, 256]); rKi = ring("kio", 1, [128, 64])
    rKb = ring("kb16", 1, [128, 256], BF16); rKi2 = ring("ki2", 1, [128, 128], BF16)
    qT = sb("qT", [128, 4 * 1024], BF16); bqT = [Buf() for _ in range(4)]; qT4 = qT[:].rearrange("p (t h q) -> p t h q", t=4, h=8)
    qiT = sb("qiT", [128, 4 * 512], BF16); bqiT = [Buf() for _ in range(4)]; qiT4 = qiT[:].rearrange("p (t h q) -> p t h q", t=4, h=4)
    wS = sb("wS", [128, 4 * 8]); bwS = [Buf() for _ in range(4)]; wS3 = wS[:].rearrange("p (t h) -> p t h", t=4)
    awS = sb("awS", [128, 4 * 8]); awS3 = awS[:].rearrange("p (t h) -> p t h", t=4)
    sgS = sb("sgS", [128, 4 * 8]); sgS3 = sgS[:].rearrange("p (t h) -> p t h", t=4)
    rDg = ring("diagS", 2, [128, 8 * 128], BF16)
    Wt = sb("bs_W", [128, 32]); bWt = Buf()
    W2t = sb("bs_W2", [128, 32]); bW2t = Buf()
    pow2 = cstt[:, CS_P2:CS_P2 + N_BISECT]
    sm = sb("smallst", [128, 16]); bsm = Buf()
    xa = sb("xa", [128, 2 * 515]); bxa = Buf(); xa3 = xa[:].rearrange("p (c t) -> p c t", c=2)
    xc = sb("xc", [128, 2 * CH]); bxc = Buf(); xc3 = xc[:].rearrange("p (c t) -> p c t", c=2)
    xcb = sb("xcb", [128, 2 * CH], BF16); bxcb = Buf(); xcb3 = xcb[:].rearrange("p (c t) -> p c t", c=2)
    junkb = xcb; bjunk = bxcb
    g_r = sb("g_r", [128, CH]); b_r = Buf()
    g_i = sb("g_i", [128, CH]); b_i = Buf()
    g_a = sb("g_a", [128, CH]); b_a = Buf()
    g_t = sb("g_t", [128, CH]); b_t = Buf()
    g_u = g_i; b_u = b_i
    g_h = sb("g_h", [128, CH]); b_h = Buf()
    rT1 = Ring([(g_a, b_a)]); rT2 = Ring([(g_t, b_t)])
    g_sg = g_r; b_sg = b_r
    actT = sb("actT", [128, 8 * CH], BF16); bact = [Buf() for _ in range(8)]; act3 = actT[:].rearrange("p (k t) -> p k t", k=8)
    mTt = sb("mTt", [128, 8 * CH], BF16); bmm = [Buf() for _ in range(8)]; m3 = mTt[:].rearrange("p (k t) -> p k t", k=8)
    rSg = Ring([(g_r, b_r), (g_i, b_i)])
    sc = sb("sc", [128, SEQ]); bsc = Buf()
    rMB = ring("MB", 2, [128, SEQ], BF16)
    junk8 = sb("junk8", [128, SEQ], mybir.dt.float8e4); bj8 = Buf()
    rR = ring("Rr", 2, [128, 512], BF16)
    rPT = ring("PTr", 2, [128, 512], BF16)
    rl = g_h; brl = b_h
    hres = sb("hres", [128, D]); bhres = Buf()
    yo = hres; byo = bhres
    lo = sb("bs_lo", [128, 1]); blo = Buf()
    wd = sb("bs_w", [128, 1]); bwd = Buf()
    mid = sb("bs_mid", [128, 1]); bmid = Buf()
    cnt = sb("bs_cnt", [128, 1]); bcnt = Buf()
    cond = sb("bs_cond", [128, 1]); bcond = Buf()
    otr = hres; botr = bhres

    for c in range(NCHUNK_RUN):
        t0 = c * CH
        P.dma("sp", lambda e: e.dma_start(out=rq3, in_=ropeq[t0:t0 + CH, :].rearrange("(t p) c -> p t c", p=128)), writes=[brq])
        P.dma("sp", lambda e: e.dma_start(out=ri3, in_=ropei[t0:t0 + CH, :].rearrange("(t p) c -> p t c", p=128)), writes=[bri])
        for tt in range(4):
            xt, bxt = rX.next()
            xh, bxh = rXh.next()
            P.dma("sp", lambda e: e.dma_start(out=xt[:], in_=x_p[t0 + tt * 128:t0 + (tt + 1) * 128, :]), writes=[bxt])
            P.op("act", lambda e: e.activation(out=junkb[:], in_=xt[:], func=AF.Square, accum_out=sm[:, 0:1]), reads=[bxt], writes=[bjunk, bsm])
            P.op("act", lambda e: e.activation(out=sm[:, 1:2], in_=sm[:, 0:1], func=AF.Sqrt, scale=1.0 / D, bias=EPS), reads=[bsm], writes=[bsm])
            P.op("dve", lambda e: e.reciprocal(out=sm[:, 2:3], in_=sm[:, 1:2]), reads=[bsm], writes=[bsm])
            P.op("dve", lambda e: e.tensor_scalar(out=xh[:], in0=xt[:], scalar1=sm[:, 2:3], scalar2=None, op0=ALU.mult), reads=[bxt, bsm], writes=[bxh])
            pt, bpt = rPST.next()
            for kc in range(8):
                P.op("pe", lambda e: e.transpose(out=pt[:, kc * 128:(kc + 1) * 128], in_=xh[:, kc * 128:(kc + 1) * 128], identity=identb[:]),
                     reads=[bxh, bidb], writes=[bpt])
            for kc in range(8):
                P.op("dve", lambda e: e.tensor_scalar(out=xn3[:, kc, tt * 128:(tt + 1) * 128], in0=pt[:, kc * 128:(kc + 1) * 128],
                                                      scalar1=A_p[:, kc:kc + 1], scalar2=B_p[:, kc:kc + 1], op0=ALU.mult, op1=ALU.add),
                     reads=[bpt, bAp, bBp], writes=[bxn])

        if STOP == 'p1':
            continue
        def p2_mm(blk, tt, w3, bw):
            ncol = 72 if blk == TM_KW else 512
            ps, bps = rPS.next()
            for kc in range(8):
                P.op("pe", lambda e: e.matmul(ps[:, 0:ncol], lhsT=xn3[:, kc, tt * 128:(tt + 1) * 128], rhs=w3[:, kc, 0:ncol],
                                              start=(kc == 0), stop=(kc == 7)), reads=[bxn, bw], writes=[bps])
            return ps, bps

        def p2_post(blk, tt, ps, bps):
            i = c * 4 + tt
            r0 = t0 + tt * 128
            t1, bt1 = rT1.next(); t2, bt2 = rT2.next()
            if blk in (TM_Q0, TM_Q1):
                qr, bqr = rQr.next()
                rope("pool", qr[:].rearrange("p (h d) -> p h d", h=4), ps[:, :].rearrange("p (h d) -> p h d", h=4),
                     rq3[:, tt, 0:128], rq3[:, tt, 128:256], 4, 128,
                     t1[:].rearrange("p (h d) -> p h d", h=4), t2[:].rearrange("p (h d) -> p h d", h=4),
                     [bps, brq], [bqr], bt1, bt2)
                pt, bpt = rPST.next()
                for h in range(4):
                    P.op("pe", lambda e: e.transpose(out=pt[:, h * 128:(h + 1) * 128], in_=qr[:, h * 128:(h + 1) * 128], identity=identb[:]),
                         reads=[bqr, bidb], writes=[bpt])
                h0 = 4 * (blk - TM_Q0)
                P.op("act", lambda e: e.copy(out=qT4[:, tt, h0:h0 + 4, :], in_=pt[:, 0:512].rearrange("p (h q) -> p h q", h=4)),
                     reads=[bpt], writes=[bqT[tt]])
            elif blk == TM_KV:
                kr, bkr = rKr.next(); vf, bvf = rVf.next(); kb, bkb = rKb.next()
                rope("pool", kr[:].rearrange("p (h d) -> p h d", h=2), ps[:, 0:256].rearrange("p (h d) -> p h d", h=2),
                     rq3[:, tt, 0:128], rq3[:, tt, 128:256], 2, 128,
                     t1[:, 0:256].rearrange("p (h d) -> p h d", h=2), t2[:, 0:256].rearrange("p (h d) -> p h d", h=2),
                     [bps, brq], [bkr], bt1, bt2)
                P.dma("act", lambda e: e.dma_start(out=nk_p[r0:r0 + 128, :], in_=kr[:]), reads=[bkr], is_out=True)
                P.op("act", lambda e: e.copy(out=vf[:], in_=ps[:, 256:512]), reads=[bps], writes=[bvf])
                P.dma("act", lambda e: e.dma_start(out=nv_p[r0:r0 + 128, :], in_=vf[:]), reads=[bvf], is_out=True)
                P.op("act", lambda e: e.copy(out=V4[:, i, :, :], in_=ps[:, 256:512].rearrange("p (g d) -> p g d", g=2)), reads=[bps], writes=[bKV[i]])
                P.op("pool", lambda e: e.tensor_copy(out=kb[:], in_=kr[:]), reads=[bkr], writes=[bkb])
                pt, bpt = rPST.next()
                for g in range(2):
                    P.op("pe", lambda e: e.transpose(out=pt[:, g * 128:(g + 1) * 128], in_=kb[:, g * 128:(g + 1) * 128], identity=identb[:]),
                         reads=[bkb, bidb], writes=[bpt])
                P.op("act", lambda e: e.copy(out=KT3[:, :, r0:r0 + 128], in_=pt[:, 0:256].rearrange("p (g q) -> p g q", g=2)),
                     reads=[bpt], writes=[bKV[i]])
            elif blk == TM_QI:
                qr, bqr = rQr.next()
                rope("pool", qr[:].rearrange("p (h d) -> p h d", h=8), ps[:, :].rearrange("p (h d) -> p h d", h=8),
                     ri3[:, tt, 0:64], ri3[:, tt, 64:128], 8, 64,
                     t1[:].rearrange("p (h d) -> p h d", h=8), t2[:].rearrange("p (h d) -> p h d", h=8),
                     [bps, bri], [bqr], bt1, bt2)
                pt, bpt = rPST.next()
                for hp in range(4):
                    P.op("pe", lambda e: e.transpose(out=pt[:, hp * 128:(hp + 1) * 128], in_=qr[:, hp * 128:(hp + 1) * 128], identity=identb[:]),
                         reads=[bqr, bidb], writes=[bpt])
                P.op("act", lambda e: e.copy(out=qiT4[:, tt, :, :], in_=pt[:, 0:512].rearrange("p (h q) -> p h q", h=4)),
                     reads=[bpt], writes=[bqiT[tt]])
            else:
                kio, bkio = rKi.next(); ki2, bki2 = rKi2.next()
                P.op("dve", lambda e: e.tensor_scalar(out=wS3[:, tt, :], in0=ps[:, 64:72], scalar1=IDX_W_SCALE, scalar2=None, op0=ALU.mult),
                     reads=[bps], writes=[bwS[tt]])
                P.op("dve", lambda e: e.tensor_scalar(out=sgS3[:, tt, :], in0=wS3[:, tt, :], scalar1=0.0, scalar2=0.5, op0=ALU.is_ge, op1=ALU.subtract),
                     reads=[bwS[tt]], writes=[bwS[tt]])
                P.op("dve", lambda e: e.scalar_tensor_tensor(out=awS3[:, tt, :], in0=wS3[:, tt, :], scalar=4.0, in1=sgS3[:, tt, :], op0=ALU.mult, op1=ALU.mult),
                     reads=[bwS[tt]], writes=[bwS[tt]])
                P.op("dve", lambda e: e.tensor_reduce(out=sm[:, 4:5], in_=ps[:, 0:64], axis=AX.X, op=ALU.add), reads=[bps], writes=[bsm])
                P.op("dve", lambda e: e.tensor_scalar(out=sm[:, 5:6], in0=sm[:, 4:5], scalar1=-1.0 / 64, scalar2=None, op0=ALU.mult), reads=[bsm], writes=[bsm])
                P.op("dve", lambda e: e.tensor_scalar(out=t1[:, 0:64], in0=ps[:, 0:64], scalar1=sm[:, 5:6], scalar2=None, op0=ALU.add),
                     reads=[bps, bsm], writes=[bt1])
                P.op("act", lambda e: e.activation(out=t2[:, 0:64], in_=t1[:, 0:64], func=AF.Square, accum_out=sm[:, 6:7]), reads=[bt1], writes=[bt2, bsm])
                P.op("act", lambda e: e.activation(out=sm[:, 7:8], in_=sm[:, 6:7], func=AF.Sqrt, scale=1.0 / 64, bias=EPS), reads=[bsm], writes=[bsm])
                P.op("dve", lambda e: e.reciprocal(out=sm[:, 8:9], in_=sm[:, 7:8]), reads=[bsm], writes=[bsm])
                P.op("dve", lambda e: e.scalar_tensor_tensor(out=t1[:, 64:128], in0=t1[:, 0:64], scalar=sm[:, 8:9], in1=idxgb_bc[:, 0:64], op0=ALU.mult, op1=ALU.mult),
                     reads=[bt1, bsm, bidxgb], writes=[bt1])
                P.op("dve", lambda e: e.tensor_tensor(out=t1[:, 128:192], in0=t1[:, 64:128], in1=idxgb_bc[:, 64:128], op=ALU.add),
                     reads=[bt1, bidxgb], writes=[bt1])
                rope("pool", kio[:].unsqueeze(1), t1[:, 128:192].unsqueeze(1), ri3[:, tt, 0:64], ri3[:, tt, 64:128], 1, 64,
                     t2[:, 64:128].unsqueeze(1), t2[:, 128:192].unsqueeze(1), [bt1, bri], [bkio], bt2, bt2)
                P.dma("act", lambda e: e.dma_start(out=nki_p[r0:r0 + 128, :], in_=kio[:]), reads=[bkio], is_out=True)
                P.op("pool", lambda e: e.tensor_copy(out=ki2[:].rearrange("p (a d) -> p a d", a=2), in_=kio[:].unsqueeze(1).to_broadcast([128, 2, 64])),
                     reads=[bkio], writes=[bki2])
                pt, bpt = rPST.next()
                P.op("pe", lambda e: e.transpose(out=pt[:, 0:128], in_=ki2[:], identity=identb[:]), reads=[bki2, bidb], writes=[bpt])
                P.op("act", lambda e: e.copy(out=kiT[:, r0:r0 + 128], in_=pt[:, 0:128]), reads=[bpt], writes=[bKV[i]])


        pend = None
        for blk in (TM_Q0, TM_Q1, TM_KV, TM_QI, TM_KW):
            w3, bw = load_tm(blk)
            for tt in range(4):
                ps, bps = p2_mm(blk, tt, w3, bw)
                if pend is not None:
                    p2_post(*pend)
                pend = (blk, tt, ps, bps)
        p2_post(*pend)
        if STOP == 'p2':
            continue
        for n in range(4):
            for c2 in range(2):
                cc = 2 * n + c2
                w3, bw = load_fm(FM_XA + cc)
                ps, bps = rPS.next()
                for kc in range(8):
                    P.op("pe", lambda e: e.matmul(ps[:, :], lhsT=w3[:, kc, :], rhs=xn3[:, kc, :], start=(kc == 0), stop=(kc == 7)),
                         reads=[bw, bxn], writes=[bps])
                P.op("pool", lambda e: e.tensor_copy(out=xa3[:, c2, 0:3], in_=hist3[:, cc, :]), reads=[bhist], writes=[bxa])
                P.op("act", lambda e: e.copy(out=xa3[:, c2, 3:515], in_=ps[:, :]), reads=[bps], writes=[bxa])
                P.op("pool", lambda e: e.tensor_copy(out=hist3[:, cc, :], in_=xa3[:, c2, 512:515]), reads=[bxa], writes=[bhist])
                wc = lambda j: chpt[:, CP_WCONV + j * 8 + cc:CP_WCONV + j * 8 + cc + 1]
                P.op("dve", lambda e: e.tensor_scalar(out=xc3[:, c2, :], in0=xa3[:, c2, 3:515], scalar1=wc(3), scalar2=chpt[:, CP_BCONV + cc:CP_BCONV + cc + 1],
                                                      op0=ALU.mult, op1=ALU.add), reads=[bxa, bchp], writes=[bxc])
                for j in range(3):
                    P.op("dve", lambda e: e.scalar_tensor_tensor(out=xc3[:, c2, :], in0=xa3[:, c2, j:j + 512], scalar=wc(j), in1=xc3[:, c2, :],
                                                                 op0=ALU.mult, op1=ALU.add), reads=[bxa, bxc, bchp], writes=[bxc])
                P.op("pool", lambda e: e.tensor_copy(out=xcb3[:, c2, :], in_=xc3[:, c2, :]), reads=[bxc], writes=[bxcb])
            for c2 in range(2):
                cc = 2 * n + c2
                for which, dst, bdst, bias0 in ((0, g_r, b_r, CP_BRA), (1, g_i, b_i, CP_BRX)):
                    ps, bps = rPS.next()
                    for k2 in range(2):
                        P.op("pe", lambda e: e.matmul(ps[:, :], lhsT=wrr5[:, which, n, k2, c2 * 128:(c2 + 1) * 128], rhs=xcb3[:, k2, :],
                                                      start=(k2 == 0), stop=(k2 == 1)), reads=[bwrr, bxcb], writes=[bps])
                    P.op("act", lambda e: e.activation(out=dst[:], in_=ps[:, :], func=AF.Sigmoid, bias=chpt[:, bias0 + cc:bias0 + cc + 1]),
                         reads=[bps, bchp], writes=[bdst])
                P.op("act", lambda e: e.activation(out=g_a[:], in_=g_r[:], func=AF.Exp, scale=clam[:, cc:cc + 1]), reads=[b_r, bclam], writes=[b_a])
                P.op("pool", lambda e: e.tensor_tensor(out=g_t[:], in0=g_a[:], in1=g_a[:], op=ALU.mult), reads=[b_a], writes=[b_t])
                P.op("pool", lambda e: e.tensor_scalar(out=g_t[:], in0=g_t[:], scalar1=-1.0, scalar2=1.0, op0=ALU.mult, op1=ALU.add), reads=[b_t], writes=[b_t])
                P.op("act", lambda e: e.activation(out=g_t[:], in_=g_t[:], func=AF.Sqrt), reads=[b_t], writes=[b_t])
                P.op("pool", lambda e: e.tensor_tensor(out=g_u[:], in0=g_i[:], in1=xc3[:, c2, :], op=ALU.mult), reads=[b_i, bxc], writes=[b_u])
                P.op("pool", lambda e: e.tensor_tensor(out=g_u[:], in0=g_u[:], in1=g_t[:], op=ALU.mult), reads=[b_u, b_t], writes=[b_u])
                P.op("dve", lambda e: e.tensor_tensor_scan(out=g_h[:], data0=g_a[:], data1=g_u[:], initial=hprev[:, cc:cc + 1], op0=ALU.mult, op1=ALU.add),
                     reads=[b_a, b_u, bhprev], writes=[b_h])
                P.op("pool", lambda e: e.tensor_copy(out=hprev[:, cc:cc + 1], in_=g_h[:, CH - 1:CH]), reads=[b_h], writes=[bhprev])
                w3, bw = load_fm(FM_GA + cc)
                ps, bps = rPS.next()
                for kc in range(8):
                    P.op("pe", lambda e: e.matmul(ps[:, :], lhsT=w3[:, kc, :], rhs=xn3[:, kc, :], start=(kc == 0), stop=(kc == 7)),
                         reads=[bw, bxn], writes=[bps])
                P.op("act", lambda e: e.activation(out=g_sg[:], in_=ps[:, :], func=AF.Silu), reads=[bps], writes=[b_sg])
                P.op("pool", lambda e: e.tensor_tensor(out=act3[:, cc, :], in0=g_h[:], in1=g_sg[:], op=ALU.mult), reads=[b_h, b_sg], writes=[bact[cc]])
        if c == NCHUNK_RUN - 1:
            for src3, nrow, dst in ((hist3, 3, ncv_p), (hprev[:].unsqueeze(2), 1, nlr_p)):
                for hb in range(2):
                    ps, bps = rPS.next()
                    for c4 in range(4):
                        cc = hb * 4 + c4
                        P.op("pe", lambda e: e.transpose(out=ps[0:nrow, c4 * 128:(c4 + 1) * 128], in_=src3[:, cc, :], identity=identf),
                             reads=[bhist, bhprev, bcst], writes=[bps])
                    P.op("act", lambda e: e.copy(out=otr[0:nrow, hb * 512:(hb + 1) * 512], in_=ps[0:nrow, :]), reads=[bps], writes=[botr])
                P.dma("act", lambda e: e.dma_start(out=dst[:, :], in_=otr[0:nrow, :]), reads=[botr], is_out=True)
        if STOP == 'p3':
            continue
        for cc in range(8):
            w3, bw = load_fm(FM_MA + cc)
            ps, bps = rPS.next()
            for kc in range(8):
                P.op("pe", lambda e: e.matmul(ps[:, :], lhsT=w3[:, kc, :], rhs=xn3[:, kc, :], start=(kc == 0), stop=(kc == 7)),
                     reads=[bw, bxn], writes=[bps])
            sg, bsg = rSg.next()
            P.op("act", lambda e: e.activation(out=sg[:], in_=ps[:, :], func=AF.Sigmoid), reads=[bps], writes=[bsg])
            w3, bw = load_fm(FM_PA + cc)
            ps, bps = rPS.next()
            for kc in range(8):
                P.op("pe", lambda e: e.matmul(ps[:, :], lhsT=w3[:, kc, :], rhs=act3[:, kc, :], start=(kc == 0), stop=(kc == 7)),
                     reads=[bw, bact[kc]], writes=[bps])
            P.op("dve", lambda e: e.tensor_tensor(out=m3[:, cc, :], in0=ps[:, :], in1=sg[:], op=ALU.mult), reads=[bps, bsg], writes=[bmm[cc]])

        if STOP == 'p4':
            continue
        def stage_A(tt):
            i = c * 4 + tt
            nk = (i + 1) * 128
            dg, bdg = rDg.next()
            dg3 = dg[:].rearrange("p (h q) -> p h q", h=8)
            P.op("pool", lambda e: e.tensor_tensor(out=dg3, in0=identb[:].unsqueeze(1).to_broadcast([128, 8, 128]),
                                                   in1=sgS3[:, tt, :].unsqueeze(2).to_broadcast([128, 8, 128]), op=ALU.mult),
                 reads=[bidb, bwS[tt]], writes=[bdg])
            pendA = None

            def accA(kb, h, k0, cols, R, bR, pacc, bpacc):
                P.op("pe", lambda e: e.matmul(pacc[:, 0:cols], lhsT=dg3[:, h, :], rhs=R[:, 0:cols], start=(h == 0), stop=(h == 7)),
                     reads=[bdg, bR], writes=[bpacc])
                if h == 7:
                    P.op("act", lambda e: e.copy(out=sc[:, k0:k0 + cols], in_=pacc[:, 0:cols]), reads=[bpacc], writes=[bsc])

            for kb in range((nk + 511) // 512):
                k0 = kb * 512
                cols = min(512, nk - k0)
                pacc, bpacc = rPACC.next()
                for h in range(8):
                    hp, h2 = h // 2, h % 2
                    ps, bps = rPS.next()
                    P.op("pe", lambda e: e.matmul(ps[:, 0:cols], lhsT=qiT4[64 * h2:64 * h2 + 64, tt, hp, :], rhs=kiT[64 * h2:64 * h2 + 64, k0:k0 + cols],
                                                  start=True, stop=True), reads=[bqiT[tt]] + bKV[k0 // 128:(k0 + cols) // 128], writes=[bps])
                    R, bR = rR.next()
                    P.op("act", lambda e: e.activation(out=R[:, 0:cols], in_=ps[:, 0:cols], func=AF.Relu, scale=awS3[:, tt, h:h + 1]),
                         reads=[bps, bwS[tt]], writes=[bR])
                    if pendA is not None:
                        accA(*pendA)
                    pendA = (kb, h, k0, cols, R, bR, pacc, bpacc)
            accA(*pendA)
            P.op("dve", lambda e: e.tensor_tensor(out=sc[:, i * 128:nk], in0=sc[:, i * 128:nk], in1=caus, op=ALU.add), reads=[bsc, bcst], writes=[bsc])

        def stage_B(tt):
            i = c * 4 + tt
            nk = (i + 1) * 128
            MB, bMB = rMB.next()
            NB = N_BISECT
            if i >= 2:
                P.op("dve", lambda e: e.tensor_reduce(out=mid[:], in_=sc[:, 0:nk], axis=AX.X, op=ALU.max), reads=[bsc], writes=[bmid])
                P.op("dve", lambda e: e.tensor_reduce(out=lo[:], in_=sc[:, 0:i * 128], axis=AX.X, op=ALU.min), reads=[bsc], writes=[blo])
                P.op("dve", lambda e: e.tensor_tensor(out=wd[:], in0=mid[:], in1=lo[:], op=ALU.subtract), reads=[bmid, blo], writes=[bwd])
                P.op("dve", lambda e: e.tensor_scalar(out=Wt[:, 0:NB], in0=pow2, scalar1=wd[:], scalar2=None, op0=ALU.mult), reads=[bcst, bwd], writes=[bWt])
                P.op("dve", lambda e: e.tensor_scalar(out=W2t[:, 0:NB], in0=pow2, scalar1=wd[:], scalar2=2.0, op0=ALU.mult, op1=ALU.mult), reads=[bcst, bwd], writes=[bW2t])
                P.op("dve", lambda e: e.tensor_tensor(out=mid[:], in0=lo[:], in1=Wt[:, 0:1], op=ALU.add), reads=[blo, bWt], writes=[bmid])
                for k in range(NB):
                    P.op("dve", lambda e: e.tensor_scalar(out=junk8[:, 0:nk], in0=sc[:, 0:nk], scalar1=mid[:], scalar2=None, op0=ALU.is_ge, op1=ALU.add,
                                                          accum_out=cnt[:], saturate=False), reads=[bsc, bmid], writes=[bj8, bcnt])
                    if k < NB - 1:
                        P.op("dve", lambda e: e.scalar_tensor_tensor(out=cond[:], in0=cnt[:], scalar=float(TOPK) - 0.5, in1=W2t[:, k + 1:k + 2], op0=ALU.is_ge, op1=ALU.mult),
                             reads=[bcnt, bW2t], writes=[bcond])
                        P.op("dve", lambda e: e.scalar_tensor_tensor(out=mid[:], in0=cond[:], scalar=Wt[:, k + 1:k + 2], in1=mid[:], op0=ALU.subtract, op1=ALU.add),
                             reads=[bcond, bWt, bmid], writes=[bmid])
                    else:
                        P.op("dve", lambda e: e.scalar_tensor_tensor(out=cond[:], in0=cnt[:], scalar=float(TOPK) - 0.5, in1=Wt[:, k:k + 1], op0=ALU.is_ge, op1=ALU.mult),
                             reads=[bcnt, bWt], writes=[bcond])
                        P.op("dve", lambda e: e.scalar_tensor_tensor(out=lo[:], in0=cond[:], scalar=Wt[:, k:k + 1], in1=mid[:], op0=ALU.subtract, op1=ALU.add),
                             reads=[bcond, bWt, bmid], writes=[blo])
                P.op("dve", lambda e: e.tensor_scalar(out=MB[:, 0:nk], in0=sc[:, 0:nk], scalar1=lo[:], scalar2=NEG, op0=ALU.is_lt, op1=ALU.mult),
                     reads=[bsc, blo], writes=[bMB])
            else:
                P.op("dve", lambda e: e.tensor_scalar(out=MB[:, 0:nk], in0=sc[:, 0:nk], scalar1=-1e29, scalar2=NEG, op0=ALU.is_lt, op1=ALU.mult),
                     reads=[bsc], writes=[bMB])
            return MB, bMB

        def stage_C(tt, MB, bMB):
            i = c * 4 + tt
            for g in range(2):
                def S_(j):
                    ps, bps = rPSS.next()
                    P.op("pe", lambda e: e.matmul(ps[:, :], lhsT=KT3[:, g, j * 128:(j + 1) * 128], rhs=qT4[:, tt, 4 * g:4 * g + 4, :],
                                                  start=True, stop=False), reads=[bKV[j], bqT[tt]], writes=[bps])
                    P.op("pe", lambda e: e.matmul(ps[:, :], lhsT=MB[:, j * 128:(j + 1) * 128], rhs=ident4[:], start=False, stop=True),
                         reads=[bMB, bid4], writes=[bps])
                    return ps, bps
                nxt = S_(0)
                for j in range(i + 1):
                    ps, bps = nxt
                    if j + 1 <= i:
                        nxt = S_(j + 1)
                    PT, bPT = rPT.next()
                    P.op("act", lambda e: e.activation(out=PT[:], in_=ps[:, :], func=AF.Exp, scale=ATT_SCALE), reads=[bps], writes=[bPT])
                    P.op("pe", lambda e: e.matmul(psO[:, :], lhsT=V4[:, j, g, :], rhs=PT[:], start=(j == 0), stop=(j == i)), reads=[bKV[j], bPT], writes=[bpsO])
                    P.op("pe", lambda e: e.matmul(psL[:, :], lhsT=onesb[:], rhs=PT[:], start=(j == 0), stop=(j == i)), reads=[bonesb, bPT], writes=[bpsL])
                P.op("dve", lambda e: e.reciprocal(out=rl[:], in_=psL[:, :]), reads=[bpsL], writes=[brl])
                P.op("dve", lambda e: e.tensor_tensor(out=act3[:, 4 * g:4 * g + 4, tt * 128:(tt + 1) * 128], in0=psO[:, :].rearrange("p (h q) -> p h q", h=4),
                                                      in1=rl[:].rearrange("p (h q) -> p h q", h=4), op=ALU.mult),
                     reads=[bpsO, brl], writes=[bact[4 * g + hh] for hh in range(4)])

        pend = None
        for tt in range(4):
            stage_A(tt)
            mb = stage_B(tt)
            if pend is not None:
                stage_C(*pend)
            pend = (tt, mb[0], mb[1])
        stage_C(*pend)
        for cc in range(8):
            w3, bw = load_fm(FM_GB + cc)
            ps, bps = rPS.next()
            for kc in range(8):
                P.op("pe", lambda e: e.matmul(ps[:, :], lhsT=w3[:, kc, :], rhs=xn3[:, kc, :], start=(kc == 0), stop=(kc == 7)),
                     reads=[bw, bxn], writes=[bps])
            sg, bsg = rSg.next()
            P.op("act", lambda e: e.activation(out=sg[:], in_=ps[:, :], func=AF.Silu), reads=[bps], writes=[bsg])
            P.op("pool", lambda e: e.tensor_tensor(out=act3[:, cc, :], in0=act3[:, cc, :], in1=sg[:], op=ALU.mult), reads=[bact[cc], bsg], writes=[bact[cc]])
        if STOP == 'p5':
            continue
        for cc in range(8):
            w3, bw = load_fm(FM_MB + cc)
            ps, bps = rPS.next()
            for kc in range(8):
                P.op("pe", lambda e: e.matmul(ps[:, :], lhsT=w3[:, kc, :], rhs=xn3[:, kc, :], start=(kc == 0), stop=(kc == 7)),
                     reads=[bw, bxn], writes=[bps])
            sg, bsg = rSg.next()
            P.op("act", lambda e: e.activation(out=sg[:], in_=ps[:, :], func=AF.Sigmoid), reads=[bps], writes=[bsg])
            w3, bw = load_fm(FM_PB + cc)
            ps, bps = rPS.next()
            for kc in range(8):
                P.op("pe", lambda e: e.matmul(ps[:, :], lhsT=w3[:, kc, :], rhs=act3[:, kc, :], start=(kc == 0), stop=(kc == 7)),
                     reads=[bw, bact[kc]], writes=[bps])
            P.op("dve", lambda e: e.tensor_tensor(out=sg[:], in0=ps[:, :], in1=sg[:], op=ALU.mult), reads=[bps, bsg], writes=[bsg])
            P.op("pool", lambda e: e.tensor_tensor(out=m3[:, cc, :], in0=m3[:, cc, :], in1=sg[:], op=ALU.add), reads=[bmm[cc], bsg], writes=[bmm[cc]])
        if STOP == 'p6':
            continue
        wo0, bwo0 = load_tm(TM_O0)
        wo1, bwo1 = load_tm(TM_O1)
        for tt in range(4):
            r0 = t0 + tt * 128
            xt, bxt = rX.next()
            P.dma("sp", lambda e: e.dma_start(out=xt[:], in_=x_p[r0:r0 + 128, :]), writes=[bxt])
            for hb, (wo, bwo) in enumerate(((wo0, bwo0), (wo1, bwo1))):
                ps, bps = rPS.next()
                for kc in range(8):
                    P.op("pe", lambda e: e.matmul(ps[:, :], lhsT=m3[:, kc, tt * 128:(tt + 1) * 128], rhs=wo[:, kc, :], start=(kc == 0), stop=(kc == 7)),
                         reads=[bmm[kc], bwo], writes=[bps])
                P.op("dve", lambda e: e.tensor_tensor(out=hres[:, hb * 512:(hb + 1) * 512], in0=ps[:, :], in1=gate_bc[:, hb * 512:(hb + 1) * 512], op=ALU.mult),
                     reads=[bps, bgate], writes=[bhres])
            P.op("pool", lambda e: e.tensor_tensor(out=hres[:], in0=hres[:], in1=xt[:], op=ALU.add), reads=[bhres, bxt], writes=[bhres])
            P.op("act", lambda e: e.activation(out=junkb[:], in_=hres[:], func=AF.Square, accum_out=sm[:, 10:11]), reads=[bhres], writes=[bjunk, bsm])
            P.op("act", lambda e: e.activation(out=sm[:, 11:12], in_=sm[:, 10:11], func=AF.Sqrt, scale=1.0 / D, bias=EPS), reads=[bsm], writes=[bsm])
            P.op("dve", lambda e: e.reciprocal(out=sm[:, 12:13], in_=sm[:, 11:12]), reads=[bsm], writes=[bsm])
            P.op("dve", lambda e: e.scalar_tensor_tensor(out=yo[:], in0=hres[:], scalar=sm[:, 12:13], in1=gfin_bc[:], op0=ALU.mult, op1=ALU.mult),
                 reads=[bhres, bsm, bgfin], writes=[byo])
            P.dma("act", lambda e: e.dma_start(out=y_p[r0:r0 + 128, :], in_=yo[:]), reads=[byo], is_out=True)


def sample_phase(nc, P, G):
    sS = G["sS"]
    sb = lambda n, shp, d=F32: G["sb"](n, shp, d, sS)
    x_s = G["x_s"]; st_conv = G["st_conv"]; st_lru = G["st_lru"]; ptab = G["ptab"]
    cache_k = G["cache_k"]; cache_v = G["cache_v"]; cache_i = G["cache_i"]; ropes = G["ropes"]
    y_s = G["y_s"]; nk_s = G["nk_s"]; nv_s = G["nv_s"]; nki_s = G["nki_s"]; ncv_s = G["ncv_s"]; nlr_s = G["nlr_s"]
    rPS = G["rPS"]; rPST = G["rPST"]
    psA = G["psA"]; bpsA = G["bpsA"]; psB = G["psB"]; bpsB = G["bpsB"]
    psS0 = G["psS0"]; bpsS0 = G["bpsS0"]; psS1 = G["psS1"]; bpsS1 = G["bpsS1"]
    psO = G["psO"]; bpsO = G["bpsO"]; psL = G["psL"]; bpsL = G["bpsL"]
    cstt = G["cstt"]; bcst = G["bcst"]; chpt = G["chpt"]; bchp = G["bchp"]
    identb = G["identb"]; bidb = G["bidb"]; identf = G["identf"]
    onesb = G["onesb"]; bonesb = G["bonesb"]; onesf = G["onesf"]; bonesf = G["bonesf"]
    clam = G["clam"]; bclam = G["bclam"]; gfin_bc = G["gfin_bc"]; bgfin = G["bgfin"]
    idxgb_bc = G["idxgb_bc"]; bidxgb = G["bidxgb"]; wrr = G["wrr"]; bwrr = G["bwrr"]
    mT = G["mT"]; bmT = G["bmT"]; gate_s = G["gate_s"]; bgs = G["bgs"]
    load_fm = G["load_fm"]; load_tm = G["load_tm"]; rope = G["rope"]
    T = NS
    I4 = identf[0:T, 0:T]
    mT3 = mT[:].rearrange("p (c t) -> p c t", t=5)
    wrr5 = wrr[:].rearrange("p (a n k c) -> p a n k c", a=2, n=4, k=2)
    Jf = cstt[:, CS_JF:CS_JF + 256]; iota_r = cstt[:, CS_IR:CS_IR + 128]; Tstrict = cstt[:, CS_TS:CS_TS + 128]

    def bc84(col0):
        return chpt[:, col0:col0 + 8].unsqueeze(2).to_broadcast([128, 8, T])

    def tt(ek, out, a, b, op, reads, writes):
        return P.op(ek, lambda e: e.tensor_tensor(out=out, in0=a, in1=b, op=op), reads=reads, writes=writes)

    def fm_to_tm(src3, ncc, nrow_in_free, dst_tile, bdst, bsrc):
        for hb in range(2):
            ps, bps = rPS.next()
            for c4 in range(4):
                cc = hb * 4 + c4
                P.op("pe", lambda e: e.transpose(out=ps[0:nrow_in_free, c4 * 128:(c4 + 1) * 128], in_=src3[:, cc, :], identity=identf),
                     reads=[bsrc, bcst], writes=[bps])
            P.op("act", lambda e: e.copy(out=dst_tile[0:nrow_in_free, hb * 512:(hb + 1) * 512], in_=ps[0:nrow_in_free, :]), reads=[bps], writes=[bdst])

    def tm_to_fm(src_tile, nrow, dst_ps, bps, bsrc, col_of):
        for kc in range(8):
            c0 = col_of(kc)
            P.op("pe", lambda e: e.transpose(out=dst_ps[:, c0:c0 + nrow], in_=src_tile[0:nrow, kc * 128:(kc + 1) * 128], identity=identf[0:nrow, 0:nrow]),
                 reads=[bsrc, bcst], writes=[bps])

    xs = sb("xs", [T, D]); bxs = Buf()
    P.dma("sp", lambda e: e.dma_start(out=xs[:], in_=x_s[:, :]), writes=[bxs])
    tm_to_fm(xs, T, psA, bpsA, bxs, lambda kc: kc * T)
    xsT = sb("xsT", [128, 8 * T]); bxsT = Buf(); xsT3 = xsT[:].rearrange("p (k t) -> p k t", t=T)
    P.op("dve", lambda e: e.tensor_copy(out=xsT[:], in_=psA[:, 0:8 * T]), reads=[bpsA], writes=[bxsT])
    tmpA = sb("tmpA", [128, 8 * T]); btA = Buf(); tmpA3 = tmpA[:].rearrange("p (k t) -> p k t", t=T)
    tmpB = sb("tmpB", [128, 8 * T]); btB = Buf(); tmpB3 = tmpB[:].rearrange("p (k t) -> p k t", t=T)
    tt("dve", tmpA[:], xsT[:], xsT[:], ALU.mult, [bxsT], [btA])
    for kc in range(8):
        P.op("pe", lambda e: e.matmul(psB[:, 0:T], lhsT=onesf[:], rhs=tmpA[:, kc * T:(kc + 1) * T], start=(kc == 0), stop=(kc == 7)),
             reads=[bonesf, btA], writes=[bpsB])
    rstd = sb("rstd_s", [128, T]); brstd = Buf()
    P.op("act", lambda e: e.activation(out=rstd[:], in_=psB[:, 0:T], func=AF.Sqrt, scale=1.0 / D, bias=EPS), reads=[bpsB], writes=[brstd])
    P.op("dve", lambda e: e.reciprocal(out=rstd[:], in_=rstd[:]), reads=[brstd], writes=[brstd])
    P.op("dve", lambda e: e.scalar_tensor_tensor(out=tmpB3, in0=mT3[:, 8:16, 1:5], scalar=1.0, in1=bc84(CP_GN), op0=ALU.add, op1=ALU.mult),
         reads=[bmT, bchp], writes=[btB])
    tt("dve", tmpA3, xsT3, rstd[:].unsqueeze(1).to_broadcast([128, 8, T]), ALU.mult, [bxsT, brstd], [btA])
    tt("dve", tmpA3, tmpA3, tmpB3, ALU.mult, [btA, btB], [btA])
    xnsT = sb("xnsT", [128, 8 * T], BF16); bxns = Buf(); xns3 = xnsT[:].rearrange("p (k t) -> p k t", t=T)
    tt("dve", xns3, tmpA3, mT3[:, 0:8, 1:5], ALU.add, [btA, bmT], [bxns])

    ztm = sb("ztm", [T, 2120]); bztm = Buf()
    off = 0
    for blk in (TM_Q0, TM_Q1, TM_KV, TM_QI, TM_KW):
        w3, bw = load_tm(blk)
        ncol = 72 if blk == TM_KW else 512
        ps, bps = rPS.next()
        for kc in range(8):
            P.op("pe", lambda e: e.matmul(ps[0:T, 0:ncol], lhsT=xns3[:, kc, :], rhs=w3[:, kc, 0:ncol], start=(kc == 0), stop=(kc == 7)),
                 reads=[bxns, bw], writes=[bps])
        P.op("act", lambda e: e.copy(out=ztm[:, off:off + ncol], in_=ps[0:T, 0:ncol]), reads=[bps], writes=[bztm])
        off += ncol
    Q0, K0, V0, QI0, KI0, WI0 = 0, 1024, 1280, 1536, 2048, 2112
    for idx in range(40):
        w3, bw = load_fm(idx)
        for kc in range(8):
            P.op("pe", lambda e: e.matmul(psO[:, idx * T:(idx + 1) * T], lhsT=w3[:, kc, :], rhs=xns3[:, kc, :], start=(kc == 0), stop=(kc == 7)),
                 reads=[bw, bxns], writes=[bpsO])
    zfm = sb("zfm", [128, 40 * T]); bzfm = Buf(); zfm3 = zfm[:].rearrange("p (c t) -> p c t", t=T)
    P.op("dve", lambda e: e.tensor_copy(out=zfm[:], in_=psO[:, 0:40 * T]), reads=[bpsO], writes=[bzfm])

    stc = sb("stc", [T * 3, D]); bstc = Buf()
    P.dma("sp", lambda e: e.dma_start(out=stc[:], in_=st_conv[:, :]), writes=[bstc])
    for t in range(T):
        P.dma("act", lambda e: e.dma_start(out=ncv_s[t, 0:2, :], in_=stc[t * 3 + 1:t * 3 + 3, :]), reads=[bstc], is_out=True)
    tm_to_fm(stc, T * 3, psL, bpsL, bstc, lambda kc: kc * 12)
    stT = sb("stT", [128, 96]); bstT = Buf(); stT4 = stT[:].rearrange("p (c t j) -> p c t j", c=8, t=T)
    P.op("dve", lambda e: e.tensor_copy(out=stT[:], in_=psL[:, 0:96]), reads=[bpsL], writes=[bstT])
    rowt = sb("rowt", [T, D]); browt = Buf()
    fm_to_tm(zfm3[:, 0:8, :], 8, T, rowt, browt, bzfm)
    P.dma("act", lambda e: e.dma_start(out=ncv_s[:, 2, :], in_=rowt[:]), reads=[browt], is_out=True)
    xcs = sb("xcs", [128, 8 * T]); bxcs = Buf(); xcs3 = xcs[:].rearrange("p (k t) -> p k t", t=T)
    tt("dve", xcs3, zfm3[:, 0:8, :], bc84(CP_WCONV + 24), ALU.mult, [bzfm, bchp], [bxcs])
    tt("dve", xcs3, xcs3, bc84(CP_BCONV), ALU.add, [bxcs, bchp], [bxcs])
    for j in range(3):
        tt("dve", tmpA3, stT4[:, :, :, j], bc84(CP_WCONV + 8 * j), ALU.mult, [bstT, bchp], [btA])
        tt("dve", xcs3, xcs3, tmpA3, ALU.add, [bxcs, btA], [bxcs])
    xcsb = sb("xcsb", [128, 8 * T], BF16); bxcsb = Buf(); xcsb3 = xcsb[:].rearrange("p (k t) -> p k t", t=T)
    P.op("dve", lambda e: e.tensor_copy(out=xcsb[:], in_=xcs[:]), reads=[bxcs], writes=[bxcsb])
    for which in range(2):
        for cc in range(8):
            n, c2 = cc // 2, cc % 2
            c0 = (which * 8 + cc) * T
            for k2 in range(2):
                P.op("pe", lambda e: e.matmul(psB[:, c0:c0 + T], lhsT=wrr5[:, which, n, k2, c2 * 128:(c2 + 1) * 128], rhs=xcsb3[:, 2 * n + k2, :],
                                              start=(k2 == 0), stop=(k2 == 1)), reads=[bwrr, bxcsb], writes=[bpsB])
    r_s = sb("r_s", [128, 8 * T]); br_s = Buf(); r_s3 = r_s[:].rearrange("p (k t) -> p k t", t=T)
    i_s = sb("i_s", [128, 8 * T]); bi_s = Buf(); i_s3 = i_s[:].rearrange("p (k t) -> p k t", t=T)
    a_s = sb("a_s", [128, 8 * T]); ba_s = Buf(); a_s3 = a_s[:].rearrange("p (k t) -> p k t", t=T)
    tt("dve", r_s3, psB[:, 0:8 * T].rearrange("p (k t) -> p k t", t=T), bc84(CP_BRA), ALU.add, [bpsB, bchp], [br_s])
    tt("dve", i_s3, psB[:, 8 * T:16 * T].rearrange("p (k t) -> p k t", t=T), bc84(CP_BRX), ALU.add, [bpsB, bchp], [bi_s])
    P.op("act", lambda e: e.activation(out=r_s[:], in_=r_s[:], func=AF.Sigmoid), reads=[br_s], writes=[br_s])
    P.op("act", lambda e: e.activation(out=i_s[:], in_=i_s[:], func=AF.Sigmoid), reads=[bi_s], writes=[bi_s])
    tt("dve", a_s3, r_s3, clam[:].unsqueeze(2).to_broadcast([128, 8, T]), ALU.mult, [br_s, bclam], [ba_s])
    P.op("act", lambda e: e.activation(out=a_s[:], in_=a_s[:], func=AF.Exp), reads=[ba_s], writes=[ba_s])
    tt("dve", tmpA[:], a_s[:], a_s[:], ALU.mult, [ba_s], [btA])
    P.op("dve", lambda e: e.tensor_scalar(out=tmpA[:], in0=tmpA[:], scalar1=-1.0, scalar2=1.0, op0=ALU.mult, op1=ALU.add), reads=[btA], writes=[btA])
    P.op("act", lambda e: e.activation(out=tmpA[:], in_=tmpA[:], func=AF.Sqrt), reads=[btA], writes=[btA])
    tt("dve", tmpB[:], i_s[:], xcs[:], ALU.mult, [bi_s, bxcs], [btB])
    tt("dve", tmpB[:], tmpB[:], tmpA[:], ALU.mult, [btB, btA], [btB])
    hst = sb("hst", [T, D]); bhst = Buf()
    P.dma("sp", lambda e: e.dma_start(out=hst[:], in_=st_lru[:, :]), writes=[bhst])
    tm_to_fm(hst, T, psA, bpsA, bhst, lambda kc: kc * T)
    h_s = sb("h_s", [128, 8 * T]); bh_s = Buf(); h_s3 = h_s[:].rearrange("p (k t) -> p k t", t=T)
    tt("dve", h_s[:], psA[:, 0:8 * T], a_s[:], ALU.mult, [bpsA, ba_s], [bh_s])
    tt("dve", h_s[:], h_s[:], tmpB[:], ALU.add, [bh_s, btB], [bh_s])
    fm_to_tm(h_s3, 8, T, rowt, browt, bh_s)
    P.dma("act", lambda e: e.dma_start(out=nlr_s[:, :], in_=rowt[:]), reads=[browt], is_out=True)
    acta = sb("acta_s", [128, 8 * T], BF16); bacta = Buf(); acta3 = acta[:].rearrange("p (k t) -> p k t", t=T)
    P.op("act", lambda e: e.activation(out=tmpA3, in_=zfm3[:, 8:16, :], func=AF.Silu), reads=[bzfm], writes=[btA])
    tt("dve", acta[:], h_s[:], tmpA[:], ALU.mult, [bh_s, btA], [bacta])
    for cc in range(8):
        w3, bw = load_fm(FM_PA + cc)
        for kc in range(8):
            P.op("pe", lambda e: e.matmul(psB[:, cc * T:(cc + 1) * T], lhsT=w3[:, kc, :], rhs=acta3[:, kc, :], start=(kc == 0), stop=(kc == 7)),
                 reads=[bw, bacta], writes=[bpsB])
    m_s = sb("m_s", [128, 8 * T]); bm_s = Buf(); m_s3 = m_s[:].rearrange("p (k t) -> p k t", t=T)
    P.op("act", lambda e: e.activation(out=tmpA3, in_=zfm3[:, 24:32, :], func=AF.Sigmoid), reads=[bzfm], writes=[btA])
    tt("dve", m_s[:], psB[:, 0:8 * T], tmpA[:], ALU.mult, [bpsB, btA], [bm_s])

    rs = sb("ropes_t", [T, 384]); brs = Buf()
    P.dma("sp", lambda e: e.dma_start(out=rs[:], in_=ropes[:, :]), writes=[brs])
    t1 = sb("s_t1", [T, 1024]); bt1 = Buf()
    t2 = sb("s_t2", [T, 1024]); bt2 = Buf()
    q_r = sb("q_r", [T, 1024]); bq_r = Buf()
    k_r = sb("k_r", [T, 256]); bk_r = Buf()
    qi_r = sb("qi_r", [T, 512]); bqi_r = Buf()
    ki_r = sb("ki_r", [T, 64]); bki_r = Buf()
    sm = sb("s_sm", [T, 16]); bsm = Buf()
    v3 = lambda ap, h: ap.rearrange("p (h d) -> p h d", h=h)
    rope("dve", v3(q_r[:], 8), v3(ztm[:, Q0:Q0 + 1024], 8), rs[:, 0:128], rs[:, 128:256], 8, 128, v3(t1[:], 8), v3(t2[:], 8), [bztm, brs], [bq_r], bt1, bt2)
    rope("dve", v3(k_r[:], 2), v3(ztm[:, K0:K0 + 256], 2), rs[:, 0:128], rs[:, 128:256], 2, 128, v3(t1[:, 0:256], 2), v3(t2[:, 0:256], 2), [bztm, brs], [bk_r], bt1, bt2)
    P.dma("act", lambda e: e.dma_start(out=nk_s[:, :], in_=k_r[:]), reads=[bk_r], is_out=True)
    P.dma("act", lambda e: e.dma_start(out=nv_s[:, :], in_=ztm[:, V0:V0 + 256]), reads=[bztm], is_out=True)
    rope("dve", v3(qi_r[:], 8), v3(ztm[:, QI0:QI0 + 512], 8), rs[:, 256:320], rs[:, 320:384], 8, 64, v3(t1[:, 0:512], 8), v3(t2[:, 0:512], 8), [bztm, brs], [bqi_r], bt1, bt2)
    P.op("dve", lambda e: e.tensor_reduce(out=sm[:, 0:1], in_=ztm[:, KI0:KI0 + 64], axis=AX.X, op=ALU.add), reads=[bztm], writes=[bsm])
    P.op("dve", lambda e: e.tensor_scalar(out=sm[:, 1:2], in0=sm[:, 0:1], scalar1=-1.0 / 64, scalar2=None, op0=ALU.mult), reads=[bsm], writes=[bsm])
    P.op("dve", lambda e: e.tensor_scalar(out=t1[:, 0:64], in0=ztm[:, KI0:KI0 + 64], scalar1=sm[:, 1:2], scalar2=None, op0=ALU.add), reads=[bztm, bsm], writes=[bt1])
    P.op("act", lambda e: e.activation(out=t2[:, 0:64], in_=t1[:, 0:64], func=AF.Square, accum_out=sm[:, 2:3]), reads=[bt1], writes=[bt2, bsm])
    P.op("act", lambda e: e.activation(out=sm[:, 3:4], in_=sm[:, 2:3], func=AF.Sqrt, scale=1.0 / 64, bias=EPS), reads=[bsm], writes=[bsm])
    P.op("dve", lambda e: e.reciprocal(out=sm[:, 4:5], in_=sm[:, 3:4]), reads=[bsm], writes=[bsm])
    P.op("dve", lambda e: e.scalar_tensor_tensor(out=t1[:, 64:128], in0=t1[:, 0:64], scalar=sm[:, 4:5], in1=idxgb_bc[0:T, 0:64], op0=ALU.mult, op1=ALU.mult),
         reads=[bt1, bsm, bidxgb], writes=[bt1])
    tt("dve", t1[:, 128:192], t1[:, 64:128], idxgb_bc[0:T, 64:128], ALU.add, [bt1, bidxgb], [bt1])
    rope("dve", ki_r[:].unsqueeze(1), t1[:, 128:192].unsqueeze(1), rs[:, 256:320], rs[:, 320:384], 1, 64,
         t2[:, 64:128].unsqueeze(1), t2[:, 128:192].unsqueeze(1), [bt1, brs], [bki_r], bt2, bt2)
    P.dma("act", lambda e: e.dma_start(out=nki_s[:, :], in_=ki_r[:]), reads=[bki_r], is_out=True)
    w_s = sb("w_s", [T, 8]); bw_s = Buf()
    P.op("dve", lambda e: e.tensor_scalar(out=w_s[:], in0=ztm[:, WI0:WI0 + 8], scalar1=IDX_W_SCALE, scalar2=None, op0=ALU.mult), reads=[bztm], writes=[bw_s])
    s8 = sb("s8", [T, 8]); bs8 = Buf()
    selfsc = sb("selfsc", [T, 1]); bselfsc = Buf()
    sl = sb("sl", [T, 8]); bsl = Buf()
    tt("dve", v3(t1[:, 0:512], 8), v3(qi_r[:], 8), ki_r[:].unsqueeze(1).to_broadcast([T, 8, 64]), ALU.mult, [bqi_r, bki_r], [bt1])
    P.op("dve", lambda e: e.tensor_reduce(out=s8[:], in_=v3(t1[:, 0:512], 8), axis=AX.X, op=ALU.add), reads=[bt1], writes=[bs8])
    P.op("dve", lambda e: e.scalar_tensor_tensor(out=s8[:], in0=s8[:], scalar=0.0, in1=w_s[:], op0=ALU.max, op1=ALU.mult), reads=[bs8, bw_s], writes=[bs8])
    P.op("dve", lambda e: e.tensor_reduce(out=selfsc[:], in_=s8[:], axis=AX.X, op=ALU.add), reads=[bs8], writes=[bselfsc])
    tt("dve", t1[:].rearrange("p (g l d) -> p g l d", g=2, l=4), q_r[:].rearrange("p (g l d) -> p g l d", g=2, l=4),
       k_r[:].rearrange("p (g d) -> p g d", g=2).unsqueeze(2).to_broadcast([T, 2, 4, 128]), ALU.mult, [bq_r, bk_r], [bt1])
    P.op("dve", lambda e: e.tensor_reduce(out=sl[:], in_=v3(t1[:], 8), axis=AX.X, op=ALU.add), reads=[bt1], writes=[bsl])
    q_b = sb("q_b", [T, 1024], BF16); bq_b = Buf()
    P.op("dve", lambda e: e.tensor_copy(out=q_b[:], in_=q_r[:]), reads=[bq_r], writes=[bq_b])
    pt, bpt = rPST.next()
    for h in range(8):
        P.op("pe", lambda e: e.transpose(out=pt[:, h * T:(h + 1) * T], in_=q_b[:, h * 128:(h + 1) * 128], identity=identb[0:T, 0:T]),
             reads=[bq_b, bidb], writes=[bpt])
    qsT = sb("qsT", [128, 8 * T], BF16); bqsT = Buf(); qsT3 = qsT[:].rearrange("p (h t) -> p h t", t=T)
    P.op("dve", lambda e: e.tensor_copy(out=qsT[:], in_=pt[:, 0:8 * T]), reads=[bpt], writes=[bqsT])
    qpad = sb("qpad", [T, 2 * 8 * 128], BF16); bqpad = Buf(); qpad4 = qpad[:].rearrange("p (a h d) -> p a h d", a=2, h=8)
    P.op("pool", lambda e: e.memset(qpad[:], 0.0), writes=[bqpad])
    P.op("dve", lambda e: e.tensor_copy(out=qpad4[:, 0, :, 0:64], in_=v3(qi_r[:], 8)), reads=[bqi_r], writes=[bqpad])
    P.op("dve", lambda e: e.tensor_copy(out=qpad4[:, 1, :, 64:128], in_=v3(qi_r[:], 8)), reads=[bqi_r], writes=[bqpad])
    pt, bpt = rPST.next()
    for a in range(2):
        for h in range(8):
            c0 = (a * 8 + h) * T
            P.op("pe", lambda e: e.transpose(out=pt[:, c0:c0 + T], in_=qpad4[:, a, h, :], identity=identb[0:T, 0:T]), reads=[bqpad, bidb], writes=[bpt])
    qiz = sb("qiz", [128, T * 16], BF16); bqiz = Buf(); qiz3 = qiz[:].rearrange("p (t a) -> p t a", t=T)
    P.op("dve", lambda e: e.tensor_copy(out=qiz3, in_=pt[:, 0:16 * T].rearrange("p (a t) -> p t a", t=T)), reads=[bpt], writes=[bqiz])
    W4 = sb("W4", [T, T * 8 + T]); bW4 = Buf()
    tt("dve", W4[:, 0:T * 8].rearrange("p (t h) -> p t h", t=T), w_s[:].unsqueeze(1).to_broadcast([T, T, 8]), I4.unsqueeze(2).to_broadcast([T, T, 8]), ALU.mult,
       [bw_s, bcst], [bW4])
    P.op("dve", lambda e: e.tensor_scalar(out=W4[:, T * 8:T * 9], in0=I4, scalar1=selfsc[:], scalar2=None, op0=ALU.mult), reads=[bcst, bselfsc], writes=[bW4])
    P.op("pe", lambda e: e.matmul(psB[:, 0:T * 9], lhsT=onesf[0:T, :], rhs=W4[:], start=True, stop=True), reads=[bonesf, bW4], writes=[bpsB])
    wbcS = sb("wbcS", [128, T * 9]); bwbcS = Buf()
    P.op("dve", lambda e: e.tensor_copy(out=wbcS[:], in_=psB[:, 0:T * 9]), reads=[bpsB], writes=[bwbcS])
    selfb = wbcS[:, T * 8:T * 9]
    pti = sb("pti", [128, T], I32); bpti = Buf()
    ptf = sb("ptf", [128, T]); bptf = Buf()
    P.dma("sp", lambda e: e.dma_start(out=pti[:], in_=ptab[:, :]), writes=[bpti])
    P.op("dve", lambda e: e.tensor_copy(out=ptf[:], in_=pti[:]), reads=[bpti], writes=[bptf])

    Gt = sb("Gt", [128, 8192]); bGt = Buf()
    kTs = sb("kTs", [128, 64 * 128], BF16); bkTs = Buf(); kTs3 = kTs[:].rearrange("p (r q) -> p r q", r=64)
    scs = sb("scs", [128, T * 128]); bscs = Buf(); scs3 = scs[:].rearrange("p (t r) -> p t r", t=T)
    tmpS = sb("tmpS", [128, 512]); btS = Buf()
    for t in range(T):
        P.dma("pool", lambda e: e.indirect_dma_start(out=Gt[:], out_offset=None, in_=cache_i[:, :],
                                                     in_offset=bass.IndirectOffsetOnAxis(ap=pti[:, t:t + 1], axis=0)), reads=[bpti], writes=[bGt])
        for r4 in range(16):
            ps, bps = rPS.next()
            for q in range(4):
                rp = r4 * 4 + q
                P.op("pe", lambda e: e.transpose(out=ps[:, q * 128:(q + 1) * 128], in_=Gt[:, rp * 128:(rp + 1) * 128], identity=identf), reads=[bGt, bcst], writes=[bps])
            if r4 % 2 == 0:
                P.op("act", lambda e: e.copy(out=kTs[:, r4 * 512:(r4 + 1) * 512], in_=ps[:, :]), reads=[bps], writes=[bkTs])
            else:
                P.op("dve", lambda e: e.tensor_copy(out=kTs[:, r4 * 512:(r4 + 1) * 512], in_=ps[:, :]), reads=[bps], writes=[bkTs])
        for half, (psX, bpsX) in enumerate(((psS0, bpsS0), (psS1, bpsS1))):
            for rr in range(64):
                r = half * 64 + rr
                rp, r2 = r // 2, r % 2
                P.op("pe", lambda e: e.matmul(psX[:, rr * 8:(rr + 1) * 8], lhsT=kTs3[:, rp, :], rhs=qiz3[:, t, r2 * 8:(r2 + 1) * 8], start=True, stop=True),
                     reads=[bkTs, bqiz], writes=[bpsX])
            P.op("dve", lambda e: e.scalar_tensor_tensor(out=tmpS[:].rearrange("p (r h) -> p r h", h=8), in0=psX[:, :].rearrange("p (r h) -> p r h", h=8), scalar=0.0,
                                                         in1=wbcS[:, t * 8:(t + 1) * 8].unsqueeze(1).to_broadcast([128, 64, 8]), op0=ALU.max, op1=ALU.mult),
                 reads=[bpsX, bwbcS], writes=[btS])
            P.op("dve", lambda e: e.tensor_reduce(out=scs3[:, t, half * 64:(half + 1) * 64], in_=tmpS[:].rearrange("p (r h) -> p r h", h=8), axis=AX.X, op=ALU.add),
                 reads=[btS], writes=[bscs])

    mx = sb("b_mx", [128, 2 * T]); bmx = Buf()
    P.op("dve", lambda e: e.tensor_reduce(out=mx[:, 0:T], in_=scs3, axis=AX.X, op=ALU.max), reads=[bscs], writes=[bmx])
    P.op("dve", lambda e: e.tensor_reduce(out=mx[:, T:2 * T], in_=scs3, axis=AX.X, op=ALU.min), reads=[bscs], writes=[bmx])
    ps, bps = rPS.next()
    P.op("pe", lambda e: e.transpose(out=ps[0:2 * T, 0:128], in_=mx[:], identity=identf), reads=[bmx, bcst], writes=[bps])
    hl = sb("b_hl", [2 * T, 2]); bhl = Buf()
    P.op("dve", lambda e: e.tensor_reduce(out=hl[:, 0:1], in_=ps[0:2 * T, 0:128], axis=AX.X, op=ALU.max), reads=[bps], writes=[bhl])
    P.op("dve", lambda e: e.tensor_reduce(out=hl[:, 1:2], in_=ps[0:2 * T, 0:128], axis=AX.X, op=ALU.min), reads=[bps], writes=[bhl])
    HL = sb("b_HL", [2 * T, 2 * T]); bHL = Buf()
    P.op("pool", lambda e: e.memset(HL[:], 0.0), writes=[bHL])
    P.op("dve", lambda e: e.tensor_scalar(out=HL[0:T, 0:T], in0=I4, scalar1=hl[0:T, 0:1], scalar2=None, op0=ALU.mult), reads=[bcst, bhl], writes=[bHL])
    ps2, bps2 = rPS.next()
    P.op("pe", lambda e: e.matmul(ps2[:, 0:T], lhsT=onesf[0:T, :], rhs=HL[0:T, 0:T], start=True, stop=True), reads=[bonesf, bHL], writes=[bps2])
    hib = sb("b_hib", [128, T]); bhib = Buf()
    tt("dve", hib[:], ps2[:, 0:T], selfb, ALU.max, [bps2, bwbcS], [bhib])
    ps, bps = rPS.next()
    P.op("pe", lambda e: e.transpose(out=ps[0:T, 0:128], in_=mx[:, T:2 * T], identity=identf), reads=[bmx, bcst], writes=[bps])
    P.op("dve", lambda e: e.tensor_reduce(out=hl[0:T, 1:2], in_=ps[0:T, 0:128], axis=AX.X, op=ALU.min), reads=[bps], writes=[bhl])
    P.op("dve", lambda e: e.tensor_scalar(out=HL[0:T, T:2 * T], in0=I4, scalar1=hl[0:T, 1:2], scalar2=None, op0=ALU.mult), reads=[bcst, bhl], writes=[bHL])
    ps2, bps2 = rPS.next()
    P.op("pe", lambda e: e.matmul(ps2[:, 0:T], lhsT=onesf[0:T, :], rhs=HL[0:T, T:2 * T], start=True, stop=True), reads=[bonesf, bHL], writes=[bps2])
    lob = sb("b_lob", [128, T]); blob = Buf()
    tt("dve", lob[:], ps2[:, 0:T], selfb, ALU.min, [bps2, bwbcS], [blob])
    wb = sb("b_wb", [128, T]); bwb = Buf()
    tt("dve", wb[:], hib[:], lob[:], ALU.subtract, [bhib, blob], [bwb])
    midb = sb("b_mid", [128, T]); bmidb = Buf()
    cmpj = sb("b_cmpj", [128, T * 128]); bcmpj = Buf(); cmpj3 = cmpj[:].rearrange("p (t r) -> p t r", t=T)
    cntp = sb("b_cntp", [128, T]); bcntp = Buf()
    tot = sb("b_tot", [128, T]); btot = Buf()
    sge = sb("b_sge", [128, T]); bsge = Buf()
    for it in range(N_BISECT_S):
        P.op("dve", lambda e: e.tensor_scalar(out=wb[:], in0=wb[:], scalar1=0.5, scalar2=None, op0=ALU.mult), reads=[bwb], writes=[bwb])
        tt("dve", midb[:], lob[:], wb[:], ALU.add, [blob, bwb], [bmidb])
        tt("dve", cmpj3, scs3, midb[:].unsqueeze(2).to_broadcast([128, T, 128]), ALU.is_ge, [bscs, bmidb], [bcmpj])
        P.op("dve", lambda e: e.tensor_reduce(out=cntp[:], in_=cmpj3, axis=AX.X, op=ALU.add), reads=[bcmpj], writes=[bcntp])
        ps, bps = rPS.next()
        P.op("pe", lambda e: e.matmul(ps[:, 0:T], lhsT=onesf[:], rhs=cntp[:], start=True, stop=True), reads=[bonesf, bcntp], writes=[bps])
        tt("dve", sge[:], selfb, midb[:], ALU.is_ge, [bwbcS, bmidb], [bsge])
        tt("dve", tot[:], ps[:, 0:T], sge[:], ALU.add, [bps, bsge], [btot])
        P.op("dve", lambda e: e.tensor_scalar(out=tot[:], in0=tot[:], scalar1=float(TOPK) - 0.5, scalar2=None, op0=ALU.is_ge), reads=[btot], writes=[btot])
        tt("dve", tot[:], tot[:], wb[:], ALU.mult, [btot, bwb], [btot])
        tt("dve", lob[:], lob[:], tot[:], ALU.add, [blob, btot], [blob])
    Msel = sb("Msel", [128, T * 128]); bMsel = Buf(); Msel3 = Msel[:].rearrange("p (t r) -> p t r", t=T)
    tt("dve", Msel3, scs3, lob[:].unsqueeze(2).to_broadcast([128, T, 128]), ALU.is_ge, [bscs, blob], [bMsel])
    thr_tm = sb("thr_tm", [T, 4]); bthr = Buf()
    tt("dve", t1[:, 0:T], lob[0:T, :], I4, ALU.mult, [blob, bcst], [bt1])
    P.op("dve", lambda e: e.tensor_reduce(out=thr_tm[:, 0:1], in_=t1[:, 0:T], axis=AX.X, op=ALU.add), reads=[bt1], writes=[bthr])
    tt("dve", thr_tm[:, 1:2], selfsc[:], thr_tm[:, 0:1], ALU.is_ge, [bselfsc, bthr], [bthr])
    P.op("dve", lambda e: e.tensor_scalar(out=thr_tm[:, 2:3], in0=thr_tm[:, 1:2], scalar1=-NEG, scalar2=NEG, op0=ALU.mult, op1=ALU.add), reads=[bthr], writes=[bthr])
    pself = sb("pself", [T, 8]); bpself = Buf()
    P.op("act", lambda e: e.activation(out=pself[:], in_=sl[:], func=AF.Exp, scale=ATT_SCALE, bias=thr_tm[:, 2:3]), reads=[bsl, bthr], writes=[bpself])

    csel = sb("csel", [128, T]); bcsel = Buf()
    P.op("dve", lambda e: e.tensor_reduce(out=csel[:], in_=Msel3, axis=AX.X, op=ALU.add), reads=[bMsel], writes=[bcsel])
    ps, bps = rPS.next()
    P.op("pe", lambda e: e.matmul(ps[:, 0:T], lhsT=Tstrict, rhs=csel[:], start=True, stop=True), reads=[bcst, bcsel], writes=[bps])
    osel = sb("osel", [128, T]); bosel = Buf()
    esel = sb("esel", [128, T]); besel = Buf()
    P.op("dve", lambda e: e.tensor_copy(out=osel[:], in_=ps[:, 0:T]), reads=[bps], writes=[bosel])
    tt("dve", esel[:], osel[:], csel[:], ALU.add, [bosel, bcsel], [besel])
    rhsT = sb("rhsT", [128, T * 130]); brhsT = Buf(); rhsT3 = rhsT[:].rearrange("p (t c) -> p t c", t=T)
    for t in range(T):
        P.op("dve", lambda e: e.tensor_tensor_scan(out=rhsT3[:, t, 0:128], data0=onesf[:], data1=Msel3[:, t, :], initial=0.0, op0=ALU.mult, op1=ALU.add),
             reads=[bonesf, bMsel], writes=[brhsT])
    tt("dve", rhsT3[:, :, 0:128], rhsT3[:, :, 0:128], Msel3, ALU.mult, [brhsT, bMsel], [brhsT])
    P.op("dve", lambda e: e.tensor_copy(out=rhsT3[:, :, 128], in_=osel[:]), reads=[bosel], writes=[brhsT])
    P.op("dve", lambda e: e.tensor_copy(out=rhsT3[:, :, 129], in_=ptf[:]), reads=[bptf], writes=[brhsT])
    Asel = sb("Asel", [128, 256]); bAsel = Buf()
    A2 = sb("A2", [128, 256]); bA2 = Buf()
    idxT = sb("idxT", [128, 2 * T], I32); bidxT = Buf()
    vbias = sb("vbias", [128, 2 * T]); bvbias = Buf()
    c4 = sb("c4", [128, 8]); bc4 = Buf()
    eqt = sb("eqt", [128, 128]); beqt = Buf()
    for t in range(T):
        P.op("dve", lambda e: e.tensor_scalar(out=Asel[:], in0=Jf, scalar1=osel[:, t:t + 1], scalar2=None, op0=ALU.is_ge), reads=[bcst, bosel], writes=[bAsel])
        P.op("dve", lambda e: e.tensor_scalar(out=A2[:], in0=Jf, scalar1=esel[:, t:t + 1], scalar2=None, op0=ALU.is_lt), reads=[bcst, besel], writes=[bA2])
        tt("dve", Asel[:], Asel[:], A2[:], ALU.mult, [bAsel, bA2], [bAsel])
        for jc in range(2):
            col = t * 2 + jc
            ps, bps = rPS.next()
            P.op("pe", lambda e: e.matmul(ps[:, 0:130], lhsT=Asel[:, jc * 128:(jc + 1) * 128], rhs=rhsT3[:, t, :], start=True, stop=True),
                 reads=[bAsel, brhsT], writes=[bps])
            tt("dve", c4[:, 0:1], cstt[:, CS_J1 + jc:CS_J1 + jc + 1], ps[:, 128:129], ALU.subtract, [bcst, bps], [bc4])
            P.op("dve", lambda e: e.tensor_scalar(out=eqt[:], in0=ps[:, 0:128], scalar1=c4[:, 0:1], scalar2=None, op0=ALU.is_equal), reads=[bps, bc4], writes=[beqt])
            P.op("dve", lambda e: e.tensor_reduce(out=c4[:, 1:2], in_=eqt[:], axis=AX.X, op=ALU.add), reads=[beqt], writes=[bc4])
            tt("dve", eqt[:], eqt[:], iota_r, ALU.mult, [beqt, bcst], [beqt])
            P.op("dve", lambda e: e.tensor_reduce(out=c4[:, 2:3], in_=eqt[:], axis=AX.X, op=ALU.add), reads=[beqt], writes=[bc4])
            P.op("dve", lambda e: e.scalar_tensor_tensor(out=c4[:, 3:4], in0=ps[:, 129:130], scalar=128.0, in1=c4[:, 2:3], op0=ALU.mult, op1=ALU.add),
                 reads=[bps, bc4], writes=[bc4])
            P.op("dve", lambda e: e.tensor_copy(out=idxT[:, col:col + 1], in_=c4[:, 3:4]), reads=[bc4], writes=[bidxT])
            P.op("dve", lambda e: e.tensor_scalar(out=vbias[:, col:col + 1], in0=c4[:, 1:2], scalar1=-NEG, scalar2=NEG, op0=ALU.mult, op1=ALU.add),
                 reads=[bc4], writes=[bvbias])

    Ksel = sb("Ksel", [128, 512]); bKsel = Buf()
    Vsel = sb("Vsel", [128, 512]); bVsel = Buf()
    Kb = sb("Kb_s", [128, 512], BF16); bKb = Buf()
    Vx = sb("Vx", [128, 4 * 129], BF16); bVx = Buf(); Vx4 = Vx[:].rearrange("p (j g d) -> p j g d", j=2, g=2)
    KselT = sb("KselT", [128, 512], BF16); bKselT = Buf(); KselT3 = KselT[:].rearrange("p (g j) -> p g j", g=2)
    PTs = sb("PTs", [128, 16], BF16); bPTs = Buf()
    vself = sb("vself", [T, 2 * 129], BF16); bvself = Buf(); vself3 = vself[:].rearrange("p (g d) -> p g d", g=2)
    pselfm = sb("pselfm", [T, 8], BF16); bpselfm = Buf()
    osb = sb("osb", [4, T * 2 * 128]); bosb = Buf(); osb4 = osb[:].rearrange("p (t g d) -> p t g d", t=T, g=2)
    rcp = sb("rcp", [4, 2]); brcp = Buf()
    P.op("pool", lambda e: e.memset(Vx[:], 1.0), writes=[bVx])
    P.op("pool", lambda e: e.memset(vself[:], 1.0), writes=[bvself])
    P.op("dve", lambda e: e.tensor_copy(out=vself3[:, :, 0:128], in_=ztm[:, V0:V0 + 256].rearrange("p (g d) -> p g d", g=2)), reads=[bztm], writes=[bvself])
    for t in range(T):
        for jc in range(2):
            col = t * 2 + jc
            P.dma("pool", lambda e: e.indirect_dma_start(out=Ksel[:, jc * 256:(jc + 1) * 256], out_offset=None, in_=cache_k[:, :],
                                                         in_offset=bass.IndirectOffsetOnAxis(ap=idxT[:, col:col + 1], axis=0)), reads=[bidxT], writes=[bKsel])
            P.dma("pool", lambda e: e.indirect_dma_start(out=Vsel[:, jc * 256:(jc + 1) * 256], out_offset=None, in_=cache_v[:, :],
                                                         in_offset=bass.IndirectOffsetOnAxis(ap=idxT[:, col:col + 1], axis=0)), reads=[bidxT], writes=[bVsel])
        P.op("dve", lambda e: e.tensor_copy(out=Kb[:], in_=Ksel[:]), reads=[bKsel], writes=[bKb])
        P.op("dve", lambda e: e.tensor_copy(out=Vx4[:, :, :, 0:128], in_=Vsel[:].rearrange("p (j g d) -> p j g d", j=2, g=2)), reads=[bVsel], writes=[bVx])
        pt, bpt = rPST.next()
        for g in range(2):
            for jc in range(2):
                c0 = (g * 2 + jc) * 128
                P.op("pe", lambda e: e.transpose(out=pt[:, c0:c0 + 128], in_=Kb[:, jc * 256 + g * 128:jc * 256 + (g + 1) * 128], identity=identb[:]),
                     reads=[bKb, bidb], writes=[bpt])
        P.op("act", lambda e: e.copy(out=KselT[:], in_=pt[:, 0:512]), reads=[bpt], writes=[bKselT])
        ps, bps = rPS.next()
        for jc in range(2):
            for g in range(2):
                c0 = (jc * 2 + g) * 4
                P.op("pe", lambda e: e.matmul(ps[:, c0:c0 + 4], lhsT=KselT3[:, g, jc * 128:(jc + 1) * 128], rhs=qsT3[:, 4 * g:4 * g + 4, t], start=True, stop=True),
                     reads=[bKselT, bqsT], writes=[bps])
        for jc in range(2):
            col = t * 2 + jc
            P.op("act", lambda e: e.activation(out=PTs[:, jc * 8:(jc + 1) * 8], in_=ps[:, jc * 8:(jc + 1) * 8], func=AF.Exp, scale=ATT_SCALE, bias=vbias[:, col:col + 1]),
                 reads=[bps, bvbias], writes=[bPTs])
        P.op("dve", lambda e: e.tensor_scalar(out=pselfm[:], in0=pself[:], scalar1=I4[:, t:t + 1], scalar2=None, op0=ALU.mult), reads=[bpself, bcst], writes=[bpselfm])
        for g in range(2):
            c0 = g * 129
            for jc in range(2):
                P.op("pe", lambda e: e.matmul(psO[0:4, c0:c0 + 129], lhsT=PTs[:, jc * 8 + 4 * g:jc * 8 + 4 * g + 4], rhs=Vx4[:, jc, g, :], start=(jc == 0), stop=False),
                     reads=[bPTs, bVx], writes=[bpsO])
            P.op("pe", lambda e: e.matmul(psO[0:4, c0:c0 + 129], lhsT=pselfm[:, 4 * g:4 * g + 4], rhs=vself3[:, g, :], start=False, stop=True),
                 reads=[bpselfm, bvself], writes=[bpsO])
        for g in range(2):
            c0 = g * 129
            P.op("dve", lambda e: e.reciprocal(out=rcp[:, g:g + 1], in_=psO[0:4, c0 + 128:c0 + 129]), reads=[bpsO], writes=[brcp])
            P.op("dve", lambda e: e.tensor_scalar(out=osb4[:, t, g, :], in0=psO[0:4, c0:c0 + 128], scalar1=rcp[:, g:g + 1], scalar2=None, op0=ALU.mult),
                 reads=[bpsO, brcp], writes=[bosb])
    ps, bps = rPS.next()
    for t in range(T):
        for g in range(2):
            c0 = (t * 2 + g) * 4
            P.op("pe", lambda e: e.transpose(out=ps[:, c0:c0 + 4], in_=osb4[:, t, g, :], identity=identf[0:4, 0:4]), reads=[bosb, bcst], writes=[bps])
    oT = sb("oT_s", [128, 8 * T]); boT = Buf(); oT3 = oT[:].rearrange("p (h t) -> p h t", t=T)
    P.op("dve", lambda e: e.tensor_copy(out=oT3, in_=ps[:, 0:32].rearrange("p (t h) -> p h t", t=T)), reads=[bps], writes=[boT])
    actb = sb("actb_s", [128, 8 * T], BF16); bactb = Buf(); actb3 = actb[:].rearrange("p (k t) -> p k t", t=T)
    P.op("act", lambda e: e.activation(out=tmpA3, in_=zfm3[:, 16:24, :], func=AF.Silu), reads=[bzfm], writes=[btA])
    tt("dve", actb[:], oT[:], tmpA[:], ALU.mult, [boT, btA], [bactb])
    for cc in range(8):
        w3, bw = load_fm(FM_PB + cc)
        for kc in range(8):
            P.op("pe", lambda e: e.matmul(psB[:, cc * T:(cc + 1) * T], lhsT=w3[:, kc, :], rhs=actb3[:, kc, :], start=(kc == 0), stop=(kc == 7)),
                 reads=[bw, bactb], writes=[bpsB])
    P.op("act", lambda e: e.activation(out=tmpA3, in_=zfm3[:, 32:40, :], func=AF.Sigmoid), reads=[bzfm], writes=[btA])
    tt("dve", tmpA[:], psB[:, 0:8 * T], tmpA[:], ALU.mult, [bpsB, btA], [btA])
    msb = sb("msb", [128, 8 * T], BF16); bmsb = Buf(); msb3 = msb[:].rearrange("p (k t) -> p k t", t=T)
    tt("dve", msb[:], m_s[:], tmpA[:], ALU.add, [bm_s, btA], [bmsb])
    hres = sb("hres_s", [T, D]); bhres = Buf()
    for hb, blk in enumerate((TM_O0, TM_O1)):
        wo, bwo = load_tm(blk)
        ps, bps = rPS.next()
        for kc in range(8):
            P.op("pe", lambda e: e.matmul(ps[0:T, :], lhsT=msb3[:, kc, :], rhs=wo[:, kc, :], start=(kc == 0), stop=(kc == 7)), reads=[bmsb, bwo], writes=[bps])
        tt("dve", hres[:, hb * 512:(hb + 1) * 512], ps[0:T, :], gate_s[:, hb * 512:(hb + 1) * 512], ALU.mult, [bps, bgs], [bhres])
    tt("dve", hres[:], hres[:], xs[:], ALU.add, [bhres, bxs], [bhres])
    P.op("act", lambda e: e.activation(out=t1[:], in_=hres[:], func=AF.Square, accum_out=sm[:, 8:9]), reads=[bhres], writes=[bt1, bsm])
    P.op("act", lambda e: e.activation(out=sm[:, 9:10], in_=sm[:, 8:9], func=AF.Sqrt, scale=1.0 / D, bias=EPS), reads=[bsm], writes=[bsm])
    P.op("dve", lambda e: e.reciprocal(out=sm[:, 10:11], in_=sm[:, 9:10]), reads=[bsm], writes=[bsm])
    P.op("dve", lambda e: e.scalar_tensor_tensor(out=hres[:], in0=hres[:], scalar=sm[:, 10:11], in1=gfin_bc[0:T, :], op0=ALU.mult, op1=ALU.mult),
         reads=[bhres, bsm, bgfin], writes=[bhres])
    P.dma("act", lambda e: e.dma_start(out=y_s[:, :], in_=hres[:]), reads=[bhres], is_out=True)


def _fm(W, c0):
    return np.ascontiguousarray(W[:, c0:c0 + 128].reshape(8, 128, 128).transpose(1, 0, 2).reshape(128, 1024))


def _tm(W, c0, n=512):
    blk = np.zeros((1024, 512), np.float32)
    blk[:, :n] = W[:, c0:c0 + n]
    return np.ascontiguousarray(blk.reshape(8, 128, 512).transpose(1, 0, 2).reshape(128, 4096))


def _vec_fm(v):
    return np.ascontiguousarray(np.asarray(v, np.float32).reshape(-1, 128).T)


def _rope_tab(pos, half):
    inv = np.float32(10000.0) ** (-(np.arange(half, dtype=np.float32)) / np.float32(half))
    ang = (pos.astype(np.float32)[:, None] * inv[None, :]).astype(np.float32)
    c, s_ = np.cos(ang).astype(np.float32), np.sin(ang).astype(np.float32)
    return np.concatenate([c, c, -s_, s_], axis=1).astype(np.float32)


def _host_shared(inp):
    f32 = np.float32
    w_in = np.asarray(inp["w_in"][0], f32)
    w_pa = np.asarray(inp["w_pa"][0], f32); w_pb = np.asarray(inp["w_pb"][0], f32); w_o = np.asarray(inp["w_o"][0], f32)
    fm = []
    for base in FM_COLS:
        for cc in range(8):
            fm.append(_fm(w_in, base + cc * 128))
    for W in (w_pa, w_pb):
        for cc in range(8):
            fm.append(_fm(W, cc * 128))
    tm = [_tm(w_in, 2048), _tm(w_in, 2560), _tm(w_in, 3072), _tm(w_in, 4608), _tm(w_in, 5120, 72), _tm(w_o, 0), _tm(w_o, 512)]
    w_ada = np.asarray(inp["w_ada"][0], f32)
    wada = np.stack([_tm(w_ada, b * 512) for b in range(6)])
    wr = []
    for W in (inp["w_ra"][0], inp["w_rx"][0]):
        W = np.asarray(W, f32).reshape(4, 2, 128, 256).transpose(2, 0, 1, 3)
        wr.append(W.reshape(128, 2048))
    w_rr = np.ascontiguousarray(np.concatenate(wr, axis=1))
    chp = np.zeros((128, NCP), f32)
    chp[:, CP_GN:CP_GN + 8] = _vec_fm(inp["g_norm"][0])
    chp[:, CP_BADA:CP_BADA + 24] = _vec_fm(inp["b_ada"][0])
    chp[:, CP_WCONV:CP_WCONV + 32] = np.asarray(inp["w_conv"][0], f32).reshape(4, 8, 128).transpose(2, 0, 1).reshape(128, 32)
    chp[:, CP_BCONV:CP_BCONV + 8] = _vec_fm(inp["b_conv"][0])
    chp[:, CP_BRA:CP_BRA + 8] = _vec_fm(inp["b_ra"][0])
    chp[:, CP_BRX:CP_BRX + 8] = _vec_fm(inp["b_rx"][0])
    chp[:, CP_LAM:CP_LAM + 8] = _vec_fm(inp["lru_lambda"][0])
    cst = np.zeros((128, CS_END), f32)
    ar = np.arange(128)
    cst[:, CS_ID:CS_ID + 128] = np.eye(128, dtype=f32)
    cst[:, CS_TS:CS_TS + 128] = (ar[:, None] < ar[None, :]).astype(f32)
    cst[:, CS_JF:CS_JF + 256] = np.arange(256, dtype=f32)[None, :]
    cst[:, CS_IR:CS_IR + 128] = ar.astype(f32)[None, :]
    cst[:, CS_CAUS:CS_CAUS + 128] = np.where(ar[None, :] <= ar[:, None], 0.0, -1e30).astype(f32)
    cst[:, CS_J1] = ar + 1
    cst[:, CS_J1 + 1] = ar + 129
    cst[:, CS_P2:CS_P2 + 32] = (0.5 ** np.arange(1, 33, dtype=np.float64)).astype(f32)[None, :]
    pos = np.arange(SEQ)
    ropeq = _rope_tab(pos, 64)
    ropei = _rope_tab(pos, 32)
    ps_ = np.full((NS,), PAST)
    ropes = np.concatenate([_rope_tab(ps_, 64), _rope_tab(ps_, 32)], axis=1)
    sh = {
        "cache_k": np.asarray(inp["cache_k"], f32).reshape(NPOOL * 128, 256),
        "cache_v": np.asarray(inp["cache_v"], f32).reshape(NPOOL * 128, 256),
        "cache_i": np.asarray(inp["cache_idx_k"], f32).reshape(NPOOL, 128 * 64),
        "w_ada": wada, "b_ada": np.asarray(inp["b_ada"], f32).reshape(1, 3 * D),
        "w_fm": np.stack(fm), "w_tm": np.stack(tm), "w_rr": w_rr, "chp": chp, "cst": cst,
        "ropeq": ropeq, "ropei": ropei, "ropes": np.ascontiguousarray(ropes),
        "g_fin": np.asarray(inp["g_final"], f32).reshape(1, D),
        "idx_gb": np.concatenate([np.asarray(inp["idx_k_norm_g"][0], f32), np.asarray(inp["idx_k_norm_b"][0], f32)]).reshape(1, 128),
    }
    return sh


_NC_CACHE = {}


def kernel(**inputs):
    f32 = np.float32
    sh = _host_shared(inputs)
    in_maps = []
    for c in range(NCORE):
        m = dict(sh)
        s0, s1 = c * NS, (c + 1) * NS
        m["x_p"] = np.ascontiguousarray(np.asarray(inputs["x_prompt"][c], f32))
        m["x_s"] = np.ascontiguousarray(np.asarray(inputs["x_sample"][s0:s1, 0], f32))
        m["c5"] = np.ascontiguousarray(np.concatenate([np.asarray(inputs["c_prompt"][c:c + 1], f32), np.asarray(inputs["c_sample"][s0:s1], f32)], axis=0))
        m["st_conv"] = np.ascontiguousarray(np.asarray(inputs["state_conv"][0, s0:s1], f32).reshape(NS * 3, D))
        m["st_lru"] = np.ascontiguousarray(np.asarray(inputs["state_rglru"][0, s0:s1], f32))
        m["ptab"] = np.ascontiguousarray(np.asarray(inputs["page_table"][s0:s1], np.int32).T)
        in_maps.append(m)
    if "nc" not in _NC_CACHE:
        _NC_CACHE["nc"] = build_program()
    nc = _NC_CACHE["nc"]
    res = run_bass_kernel_spmd(nc, in_maps, core_ids=list(range(NCORE)))
    R = res.results
    cat = lambda k: np.stack([np.asarray(R[c][k], f32) for c in range(NCORE)])
    y_prompt = cat("y_p")
    y_sample = cat("y_s").reshape(NCORE * NS, 1, D)
    nk_p = cat("nk_p").reshape(1, NCORE, SEQ, 2, 128)
    nv_p = cat("nv_p").reshape(1, NCORE, SEQ, 2, 128)
    nki_p = cat("nki_p").reshape(1, NCORE, SEQ, 64)
    ncv_p = cat("ncv_p").reshape(1, NCORE, 3, D)
    nlr_p = cat("nlr_p").reshape(1, NCORE, D)
    nk_s = cat("nk_s").reshape(1, NCORE * NS, 1, 2, 128)
    nv_s = cat("nv_s").reshape(1, NCORE * NS, 1, 2, 128)
    nki_s = cat("nki_s").reshape(1, NCORE * NS, 1, 64)
    ncv_s = cat("ncv_s").reshape(1, NCORE * NS, 3, D)
    nlr_s = cat("nlr_s").reshape(1, NCORE * NS, D)
    return (y_prompt, y_sample, nk_p, nv_p, nki_p, ncv_p, nlr_p, nk_s, nv_s, nki_s, ncv_s, nlr_s)
```

```python
import numpy as np
import concourse.bass as bass
import concourse.mybir as mybir
from concourse.bass_utils import run_bass_kernel_spmd
from contextlib import ExitStack

F32 = mybir.dt.float32
BF16 = mybir.dt.bfloat16
I32 = mybir.dt.int32
AF = mybir.ActivationFunctionType
ALU = mybir.AluOpType
AX = mybir.AxisListType

D = 1024
SEQ = 4096
NCORE = 8
NS = 4
PAST = 16384
NPAGE = 128
NPOOL = 5120
D_IN = 7240
EPS = 1e-6
IDX_W_SCALE = 512.0 ** -0.5
ATT_SCALE = 128.0 ** -0.5
TOPK = 256
NEG = -30000.0
CH = 512
NCHUNK = SEQ // CH
N_BISECT = 14
N_BISECT_S = 20

FM_XA, FM_GA, FM_GB, FM_MA, FM_MB, FM_PA, FM_PB = 0, 8, 16, 24, 32, 40, 48
NFM = 56
FM_COLS = [0, 1024, 3584, 5192, 6216]
TM_Q0, TM_Q1, TM_KV, TM_QI, TM_KW, TM_O0, TM_O1 = range(7)
NTM = 7
TM_COLS = [2048, 2560, 3072, 4608, 5120]

CP_GN, CP_BADA, CP_WCONV, CP_BCONV, CP_BRA, CP_BRX, CP_LAM = 0, 8, 32, 64, 72, 80, 88
NCP = 96
CS_ID, CS_TS, CS_JF, CS_IR, CS_CAUS, CS_J1, CS_P2, CS_END = 0, 128, 256, 512, 640, 768, 770, 802

DO_SAMPLE = True
DO_ATTN = True
STOP = 'all'
NCHUNK_RUN = NCHUNK


class Buf:
    __slots__ = ("name", "w", "r")

    def __init__(self, name=""):
        self.name = name
        self.w = None
        self.r = []


class Prog:
    EPOCH = 30000

    def __init__(self, nc, es, n_dma_sems=12):
        self.nc = nc
        self.es = es
        self.engs = {"pe": nc.tensor, "act": nc.scalar, "dve": nc.vector, "pool": nc.gpsimd, "sp": nc.sync}
        self.cnt = {k: 0 for k in self.engs}
        self.sems = {k: [es.enter_context(nc.semaphore("s_" + k))] for k in self.engs}
        self.waited = {k: {} for k in self.engs}
        self.dsems = {}
        self.dcnt = {}
        self.dnext = {}
        for q in ("sp", "act", "pool"):
            self.dsems[q] = [es.enter_context(nc.semaphore("d%s%d" % (q, i))) for i in range(n_dma_sems)]
            self.dcnt[q] = [0] * n_dma_sems
            self.dnext[q] = 0
        self.ninst = 0
        self.out_toks = []

    def _wait(self, ek, tok):
        if tok is None:
            return
        sem, val = tok
        key = id(sem)
        if self.waited[ek].get(key, 0) >= val:
            return
        self.engs[ek].wait_ge(sem, val)
        self.waited[ek][key] = val

    def _same(self, ek, tok):
        if tok is None:
            return False
        sem = tok[0]
        for s_ in self.sems[ek]:
            if sem is s_:
                return True
        return False

    def _deps(self, ek, reads, writes):
        pe = (ek == "pe")
        for b in reads:
            if pe and self._same(ek, b.w):
                continue
            self._wait(ek, b.w)
        for b in writes:
            if not (pe and self._same(ek, b.w)):
                self._wait(ek, b.w)
            for t in b.r:
                if not (pe and self._same(ek, t)):
                    self._wait(ek, t)

    def _mark(self, tok, reads, writes):
        for b in reads:
            b.r.append(tok)
            if len(b.r) > 64:
                b.r = b.r[-48:]
        for b in writes:
            b.w = tok
            b.r = []

    def op(self, ek, fn, reads=(), writes=()):
        self._deps(ek, reads, writes)
        ins = fn(self.engs[ek])
        if self.cnt[ek] >= self.EPOCH:
            self.sems[ek].append(self.es.enter_context(self.nc.semaphore("s_%s_%d" % (ek, len(self.sems[ek])))))
            self.cnt[ek] = 0
        sem = self.sems[ek][-1]
        self.cnt[ek] += 1
        ins.then_inc(sem, 1)
        tok = (sem, self.cnt[ek])
        self._mark(tok, reads, writes)
        self.ninst += 1
        return tok

    def dma(self, ek, fn, reads=(), writes=(), is_out=False):
        q = ek
        i = self.dnext[q]
        self.dnext[q] = (i + 1) % len(self.dsems[q])
        sem = self.dsems[q][i]
        if self.dcnt[q][i] > 0:
            self._wait(ek, (sem, self.dcnt[q][i]))
        self._deps(ek, reads, writes)
        ins = fn(self.engs[ek])
        self.dcnt[q][i] += 16
        ins.then_inc(sem, 16)
        tok = (sem, self.dcnt[q][i])
        self._mark(tok, reads, writes)
        self.ninst += 1
        if is_out:
            self.out_toks.append(tok)
        return tok

    def all_tokens(self):
        toks = []
        for k in self.engs:
            if self.cnt[k] > 0:
                toks.append((self.sems[k][-1], self.cnt[k]))
        for q in self.dsems:
            for s, c in zip(self.dsems[q], self.dcnt[q]):
                if c > 0:
                    toks.append((s, c))
        return toks

    def barrier(self):
        toks = self.all_tokens()
        for ek in self.engs:
            for t in toks:
                self._wait(ek, t)

    def finish(self):
        toks = self.all_tokens()
        for t in toks:
            self._wait("sp", t)


class Ring:
    def __init__(self, items):
        self.items = items
        self.i = 0

    def next(self):
        it = self.items[self.i]
        self.i = (self.i + 1) % len(self.items)
        return it


def build_program():
    nc = bass.Bass("TRN2", target_bir_lowering=False)
    dt_in = lambda n, s, d=F32: nc.dram_tensor(n, s, d, kind="ExternalInput").ap()
    dt_out = lambda n, s, d=F32: nc.dram_tensor(n, s, d, kind="ExternalOutput").ap()

    x_p = dt_in("x_p", [SEQ, D])
    x_s = dt_in("x_s", [NS, D])
    c5 = dt_in("c5", [1 + NS, D])
    cache_k = dt_in("cache_k", [NPOOL * 128, 256])
    cache_v = dt_in("cache_v", [NPOOL * 128, 256])
    cache_i = dt_in("cache_i", [NPOOL, 128 * 64])
    st_conv = dt_in("st_conv", [NS * 3, D])
    st_lru = dt_in("st_lru", [NS, D])
    ptab = dt_in("ptab", [128, NS], I32)
    w_ada = dt_in("w_ada", [6, 128, 8 * 512])
    b_ada = dt_in("b_ada", [1, 3 * D])
    w_fm = dt_in("w_fm", [NFM, 128, 1024])
    w_tm = dt_in("w_tm", [NTM, 128, 4096])
    w_rr = dt_in("w_rr", [128, 2 * 2048])
    chp = dt_in("chp", [128, NCP])
    cst = dt_in("cst", [128, CS_END])
    ropeq = dt_in("ropeq", [SEQ, 256])
    ropei = dt_in("ropei", [SEQ, 128])
    ropes = dt_in("ropes", [NS, 384])
    g_fin = dt_in("g_fin", [1, D])
    idx_gb = dt_in("idx_gb", [1, 128])

    y_p = dt_out("y_p", [SEQ, D])
    y_s = dt_out("y_s", [NS, D])
    nk_p = dt_out("nk_p", [SEQ, 256])
    nv_p = dt_out("nv_p", [SEQ, 256])
    nki_p = dt_out("nki_p", [SEQ, 64])
    ncv_p = dt_out("ncv_p", [3, D])
    nlr_p = dt_out("nlr_p", [1, D])
    nk_s = dt_out("nk_s", [NS, 256])
    nv_s = dt_out("nv_s", [NS, 256])
    nki_s = dt_out("nki_s", [NS, 64])
    ncv_s = dt_out("ncv_s", [NS, 3, D])
    nlr_s = dt_out("nlr_s", [NS, D])

    wfm = nc.dram_tensor("wfm_bf", [NFM, 128, 1024], BF16, kind="Internal").ap()
    wtm = nc.dram_tensor("wtm_bf", [NTM, 128, 4096], BF16, kind="Internal").ap()

    with ExitStack() as es:
        P = Prog(nc, es)

        def sb(name, shape, dtype=F32, scope=es):
            return scope.enter_context(nc.sbuf_tensor(name, shape, dtype))

        def ring(name, n, shape, dtype=F32, scope=es):
            return Ring([(sb("%s%d" % (name, i), shape, dtype, scope), Buf(name)) for i in range(n)])

        OPQ = ["act", "dve", "pool"]

        psA = es.enter_context(nc.psum_tensor("psA", [128, 512], F32)); bpsA = Buf()
        psB = es.enter_context(nc.psum_tensor("psB", [128, 512], F32)); bpsB = Buf()
        psS0 = es.enter_context(nc.psum_tensor("psS0", [128, 512], F32)); bpsS0 = Buf()
        psS1 = es.enter_context(nc.psum_tensor("psS1", [128, 512], F32)); bpsS1 = Buf()
        psO = es.enter_context(nc.psum_tensor("psO", [128, 512], F32)); bpsO = Buf()
        psL = es.enter_context(nc.psum_tensor("psL", [128, 512], F32)); bpsL = Buf()
        psT0 = es.enter_context(nc.psum_tensor("psT0", [128, 512], F32)); bpsT0 = Buf()
        psT1 = es.enter_context(nc.psum_tensor("psT1", [128, 512], F32)); bpsT1 = Buf()
        rPS = Ring([(psA, bpsA), (psB, bpsB)])
        rPSS = Ring([(psS0, bpsS0), (psS1, bpsS1)])
        rPST = Ring([(psT0[:].bitcast(BF16), bpsT0), (psT1[:].bitcast(BF16), bpsT1)])
        rPACC = Ring([(psT0, bpsT0), (psT1, bpsT1)])

        cstt = sb("cstt", [128, CS_END]); bcst = Buf()
        chpt = sb("chpt", [128, NCP]); bchp = Buf()
        identb = sb("identb", [128, 128], BF16); bidb = Buf()
        ident4 = sb("ident4", [128, 512], BF16); bid4 = Buf()
        onesb = sb("onesb", [128, 128], BF16); bonesb = Buf()
        onesf = sb("onesf", [128, 128]); bonesf = Buf()
        clam = sb("clam", [128, 8]); bclam = Buf()
        gate_bc = sb("gate_bc", [128, D]); bgate = Buf()
        gfin_bc = sb("gfin_bc", [128, D]); bgfin = Buf()
        idxgb_bc = sb("idxgb_bc", [128, 128]); bidxgb = Buf()
        wrr = sb("wrr", [128, 4096], BF16); bwrr = Buf()
        A_p = sb("A_p", [128, 8]); bAp = Buf()
        B_p = sb("B_p", [128, 8]); bBp = Buf()
        mT = sb("mT", [128, 24 * 5]); bmT = Buf()
        silucT = sb("silucT", [128, 8 * 5], BF16); bsil = Buf()
        identf = cstt[:, CS_ID:CS_ID + 128]
        rFM = ring("wfmr", 4, [128, 1024], BF16)
        rTM = ring("wtmr", 2, [128, 4096], BF16)
        sS = es.enter_context(ExitStack())
        gate_s = sb("gate_s", [NS, D], F32, sS); bgs = Buf()

        P.dma("sp", lambda e: e.dma_start(out=cstt[:], in_=cst[:, :]), writes=[bcst])
        P.dma("sp", lambda e: e.dma_start(out=chpt[:], in_=chp[:, :]), writes=[bchp])
        P.dma("sp", lambda e: e.dma_start(out=gfin_bc[:], in_=g_fin[0:1, :].broadcast_to([128, D])), writes=[bgfin])
        P.dma("sp", lambda e: e.dma_start(out=gate_bc[:], in_=b_ada[0:1, 2 * D:3 * D].broadcast_to([128, D])), writes=[bgate])
        P.dma("sp", lambda e: e.dma_start(out=gate_s[:], in_=b_ada[0:1, 2 * D:3 * D].broadcast_to([NS, D])), writes=[bgs])
        P.dma("sp", lambda e: e.dma_start(out=idxgb_bc[:], in_=idx_gb[0:1, :].broadcast_to([128, 128])), writes=[bidxgb])
        P.op("dve", lambda e: e.tensor_copy(out=identb[:], in_=identf), reads=[bcst], writes=[bidb])
        for r4 in range(4):
            P.op("pool", lambda e: e.tensor_copy(out=ident4[:, r4 * 128:(r4 + 1) * 128], in_=identf), reads=[bcst], writes=[bid4])
        P.op("pool", lambda e: e.memset(onesb[:], 1.0), writes=[bonesb])
        P.op("pool", lambda e: e.memset(onesf[:], 1.0), writes=[bonesf])
        P.op("act", lambda e: e.activation(out=clam[:], in_=chpt[:, CP_LAM:CP_LAM + 8], func=AF.Exp, scale=-1.0), reads=[bchp], writes=[bclam])
        P.op("act", lambda e: e.activation(out=clam[:], in_=clam[:], func=AF.Ln, bias=1.0), reads=[bclam], writes=[bclam])
        P.op("dve", lambda e: e.tensor_scalar(out=clam[:], in0=clam[:], scalar1=-8.0, scalar2=None, op0=ALU.mult), reads=[bclam], writes=[bclam])

        with ExitStack() as s0:
            rst = ring("w0st", 2, [128, 4096], F32, s0)
            rsb = ring("w0sb", 2, [128, 4096], BF16, s0)
            k = 0
            for src, dst, n in ((w_fm, wfm, NFM // 4), (w_tm, wtm, NTM)):
                for b in range(n):
                    st, bst = rst.next()
                    sbf, bsbf = rsb.next()
                    if src is w_fm:
                        sap = src[4 * b:4 * b + 4].rearrange("n p f -> p n f")
                        dap = dst[4 * b:4 * b + 4].rearrange("n p f -> p n f")
                        tap_s = st[:].rearrange("p (n f) -> p n f", n=4)
                        tap_b = sbf[:].rearrange("p (n f) -> p n f", n=4)
                    else:
                        sap, dap, tap_s, tap_b = src[b], dst[b], st[:], sbf[:]
                    P.dma("sp", lambda e: e.dma_start(out=tap_s, in_=sap), writes=[bst])
                    ek = OPQ[k % 3]; k += 1
                    if ek == "act":
                        P.op(ek, lambda e: e.copy(out=sbf[:], in_=st[:]), reads=[bst], writes=[bsbf])
                    else:
                        P.op(ek, lambda e: e.tensor_copy(out=sbf[:], in_=st[:]), reads=[bst], writes=[bsbf])
                    P.dma("act", lambda e: e.dma_start(out=dap, in_=tap_b), reads=[bsbf])
            st, bst = rst.next()
            P.dma("sp", lambda e: e.dma_start(out=st[:], in_=w_rr[:, :]), writes=[bst])
            P.op("dve", lambda e: e.tensor_copy(out=wrr[:], in_=st[:]), reads=[bst], writes=[bwrr])

            c5t = sb("c5t", [1 + NS, D], F32, s0); bc5 = Buf()
            P.dma("sp", lambda e: e.dma_start(out=c5t[:], in_=c5[:, :]), writes=[bc5])
            P.op("act", lambda e: e.activation(out=c5t[:], in_=c5t[:], func=AF.Silu), reads=[bc5], writes=[bc5])
            for kc in range(8):
                P.op("pe", lambda e: e.transpose(out=psA[:, kc * 5:kc * 5 + 5], in_=c5t[:, kc * 128:(kc + 1) * 128], identity=identf[0:5, 0:5]),
                     reads=[bc5, bcst], writes=[bpsA])
            P.op("dve", lambda e: e.tensor_copy(out=silucT[:], in_=psA[:, 0:40]), reads=[bpsA], writes=[bsil])
            silrep = sb("silrep", [128, 8 * 128], BF16, s0); bsilrep = Buf()
            sil3 = silucT[:].rearrange("p (k t) -> p k t", t=5)
            P.op("dve", lambda e: e.tensor_copy(out=silrep[:].rearrange("p (k m) -> p k m", m=128),
                                                in_=sil3[:, :, 0:1].to_broadcast([128, 8, 128])), reads=[bsil], writes=[bsilrep])
            for blk in range(6):
                st, bst = rst.next()
                sbf, bsbf = rsb.next()
                P.dma("sp", lambda e: e.dma_start(out=st[:], in_=w_ada[blk]), writes=[bst])
                P.op(OPQ[blk % 3], (lambda e: e.copy(out=sbf[:], in_=st[:])) if blk % 3 == 0 else (lambda e: e.tensor_copy(out=sbf[:], in_=st[:])),
                     reads=[bst], writes=[bsbf])
                w3 = sbf[:].rearrange("p (k c) -> p k c", c=512)
                for q in range(4):
                    cc = blk * 4 + q
                    for kc in range(8):
                        P.op("pe", lambda e: e.matmul(psB[:, cc * 5:cc * 5 + 5], lhsT=w3[:, kc, q * 128:(q + 1) * 128], rhs=sil3[:, kc, :],
                                                      start=(kc == 0), stop=(kc == 7)), reads=[bsbf, bsil], writes=[bpsB])
                if blk >= 4:
                    hb = blk - 4
                    ps, bps = rPSS.next()
                    for kc in range(8):
                        P.op("pe", lambda e: e.matmul(ps[:, :], lhsT=silrep[:, kc * 128:(kc + 1) * 128], rhs=w3[:, kc, :],
                                                      start=(kc == 0), stop=(kc == 7)), reads=[bsbf, bsilrep], writes=[bps])
                    P.op("dve", lambda e: e.tensor_tensor(out=gate_bc[:, hb * 512:(hb + 1) * 512], in0=ps[:, :], in1=gate_bc[:, hb * 512:(hb + 1) * 512], op=ALU.add),
                         reads=[bps, bgate], writes=[bgate])
                    ps, bps = rPSS.next()
                    for kc in range(8):
                        P.op("pe", lambda e: e.matmul(ps[0:NS, :], lhsT=sil3[:, kc, 1:5], rhs=w3[:, kc, :],
                                                      start=(kc == 0), stop=(kc == 7)), reads=[bsbf, bsil], writes=[bps])
                    P.op("dve", lambda e: e.tensor_tensor(out=gate_s[:, hb * 512:(hb + 1) * 512], in0=ps[0:NS, :], in1=gate_s[:, hb * 512:(hb + 1) * 512], op=ALU.add),
                         reads=[bps, bgs], writes=[bgs])
            mT3 = mT[:].rearrange("p (c t) -> p c t", t=5)
            P.op("dve", lambda e: e.tensor_tensor(out=mT3, in0=psB[:, 0:120].rearrange("p (c t) -> p c t", t=5),
                                                  in1=chpt[:, CP_BADA:CP_BADA + 24].unsqueeze(2).to_broadcast([128, 24, 5]), op=ALU.add),
                 reads=[bpsB, bchp], writes=[bmT])
            P.op("dve", lambda e: e.scalar_tensor_tensor(out=A_p[:], in0=mT3[:, 8:16, 0], scalar=1.0, in1=chpt[:, CP_GN:CP_GN + 8], op0=ALU.add, op1=ALU.mult),
                 reads=[bmT, bchp], writes=[bAp])
            P.op("dve", lambda e: e.tensor_copy(out=B_p[:], in_=mT3[:, 0:8, 0]), reads=[bmT], writes=[bBp])
        P.barrier()


        def load_fm(idx):
            t, b = rFM.next()
            P.dma("sp", lambda e: e.dma_start(out=t[:], in_=wfm[idx]), writes=[b])
            return t[:].rearrange("p (k c) -> p k c", c=128), b

        def load_tm(idx):
            t, b = rTM.next()
            P.dma("sp", lambda e: e.dma_start(out=t[:], in_=wtm[idx]), writes=[b])
            return t[:].rearrange("p (k c) -> p k c", c=512), b

        def rope(ek2, out_ap, x_ap, cosf, sinf, H, Dh, t1, t2, reads, writes, bt1, bt2):
            hf = Dh // 2
            p = x_ap.shape[0]
            cb = cosf.unsqueeze(1).to_broadcast([p, H, Dh])
            s1 = sinf[:, 0:hf].unsqueeze(1).to_broadcast([p, H, hf])
            s2 = sinf[:, hf:Dh].unsqueeze(1).to_broadcast([p, H, hf])
            P.op("dve", lambda e: e.tensor_tensor(out=t1, in0=x_ap, in1=cb, op=ALU.mult), reads=reads, writes=[bt1])
            P.op("dve", lambda e: e.tensor_tensor(out=t2[:, :, 0:hf], in0=x_ap[:, :, hf:Dh], in1=s1, op=ALU.mult), reads=reads, writes=[bt2])
            P.op("dve", lambda e: e.tensor_tensor(out=t2[:, :, hf:Dh], in0=x_ap[:, :, 0:hf], in1=s2, op=ALU.mult), reads=reads, writes=[bt2])
            return P.op(ek2, lambda e: e.tensor_tensor(out=out_ap, in0=t1, in1=t2, op=ALU.add), reads=[bt1, bt2], writes=writes)

        if DO_SAMPLE:
            sample_phase(nc, P, locals())
            P.barrier()
        sS.close()

        if STOP != 'p0':
            prompt_phase(nc, P, locals())
        P.finish()
        print("ninst", P.ninst, {k: (len(P.sems[k]) - 1) * P.EPOCH + P.cnt[k] for k in P.cnt})
    return nc


def prompt_phase(nc, P, G):
    es = G["es"]; sb = G["sb"]; ring = G["ring"]
    x_p = G["x_p"]; y_p = G["y_p"]; nk_p = G["nk_p"]; nv_p = G["nv_p"]; nki_p = G["nki_p"]; ncv_p = G["ncv_p"]; nlr_p = G["nlr_p"]
    ropeq = G["ropeq"]; ropei = G["ropei"]
    rPS = G["rPS"]; rPSS = G["rPSS"]; rPST = G["rPST"]; rPACC = G["rPACC"]
    psO = G["psO"]; bpsO = G["bpsO"]; psL = G["psL"]; bpsL = G["bpsL"]
    cstt = G["cstt"]; bcst = G["bcst"]; chpt = G["chpt"]; bchp = G["bchp"]
    identb = G["identb"]; bidb = G["bidb"]; ident4 = G["ident4"]; bid4 = G["bid4"]
    onesb = G["onesb"]; bonesb = G["bonesb"]; identf = G["identf"]
    clam = G["clam"]; bclam = G["bclam"]
    gate_bc = G["gate_bc"]; bgate = G["bgate"]; gfin_bc = G["gfin_bc"]; bgfin = G["bgfin"]
    idxgb_bc = G["idxgb_bc"]; bidxgb = G["bidxgb"]
    wrr = G["wrr"]; bwrr = G["bwrr"]; A_p = G["A_p"]; bAp = G["bAp"]; B_p = G["B_p"]; bBp = G["bBp"]
    load_fm = G["load_fm"]; load_tm = G["load_tm"]; rope = G["rope"]
    caus = cstt[:, CS_CAUS:CS_CAUS + 128]
    wrr5 = wrr[:].rearrange("p (a n k c) -> p a n k c", a=2, n=4, k=2)

    KT = sb("KT", [128, 2 * SEQ], BF16); KT3 = KT[:].rearrange("p (g t) -> p g t", g=2)
    Vres = sb("Vres", [128, 32 * 256], BF16); V4 = Vres[:].rearrange("p (i g d) -> p i g d", i=32, g=2)
    kiT = sb("kiT", [128, SEQ], BF16)
    bKV = [Buf() for _ in range(32)]
    hist = sb("hist", [128, 8 * 3]); bhist = Buf(); hist3 = hist[:].rearrange("p (c j) -> p c j", j=3)
    hprev = sb("hprev", [128, 8]); bhprev = Buf()
    P.op("pool", lambda e: e.memset(hist[:], 0.0), writes=[bhist])
    P.op("pool", lambda e: e.memset(hprev[:], 0.0), writes=[bhprev])

    rX = ring("xt", 1, [128, D])
    rXh = ring("xh", 1, [128, D], BF16)
    xnT = sb("xnT", [128, 8 * CH], BF16); bxn = Buf(); xn3 = xnT[:].rearrange("p (k t) -> p k t", k=8)
    rq = sb("rq", [128, 4 * 256]); brq = Buf(); rq3 = rq[:].rearrange("p (t c) -> p t c", t=4)
    ri = sb("ri", [128, 4 * 128]); bri = Buf(); ri3 = ri[:].rearrange("p (t c) -> p t c", t=4)
    rQr = ring("qrot", 1, [128, 512], BF16)
    rKr = ring("krot", 1, [128, 256]); rVf = ring("vf", 1, [128, 256]); rKi = ring("kio", 1, [128, 64])
    rKb = ring("kb16", 1, [128, 256], BF16); rKi2 = ring("ki2", 1, [128, 128], BF16)
    qT = sb("qT", [128, 4 * 1024], BF16); bqT = [Buf() for _ in range(4)]; qT4 = qT[:].rearrange("p (t h q) -> p t h q", t=4, h=8)
    qiT = sb("qiT", [128, 4 * 512], BF16); bqiT = [Buf() for _ in range(4)]; qiT4 = qiT[:].rearrange("p (t h q) -> p t h q", t=4, h=4)
    wS = sb("wS", [128, 4 * 8]); bwS = [Buf() for _ in range(4)]; wS3 = wS[:].rearrange("p (t h) -> p t h", t=4)
    awS = sb("awS", [128, 4 * 8]); awS3 = awS[:].rearrange("p (t h) -> p t h", t=4)
    sgS = sb("sgS", [128, 4 * 8]); sgS3 = sgS[:].rearrange("p (t h) -> p t h", t=4)
    rDg = ring("diagS", 2, [128, 8 * 128], BF16)
    Wt = sb("bs_W", [128, 32]); bWt = Buf()
    W2t = sb("bs_W2", [128, 32]); bW2t = Buf()
    pow2 = cstt[:, CS_P2:CS_P2 + N_BISECT]
    sm = sb("smallst", [128, 16]); bsm = Buf()
    xa = sb("xa", [128, 2 * 515]); bxa = Buf(); xa3 = xa[:].rearrange("p (c t) -> p c t", c=2)
    xc = sb("xc", [128, 2 * CH]); bxc = Buf(); xc3 = xc[:].rearrange("p (c t) -> p c t", c=2)
    xcb = sb("xcb", [128, 2 * CH], BF16); bxcb = Buf(); xcb3 = xcb[:].rearrange("p (c t) -> p c t", c=2)
    junkb = xcb; bjunk = bxcb
    g_r = sb("g_r", [128, CH]); b_r = Buf()
    g_i = sb("g_i", [128, CH]); b_i = Buf()
    g_a = sb("g_a", [128, CH]); b_a = Buf()
    g_t = sb("g_t", [128, CH]); b_t = Buf()
    g_u = g_i; b_u = b_i
    g_h = sb("g_h", [128, CH]); b_h = Buf()
    rT1 = Ring([(g_a, b_a)]); rT2 = Ring([(g_t, b_t)])
    g_sg = g_r; b_sg = b_r
    actT = sb("actT", [128, 8 * CH], BF16); bact = [Buf() for _ in range(8)]; act3 = actT[:].rearrange("p (k t) -> p k t", k=8)
    mTt = sb("mTt", [128, 8 * CH], BF16); bmm = [Buf() for _ in range(8)]; m3 = mTt[:].rearrange("p (k t) -> p k t", k=8)
    rSg = Ring([(g_r, b_r), (g_i, b_i)])
    sc = sb("sc", [128, SEQ]); bsc = Buf()
    rMB = ring("MB", 2, [128, SEQ], BF16)
    junk8 = sb("junk8", [128, SEQ], mybir.dt.float8e4); bj8 = Buf()
    rR = ring("Rr", 2, [128, 512], BF16)
    rPT = ring("PTr", 2, [128, 512], BF16)
    rl = g_h; brl = b_h
    hres = sb("hres", [128, D]); bhres = Buf()
    yo = hres; byo = bhres
    lo = sb("bs_lo", [128, 1]); blo = Buf()
    wd = sb("bs_w", [128, 1]); bwd = Buf()
    mid = sb("bs_mid", [128, 1]); bmid = Buf()
    cnt = sb("bs_cnt", [128, 1]); bcnt = Buf()
    cond = sb("bs_cond", [128, 1]); bcond = Buf()
    otr = hres; botr = bhres

    for c in range(NCHUNK_RUN):
        t0 = c * CH
        P.dma("sp", lambda e: e.dma_start(out=rq3, in_=ropeq[t0:t0 + CH, :].rearrange("(t p) c -> p t c", p=128)), writes=[brq])
        P.dma("sp", lambda e: e.dma_start(out=ri3, in_=ropei[t0:t0 + CH, :].rearrange("(t p) c -> p t c", p=128)), writes=[bri])
        for tt in range(4):
            xt, bxt = rX.next()
            xh, bxh = rXh.next()
            P.dma("sp", lambda e: e.dma_start(out=xt[:], in_=x_p[t0 + tt * 128:t0 + (tt + 1) * 128, :]), writes=[bxt])
            P.op("act", lambda e: e.activation(out=junkb[:], in_=xt[:], func=AF.Square, accum_out=sm[:, 0:1]), reads=[bxt], writes=[bjunk, bsm])
            P.op("act", lambda e: e.activation(out=sm[:, 1:2], in_=sm[:, 0:1], func=AF.Sqrt, scale=1.0 / D, bias=EPS), reads=[bsm], writes=[bsm])
            P.op("dve", lambda e: e.reciprocal(out=sm[:, 2:3], in_=sm[:, 1:2]), reads=[bsm], writes=[bsm])
            P.op("dve", lambda e: e.tensor_scalar(out=xh[:], in0=xt[:], scalar1=sm[:, 2:3], scalar2=None, op0=ALU.mult), reads=[bxt, bsm], writes=[bxh])
            pt, bpt = rPST.next()
            for kc in range(8):
                P.op("pe", lambda e: e.transpose(out=pt[:, kc * 128:(kc + 1) * 128], in_=xh[:, kc * 128:(kc + 1) * 128], identity=identb[:]),
                     reads=[bxh, bidb], writes=[bpt])
            for kc in range(8):
                P.op("dve", lambda e: e.tensor_scalar(out=xn3[:, kc, tt * 128:(tt + 1) * 128], in0=pt[:, kc * 128:(kc + 1) * 128],
                                                      scalar1=A_p[:, kc:kc + 1], scalar2=B_p[:, kc:kc + 1], op0=ALU.mult, op1=ALU.add),
                     reads=[bpt, bAp, bBp], writes=[bxn])

        if STOP == 'p1':
            continue
        def p2_mm(blk, tt, w3, bw):
            ncol = 72 if blk == TM_KW else 512
            ps, bps = rPS.next()
            for kc in range(8):
                P.op("pe", lambda e: e.matmul(ps[:, 0:ncol], lhsT=xn3[:, kc, tt * 128:(tt + 1) * 128], rhs=w3[:, kc, 0:ncol],
                                              start=(kc == 0), stop=(kc == 7)), reads=[bxn, bw], writes=[bps])
            return ps, bps

        def p2_post(blk, tt, ps, bps):
            i = c * 4 + tt
            r0 = t0 + tt * 128
            t1, bt1 = rT1.next(); t2, bt2 = rT2.next()
            if blk in (TM_Q0, TM_Q1):
                qr, bqr = rQr.next()
                rope("pool", qr[:].rearrange("p (h d) -> p h d", h=4), ps[:, :].rearrange("p (h d) -> p h d", h=4),
                     rq3[:, tt, 0:128], rq3[:, tt, 128:256], 4, 128,
                     t1[:].rearrange("p (h d) -> p h d", h=4), t2[:].rearrange("p (h d) -> p h d", h=4),
                     [bps, brq], [bqr], bt1, bt2)
                pt, bpt = rPST.next()
                for h in range(4):
                    P.op("pe", lambda e: e.transpose(out=pt[:, h * 128:(h + 1) * 128], in_=qr[:, h * 128:(h + 1) * 128], identity=identb[:]),
                         reads=[bqr, bidb], writes=[bpt])
                h0 = 4 * (blk - TM_Q0)
                P.op("act", lambda e: e.copy(out=qT4[:, tt, h0:h0 + 4, :], in_=pt[:, 0:512].rearrange("p (h q) -> p h q", h=4)),
                     reads=[bpt], writes=[bqT[tt]])
            elif blk == TM_KV:
                kr, bkr = rKr.next(); vf, bvf = rVf.next(); kb, bkb = rKb.next()
                rope("pool", kr[:].rearrange("p (h d) -> p h d", h=2), ps[:, 0:256].rearrange("p (h d) -> p h d", h=2),
                     rq3[:, tt, 0:128], rq3[:, tt, 128:256], 2, 128,
                     t1[:, 0:256].rearrange("p (h d) -> p h d", h=2), t2[:, 0:256].rearrange("p (h d) -> p h d", h=2),
                     [bps, brq], [bkr], bt1, bt2)
                P.dma("act", lambda e: e.dma_start(out=nk_p[r0:r0 + 128, :], in_=kr[:]), reads=[bkr], is_out=True)
                P.op("act", lambda e: e.copy(out=vf[:], in_=ps[:, 256:512]), reads=[bps], writes=[bvf])
                P.dma("act", lambda e: e.dma_start(out=nv_p[r0:r0 + 128, :], in_=vf[:]), reads=[bvf], is_out=True)
                P.op("act", lambda e: e.copy(out=V4[:, i, :, :], in_=ps[:, 256:512].rearrange("p (g d) -> p g d", g=2)), reads=[bps], writes=[bKV[i]])
                P.op("pool", lambda e: e.tensor_copy(out=kb[:], in_=kr[:]), reads=[bkr], writes=[bkb])
                pt, bpt = rPST.next()
                for g in range(2):
                    P.op("pe", lambda e: e.transpose(out=pt[:, g * 128:(g + 1) * 128], in_=kb[:, g * 128:(g + 1) * 128], identity=identb[:]),
                         reads=[bkb, bidb], writes=[bpt])
                P.op("act", lambda e: e.copy(out=KT3[:, :, r0:r0 + 128], in_=pt[:, 0:256].rearrange("p (g q) -> p g q", g=2)),
                     reads=[bpt], writes=[bKV[i]])
            elif blk == TM_QI:
                qr, bqr = rQr.next()
                rope("pool", qr[:].rearrange("p (h d) -> p h d", h=8), ps[:, :].rearrange("p (h d) -> p h d", h=8),
                     ri3[:, tt, 0:64], ri3[:, tt, 64:128], 8, 64,
                     t1[:].rearrange("p (h d) -> p h d", h=8), t2[:].rearrange("p (h d) -> p h d", h=8),
                     [bps, bri], [bqr], bt1, bt2)
                pt, bpt = rPST.next()
                for hp in range(4):
                    P.op("pe", lambda e: e.transpose(out=pt[:, hp * 128:(hp + 1) * 128], in_=qr[:, hp * 128:(hp + 1) * 128], identity=identb[:]),
                         reads=[bqr, bidb], writes=[bpt])
                P.op("act", lambda e: e.copy(out=qiT4[:, tt, :, :], in_=pt[:, 0:512].rearrange("p (h q) -> p h q", h=4)),
                     reads=[bpt], writes=[bqiT[tt]])
            else:
                kio, bkio = rKi.next(); ki2, bki2 = rKi2.next()
                P.op("dve", lambda e: e.tensor_scalar(out=wS3[:, tt, :], in0=ps[:, 64:72], scalar1=IDX_W_SCALE, scalar2=None, op0=ALU.mult),
                     reads=[bps], writes=[bwS[tt]])
                P.op("dve", lambda e: e.tensor_scalar(out=sgS3[:, tt, :], in0=wS3[:, tt, :], scalar1=0.0, scalar2=0.5, op0=ALU.is_ge, op1=ALU.subtract),
                     reads=[bwS[tt]], writes=[bwS[tt]])
                P.op("dve", lambda e: e.scalar_tensor_tensor(out=awS3[:, tt, :], in0=wS3[:, tt, :], scalar=4.0, in1=sgS3[:, tt, :], op0=ALU.mult, op1=ALU.mult),
                     reads=[bwS[tt]], writes=[bwS[tt]])
                P.op("dve", lambda e: e.tensor_reduce(out=sm[:, 4:5], in_=ps[:, 0:64], axis=AX.X, op=ALU.add), reads=[bps], writes=[bsm])
                P.op("dve", lambda e: e.tensor_scalar(out=sm[:, 5:6], in0=sm[:, 4:5], scalar1=-1.0 / 64, scalar2=None, op0=ALU.mult), reads=[bsm], writes=[bsm])
                P.op("dve", lambda e: e.tensor_scalar(out=t1[:, 0:64], in0=ps[:, 0:64], scalar1=sm[:, 5:6], scalar2=None, op0=ALU.add),
                     reads=[bps, bsm], writes=[bt1])
                P.op("act", lambda e: e.activation(out=t2[:, 0:64], in_=t1[:, 0:64], func=AF.Square, accum_out=sm[:, 6:7]), reads=[bt1], writes=[bt2, bsm])
                P.op("act", lambda e: e.activation(out=sm[:, 7:8], in_=sm[:, 6:7], func=AF.Sqrt, scale=1.0 / 64, bias=EPS), reads=[bsm], writes=[bsm])
                P.op("dve", lambda e: e.reciprocal(out=sm[:, 8:9], in_=sm[:, 7:8]), reads=[bsm], writes=[bsm])
                P.op("dve", lambda e: e.scalar_tensor_tensor(out=t1[:, 64:128], in0=t1[:, 0:64], scalar=sm[:, 8:9], in1=idxgb_bc[:, 0:64], op0=ALU.mult, op1=ALU.mult),
                     reads=[bt1, bsm, bidxgb], writes=[bt1])
                P.op("dve", lambda e: e.tensor_tensor(out=t1[:, 128:192], in0=t1[:, 64:128], in1=idxgb_bc[:, 64:128], op=ALU.add),
                     reads=[bt1, bidxgb], writes=[bt1])
                rope("pool", kio[:].unsqueeze(1), t1[:, 128:192].unsqueeze(1), ri3[:, tt, 0:64], ri3[:, tt, 64:128], 1, 64,
                     t2[:, 64:128].unsqueeze(1), t2[:, 128:192].unsqueeze(1), [bt1, bri], [bkio], bt2, bt2)
                P.dma("act", lambda e: e.dma_start(out=nki_p[r0:r0 + 128, :], in_=kio[:]), reads=[bkio], is_out=True)
                P.op("pool", lambda e: e.tensor_copy(out=ki2[:].rearrange("p (a d) -> p a d", a=2), in_=kio[:].unsqueeze(1).to_broadcast([128, 2, 64])),
                     reads=[bkio], writes=[bki2])
                pt, bpt = rPST.next()
                P.op("pe", lambda e: e.transpose(out=pt[:, 0:128], in_=ki2[:], identity=identb[:]), reads=[bki2, bidb], writes=[bpt])
                P.op("act", lambda e: e.copy(out=kiT[:, r0:r0 + 128], in_=pt[:, 0:128]), reads=[bpt], writes=[bKV[i]])


        pend = None
        for blk in (TM_Q0, TM_Q1, TM_KV, TM_QI, TM_KW):
            w3, bw = load_tm(blk)
            for tt in range(4):
                ps, bps = p2_mm(blk, tt, w3, bw)
                if pend is not None:
                    p2_post(*pend)
                pend = (blk, tt, ps, bps)
        p2_post(*pend)
        if STOP == 'p2':
            continue
        for n in range(4):
            for c2 in range(2):
                cc = 2 * n + c2
                w3, bw = load_fm(FM_XA + cc)
                ps, bps = rPS.next()
                for kc in range(8):
                    P.op("pe", lambda e: e.matmul(ps[:, :], lhsT=w3[:, kc, :], rhs=xn3[:, kc, :], start=(kc == 0), stop=(kc == 7)),
                         reads=[bw, bxn], writes=[bps])
                P.op("pool", lambda e: e.tensor_copy(out=xa3[:, c2, 0:3], in_=hist3[:, cc, :]), reads=[bhist], writes=[bxa])
                P.op("act", lambda e: e.copy(out=xa3[:, c2, 3:515], in_=ps[:, :]), reads=[bps], writes=[bxa])
                P.op("pool", lambda e: e.tensor_copy(out=hist3[:, cc, :], in_=xa3[:, c2, 512:515]), reads=[bxa], writes=[bhist])
                wc = lambda j: chpt[:, CP_WCONV + j * 8 + cc:CP_WCONV + j * 8 + cc + 1]
                P.op("dve", lambda e: e.tensor_scalar(out=xc3[:, c2, :], in0=xa3[:, c2, 3:515], scalar1=wc(3), scalar2=chpt[:, CP_BCONV + cc:CP_BCONV + cc + 1],
                                                      op0=ALU.mult, op1=ALU.add), reads=[bxa, bchp], writes=[bxc])
                for j in range(3):
                    P.op("dve", lambda e: e.scalar_tensor_tensor(out=xc3[:, c2, :], in0=xa3[:, c2, j:j + 512], scalar=wc(j), in1=xc3[:, c2, :],
                                                                 op0=ALU.mult, op1=ALU.add), reads=[bxa, bxc, bchp], writes=[bxc])
                P.op("pool", lambda e: e.tensor_copy(out=xcb3[:, c2, :], in_=xc3[:, c2, :]), reads=[bxc], writes=[bxcb])
            for c2 in range(2):
                cc = 2 * n + c2
                for which, dst, bdst, bias0 in ((0, g_r, b_r, CP_BRA), (1, g_i, b_i, CP_BRX)):
                    ps, bps = rPS.next()
                    for k2 in range(2):
                        P.op("pe", lambda e: e.matmul(ps[:, :], lhsT=wrr5[:, which, n, k2, c2 * 128:(c2 + 1) * 128], rhs=xcb3[:, k2, :],
                                                      start=(k2 == 0), stop=(k2 == 1)), reads=[bwrr, bxcb], writes=[bps])
                    P.op("act", lambda e: e.activation(out=dst[:], in_=ps[:, :], func=AF.Sigmoid, bias=chpt[:, bias0 + cc:bias0 + cc + 1]),
                         reads=[bps, bchp], writes=[bdst])
                P.op("dve", lambda e: e.tensor_tensor(out=g_u[:], in0=g_i[:], in1=xc3[:, c2, :], op=ALU.mult), reads=[b_i, bxc], writes=[b_u])
                P.op("act", lambda e: e.activation(out=g_a[:], in_=g_r[:], func=AF.Exp, scale=clam[:, cc:cc + 1]), reads=[b_r, bclam], writes=[b_a])
                P.op("dve", lambda e: e.scalar_tensor_tensor(out=g_t[:], in0=g_a[:], scalar=-1.0, in1=g_a[:], op0=ALU.mult, op1=ALU.mult), reads=[b_a], writes=[b_t])
                P.op("act", lambda e: e.activation(out=g_t[:], in_=g_t[:], func=AF.Sqrt, bias=1.0), reads=[b_t], writes=[b_t])
                P.op("dve", lambda e: e.tensor_tensor(out=g_u[:], in0=g_u[:], in1=g_t[:], op=ALU.mult), reads=[b_u, b_t], writes=[b_u])
                P.op("dve", lambda e: e.tensor_tensor_scan(out=g_h[:], data0=g_a[:], data1=g_u[:], initial=hprev[:, cc:cc + 1], op0=ALU.mult, op1=ALU.add),
                     reads=[b_a, b_u, bhprev], writes=[b_h])
                P.op("dve", lambda e: e.tensor_copy(out=hprev[:, cc:cc + 1], in_=g_h[:, CH - 1:CH]), reads=[b_h], writes=[bhprev])
                w3, bw = load_fm(FM_GA + cc)
                ps, bps = rPS.next()
                for kc in range(8):
                    P.op("pe", lambda e: e.matmul(ps[:, :], lhsT=w3[:, kc, :], rhs=xn3[:, kc, :], start=(kc == 0), stop=(kc == 7)),
                         reads=[bw, bxn], writes=[bps])
                P.op("act", lambda e: e.activation(out=g_sg[:], in_=ps[:, :], func=AF.Silu), reads=[bps], writes=[b_sg])
                P.op("dve", lambda e: e.tensor_tensor(out=act3[:, cc, :], in0=g_h[:], in1=g_sg[:], op=ALU.mult), reads=[b_h, b_sg], writes=[bact[cc]])
        if c == NCHUNK_RUN - 1:
            for src3, nrow, dst in ((hist3, 3, ncv_p), (hprev[:].unsqueeze(2), 1, nlr_p)):
                for hb in range(2):
                    ps, bps = rPS.next()
                    for c4 in range(4):
                        cc = hb * 4 + c4
                        P.op("pe", lambda e: e.transpose(out=ps[0:nrow, c4 * 128:(c4 + 1) * 128], in_=src3[:, cc, :], identity=identf),
                             reads=[bhist, bhprev, bcst], writes=[bps])
                    P.op("act", lambda e: e.copy(out=otr[0:nrow, hb * 512:(hb + 1) * 512], in_=ps[0:nrow, :]), reads=[bps], writes=[botr])
                P.dma("act", lambda e: e.dma_start(out=dst[:, :], in_=otr[0:nrow, :]), reads=[botr], is_out=True)
        if STOP == 'p3':
            continue
        for cc in range(8):
            w3, bw = load_fm(FM_MA + cc)
            ps, bps = rPS.next()
            for kc in range(8):
                P.op("pe", lambda e: e.matmul(ps[:, :], lhsT=w3[:, kc, :], rhs=xn3[:, kc, :], start=(kc == 0), stop=(kc == 7)),
                     reads=[bw, bxn], writes=[bps])
            sg, bsg = rSg.next()
            P.op("act", lambda e: e.activation(out=sg[:], in_=ps[:, :], func=AF.Sigmoid), reads=[bps], writes=[bsg])
            w3, bw = load_fm(FM_PA + cc)
            ps, bps = rPS.next()
            for kc in range(8):
                P.op("pe", lambda e: e.matmul(ps[:, :], lhsT=w3[:, kc, :], rhs=act3[:, kc, :], start=(kc == 0), stop=(kc == 7)),
                     reads=[bw, bact[kc]], writes=[bps])
            P.op("dve", lambda e: e.tensor_tensor(out=m3[:, cc, :], in0=ps[:, :], in1=sg[:], op=ALU.mult), reads=[bps, bsg], writes=[bmm[cc]])

        if STOP == 'p4':
            continue
        def stage_A(tt):
            i = c * 4 + tt
            nk = (i + 1) * 128
            dg, bdg = rDg.next()
            dg3 = dg[:].rearrange("p (h q) -> p h q", h=8)
            P.op("pool", lambda e: e.tensor_tensor(out=dg3, in0=identb[:].unsqueeze(1).to_broadcast([128, 8, 128]),
                                                   in1=sgS3[:, tt, :].unsqueeze(2).to_broadcast([128, 8, 128]), op=ALU.mult),
                 reads=[bidb, bwS[tt]], writes=[bdg])
            pendA = None

            def accA(kb, h, k0, cols, R, bR, pacc, bpacc):
                P.op("pe", lambda e: e.matmul(pacc[:, 0:cols], lhsT=dg3[:, h, :], rhs=R[:, 0:cols], start=(h == 0), stop=(h == 7)),
                     reads=[bdg, bR], writes=[bpacc])
                if h == 7:
                    P.op("act", lambda e: e.copy(out=sc[:, k0:k0 + cols], in_=pacc[:, 0:cols]), reads=[bpacc], writes=[bsc])

            for kb in range((nk + 511) // 512):
                k0 = kb * 512
                cols = min(512, nk - k0)
                pacc, bpacc = rPACC.next()
                for h in range(8):
                    hp, h2 = h // 2, h % 2
                    ps, bps = rPS.next()
                    P.op("pe", lambda e: e.matmul(ps[:, 0:cols], lhsT=qiT4[64 * h2:64 * h2 + 64, tt, hp, :], rhs=kiT[64 * h2:64 * h2 + 64, k0:k0 + cols],
                                                  start=True, stop=True), reads=[bqiT[tt]] + bKV[k0 // 128:(k0 + cols) // 128], writes=[bps])
                    R, bR = rR.next()
                    P.op("act", lambda e: e.activation(out=R[:, 0:cols], in_=ps[:, 0:cols], func=AF.Relu, scale=awS3[:, tt, h:h + 1]),
                         reads=[bps, bwS[tt]], writes=[bR])
                    if pendA is not None:
                        accA(*pendA)
                    pendA = (kb, h, k0, cols, R, bR, pacc, bpacc)
            accA(*pendA)
            P.op("dve", lambda e: e.tensor_tensor(out=sc[:, i * 128:nk], in0=sc[:, i * 128:nk], in1=caus, op=ALU.add), reads=[bsc, bcst], writes=[bsc])

        def stage_B(tt):
            i = c * 4 + tt
            nk = (i + 1) * 128
            MB, bMB = rMB.next()
            NB = N_BISECT
            if i >= 2:
                P.op("dve", lambda e: e.tensor_reduce(out=mid[:], in_=sc[:, 0:nk], axis=AX.X, op=ALU.max), reads=[bsc], writes=[bmid])
                P.op("dve", lambda e: e.tensor_reduce(out=lo[:], in_=sc[:, 0:i * 128], axis=AX.X, op=ALU.min), reads=[bsc], writes=[blo])
                P.op("dve", lambda e: e.tensor_tensor(out=wd[:], in0=mid[:], in1=lo[:], op=ALU.subtract), reads=[bmid, blo], writes=[bwd])
                P.op("dve", lambda e: e.tensor_scalar(out=Wt[:, 0:NB], in0=pow2, scalar1=wd[:], scalar2=None, op0=ALU.mult), reads=[bcst, bwd], writes=[bWt])
                P.op("dve", lambda e: e.tensor_scalar(out=W2t[:, 0:NB], in0=pow2, scalar1=wd[:], scalar2=2.0, op0=ALU.mult, op1=ALU.mult), reads=[bcst, bwd], writes=[bW2t])
                P.op("dve", lambda e: e.tensor_tensor(out=mid[:], in0=lo[:], in1=Wt[:, 0:1], op=ALU.add), reads=[blo, bWt], writes=[bmid])
                for k in range(NB):
                    P.op("dve", lambda e: e.tensor_scalar(out=junk8[:, 0:nk], in0=sc[:, 0:nk], scalar1=mid[:], scalar2=None, op0=ALU.is_ge, op1=ALU.add,
                                                          accum_out=cnt[:], saturate=False), reads=[bsc, bmid], writes=[bj8, bcnt])
                    if k < NB - 1:
                        P.op("dve", lambda e: e.scalar_tensor_tensor(out=cond[:], in0=cnt[:], scalar=float(TOPK) - 0.5, in1=W2t[:, k + 1:k + 2], op0=ALU.is_ge, op1=ALU.mult),
                             reads=[bcnt, bW2t], writes=[bcond])
                        P.op("dve", lambda e: e.scalar_tensor_tensor(out=mid[:], in0=cond[:], scalar=Wt[:, k + 1:k + 2], in1=mid[:], op0=ALU.subtract, op1=ALU.add),
                             reads=[bcond, bWt, bmid], writes=[bmid])
                    else:
                        P.op("dve", lambda e: e.scalar_tensor_tensor(out=cond[:], in0=cnt[:], scalar=float(TOPK) - 0.5, in1=Wt[:, k:k + 1], op0=ALU.is_ge, op1=ALU.mult),
                             reads=[bcnt, bWt], writes=[bcond])
                        P.op("dve", lambda e: e.scalar_tensor_tensor(out=lo[:], in0=cond[:], scalar=Wt[:, k:k + 1], in1=mid[:], op0=ALU.subtract, op1=ALU.add),
                             reads=[bcond, bWt, bmid], writes=[blo])
                P.op("dve", lambda e: e.tensor_scalar(out=MB[:, 0:nk], in0=sc[:, 0:nk], scalar1=lo[:], scalar2=NEG, op0=ALU.is_lt, op1=ALU.mult),
                     reads=[bsc, blo], writes=[bMB])
            else:
                P.op("dve", lambda e: e.tensor_scalar(out=MB[:, 0:nk], in0=sc[:, 0:nk], scalar1=-1e29, scalar2=NEG, op0=ALU.is_lt, op1=ALU.mult),
                     reads=[bsc], writes=[bMB])
            return MB, bMB

        def stage_C(tt, MB, bMB):
            i = c * 4 + tt
            for g in range(2):
                def S_(j):
                    ps, bps = rPSS.next()
                    P.op("pe", lambda e: e.matmul(ps[:, :], lhsT=KT3[:, g, j * 128:(j + 1) * 128], rhs=qT4[:, tt, 4 * g:4 * g + 4, :],
                                                  start=True, stop=False), reads=[bKV[j], bqT[tt]], writes=[bps])
                    P.op("pe", lambda e: e.matmul(ps[:, :], lhsT=MB[:, j * 128:(j + 1) * 128], rhs=ident4[:], start=False, stop=True),
                         reads=[bMB, bid4], writes=[bps])
                    return ps, bps
                nxt = S_(0)
                for j in range(i + 1):
                    ps, bps = nxt
                    if j + 1 <= i:
                        nxt = S_(j + 1)
                    PT, bPT = rPT.next()
                    P.op("act", lambda e: e.activation(out=PT[:], in_=ps[:, :], func=AF.Exp, scale=ATT_SCALE), reads=[bps], writes=[bPT])
                    P.op("pe", lambda e: e.matmul(psO[:, :], lhsT=V4[:, j, g, :], rhs=PT[:], start=(j == 0), stop=(j == i)), reads=[bKV[j], bPT], writes=[bpsO])
                    P.op("pe", lambda e: e.matmul(psL[:, :], lhsT=onesb[:], rhs=PT[:], start=(j == 0), stop=(j == i)), reads=[bonesb, bPT], writes=[bpsL])
                bo = [bact[4 * g + hh] for hh in range(4)]
                P.op("act", lambda e: e.activation(out=rl[:], in_=psL[:, :], func=AF.Ln), reads=[bpsL], writes=[brl])
                P.op("act", lambda e: e.activation(out=rl[:], in_=rl[:], func=AF.Exp, scale=-1.0), reads=[brl], writes=[brl])
                P.op("act", lambda e: e.copy(out=act3[:, 4 * g:4 * g + 4, tt * 128:(tt + 1) * 128], in_=psO[:, :].rearrange("p (h q) -> p h q", h=4)),
                     reads=[bpsO], writes=bo)
                P.op("pool", lambda e: e.tensor_tensor(out=act3[:, 4 * g:4 * g + 4, tt * 128:(tt + 1) * 128], in0=act3[:, 4 * g:4 * g + 4, tt * 128:(tt + 1) * 128],
                                                       in1=rl[:].rearrange("p (h q) -> p h q", h=4), op=ALU.mult),
                     reads=bo + [brl], writes=bo)

        pend = None
        for tt in range(4):
            stage_A(tt)
            mb = stage_B(tt)
            if pend is not None:
                stage_C(*pend)
            pend = (tt, mb[0], mb[1])
        stage_C(*pend)
        for cc in range(8):
            w3, bw = load_fm(FM_GB + cc)
            ps, bps = rPS.next()
            for kc in range(8):
                P.op("pe", lambda e: e.matmul(ps[:, :], lhsT=w3[:, kc, :], rhs=xn3[:, kc, :], start=(kc == 0), stop=(kc == 7)),
                     reads=[bw, bxn], writes=[bps])
            sg, bsg = rSg.next()
            P.op("act", lambda e: e.activation(out=sg[:], in_=ps[:, :], func=AF.Silu), reads=[bps], writes=[bsg])
            P.op("pool", lambda e: e.tensor_tensor(out=act3[:, cc, :], in0=act3[:, cc, :], in1=sg[:], op=ALU.mult), reads=[bact[cc], bsg], writes=[bact[cc]])
        if STOP == 'p5':
            continue
        for cc in range(8):
            w3, bw = load_fm(FM_MB + cc)
            ps, bps = rPS.next()
            for kc in range(8):
                P.op("pe", lambda e: e.matmul(ps[:, :], lhsT=w3[:, kc, :], rhs=xn3[:, kc, :], start=(kc == 0), stop=(kc == 7)),
                     reads=[bw, bxn], writes=[bps])
            sg, bsg = rSg.next()
            P.op("act", lambda e: e.activation(out=sg[:], in_=ps[:, :], func=AF.Sigmoid), reads=[bps], writes=[bsg])
            w3, bw = load_fm(FM_PB + cc)
            ps, bps = rPS.next()
            for kc in range(8):
                P.op("pe", lambda e: e.matmul(ps[:, :], lhsT=w3[:, kc, :], rhs=act3[:, kc, :], start=(kc == 0), stop=(kc == 7)),
                     reads=[bw, bact[kc]], writes=[bps])
            P.op("dve", lambda e: e.tensor_tensor(out=sg[:], in0=ps[:, :], in1=sg[:], op=ALU.mult), reads=[bps, bsg], writes=[bsg])
            P.op("pool", lambda e: e.tensor_tensor(out=m3[:, cc, :], in0=m3[:, cc, :], in1=sg[:], op=ALU.add), reads=[bmm[cc], bsg], writes=[bmm[cc]])
        if STOP == 'p6':
            continue
        wo0, bwo0 = load_tm(TM_O0)
        wo1, bwo1 = load_tm(TM_O1)
        for tt in range(4):
            r0 = t0 + tt * 128
            xt, bxt = rX.next()
            P.dma("sp", lambda e: e.dma_start(out=xt[:], in_=x_p[r0:r0 + 128, :]), writes=[bxt])
            for hb, (wo, bwo) in enumerate(((wo0, bwo0), (wo1, bwo1))):
                ps, bps = rPS.next()
                for kc in range(8):
                    P.op("pe", lambda e: e.matmul(ps[:, :], lhsT=m3[:, kc, tt * 128:(tt + 1) * 128], rhs=wo[:, kc, :], start=(kc == 0), stop=(kc == 7)),
                         reads=[bmm[kc], bwo], writes=[bps])
                P.op("dve", lambda e: e.tensor_tensor(out=hres[:, hb * 512:(hb + 1) * 512], in0=ps[:, :], in1=gate_bc[:, hb * 512:(hb + 1) * 512], op=ALU.mult),
                     reads=[bps, bgate], writes=[bhres])
            P.op("pool", lambda e: e.tensor_tensor(out=hres[:], in0=hres[:], in1=xt[:], op=ALU.add), reads=[bhres, bxt], writes=[bhres])
            P.op("act", lambda e: e.activation(out=junkb[:], in_=hres[:], func=AF.Square, accum_out=sm[:, 10:11]), reads=[bhres], writes=[bjunk, bsm])
            P.op("act", lambda e: e.activation(out=sm[:, 11:12], in_=sm[:, 10:11], func=AF.Sqrt, scale=1.0 / D, bias=EPS), reads=[bsm], writes=[bsm])
            P.op("dve", lambda e: e.reciprocal(out=sm[:, 12:13], in_=sm[:, 11:12]), reads=[bsm], writes=[bsm])
            P.op("dve", lambda e: e.scalar_tensor_tensor(out=yo[:], in0=hres[:], scalar=sm[:, 12:13], in1=gfin_bc[:], op0=ALU.mult, op1=ALU.mult),
                 reads=[bhres, bsm, bgfin], writes=[byo])
            P.dma("act", lambda e: e.dma_start(out=y_p[r0:r0 + 128, :], in_=yo[:]), reads=[byo], is_out=True)


def sample_phase(nc, P, G):
    sS = G["sS"]
    sb = lambda n, shp, d=F32: G["sb"](n, shp, d, sS)
    x_s = G["x_s"]; st_conv = G["st_conv"]; st_lru = G["st_lru"]; ptab = G["ptab"]
    cache_k = G["cache_k"]; cache_v = G["cache_v"]; cache_i = G["cache_i"]; ropes = G["ropes"]
    y_s = G["y_s"]; nk_s = G["nk_s"]; nv_s = G["nv_s"]; nki_s = G["nki_s"]; ncv_s = G["ncv_s"]; nlr_s = G["nlr_s"]
    rPS = G["rPS"]; rPST = G["rPST"]
    psA = G["psA"]; bpsA = G["bpsA"]; psB = G["psB"]; bpsB = G["bpsB"]
    psS0 = G["psS0"]; bpsS0 = G["bpsS0"]; psS1 = G["psS1"]; bpsS1 = G["bpsS1"]
    psO = G["psO"]; bpsO = G["bpsO"]; psL = G["psL"]; bpsL = G["bpsL"]
    cstt = G["cstt"]; bcst = G["bcst"]; chpt = G["chpt"]; bchp = G["bchp"]
    identb = G["identb"]; bidb = G["bidb"]; identf = G["identf"]
    onesb = G["onesb"]; bonesb = G["bonesb"]; onesf = G["onesf"]; bonesf = G["bonesf"]
    clam = G["clam"]; bclam = G["bclam"]; gfin_bc = G["gfin_bc"]; bgfin = G["bgfin"]
    idxgb_bc = G["idxgb_bc"]; bidxgb = G["bidxgb"]; wrr = G["wrr"]; bwrr = G["bwrr"]
    mT = G["mT"]; bmT = G["bmT"]; gate_s = G["gate_s"]; bgs = G["bgs"]
    load_fm = G["load_fm"]; load_tm = G["load_tm"]; rope = G["rope"]
    T = NS
    I4 = identf[0:T, 0:T]
    mT3 = mT[:].rearrange("p (c t) -> p c t", t=5)
    wrr5 = wrr[:].rearrange("p (a n k c) -> p a n k c", a=2, n=4, k=2)
    Jf = cstt[:, CS_JF:CS_JF + 256]; iota_r = cstt[:, CS_IR:CS_IR + 128]; Tstrict = cstt[:, CS_TS:CS_TS + 128]

    def bc84(col0):
        return chpt[:, col0:col0 + 8].unsqueeze(2).to_broadcast([128, 8, T])

    def tt(ek, out, a, b, op, reads, writes):
        return P.op(ek, lambda e: e.tensor_tensor(out=out, in0=a, in1=b, op=op), reads=reads, writes=writes)

    def fm_to_tm(src3, ncc, nrow_in_free, dst_tile, bdst, bsrc):
        for hb in range(2):
            ps, bps = rPS.next()
            for c4 in range(4):
                cc = hb * 4 + c4
                P.op("pe", lambda e: e.transpose(out=ps[0:nrow_in_free, c4 * 128:(c4 + 1) * 128], in_=src3[:, cc, :], identity=identf),
                     reads=[bsrc, bcst], writes=[bps])
            P.op("act", lambda e: e.copy(out=dst_tile[0:nrow_in_free, hb * 512:(hb + 1) * 512], in_=ps[0:nrow_in_free, :]), reads=[bps], writes=[bdst])

    def tm_to_fm(src_tile, nrow, dst_ps, bps, bsrc, col_of):
        for kc in range(8):
            c0 = col_of(kc)
            P.op("pe", lambda e: e.transpose(out=dst_ps[:, c0:c0 + nrow], in_=src_tile[0:nrow, kc * 128:(kc + 1) * 128], identity=identf[0:nrow, 0:nrow]),
                 reads=[bsrc, bcst], writes=[bps])

    xs = sb("xs", [T, D]); bxs = Buf()
    P.dma("sp", lambda e: e.dma_start(out=xs[:], in_=x_s[:, :]), writes=[bxs])
    tm_to_fm(xs, T, psA, bpsA, bxs, lambda kc: kc * T)
    xsT = sb("xsT", [128, 8 * T]); bxsT = Buf(); xsT3 = xsT[:].rearrange("p (k t) -> p k t", t=T)
    P.op("dve", lambda e: e.tensor_copy(out=xsT[:], in_=psA[:, 0:8 * T]), reads=[bpsA], writes=[bxsT])
    tmpA = sb("tmpA", [128, 8 * T]); btA = Buf(); tmpA3 = tmpA[:].rearrange("p (k t) -> p k t", t=T)
    tmpB = sb("tmpB", [128, 8 * T]); btB = Buf(); tmpB3 = tmpB[:].rearrange("p (k t) -> p k t", t=T)
    tt("dve", tmpA[:], xsT[:], xsT[:], ALU.mult, [bxsT], [btA])
    for kc in range(8):
        P.op("pe", lambda e: e.matmul(psB[:, 0:T], lhsT=onesf[:], rhs=tmpA[:, kc * T:(kc + 1) * T], start=(kc == 0), stop=(kc == 7)),
             reads=[bonesf, btA], writes=[bpsB])
    rstd = sb("rstd_s", [128, T]); brstd = Buf()
    P.op("act", lambda e: e.activation(out=rstd[:], in_=psB[:, 0:T], func=AF.Sqrt, scale=1.0 / D, bias=EPS), reads=[bpsB], writes=[brstd])
    P.op("dve", lambda e: e.reciprocal(out=rstd[:], in_=rstd[:]), reads=[brstd], writes=[brstd])
    P.op("dve", lambda e: e.scalar_tensor_tensor(out=tmpB3, in0=mT3[:, 8:16, 1:5], scalar=1.0, in1=bc84(CP_GN), op0=ALU.add, op1=ALU.mult),
         reads=[bmT, bchp], writes=[btB])
    tt("dve", tmpA3, xsT3, rstd[:].unsqueeze(1).to_broadcast([128, 8, T]), ALU.mult, [bxsT, brstd], [btA])
    tt("dve", tmpA3, tmpA3, tmpB3, ALU.mult, [btA, btB], [btA])
    xnsT = sb("xnsT", [128, 8 * T], BF16); bxns = Buf(); xns3 = xnsT[:].rearrange("p (k t) -> p k t", t=T)
    tt("dve", xns3, tmpA3, mT3[:, 0:8, 1:5], ALU.add, [btA, bmT], [bxns])

    ztm = sb("ztm", [T, 2120]); bztm = Buf()
    off = 0
    for blk in (TM_Q0, TM_Q1, TM_KV, TM_QI, TM_KW):
        w3, bw = load_tm(blk)
        ncol = 72 if blk == TM_KW else 512
        ps, bps = rPS.next()
        for kc in range(8):
            P.op("pe", lambda e: e.matmul(ps[0:T, 0:ncol], lhsT=xns3[:, kc, :], rhs=w3[:, kc, 0:ncol], start=(kc == 0), stop=(kc == 7)),
                 reads=[bxns, bw], writes=[bps])
        P.op("act", lambda e: e.copy(out=ztm[:, off:off + ncol], in_=ps[0:T, 0:ncol]), reads=[bps], writes=[bztm])
        off += ncol
    Q0, K0, V0, QI0, KI0, WI0 = 0, 1024, 1280, 1536, 2048, 2112
    for idx in range(40):
        w3, bw = load_fm(idx)
        for kc in range(8):
            P.op("pe", lambda e: e.matmul(psO[:, idx * T:(idx + 1) * T], lhsT=w3[:, kc, :], rhs=xns3[:, kc, :], start=(kc == 0), stop=(kc == 7)),
                 reads=[bw, bxns], writes=[bpsO])
    zfm = sb("zfm", [128, 40 * T]); bzfm = Buf(); zfm3 = zfm[:].rearrange("p (c t) -> p c t", t=T)
    P.op("dve", lambda e: e.tensor_copy(out=zfm[:], in_=psO[:, 0:40 * T]), reads=[bpsO], writes=[bzfm])

    stc = sb("stc", [T * 3, D]); bstc = Buf()
    P.dma("sp", lambda e: e.dma_start(out=stc[:], in_=st_conv[:, :]), writes=[bstc])
    for t in range(T):
        P.dma("act", lambda e: e.dma_start(out=ncv_s[t, 0:2, :], in_=stc[t * 3 + 1:t * 3 + 3, :]), reads=[bstc], is_out=True)
    tm_to_fm(stc, T * 3, psL, bpsL, bstc, lambda kc: kc * 12)
    stT = sb("stT", [128, 96]); bstT = Buf(); stT4 = stT[:].rearrange("p (c t j) -> p c t j", c=8, t=T)
    P.op("dve", lambda e: e.tensor_copy(out=stT[:], in_=psL[:, 0:96]), reads=[bpsL], writes=[bstT])
    rowt = sb("rowt", [T, D]); browt = Buf()
    fm_to_tm(zfm3[:, 0:8, :], 8, T, rowt, browt, bzfm)
    P.dma("act", lambda e: e.dma_start(out=ncv_s[:, 2, :], in_=rowt[:]), reads=[browt], is_out=True)
    xcs = sb("xcs", [128, 8 * T]); bxcs = Buf(); xcs3 = xcs[:].rearrange("p (k t) -> p k t", t=T)
    tt("dve", xcs3, zfm3[:, 0:8, :], bc84(CP_WCONV + 24), ALU.mult, [bzfm, bchp], [bxcs])
    tt("dve", xcs3, xcs3, bc84(CP_BCONV), ALU.add, [bxcs, bchp], [bxcs])
    for j in range(3):
        tt("dve", tmpA3, stT4[:, :, :, j], bc84(CP_WCONV + 8 * j), ALU.mult, [bstT, bchp], [btA])
        tt("dve", xcs3, xcs3, tmpA3, ALU.add, [bxcs, btA], [bxcs])
    xcsb = sb("xcsb", [128, 8 * T], BF16); bxcsb = Buf(); xcsb3 = xcsb[:].rearrange("p (k t) -> p k t", t=T)
    P.op("dve", lambda e: e.tensor_copy(out=xcsb[:], in_=xcs[:]), reads=[bxcs], writes=[bxcsb])
    for which in range(2):
        for cc in range(8):
            n, c2 = cc // 2, cc % 2
            c0 = (which * 8 + cc) * T
            for k2 in range(2):
                P.op("pe", lambda e: e.matmul(psB[:, c0:c0 + T], lhsT=wrr5[:, which, n, k2, c2 * 128:(c2 + 1) * 128], rhs=xcsb3[:, 2 * n + k2, :],
                                              start=(k2 == 0), stop=(k2 == 1)), reads=[bwrr, bxcsb], writes=[bpsB])
    r_s = sb("r_s", [128, 8 * T]); br_s = Buf(); r_s3 = r_s[:].rearrange("p (k t) -> p k t", t=T)
    i_s = sb("i_s", [128, 8 * T]); bi_s = Buf(); i_s3 = i_s[:].rearrange("p (k t) -> p k t", t=T)
    a_s = sb("a_s", [128, 8 * T]); ba_s = Buf(); a_s3 = a_s[:].rearrange("p (k t) -> p k t", t=T)
    tt("dve", r_s3, psB[:, 0:8 * T].rearrange("p (k t) -> p k t", t=T), bc84(CP_BRA), ALU.add, [bpsB, bchp], [br_s])
    tt("dve", i_s3, psB[:, 8 * T:16 * T].rearrange("p (k t) -> p k t", t=T), bc84(CP_BRX), ALU.add, [bpsB, bchp], [bi_s])
    P.op("act", lambda e: e.activation(out=r_s[:], in_=r_s[:], func=AF.Sigmoid), reads=[br_s], writes=[br_s])
    P.op("act", lambda e: e.activation(out=i_s[:], in_=i_s[:], func=AF.Sigmoid), reads=[bi_s], writes=[bi_s])
    tt("dve", a_s3, r_s3, clam[:].unsqueeze(2).to_broadcast([128, 8, T]), ALU.mult, [br_s, bclam], [ba_s])
    P.op("act", lambda e: e.activation(out=a_s[:], in_=a_s[:], func=AF.Exp), reads=[ba_s], writes=[ba_s])
    tt("dve", tmpA[:], a_s[:], a_s[:], ALU.mult, [ba_s], [btA])
    P.op("dve", lambda e: e.tensor_scalar(out=tmpA[:], in0=tmpA[:], scalar1=-1.0, scalar2=1.0, op0=ALU.mult, op1=ALU.add), reads=[btA], writes=[btA])
    P.op("act", lambda e: e.activation(out=tmpA[:], in_=tmpA[:], func=AF.Sqrt), reads=[btA], writes=[btA])
    tt("dve", tmpB[:], i_s[:], xcs[:], ALU.mult, [bi_s, bxcs], [btB])
    tt("dve", tmpB[:], tmpB[:], tmpA[:], ALU.mult, [btB, btA], [btB])
    hst = sb("hst", [T, D]); bhst = Buf()
    P.dma("sp", lambda e: e.dma_start(out=hst[:], in_=st_lru[:, :]), writes=[bhst])
    tm_to_fm(hst, T, psA, bpsA, bhst, lambda kc: kc * T)
    h_s = sb("h_s", [128, 8 * T]); bh_s = Buf(); h_s3 = h_s[:].rearrange("p (k t) -> p k t", t=T)
    tt("dve", h_s[:], psA[:, 0:8 * T], a_s[:], ALU.mult, [bpsA, ba_s], [bh_s])
    tt("dve", h_s[:], h_s[:], tmpB[:], ALU.add, [bh_s, btB], [bh_s])
    fm_to_tm(h_s3, 8, T, rowt, browt, bh_s)
    P.dma("act", lambda e: e.dma_start(out=nlr_s[:, :], in_=rowt[:]), reads=[browt], is_out=True)
    acta = sb("acta_s", [128, 8 * T], BF16); bacta = Buf(); acta3 = acta[:].rearrange("p (k t) -> p k t", t=T)
    P.op("act", lambda e: e.activation(out=tmpA3, in_=zfm3[:, 8:16, :], func=AF.Silu), reads=[bzfm], writes=[btA])
    tt("dve", acta[:], h_s[:], tmpA[:], ALU.mult, [bh_s, btA], [bacta])
    for cc in range(8):
        w3, bw = load_fm(FM_PA + cc)
        for kc in range(8):
            P.op("pe", lambda e: e.matmul(psB[:, cc * T:(cc + 1) * T], lhsT=w3[:, kc, :], rhs=acta3[:, kc, :], start=(kc == 0), stop=(kc == 7)),
                 reads=[bw, bacta], writes=[bpsB])
    m_s = sb("m_s", [128, 8 * T]); bm_s = Buf(); m_s3 = m_s[:].rearrange("p (k t) -> p k t", t=T)
    P.op("act", lambda e: e.activation(out=tmpA3, in_=zfm3[:, 24:32, :], func=AF.Sigmoid), reads=[bzfm], writes=[btA])
    tt("dve", m_s[:], psB[:, 0:8 * T], tmpA[:], ALU.mult, [bpsB, btA], [bm_s])

    rs = sb("ropes_t", [T, 384]); brs = Buf()
    P.dma("sp", lambda e: e.dma_start(out=rs[:], in_=ropes[:, :]), writes=[brs])
    t1 = sb("s_t1", [T, 1024]); bt1 = Buf()
    t2 = sb("s_t2", [T, 1024]); bt2 = Buf()
    q_r = sb("q_r", [T, 1024]); bq_r = Buf()
    k_r = sb("k_r", [T, 256]); bk_r = Buf()
    qi_r = sb("qi_r", [T, 512]); bqi_r = Buf()
    ki_r = sb("ki_r", [T, 64]); bki_r = Buf()
    sm = sb("s_sm", [T, 16]); bsm = Buf()
    v3 = lambda ap, h: ap.rearrange("p (h d) -> p h d", h=h)
    rope("dve", v3(q_r[:], 8), v3(ztm[:, Q0:Q0 + 1024], 8), rs[:, 0:128], rs[:, 128:256], 8, 128, v3(t1[:], 8), v3(t2[:], 8), [bztm, brs], [bq_r], bt1, bt2)
    rope("dve", v3(k_r[:], 2), v3(ztm[:, K0:K0 + 256], 2), rs[:, 0:128], rs[:, 128:256], 2, 128, v3(t1[:, 0:256], 2), v3(t2[:, 0:256], 2), [bztm, brs], [bk_r], bt1, bt2)
    P.dma("act", lambda e: e.dma_start(out=nk_s[:, :], in_=k_r[:]), reads=[bk_r], is_out=True)
    P.dma("act", lambda e: e.dma_start(out=nv_s[:, :], in_=ztm[:, V0:V0 + 256]), reads=[bztm], is_out=True)
    rope("dve", v3(qi_r[:], 8), v3(ztm[:, QI0:QI0 + 512], 8), rs[:, 256:320], rs[:, 320:384], 8, 64, v3(t1[:, 0:512], 8), v3(t2[:, 0:512], 8), [bztm, brs], [bqi_r], bt1, bt2)
    P.op("dve", lambda e: e.tensor_reduce(out=sm[:, 0:1], in_=ztm[:, KI0:KI0 + 64], axis=AX.X, op=ALU.add), reads=[bztm], writes=[bsm])
    P.op("dve", lambda e: e.tensor_scalar(out=sm[:, 1:2], in0=sm[:, 0:1], scalar1=-1.0 / 64, scalar2=None, op0=ALU.mult), reads=[bsm], writes=[bsm])
    P.op("dve", lambda e: e.tensor_scalar(out=t1[:, 0:64], in0=ztm[:, KI0:KI0 + 64], scalar1=sm[:, 1:2], scalar2=None, op0=ALU.add), reads=[bztm, bsm], writes=[bt1])
    P.op("act", lambda e: e.activation(out=t2[:, 0:64], in_=t1[:, 0:64], func=AF.Square, accum_out=sm[:, 2:3]), reads=[bt1], writes=[bt2, bsm])
    P.op("act", lambda e: e.activation(out=sm[:, 3:4], in_=sm[:, 2:3], func=AF.Sqrt, scale=1.0 / 64, bias=EPS), reads=[bsm], writes=[bsm])
    P.op("dve", lambda e: e.reciprocal(out=sm[:, 4:5], in_=sm[:, 3:4]), reads=[bsm], writes=[bsm])
    P.op("dve", lambda e: e.scalar_tensor_tensor(out=t1[:, 64:128], in0=t1[:, 0:64], scalar=sm[:, 4:5], in1=idxgb_bc[0:T, 0:64], op0=ALU.mult, op1=ALU.mult),
         reads=[bt1, bsm, bidxgb], writes=[bt1])
    tt("dve", t1[:, 128:192], t1[:, 64:128], idxgb_bc[0:T, 64:128], ALU.add, [bt1, bidxgb], [bt1])
    rope("dve", ki_r[:].unsqueeze(1), t1[:, 128:192].unsqueeze(1), rs[:, 256:320], rs[:, 320:384], 1, 64,
         t2[:, 64:128].unsqueeze(1), t2[:, 128:192].unsqueeze(1), [bt1, brs], [bki_r], bt2, bt2)
    P.dma("act", lambda e: e.dma_start(out=nki_s[:, :], in_=ki_r[:]), reads=[bki_r], is_out=True)
    w_s = sb("w_s", [T, 8]); bw_s = Buf()
    P.op("dve", lambda e: e.tensor_scalar(out=w_s[:], in0=ztm[:, WI0:WI0 + 8], scalar1=IDX_W_SCALE, scalar2=None, op0=ALU.mult), reads=[bztm], writes=[bw_s])
    s8 = sb("s8", [T, 8]); bs8 = Buf()
    selfsc = sb("selfsc", [T, 1]); bselfsc = Buf()
    sl = sb("sl", [T, 8]); bsl = Buf()
    tt("dve", v3(t1[:, 0:512], 8), v3(qi_r[:], 8), ki_r[:].unsqueeze(1).to_broadcast([T, 8, 64]), ALU.mult, [bqi_r, bki_r], [bt1])
    P.op("dve", lambda e: e.tensor_reduce(out=s8[:], in_=v3(t1[:, 0:512], 8), axis=AX.X, op=ALU.add), reads=[bt1], writes=[bs8])
    P.op("dve", lambda e: e.scalar_tensor_tensor(out=s8[:], in0=s8[:], scalar=0.0, in1=w_s[:], op0=ALU.max, op1=ALU.mult), reads=[bs8, bw_s], writes=[bs8])
    P.op("dve", lambda e: e.tensor_reduce(out=selfsc[:], in_=s8[:], axis=AX.X, op=ALU.add), reads=[bs8], writes=[bselfsc])
    tt("dve", t1[:].rearrange("p (g l d) -> p g l d", g=2, l=4), q_r[:].rearrange("p (g l d) -> p g l d", g=2, l=4),
       k_r[:].rearrange("p (g d) -> p g d", g=2).unsqueeze(2).to_broadcast([T, 2, 4, 128]), ALU.mult, [bq_r, bk_r], [bt1])
    P.op("dve", lambda e: e.tensor_reduce(out=sl[:], in_=v3(t1[:], 8), axis=AX.X, op=ALU.add), reads=[bt1], writes=[bsl])
    q_b = sb("q_b", [T, 1024], BF16); bq_b = Buf()
    P.op("dve", lambda e: e.tensor_copy(out=q_b[:], in_=q_r[:]), reads=[bq_r], writes=[bq_b])
    pt, bpt = rPST.next()
    for h in range(8):
        P.op("pe", lambda e: e.transpose(out=pt[:, h * T:(h + 1) * T], in_=q_b[:, h * 128:(h + 1) * 128], identity=identb[0:T, 0:T]),
             reads=[bq_b, bidb], writes=[bpt])
    qsT = sb("qsT", [128, 8 * T], BF16); bqsT = Buf(); qsT3 = qsT[:].rearrange("p (h t) -> p h t", t=T)
    P.op("dve", lambda e: e.tensor_copy(out=qsT[:], in_=pt[:, 0:8 * T]), reads=[bpt], writes=[bqsT])
    qpad = sb("qpad", [T, 2 * 8 * 128], BF16); bqpad = Buf(); qpad4 = qpad[:].rearrange("p (a h d) -> p a h d", a=2, h=8)
    P.op("pool", lambda e: e.memset(qpad[:], 0.0), writes=[bqpad])
    P.op("dve", lambda e: e.tensor_copy(out=qpad4[:, 0, :, 0:64], in_=v3(qi_r[:], 8)), reads=[bqi_r], writes=[bqpad])
    P.op("dve", lambda e: e.tensor_copy(out=qpad4[:, 1, :, 64:128], in_=v3(qi_r[:], 8)), reads=[bqi_r], writes=[bqpad])
    pt, bpt = rPST.next()
    for a in range(2):
        for h in range(8):
            c0 = (a * 8 + h) * T
            P.op("pe", lambda e: e.transpose(out=pt[:, c0:c0 + T], in_=qpad4[:, a, h, :], identity=identb[0:T, 0:T]), reads=[bqpad, bidb], writes=[bpt])
    qiz = sb("qiz", [128, T * 16], BF16); bqiz = Buf(); qiz3 = qiz[:].rearrange("p (t a) -> p t a", t=T)
    P.op("dve", lambda e: e.tensor_copy(out=qiz3, in_=pt[:, 0:16 * T].rearrange("p (a t) -> p t a", t=T)), reads=[bpt], writes=[bqiz])
    W4 = sb("W4", [T, T * 8 + T]); bW4 = Buf()
    tt("dve", W4[:, 0:T * 8].rearrange("p (t h) -> p t h", t=T), w_s[:].unsqueeze(1).to_broadcast([T, T, 8]), I4.unsqueeze(2).to_broadcast([T, T, 8]), ALU.mult,
       [bw_s, bcst], [bW4])
    P.op("dve", lambda e: e.tensor_scalar(out=W4[:, T * 8:T * 9], in0=I4, scalar1=selfsc[:], scalar2=None, op0=ALU.mult), reads=[bcst, bselfsc], writes=[bW4])
    P.op("pe", lambda e: e.matmul(psB[:, 0:T * 9], lhsT=onesf[0:T, :], rhs=W4[:], start=True, stop=True), reads=[bonesf, bW4], writes=[bpsB])
    wbcS = sb("wbcS", [128, T * 9]); bwbcS = Buf()
    P.op("dve", lambda e: e.tensor_copy(out=wbcS[:], in_=psB[:, 0:T * 9]), reads=[bpsB], writes=[bwbcS])
    selfb = wbcS[:, T * 8:T * 9]
    pti = sb("pti", [128, T], I32); bpti = Buf()
    ptf = sb("ptf", [128, T]); bptf = Buf()
    P.dma("sp", lambda e: e.dma_start(out=pti[:], in_=ptab[:, :]), writes=[bpti])
    P.op("dve", lambda e: e.tensor_copy(out=ptf[:], in_=pti[:]), reads=[bpti], writes=[bptf])

    Gt = sb("Gt", [128, 8192]); bGt = Buf()
    kTs = sb("kTs", [128, 64 * 128], BF16); bkTs = Buf(); kTs3 = kTs[:].rearrange("p (r q) -> p r q", r=64)
    scs = sb("scs", [128, T * 128]); bscs = Buf(); scs3 = scs[:].rearrange("p (t r) -> p t r", t=T)
    tmpS = sb("tmpS", [128, 512]); btS = Buf()
    for t in range(T):
        P.dma("pool", lambda e: e.indirect_dma_start(out=Gt[:], out_offset=None, in_=cache_i[:, :],
                                                     in_offset=bass.IndirectOffsetOnAxis(ap=pti[:, t:t + 1], axis=0)), reads=[bpti], writes=[bGt])
        for r4 in range(16):
            ps, bps = rPS.next()
            for q in range(4):
                rp = r4 * 4 + q
                P.op("pe", lambda e: e.transpose(out=ps[:, q * 128:(q + 1) * 128], in_=Gt[:, rp * 128:(rp + 1) * 128], identity=identf), reads=[bGt, bcst], writes=[bps])
            if r4 % 2 == 0:
                P.op("act", lambda e: e.copy(out=kTs[:, r4 * 512:(r4 + 1) * 512], in_=ps[:, :]), reads=[bps], writes=[bkTs])
            else:
                P.op("dve", lambda e: e.tensor_copy(out=kTs[:, r4 * 512:(r4 + 1) * 512], in_=ps[:, :]), reads=[bps], writes=[bkTs])
        for half, (psX, bpsX) in enumerate(((psS0, bpsS0), (psS1, bpsS1))):
            for rr in range(64):
                r = half * 64 + rr
                rp, r2 = r // 2, r % 2
                P.op("pe", lambda e: e.matmul(psX[:, rr * 8:(rr + 1) * 8], lhsT=kTs3[:, rp, :], rhs=qiz3[:, t, r2 * 8:(r2 + 1) * 8], start=True, stop=True),
                     reads=[bkTs, bqiz], writes=[bpsX])
            P.op("dve", lambda e: e.scalar_tensor_tensor(out=tmpS[:].rearrange("p (r h) -> p r h", h=8), in0=psX[:, :].rearrange("p (r h) -> p r h", h=8), scalar=0.0,
                                                         in1=wbcS[:, t * 8:(t + 1) * 8].unsqueeze(1).to_broadcast([128, 64, 8]), op0=ALU.max, op1=ALU.mult),
                 reads=[bpsX, bwbcS], writes=[btS])
            P.op("dve", lambda e: e.tensor_reduce(out=scs3[:, t, half * 64:(half + 1) * 64], in_=tmpS[:].rearrange("p (r h) -> p r h", h=8), axis=AX.X, op=ALU.add),
                 reads=[btS], writes=[bscs])

    mx = sb("b_mx", [128, 2 * T]); bmx = Buf()
    P.op("dve", lambda e: e.tensor_reduce(out=mx[:, 0:T], in_=scs3, axis=AX.X, op=ALU.max), reads=[bscs], writes=[bmx])
    P.op("dve", lambda e: e.tensor_reduce(out=mx[:, T:2 * T], in_=scs3, axis=AX.X, op=ALU.min), reads=[bscs], writes=[bmx])
    ps, bps = rPS.next()
    P.op("pe", lambda e: e.transpose(out=ps[0:2 * T, 0:128], in_=mx[:], identity=identf), reads=[bmx, bcst], writes=[bps])
    hl = sb("b_hl", [2 * T, 2]); bhl = Buf()
    P.op("dve", lambda e: e.tensor_reduce(out=hl[:, 0:1], in_=ps[0:2 * T, 0:128], axis=AX.X, op=ALU.max), reads=[bps], writes=[bhl])
    P.op("dve", lambda e: e.tensor_reduce(out=hl[:, 1:2], in_=ps[0:2 * T, 0:128], axis=AX.X, op=ALU.min), reads=[bps], writes=[bhl])
    HL = sb("b_HL", [2 * T, 2 * T]); bHL = Buf()
    P.op("pool", lambda e: e.memset(HL[:], 0.0), writes=[bHL])
    P.op("dve", lambda e: e.tensor_scalar(out=HL[0:T, 0:T], in0=I4, scalar1=hl[0:T, 0:1], scalar2=None, op0=ALU.mult), reads=[bcst, bhl], writes=[bHL])
    ps2, bps2 = rPS.next()
    P.op("pe", lambda e: e.matmul(ps2[:, 0:T], lhsT=onesf[0:T, :], rhs=HL[0:T, 0:T], start=True, stop=True), reads=[bonesf, bHL], writes=[bps2])
    hib = sb("b_hib", [128, T]); bhib = Buf()
    tt("dve", hib[:], ps2[:, 0:T], selfb, ALU.max, [bps2, bwbcS], [bhib])
    ps, bps = rPS.next()
    P.op("pe", lambda e: e.transpose(out=ps[0:T, 0:128], in_=mx[:, T:2 * T], identity=identf), reads=[bmx, bcst], writes=[bps])
    P.op("dve", lambda e: e.tensor_reduce(out=hl[0:T, 1:2], in_=ps[0:T, 0:128], axis=AX.X, op=ALU.min), reads=[bps], writes=[bhl])
    P.op("dve", lambda e: e.tensor_scalar(out=HL[0:T, T:2 * T], in0=I4, scalar1=hl[0:T, 1:2], scalar2=None, op0=ALU.mult), reads=[bcst, bhl], writes=[bHL])
    ps2, bps2 = rPS.next()
    P.op("pe", lambda e: e.matmul(ps2[:, 0:T], lhsT=onesf[0:T, :], rhs=HL[0:T, T:2 * T], start=True, stop=True), reads=[bonesf, bHL], writes=[bps2])
    lob = sb("b_lob", [128, T]); blob = Buf()
    tt("dve", lob[:], ps2[:, 0:T], selfb, ALU.min, [bps2, bwbcS], [blob])
    wb = sb("b_wb", [128, T]); bwb = Buf()
    tt("dve", wb[:], hib[:], lob[:], ALU.subtract, [bhib, blob], [bwb])
    midb = sb("b_mid", [128, T]); bmidb = Buf()
    cmpj = sb("b_cmpj", [128, T * 128]); bcmpj = Buf(); cmpj3 = cmpj[:].rearrange("p (t r) -> p t r", t=T)
    cntp = sb("b_cntp", [128, T]); bcntp = Buf()
    tot = sb("b_tot", [128, T]); btot = Buf()
    sge = sb("b_sge", [128, T]); bsge = Buf()
    for it in range(N_BISECT_S):
        P.op("dve", lambda e: e.tensor_scalar(out=wb[:], in0=wb[:], scalar1=0.5, scalar2=None, op0=ALU.mult), reads=[bwb], writes=[bwb])
        tt("dve", midb[:], lob[:], wb[:], ALU.add, [blob, bwb], [bmidb])
        tt("dve", cmpj3, scs3, midb[:].unsqueeze(2).to_broadcast([128, T, 128]), ALU.is_ge, [bscs, bmidb], [bcmpj])
        P.op("dve", lambda e: e.tensor_reduce(out=cntp[:], in_=cmpj3, axis=AX.X, op=ALU.add), reads=[bcmpj], writes=[bcntp])
        ps, bps = rPS.next()
        P.op("pe", lambda e: e.matmul(ps[:, 0:T], lhsT=onesf[:], rhs=cntp[:], start=True, stop=True), reads=[bonesf, bcntp], writes=[bps])
        tt("dve", sge[:], selfb, midb[:], ALU.is_ge, [bwbcS, bmidb], [bsge])
        tt("dve", tot[:], ps[:, 0:T], sge[:], ALU.add, [bps, bsge], [btot])
        P.op("dve", lambda e: e.tensor_scalar(out=tot[:], in0=tot[:], scalar1=float(TOPK) - 0.5, scalar2=None, op0=ALU.is_ge), reads=[btot], writes=[btot])
        tt("dve", tot[:], tot[:], wb[:], ALU.mult, [btot, bwb], [btot])
        tt("dve", lob[:], lob[:], tot[:], ALU.add, [blob, btot], [blob])
    Msel = sb("Msel", [128, T * 128]); bMsel = Buf(); Msel3 = Msel[:].rearrange("p (t r) -> p t r", t=T)
    tt("dve", Msel3, scs3, lob[:].unsqueeze(2).to_broadcast([128, T, 128]), ALU.is_ge, [bscs, blob], [bMsel])
    thr_tm = sb("thr_tm", [T, 4]); bthr = Buf()
    tt("dve", t1[:, 0:T], lob[0:T, :], I4, ALU.mult, [blob, bcst], [bt1])
    P.op("dve", lambda e: e.tensor_reduce(out=thr_tm[:, 0:1], in_=t1[:, 0:T], axis=AX.X, op=ALU.add), reads=[bt1], writes=[bthr])
    tt("dve", thr_tm[:, 1:2], selfsc[:], thr_tm[:, 0:1], ALU.is_ge, [bselfsc, bthr], [bthr])
    P.op("dve", lambda e: e.tensor_scalar(out=thr_tm[:, 2:3], in0=thr_tm[:, 1:2], scalar1=-NEG, scalar2=NEG, op0=ALU.mult, op1=ALU.add), reads=[bthr], writes=[bthr])
    pself = sb("pself", [T, 8]); bpself = Buf()
    P.op("act", lambda e: e.activation(out=pself[:], in_=sl[:], func=AF.Exp, scale=ATT_SCALE, bias=thr_tm[:, 2:3]), reads=[bsl, bthr], writes=[bpself])

    csel = sb("csel", [128, T]); bcsel = Buf()
    P.op("dve", lambda e: e.tensor_reduce(out=csel[:], in_=Msel3, axis=AX.X, op=ALU.add), reads=[bMsel], writes=[bcsel])
    ps, bps = rPS.next()
    P.op("pe", lambda e: e.matmul(ps[:, 0:T], lhsT=Tstrict, rhs=csel[:], start=True, stop=True), reads=[bcst, bcsel], writes=[bps])
    osel = sb("osel", [128, T]); bosel = Buf()
    esel = sb("esel", [128, T]); besel = Buf()
    P.op("dve", lambda e: e.tensor_copy(out=osel[:], in_=ps[:, 0:T]), reads=[bps], writes=[bosel])
    tt("dve", esel[:], osel[:], csel[:], ALU.add, [bosel, bcsel], [besel])
    rhsT = sb("rhsT", [128, T * 130]); brhsT = Buf(); rhsT3 = rhsT[:].rearrange("p (t c) -> p t c", t=T)
    for t in range(T):
        P.op("dve", lambda e: e.tensor_tensor_scan(out=rhsT3[:, t, 0:128], data0=onesf[:], data1=Msel3[:, t, :], initial=0.0, op0=ALU.mult, op1=ALU.add),
             reads=[bonesf, bMsel], writes=[brhsT])
    tt("dve", rhsT3[:, :, 0:128], rhsT3[:, :, 0:128], Msel3, ALU.mult, [brhsT, bMsel], [brhsT])
    P.op("dve", lambda e: e.tensor_copy(out=rhsT3[:, :, 128], in_=osel[:]), reads=[bosel], writes=[brhsT])
    P.op("dve", lambda e: e.tensor_copy(out=rhsT3[:, :, 129], in_=ptf[:]), reads=[bptf], writes=[brhsT])
    Asel = sb("Asel", [128, 256]); bAsel = Buf()
    A2 = sb("A2", [128, 256]); bA2 = Buf()
    idxT = sb("idxT", [128, 2 * T], I32); bidxT = Buf()
    vbias = sb("vbias", [128, 2 * T]); bvbias = Buf()
    c4 = sb("c4", [128, 8]); bc4 = Buf()
    eqt = sb("eqt", [128, 128]); beqt = Buf()
    for t in range(T):
        P.op("dve", lambda e: e.tensor_scalar(out=Asel[:], in0=Jf, scalar1=osel[:, t:t + 1], scalar2=None, op0=ALU.is_ge), reads=[bcst, bosel], writes=[bAsel])
        P.op("dve", lambda e: e.tensor_scalar(out=A2[:], in0=Jf, scalar1=esel[:, t:t + 1], scalar2=None, op0=ALU.is_lt), reads=[bcst, besel], writes=[bA2])
        tt("dve", Asel[:], Asel[:], A2[:], ALU.mult, [bAsel, bA2], [bAsel])
        for jc in range(2):
            col = t * 2 + jc
            ps, bps = rPS.next()
            P.op("pe", lambda e: e.matmul(ps[:, 0:130], lhsT=Asel[:, jc * 128:(jc + 1) * 128], rhs=rhsT3[:, t, :], start=True, stop=True),
                 reads=[bAsel, brhsT], writes=[bps])
            tt("dve", c4[:, 0:1], cstt[:, CS_J1 + jc:CS_J1 + jc + 1], ps[:, 128:129], ALU.subtract, [bcst, bps], [bc4])
            P.op("dve", lambda e: e.tensor_scalar(out=eqt[:], in0=ps[:, 0:128], scalar1=c4[:, 0:1], scalar2=None, op0=ALU.is_equal), reads=[bps, bc4], writes=[beqt])
            P.op("dve", lambda e: e.tensor_reduce(out=c4[:, 1:2], in_=eqt[:], axis=AX.X, op=ALU.add), reads=[beqt], writes=[bc4])
            tt("dve", eqt[:], eqt[:], iota_r, ALU.mult, [beqt, bcst], [beqt])
            P.op("dve", lambda e: e.tensor_reduce(out=c4[:, 2:3], in_=eqt[:], axis=AX.X, op=ALU.add), reads=[beqt], writes=[bc4])
            P.op("dve", lambda e: e.scalar_tensor_tensor(out=c4[:, 3:4], in0=ps[:, 129:130], scalar=128.0, in1=c4[:, 2:3], op0=ALU.mult, op1=ALU.add),
                 reads=[bps, bc4], writes=[bc4])
            P.op("dve", lambda e: e.tensor_copy(out=idxT[:, col:col + 1], in_=c4[:, 3:4]), reads=[bc4], writes=[bidxT])
            P.op("dve", lambda e: e.tensor_scalar(out=vbias[:, col:col + 1], in0=c4[:, 1:2], scalar1=-NEG, scalar2=NEG, op0=ALU.mult, op1=ALU.add),
                 reads=[bc4], writes=[bvbias])

    Ksel = sb("Ksel", [128, 512]); bKsel = Buf()
    Vsel = sb("Vsel", [128, 512]); bVsel = Buf()
    Kb = sb("Kb_s", [128, 512], BF16); bKb = Buf()
    Vx = sb("Vx", [128, 4 * 129], BF16); bVx = Buf(); Vx4 = Vx[:].rearrange("p (j g d) -> p j g d", j=2, g=2)
    KselT = sb("KselT", [128, 512], BF16); bKselT = Buf(); KselT3 = KselT[:].rearrange("p (g j) -> p g j", g=2)
    PTs = sb("PTs", [128, 16], BF16); bPTs = Buf()
    vself = sb("vself", [T, 2 * 129], BF16); bvself = Buf(); vself3 = vself[:].rearrange("p (g d) -> p g d", g=2)
    pselfm = sb("pselfm", [T, 8], BF16); bpselfm = Buf()
    osb = sb("osb", [4, T * 2 * 128]); bosb = Buf(); osb4 = osb[:].rearrange("p (t g d) -> p t g d", t=T, g=2)
    rcp = sb("rcp", [4, 2]); brcp = Buf()
    P.op("pool", lambda e: e.memset(Vx[:], 1.0), writes=[bVx])
    P.op("pool", lambda e: e.memset(vself[:], 1.0), writes=[bvself])
    P.op("dve", lambda e: e.tensor_copy(out=vself3[:, :, 0:128], in_=ztm[:, V0:V0 + 256].rearrange("p (g d) -> p g d", g=2)), reads=[bztm], writes=[bvself])
    for t in range(T):
        for jc in range(2):
            col = t * 2 + jc
            P.dma("pool", lambda e: e.indirect_dma_start(out=Ksel[:, jc * 256:(jc + 1) * 256], out_offset=None, in_=cache_k[:, :],
                                                         in_offset=bass.IndirectOffsetOnAxis(ap=idxT[:, col:col + 1], axis=0)), reads=[bidxT], writes=[bKsel])
            P.dma("pool", lambda e: e.indirect_dma_start(out=Vsel[:, jc * 256:(jc + 1) * 256], out_offset=None, in_=cache_v[:, :],
                                                         in_offset=bass.IndirectOffsetOnAxis(ap=idxT[:, col:col + 1], axis=0)), reads=[bidxT], writes=[bVsel])
        P.op("dve", lambda e: e.tensor_copy(out=Kb[:], in_=Ksel[:]), reads=[bKsel], writes=[bKb])
        P.op("dve", lambda e: e.tensor_copy(out=Vx4[:, :, :, 0:128], in_=Vsel[:].rearrange("p (j g d) -> p j g d", j=2, g=2)), reads=[bVsel], writes=[bVx])
        pt, bpt = rPST.next()
        for g in range(2):
            for jc in range(2):
                c0 = (g * 2 + jc) * 128
                P.op("pe", lambda e: e.transpose(out=pt[:, c0:c0 + 128], in_=Kb[:, jc * 256 + g * 128:jc * 256 + (g + 1) * 128], identity=identb[:]),
                     reads=[bKb, bidb], writes=[bpt])
        P.op("act", lambda e: e.copy(out=KselT[:], in_=pt[:, 0:512]), reads=[bpt], writes=[bKselT])
        ps, bps = rPS.next()
        for jc in range(2):
            for g in range(2):
                c0 = (jc * 2 + g) * 4
                P.op("pe", lambda e: e.matmul(ps[:, c0:c0 + 4], lhsT=KselT3[:, g, jc * 128:(jc + 1) * 128], rhs=qsT3[:, 4 * g:4 * g + 4, t], start=True, stop=True),
                     reads=[bKselT, bqsT], writes=[bps])
        for jc in range(2):
            col = t * 2 + jc
            P.op("act", lambda e: e.activation(out=PTs[:, jc * 8:(jc + 1) * 8], in_=ps[:, jc * 8:(jc + 1) * 8], func=AF.Exp, scale=ATT_SCALE, bias=vbias[:, col:col + 1]),
                 reads=[bps, bvbias], writes=[bPTs])
        P.op("dve", lambda e: e.tensor_scalar(out=pselfm[:], in0=pself[:], scalar1=I4[:, t:t + 1], scalar2=None, op0=ALU.mult), reads=[bpself, bcst], writes=[bpselfm])
        for g in range(2):
            c0 = g * 129
            for jc in range(2):
                P.op("pe", lambda e: e.matmul(psO[0:4, c0:c0 + 129], lhsT=PTs[:, jc * 8 + 4 * g:jc * 8 + 4 * g + 4], rhs=Vx4[:, jc, g, :], start=(jc == 0), stop=False),
                     reads=[bPTs, bVx], writes=[bpsO])
            P.op("pe", lambda e: e.matmul(psO[0:4, c0:c0 + 129], lhsT=pselfm[:, 4 * g:4 * g + 4], rhs=vself3[:, g, :], start=False, stop=True),
                 reads=[bpselfm, bvself], writes=[bpsO])
        for g in range(2):
            c0 = g * 129
            P.op("dve", lambda e: e.reciprocal(out=rcp[:, g:g + 1], in_=psO[0:4, c0 + 128:c0 + 129]), reads=[bpsO], writes=[brcp])
            P.op("dve", lambda e: e.tensor_scalar(out=osb4[:, t, g, :], in0=psO[0:4, c0:c0 + 128], scalar1=rcp[:, g:g + 1], scalar2=None, op0=ALU.mult),
                 reads=[bpsO, brcp], writes=[bosb])
    ps, bps = rPS.next()
    for t in range(T):
        for g in range(2):
            c0 = (t * 2 + g) * 4
            P.op("pe", lambda e: e.transpose(out=ps[:, c0:c0 + 4], in_=osb4[:, t, g, :], identity=identf[0:4, 0:4]), reads=[bosb, bcst], writes=[bps])
    oT = sb("oT_s", [128, 8 * T]); boT = Buf(); oT3 = oT[:].rearrange("p (h t) -> p h t", t=T)
    P.op("dve", lambda e: e.tensor_copy(out=oT3, in_=ps[:, 0:32].rearrange("p (t h) -> p h t", t=T)), reads=[bps], writes=[boT])
    actb = sb("actb_s", [128, 8 * T], BF16); bactb = Buf(); actb3 = actb[:].rearrange("p (k t) -> p k t", t=T)
    P.op("act", lambda e: e.activation(out=tmpA3, in_=zfm3[:, 16:24, :], func=AF.Silu), reads=[bzfm], writes=[btA])
    tt("dve", actb[:], oT[:], tmpA[:], ALU.mult, [boT, btA], [bactb])
    for cc in range(8):
        w3, bw = load_fm(FM_PB + cc)
        for kc in range(8):
            P.op("pe", lambda e: e.matmul(psB[:, cc * T:(cc + 1) * T], lhsT=w3[:, kc, :], rhs=actb3[:, kc, :], start=(kc == 0), stop=(kc == 7)),
                 reads=[bw, bactb], writes=[bpsB])
    P.op("act", lambda e: e.activation(out=tmpA3, in_=zfm3[:, 32:40, :], func=AF.Sigmoid), reads=[bzfm], writes=[btA])
    tt("dve", tmpA[:], psB[:, 0:8 * T], tmpA[:], ALU.mult, [bpsB, btA], [btA])
    msb = sb("msb", [128, 8 * T], BF16); bmsb = Buf(); msb3 = msb[:].rearrange("p (k t) -> p k t", t=T)
    tt("dve", msb[:], m_s[:], tmpA[:], ALU.add, [bm_s, btA], [bmsb])
    hres = sb("hres_s", [T, D]); bhres = Buf()
    for hb, blk in enumerate((TM_O0, TM_O1)):
        wo, bwo = load_tm(blk)
        ps, bps = rPS.next()
        for kc in range(8):
            P.op("pe", lambda e: e.matmul(ps[0:T, :], lhsT=msb3[:, kc, :], rhs=wo[:, kc, :], start=(kc == 0), stop=(kc == 7)), reads=[bmsb, bwo], writes=[bps])
        tt("dve", hres[:, hb * 512:(hb + 1) * 512], ps[0:T, :], gate_s[:, hb * 512:(hb + 1) * 512], ALU.mult, [bps, bgs], [bhres])
    tt("dve", hres[:], hres[:], xs[:], ALU.add, [bhres, bxs], [bhres])
    P.op("act", lambda e: e.activation(out=t1[:], in_=hres[:], func=AF.Square, accum_out=sm[:, 8:9]), reads=[bhres], writes=[bt1, bsm])
    P.op("act", lambda e: e.activation(out=sm[:, 9:10], in_=sm[:, 8:9], func=AF.Sqrt, scale=1.0 / D, bias=EPS), reads=[bsm], writes=[bsm])
    P.op("dve", lambda e: e.reciprocal(out=sm[:, 10:11], in_=sm[:, 9:10]), reads=[bsm], writes=[bsm])
    P.op("dve", lambda e: e.scalar_tensor_tensor(out=hres[:], in0=hres[:], scalar=sm[:, 10:11], in1=gfin_bc[0:T, :], op0=ALU.mult, op1=ALU.mult),
         reads=[bhres, bsm, bgfin], writes=[bhres])
    P.dma("act", lambda e: e.dma_start(out=y_s[:, :], in_=hres[:]), reads=[bhres], is_out=True)


def _fm(W, c0):
    return np.ascontiguousarray(W[:, c0:c0 + 128].reshape(8, 128, 128).transpose(1, 0, 2).reshape(128, 1024))


def _tm(W, c0, n=512):
    blk = np.zeros((1024, 512), np.float32)
    blk[:, :n] = W[:, c0:c0 + n]
    return np.ascontiguousarray(blk.reshape(8, 128, 512).transpose(1, 0, 2).reshape(128, 4096))


def _vec_fm(v):
    return np.ascontiguousarray(np.asarray(v, np.float32).reshape(-1, 128).T)


def _rope_tab(pos, half):
    inv = np.float32(10000.0) ** (-(np.arange(half, dtype=np.float32)) / np.float32(half))
    ang = (pos.astype(np.float32)[:, None] * inv[None, :]).astype(np.float32)
    c, s_ = np.cos(ang).astype(np.float32), np.sin(ang).astype(np.float32)
    return np.concatenate([c, c, -s_, s_], axis=1).astype(np.float32)


def _host_shared(inp):
    f32 = np.float32
    w_in = np.asarray(inp["w_in"][0], f32)
    w_pa = np.asarray(inp["w_pa"][0], f32); w_pb = np.asarray(inp["w_pb"][0], f32); w_o = np.asarray(inp["w_o"][0], f32)
    fm = []
    for base in FM_COLS:
        for cc in range(8):
            fm.append(_fm(w_in, base + cc * 128))
    for W in (w_pa, w_pb):
        for cc in range(8):
            fm.append(_fm(W, cc * 128))
    tm = [_tm(w_in, 2048), _tm(w_in, 2560), _tm(w_in, 3072), _tm(w_in, 4608), _tm(w_in, 5120, 72), _tm(w_o, 0), _tm(w_o, 512)]
    w_ada = np.asarray(inp["w_ada"][0], f32)
    wada = np.stack([_tm(w_ada, b * 512) for b in range(6)])
    wr = []
    for W in (inp["w_ra"][0], inp["w_rx"][0]):
        W = np.asarray(W, f32).reshape(4, 2, 128, 256).transpose(2, 0, 1, 3)
        wr.append(W.reshape(128, 2048))
    w_rr = np.ascontiguousarray(np.concatenate(wr, axis=1))
    chp = np.zeros((128, NCP), f32)
    chp[:, CP_GN:CP_GN + 8] = _vec_fm(inp["g_norm"][0])
    chp[:, CP_BADA:CP_BADA + 24] = _vec_fm(inp["b_ada"][0])
    chp[:, CP_WCONV:CP_WCONV + 32] = np.asarray(inp["w_conv"][0], f32).reshape(4, 8, 128).transpose(2, 0, 1).reshape(128, 32)
    chp[:, CP_BCONV:CP_BCONV + 8] = _vec_fm(inp["b_conv"][0])
    chp[:, CP_BRA:CP_BRA + 8] = _vec_fm(inp["b_ra"][0])
    chp[:, CP_BRX:CP_BRX + 8] = _vec_fm(inp["b_rx"][0])
    chp[:, CP_LAM:CP_LAM + 8] = _vec_fm(inp["lru_lambda"][0])
    cst = np.zeros((128, CS_END), f32)
    ar = np.arange(128)
    cst[:, CS_ID:CS_ID + 128] = np.eye(128, dtype=f32)
    cst[:, CS_TS:CS_TS + 128] = (ar[:, None] < ar[None, :]).astype(f32)
    cst[:, CS_JF:CS_JF + 256] = np.arange(256, dtype=f32)[None, :]
    cst[:, CS_IR:CS_IR + 128] = ar.astype(f32)[None, :]
    cst[:, CS_CAUS:CS_CAUS + 128] = np.where(ar[None, :] <= ar[:, None], 0.0, -1e30).astype(f32)
    cst[:, CS_J1] = ar + 1
    cst[:, CS_J1 + 1] = ar + 129
    cst[:, CS_P2:CS_P2 + 32] = (0.5 ** np.arange(1, 33, dtype=np.float64)).astype(f32)[None, :]
    pos = np.arange(SEQ)
    ropeq = _rope_tab(pos, 64)
    ropei = _rope_tab(pos, 32)
    ps_ = np.full((NS,), PAST)
    ropes = np.concatenate([_rope_tab(ps_, 64), _rope_tab(ps_, 32)], axis=1)
    sh = {
        "cache_k": np.asarray(inp["cache_k"], f32).reshape(NPOOL * 128, 256),
        "cache_v": np.asarray(inp["cache_v"], f32).reshape(NPOOL * 128, 256),
        "cache_i": np.asarray(inp["cache_idx_k"], f32).reshape(NPOOL, 128 * 64),
        "w_ada": wada, "b_ada": np.asarray(inp["b_ada"], f32).reshape(1, 3 * D),
        "w_fm": np.stack(fm), "w_tm": np.stack(tm), "w_rr": w_rr, "chp": chp, "cst": cst,
        "ropeq": ropeq, "ropei": ropei, "ropes": np.ascontiguousarray(ropes),
        "g_fin": np.asarray(inp["g_final"], f32).reshape(1, D),
        "idx_gb": np.concatenate([np.asarray(inp["idx_k_norm_g"][0], f32), np.asarray(inp["idx_k_norm_b"][0], f32)]).reshape(1, 128),
    }
    return sh


_NC_CACHE = {}


def kernel(**inputs):
    f32 = np.float32
    sh = _host_shared(inputs)
    in_maps = []
    for c in range(NCORE):
        m = dict(sh)
        s0, s1 = c * NS, (c + 1) * NS
        m["x_p"] = np.ascontiguousarray(np.asarray(inputs["x_prompt"][c], f32))
        m["x_s"] = np.ascontiguousarray(np.asarray(inputs["x_sample"][s0:s1, 0], f32))
        m["c5"] = np.ascontiguousarray(np.concatenate([np.asarray(inputs["c_prompt"][c:c + 1], f32), np.asarray(inputs["c_sample"][s0:s1], f32)], axis=0))
        m["st_conv"] = np.ascontiguousarray(np.asarray(inputs["state_conv"][0, s0:s1], f32).reshape(NS * 3, D))
        m["st_lru"] = np.ascontiguousarray(np.asarray(inputs["state_rglru"][0, s0:s1], f32))
        m["ptab"] = np.ascontiguousarray(np.asarray(inputs["page_table"][s0:s1], np.int32).T)
        in_maps.append(m)
    if "nc" not in _NC_CACHE:
        _NC_CACHE["nc"] = build_program()
    nc = _NC_CACHE["nc"]
    res = run_bass_kernel_spmd(nc, in_maps, core_ids=list(range(NCORE)))
    R = res.results
    cat = lambda k: np.stack([np.asarray(R[c][k], f32) for c in range(NCORE)])
    y_prompt = cat("y_p")
    y_sample = cat("y_s").reshape(NCORE * NS, 1, D)
    nk_p = cat("nk_p").reshape(1, NCORE, SEQ, 2, 128)
    nv_p = cat("nv_p").reshape(1, NCORE, SEQ, 2, 128)
    nki_p = cat("nki_p").reshape(1, NCORE, SEQ, 64)
    ncv_p = cat("ncv_p").reshape(1, NCORE, 3, D)
    nlr_p = cat("nlr_p").reshape(1, NCORE, D)
    nk_s = cat("nk_s").reshape(1, NCORE * NS, 1, 2, 128)
    nv_s = cat("nv_s").reshape(1, NCORE * NS, 1, 2, 128)
    nki_s = cat("nki_s").reshape(1, NCORE * NS, 1, 64)
    ncv_s = cat("ncv_s").reshape(1, NCORE * NS, 3, D)
    nlr_s = cat("nlr_s").reshape(1, NCORE * NS, D)
    return (y_prompt, y_sample, nk_p, nv_p, nki_p, ncv_p, nlr_p, nk_s, nv_s, nki_s, ncv_s, nlr_s)
```

```python
import numpy as np
import concourse.bass as bass
import concourse.mybir as mybir
from concourse.bass_utils import run_bass_kernel_spmd
from contextlib import ExitStack

F32 = mybir.dt.float32
BF16 = mybir.dt.bfloat16
I32 = mybir.dt.int32
AF = mybir.ActivationFunctionType
ALU = mybir.AluOpType
AX = mybir.AxisListType

D = 1024
SEQ = 4096
NCORE = 8
NS = 4
PAST = 16384
NPAGE = 128
NPOOL = 5120
D_IN = 7240
EPS = 1e-6
IDX_W_SCALE = 512.0 ** -0.5
ATT_SCALE = 128.0 ** -0.5
TOPK = 256
NEG = -30000.0
NEG_MB = -240.0
CH = 512
NCHUNK = SEQ // CH
N_BISECT = 14
N_BISECT_S = 20

FM_XA, FM_GA, FM_GB, FM_MA, FM_MB, FM_PA, FM_PB = 0, 8, 16, 24, 32, 40, 48
NFM = 56
FM_COLS = [0, 1024, 3584, 5192, 6216]
TM_Q0, TM_Q1, TM_KV, TM_QI, TM_KW, TM_O0, TM_O1 = range(7)
NTM = 7
TM_COLS = [2048, 2560, 3072, 4608, 5120]

CP_GN, CP_BADA, CP_WCONV, CP_BCONV, CP_BRA, CP_BRX, CP_LAM = 0, 8, 32, 64, 72, 80, 88
NCP = 96
CS_ID, CS_TS, CS_JF, CS_IR, CS_CAUS, CS_J1, CS_P2, CS_END = 0, 128, 256, 512, 640, 768, 770, 802

DO_SAMPLE = True
DO_ATTN = True
STOP = 'all'
NCHUNK_RUN = NCHUNK


class Buf:
    __slots__ = ("name", "w", "r")

    def __init__(self, name=""):
        self.name = name
        self.w = None
        self.r = []


class Prog:
    EPOCH = 30000

    def __init__(self, nc, es, n_dma_sems=12):
        self.nc = nc
        self.es = es
        self.engs = {"pe": nc.tensor, "act": nc.scalar, "dve": nc.vector, "pool": nc.gpsimd, "sp": nc.sync}
        self.cnt = {k: 0 for k in self.engs}
        self.sems = {k: [es.enter_context(nc.semaphore("s_" + k))] for k in self.engs}
        self.waited = {k: {} for k in self.engs}
        self.dsems = {}
        self.dcnt = {}
        self.dnext = {}
        for q in ("sp", "act", "pool"):
            self.dsems[q] = [es.enter_context(nc.semaphore("d%s%d" % (q, i))) for i in range(n_dma_sems)]
            self.dcnt[q] = [0] * n_dma_sems
            self.dnext[q] = 0
        self.ninst = 0
        self.out_toks = []

    def _wait(self, ek, tok):
        if tok is None:
            return
        sem, val = tok
        key = id(sem)
        if self.waited[ek].get(key, 0) >= val:
            return
        self.engs[ek].wait_ge(sem, val)
        self.waited[ek][key] = val

    def _same(self, ek, tok):
        if tok is None:
            return False
        sem = tok[0]
        for s_ in self.sems[ek]:
            if sem is s_:
                return True
        return False

    def _deps(self, ek, reads, writes):
        pe = (ek == "pe")
        for b in reads:
            if pe and self._same(ek, b.w):
                continue
            self._wait(ek, b.w)
        for b in writes:
            if not (pe and self._same(ek, b.w)):
                self._wait(ek, b.w)
            for t in b.r:
                if not (pe and self._same(ek, t)):
                    self._wait(ek, t)

    def _mark(self, tok, reads, writes):
        for b in reads:
            b.r.append(tok)
            if len(b.r) > 64:
                b.r = b.r[-48:]
        for b in writes:
            b.w = tok
            b.r = []

    def op(self, ek, fn, reads=(), writes=()):
        self._deps(ek, reads, writes)
        ins = fn(self.engs[ek])
        if self.cnt[ek] >= self.EPOCH:
            self.sems[ek].append(self.es.enter_context(self.nc.semaphore("s_%s_%d" % (ek, len(self.sems[ek])))))
            self.cnt[ek] = 0
        sem = self.sems[ek][-1]
        self.cnt[ek] += 1
        ins.then_inc(sem, 1)
        tok = (sem, self.cnt[ek])
        self._mark(tok, reads, writes)
        self.ninst += 1
        return tok

    def dma(self, ek, fn, reads=(), writes=(), is_out=False):
        q = ek
        i = self.dnext[q]
        self.dnext[q] = (i + 1) % len(self.dsems[q])
        sem = self.dsems[q][i]
        if self.dcnt[q][i] > 0:
            self._wait(ek, (sem, self.dcnt[q][i]))
        self._deps(ek, reads, writes)
        ins = fn(self.engs[ek])
        self.dcnt[q][i] += 16
        ins.then_inc(sem, 16)
        tok = (sem, self.dcnt[q][i])
        self._mark(tok, reads, writes)
        self.ninst += 1
        if is_out:
            self.out_toks.append(tok)
        return tok

    def all_tokens(self):
        toks = []
        for k in self.engs:
            if self.cnt[k] > 0:
                toks.append((self.sems[k][-1], self.cnt[k]))
        for q in self.dsems:
            for s, c in zip(self.dsems[q], self.dcnt[q]):
                if c > 0:
                    toks.append((s, c))
        return toks

    def barrier(self):
        toks = self.all_tokens()
        for ek in self.engs:
            for t in toks:
                self._wait(ek, t)

    def finish(self):
        toks = self.all_tokens()
        for t in toks:
            self._wait("sp", t)


class Ring:
    def __init__(self, items):
        self.items = items
        self.i = 0

    def next(self):
        it = self.items[self.i]
        self.i = (self.i + 1) % len(self.items)
        return it


def build_program():
    nc = bass.Bass("TRN2", target_bir_lowering=False)
    dt_in = lambda n, s, d=F32: nc.dram_tensor(n, s, d, kind="ExternalInput").ap()
    dt_out = lambda n, s, d=F32: nc.dram_tensor(n, s, d, kind="ExternalOutput").ap()

    x_p = dt_in("x_p", [SEQ, D])
    x_s = dt_in("x_s", [NS, D])
    c5 = dt_in("c5", [1 + NS, D])
    cache_k = dt_in("cache_k", [NPOOL * 128, 256])
    cache_v = dt_in("cache_v", [NPOOL * 128, 256])
    cache_i = dt_in("cache_i", [NPOOL, 128 * 64])
    st_conv = dt_in("st_conv", [NS * 3, D])
    st_lru = dt_in("st_lru", [NS, D])
    ptab = dt_in("ptab", [128, NS], I32)
    w_ada = dt_in("w_ada", [6, 128, 8 * 512])
    b_ada = dt_in("b_ada", [1, 3 * D])
    w_fm = dt_in("w_fm", [NFM, 128, 1024])
    w_tm = dt_in("w_tm", [NTM, 128, 4096])
    w_rr = dt_in("w_rr", [128, 2 * 2048])
    chp = dt_in("chp", [128, NCP])
    cst = dt_in("cst", [128, CS_END])
    ropeq = dt_in("ropeq", [SEQ, 256])
    ropei = dt_in("ropei", [SEQ, 128])
    ropes = dt_in("ropes", [NS, 384])
    g_fin = dt_in("g_fin", [1, D])
    idx_gb = dt_in("idx_gb", [1, 128])

    y_p = dt_out("y_p", [SEQ, D])
    y_s = dt_out("y_s", [NS, D])
    nk_p = dt_out("nk_p", [SEQ, 256])
    nv_p = dt_out("nv_p", [SEQ, 256])
    nki_p = dt_out("nki_p", [SEQ, 64])
    ncv_p = dt_out("ncv_p", [3, D])
    nlr_p = dt_out("nlr_p", [1, D])
    nk_s = dt_out("nk_s", [NS, 256])
    nv_s = dt_out("nv_s", [NS, 256])
    nki_s = dt_out("nki_s", [NS, 64])
    ncv_s = dt_out("ncv_s", [NS, 3, D])
    nlr_s = dt_out("nlr_s", [NS, D])

    wfm = nc.dram_tensor("wfm_bf", [NFM, 128, 1024], BF16, kind="Internal").ap()
    wtm = nc.dram_tensor("wtm_bf", [NTM, 128, 4096], BF16, kind="Internal").ap()

    with ExitStack() as es:
        P = Prog(nc, es)

        def sb(name, shape, dtype=F32, scope=es):
            return scope.enter_context(nc.sbuf_tensor(name, shape, dtype))

        def ring(name, n, shape, dtype=F32, scope=es):
            return Ring([(sb("%s%d" % (name, i), shape, dtype, scope), Buf(name)) for i in range(n)])

        OPQ = ["act", "dve", "pool"]

        psA = es.enter_context(nc.psum_tensor("psA", [128, 512], F32)); bpsA = Buf()
        psB = es.enter_context(nc.psum_tensor("psB", [128, 512], F32)); bpsB = Buf()
        psS0 = es.enter_context(nc.psum_tensor("psS0", [128, 512], F32)); bpsS0 = Buf()
        psS1 = es.enter_context(nc.psum_tensor("psS1", [128, 512], F32)); bpsS1 = Buf()
        psO = es.enter_context(nc.psum_tensor("psO", [128, 512], F32)); bpsO = Buf()
        psL = es.enter_context(nc.psum_tensor("psL", [128, 512], F32)); bpsL = Buf()
        psT0 = es.enter_context(nc.psum_tensor("psT0", [128, 512], F32)); bpsT0 = Buf()
        psT1 = es.enter_context(nc.psum_tensor("psT1", [128, 512], F32)); bpsT1 = Buf()
        rPS = Ring([(psA, bpsA), (psB, bpsB)])
        rPSS = Ring([(psS0, bpsS0), (psS1, bpsS1)])
        rPST = Ring([(psT0[:].bitcast(BF16), bpsT0), (psT1[:].bitcast(BF16), bpsT1)])
        rPACC = Ring([(psT0, bpsT0), (psT1, bpsT1)])

        cstt = sb("cstt", [128, CS_END]); bcst = Buf()
        chpt = sb("chpt", [128, NCP]); bchp = Buf()
        identb = sb("identb", [128, 128], BF16); bidb = Buf()
        ident4 = sb("ident4", [128, 512], BF16); bid4 = Buf()
        onesb = sb("onesb", [128, 128], BF16); bonesb = Buf()
        onesf = sb("onesf", [128, 128]); bonesf = Buf()
        clam = sb("clam", [128, 8]); bclam = Buf()
        gate_bc = sb("gate_bc", [128, D]); bgate = Buf()
        gfin_bc = sb("gfin_bc", [128, D]); bgfin = Buf()
        idxgb_bc = sb("idxgb_bc", [128, 128]); bidxgb = Buf()
        wrr = sb("wrr", [128, 4096], BF16); bwrr = Buf()
        A_p = sb("A_p", [128, 8]); bAp = Buf()
        B_p = sb("B_p", [128, 8]); bBp = Buf()
        mT = sb("mT", [128, 24 * 5]); bmT = Buf()
        silucT = sb("silucT", [128, 8 * 5], BF16); bsil = Buf()
        identf = cstt[:, CS_ID:CS_ID + 128]
        rFM = ring("wfmr", 4, [128, 1024], BF16)
        rTM = ring("wtmr", 2, [128, 4096], BF16)
        sS = es.enter_context(ExitStack())
        gate_s = sb("gate_s", [NS, D], F32, sS); bgs = Buf()

        P.dma("sp", lambda e: e.dma_start(out=cstt[:], in_=cst[:, :]), writes=[bcst])
        P.dma("sp", lambda e: e.dma_start(out=chpt[:], in_=chp[:, :]), writes=[bchp])
        P.dma("sp", lambda e: e.dma_start(out=gfin_bc[:], in_=g_fin[0:1, :].broadcast_to([128, D])), writes=[bgfin])
        P.dma("sp", lambda e: e.dma_start(out=gate_bc[:], in_=b_ada[0:1, 2 * D:3 * D].broadcast_to([128, D])), writes=[bgate])
        P.dma("sp", lambda e: e.dma_start(out=gate_s[:], in_=b_ada[0:1, 2 * D:3 * D].broadcast_to([NS, D])), writes=[bgs])
        P.dma("sp", lambda e: e.dma_start(out=idxgb_bc[:], in_=idx_gb[0:1, :].broadcast_to([128, 128])), writes=[bidxgb])
        P.op("dve", lambda e: e.tensor_copy(out=identb[:], in_=identf), reads=[bcst], writes=[bidb])
        for r4 in range(4):
            P.op("pool", lambda e: e.tensor_copy(out=ident4[:, r4 * 128:(r4 + 1) * 128], in_=identf), reads=[bcst], writes=[bid4])
        P.op("pool", lambda e: e.memset(onesb[:], 1.0), writes=[bonesb])
        P.op("pool", lambda e: e.memset(onesf[:], 1.0), writes=[bonesf])
        P.op("act", lambda e: e.activation(out=clam[:], in_=chpt[:, CP_LAM:CP_LAM + 8], func=AF.Exp, scale=-1.0), reads=[bchp], writes=[bclam])
        P.op("act", lambda e: e.activation(out=clam[:], in_=clam[:], func=AF.Ln, bias=1.0), reads=[bclam], writes=[bclam])
        P.op("dve", lambda e: e.tensor_scalar(out=clam[:], in0=clam[:], scalar1=-8.0, scalar2=None, op0=ALU.mult), reads=[bclam], writes=[bclam])

        with ExitStack() as s0:
            rst = ring("w0st", 2, [128, 4096], F32, s0)
            rsb = ring("w0sb", 2, [128, 4096], BF16, s0)
            k = 0
            for src, dst, n in ((w_fm, wfm, NFM // 4), (w_tm, wtm, NTM)):
                for b in range(n):
                    st, bst = rst.next()
                    sbf, bsbf = rsb.next()
                    if src is w_fm:
                        sap = src[4 * b:4 * b + 4].rearrange("n p f -> p n f")
                        dap = dst[4 * b:4 * b + 4].rearrange("n p f -> p n f")
                        tap_s = st[:].rearrange("p (n f) -> p n f", n=4)
                        tap_b = sbf[:].rearrange("p (n f) -> p n f", n=4)
                    else:
                        sap, dap, tap_s, tap_b = src[b], dst[b], st[:], sbf[:]
                    P.dma("sp", lambda e: e.dma_start(out=tap_s, in_=sap), writes=[bst])
                    ek = OPQ[k % 3]; k += 1
                    if ek == "act":
                        P.op(ek, lambda e: e.copy(out=sbf[:], in_=st[:]), reads=[bst], writes=[bsbf])
                    else:
                        P.op(ek, lambda e: e.tensor_copy(out=sbf[:], in_=st[:]), reads=[bst], writes=[bsbf])
                    P.dma("act", lambda e: e.dma_start(out=dap, in_=tap_b), reads=[bsbf])
            st, bst = rst.next()
            P.dma("sp", lambda e: e.dma_start(out=st[:], in_=w_rr[:, :]), writes=[bst])
            P.op("dve", lambda e: e.tensor_copy(out=wrr[:], in_=st[:]), reads=[bst], writes=[bwrr])

            c5t = sb("c5t", [1 + NS, D], F32, s0); bc5 = Buf()
            P.dma("sp", lambda e: e.dma_start(out=c5t[:], in_=c5[:, :]), writes=[bc5])
            P.op("act", lambda e: e.activation(out=c5t[:], in_=c5t[:], func=AF.Silu), reads=[bc5], writes=[bc5])
            for kc in range(8):
                P.op("pe", lambda e: e.transpose(out=psA[:, kc * 5:kc * 5 + 5], in_=c5t[:, kc * 128:(kc + 1) * 128], identity=identf[0:5, 0:5]),
                     reads=[bc5, bcst], writes=[bpsA])
            P.op("dve", lambda e: e.tensor_copy(out=silucT[:], in_=psA[:, 0:40]), reads=[bpsA], writes=[bsil])
            silrep = sb("silrep", [128, 8 * 128], BF16, s0); bsilrep = Buf()
            sil3 = silucT[:].rearrange("p (k t) -> p k t", t=5)
            P.op("dve", lambda e: e.tensor_copy(out=silrep[:].rearrange("p (k m) -> p k m", m=128),
                                                in_=sil3[:, :, 0:1].to_broadcast([128, 8, 128])), reads=[bsil], writes=[bsilrep])
            for blk in range(6):
                st, bst = rst.next()
                sbf, bsbf = rsb.next()
                P.dma("sp", lambda e: e.dma_start(out=st[:], in_=w_ada[blk]), writes=[bst])
                P.op(OPQ[blk % 3], (lambda e: e.copy(out=sbf[:], in_=st[:])) if blk % 3 == 0 else (lambda e: e.tensor_copy(out=sbf[:], in_=st[:])),
                     reads=[bst], writes=[bsbf])
                w3 = sbf[:].rearrange("p (k c) -> p k c", c=512)
                for q in range(4):
                    cc = blk * 4 + q
                    for kc in range(8):
                        P.op("pe", lambda e: e.matmul(psB[:, cc * 5:cc * 5 + 5], lhsT=w3[:, kc, q * 128:(q + 1) * 128], rhs=sil3[:, kc, :],
                                                      start=(kc == 0), stop=(kc == 7)), reads=[bsbf, bsil], writes=[bpsB])
                if blk >= 4:
                    hb = blk - 4
                    ps, bps = rPSS.next()
                    for kc in range(8):
                        P.op("pe", lambda e: e.matmul(ps[:, :], lhsT=silrep[:, kc * 128:(kc + 1) * 128], rhs=w3[:, kc, :],
                                                      start=(kc == 0), stop=(kc == 7)), reads=[bsbf, bsilrep], writes=[bps])
                    P.op("dve", lambda e: e.tensor_tensor(out=gate_bc[:, hb * 512:(hb + 1) * 512], in0=ps[:, :], in1=gate_bc[:, hb * 512:(hb + 1) * 512], op=ALU.add),
                         reads=[bps, bgate], writes=[bgate])
                    ps, bps = rPSS.next()
                    for kc in range(8):
                        P.op("pe", lambda e: e.matmul(ps[0:NS, :], lhsT=sil3[:, kc, 1:5], rhs=w3[:, kc, :],
                                                      start=(kc == 0), stop=(kc == 7)), reads=[bsbf, bsil], writes=[bps])
                    P.op("dve", lambda e: e.tensor_tensor(out=gate_s[:, hb * 512:(hb + 1) * 512], in0=ps[0:NS, :], in1=gate_s[:, hb * 512:(hb + 1) * 512], op=ALU.add),
                         reads=[bps, bgs], writes=[bgs])
            mT3 = mT[:].rearrange("p (c t) -> p c t", t=5)
            P.op("dve", lambda e: e.tensor_tensor(out=mT3, in0=psB[:, 0:120].rearrange("p (c t) -> p c t", t=5),
                                                  in1=chpt[:, CP_BADA:CP_BADA + 24].unsqueeze(2).to_broadcast([128, 24, 5]), op=ALU.add),
                 reads=[bpsB, bchp], writes=[bmT])
            P.op("dve", lambda e: e.scalar_tensor_tensor(out=A_p[:], in0=mT3[:, 8:16, 0], scalar=1.0, in1=chpt[:, CP_GN:CP_GN + 8], op0=ALU.add, op1=ALU.mult),
                 reads=[bmT, bchp], writes=[bAp])
            P.op("dve", lambda e: e.tensor_copy(out=B_p[:], in_=mT3[:, 0:8, 0]), reads=[bmT], writes=[bBp])
        P.barrier()


        def load_fm(idx):
            t, b = rFM.next()
            P.dma("sp", lambda e: e.dma_start(out=t[:], in_=wfm[idx]), writes=[b])
            return t[:].rearrange("p (k c) -> p k c", c=128), b

        def load_tm(idx):
            t, b = rTM.next()
            P.dma("sp", lambda e: e.dma_start(out=t[:], in_=wtm[idx]), writes=[b])
            return t[:].rearrange("p (k c) -> p k c", c=512), b

        def rope(ek2, out_ap, x_ap, cosf, sinf, H, Dh, t1, t2, reads, writes, bt1, bt2):
            hf = Dh // 2
            p = x_ap.shape[0]
            cb = cosf.unsqueeze(1).to_broadcast([p, H, Dh])
            s1 = sinf[:, 0:hf].unsqueeze(1).to_broadcast([p, H, hf])
            s2 = sinf[:, hf:Dh].unsqueeze(1).to_broadcast([p, H, hf])
            P.op("dve", lambda e: e.tensor_tensor(out=t1, in0=x_ap, in1=cb, op=ALU.mult), reads=reads, writes=[bt1])
            P.op("dve", lambda e: e.tensor_tensor(out=t2[:, :, 0:hf], in0=x_ap[:, :, hf:Dh], in1=s1, op=ALU.mult), reads=reads, writes=[bt2])
            P.op("dve", lambda e: e.tensor_tensor(out=t2[:, :, hf:Dh], in0=x_ap[:, :, 0:hf], in1=s2, op=ALU.mult), reads=reads, writes=[bt2])
            return P.op(ek2, lambda e: e.tensor_tensor(out=out_ap, in0=t1, in1=t2, op=ALU.add), reads=[bt1, bt2], writes=writes)

        if DO_SAMPLE:
            sample_phase(nc, P, locals())
            P.barrier()
        sS.close()

        if STOP != 'p0':
            prompt_phase(nc, P, locals())
        P.finish()
        print("ninst", P.ninst, {k: (len(P.sems[k]) - 1) * P.EPOCH + P.cnt[k] for k in P.cnt})
    return nc


def prompt_phase(nc, P, G):
    es = G["es"]; sb = G["sb"]; ring = G["ring"]
    x_p = G["x_p"]; y_p = G["y_p"]; nk_p = G["nk_p"]; nv_p = G["nv_p"]; nki_p = G["nki_p"]; ncv_p = G["ncv_p"]; nlr_p = G["nlr_p"]
    ropeq = G["ropeq"]; ropei = G["ropei"]
    rPS = G["rPS"]; rPSS = G["rPSS"]; rPST = G["rPST"]; rPACC = G["rPACC"]
    psO = G["psO"]; bpsO = G["bpsO"]; psL = G["psL"]; bpsL = G["bpsL"]
    cstt = G["cstt"]; bcst = G["bcst"]; chpt = G["chpt"]; bchp = G["bchp"]
    identb = G["identb"]; bidb = G["bidb"]; ident4 = G["ident4"]; bid4 = G["bid4"]
    onesb = G["onesb"]; bonesb = G["bonesb"]; identf = G["identf"]
    clam = G["clam"]; bclam = G["bclam"]
    gate_bc = G["gate_bc"]; bgate = G["bgate"]; gfin_bc = G["gfin_bc"]; bgfin = G["bgfin"]
    idxgb_bc = G["idxgb_bc"]; bidxgb = G["bidxgb"]
    wrr = G["wrr"]; bwrr = G["bwrr"]; A_p = G["A_p"]; bAp = G["bAp"]; B_p = G["B_p"]; bBp = G["bBp"]
    load_fm = G["load_fm"]; load_tm = G["load_tm"]; rope = G["rope"]
    caus = cstt[:, CS_CAUS:CS_CAUS + 128]
    wrr5 = wrr[:].rearrange("p (a n k c) -> p a n k c", a=2, n=4, k=2)

    KT = sb("KT", [128, 2 * SEQ], BF16); KT3 = KT[:].rearrange("p (g t) -> p g t", g=2)
    Vres = sb("Vres", [128, 32 * 256], BF16); V4 = Vres[:].rearrange("p (i g d) -> p i g d", i=32, g=2)
    kiT = sb("kiT", [128, SEQ], BF16)
    bKV = [Buf() for _ in range(32)]
    hist = sb("hist", [128, 8 * 3]); bhist = Buf(); hist3 = hist[:].rearrange("p (c j) -> p c j", j=3)
    hprev = sb("hprev", [128, 8]); bhprev = Buf()
    P.op("pool", lambda e: e.memset(hist[:], 0.0), writes=[bhist])
    P.op("pool", lambda e: e.memset(hprev[:], 0.0), writes=[bhprev])

    rX = ring("xt", 1, [128, D])
    rXh = ring("xh", 1, [128, D], BF16)
    xnT = sb("xnT", [128, 8 * CH], BF16); bxn = Buf(); xn3 = xnT[:].rearrange("p (k t) -> p k t", k=8)
    rq = sb("rq", [128, 4 * 256]); brq = Buf(); rq3 = rq[:].rearrange("p (t c) -> p t c", t=4)
    ri = sb("ri", [128, 4 * 128]); bri = Buf(); ri3 = ri[:].rearrange("p (t c) -> p t c", t=4)
    rQr = ring("qrot", 1, [128, 512], BF16)
    rKr = ring("krot", 1, [128, 256]); rVf = ring("vf", 1, [128, 256]); rKi = ring("kio", 1, [128, 64])
    rKb = ring("kb16", 1, [128, 256], BF16); rKi2 = ring("ki2", 1, [128, 128], BF16)
    qT = sb("qT", [128, 4 * 1024], BF16); bqT = [Buf() for _ in range(4)]; qT4 = qT[:].rearrange("p (t h q) -> p t h q", t=4, h=8)
    qiT = sb("qiT", [128, 4 * 512], BF16); bqiT = [Buf() for _ in range(4)]; qiT4 = qiT[:].rearrange("p (t h q) -> p t h q", t=4, h=4)
    wS = sb("wS", [128, 4 * 8]); bwS = [Buf() for _ in range(4)]; wS3 = wS[:].rearrange("p (t h) -> p t h", t=4)
    awS = sb("awS", [128, 4 * 8]); awS3 = awS[:].rearrange("p (t h) -> p t h", t=4)
    sgS = sb("sgS", [128, 4 * 8]); sgS3 = sgS[:].rearrange("p (t h) -> p t h", t=4)
    rDg = ring("diagS", 2, [128, 8 * 128], BF16)
    Wt = sb("bs_W", [128, 32]); bWt = Buf()
    W2t = sb("bs_W2", [128, 32]); bW2t = Buf()
    pow2 = cstt[:, CS_P2:CS_P2 + N_BISECT]
    sm = sb("smallst", [128, 16]); bsm = Buf()
    xa = sb("xa", [128, 2 * 515]); bxa = [Buf(), Buf()]; xa3 = xa[:].rearrange("p (c t) -> p c t", c=2)
    xc = sb("xc", [128, 2 * CH]); bxc = [Buf(), Buf()]; xc3 = xc[:].rearrange("p (c t) -> p c t", c=2)
    xcb = sb("xcb", [128, 2 * CH], BF16); bxcb = [Buf(), Buf()]; xcb3 = xcb[:].rearrange("p (c t) -> p c t", c=2)
    junkb = xcb; bjunk = bxcb[0]
    g_r = [sb("g_r%d" % q, [128, CH]) for q in range(2)]; b_r = [Buf(), Buf()]
    g_i = [sb("g_i%d" % q, [128, CH]) for q in range(2)]; b_i = [Buf(), Buf()]
    g_a = [sb("g_a%d" % q, [128, CH]) for q in range(2)]; b_a = [Buf(), Buf()]
    g_t = [sb("g_t%d" % q, [128, CH]) for q in range(2)]; b_t = [Buf(), Buf()]
    g_h = [sb("g_h%d" % q, [128, CH]) for q in range(2)]; b_h = [Buf(), Buf()]
    rT1 = Ring([(g_a[0], b_a[0])]); rT2 = Ring([(g_t[0], b_t[0])])
    rP3 = Ring([(G["psA"], G["bpsA"]), (G["psB"], G["bpsB"]), (G["psS0"], G["bpsS0"]), (G["psS1"], G["bpsS1"])])
    actT = sb("actT", [128, 8 * CH], BF16); bact = [Buf() for _ in range(8)]; act3 = actT[:].rearrange("p (k t) -> p k t", k=8)
    mTt = sb("mTt", [128, 8 * CH], BF16); bmm = [Buf() for _ in range(8)]; m3 = mTt[:].rearrange("p (k t) -> p k t", k=8)
    rSg = Ring([(g_r[0], b_r[0]), (g_i[0], b_i[0]), (g_r[1], b_r[1]), (g_i[1], b_i[1])])
    sc = sb("sc", [128, SEQ]); bsc = Buf()
    rMB = ring("MB", 2, [128, SEQ], mybir.dt.float8e4)
    rR = ring("Rr", 2, [128, 512], BF16)
    rPT = ring("PTr", 2, [128, 512], BF16)
    rl = g_h[0]; brl = b_h[0]
    hres = sb("hres", [128, D]); bhres = Buf()
    yo = hres; byo = bhres
    lo = sb("bs_lo", [128, 1]); blo = Buf()
    wd = sb("bs_w", [128, 1]); bwd = Buf()
    mid = sb("bs_mid", [128, 1]); bmid = Buf()
    cnt = sb("bs_cnt", [128, 1]); bcnt = Buf()
    cond = sb("bs_cond", [128, 1]); bcond = Buf()
    otr = hres; botr = bhres

    for c in range(NCHUNK_RUN):
        t0 = c * CH
        P.dma("sp", lambda e: e.dma_start(out=rq3, in_=ropeq[t0:t0 + CH, :].rearrange("(t p) c -> p t c", p=128)), writes=[brq])
        P.dma("sp", lambda e: e.dma_start(out=ri3, in_=ropei[t0:t0 + CH, :].rearrange("(t p) c -> p t c", p=128)), writes=[bri])
        for tt in range(4):
            xt, bxt = rX.next()
            xh, bxh = rXh.next()
            P.dma("sp", lambda e: e.dma_start(out=xt[:], in_=x_p[t0 + tt * 128:t0 + (tt + 1) * 128, :]), writes=[bxt])
            P.op("act", lambda e: e.activation(out=junkb[:], in_=xt[:], func=AF.Square, accum_out=sm[:, 0:1]), reads=[bxt], writes=[bxcb[0], bxcb[1], bsm])
            P.op("act", lambda e: e.activation(out=sm[:, 1:2], in_=sm[:, 0:1], func=AF.Sqrt, scale=1.0 / D, bias=EPS), reads=[bsm], writes=[bsm])
            P.op("dve", lambda e: e.reciprocal(out=sm[:, 2:3], in_=sm[:, 1:2]), reads=[bsm], writes=[bsm])
            P.op("dve", lambda e: e.tensor_scalar(out=xh[:], in0=xt[:], scalar1=sm[:, 2:3], scalar2=None, op0=ALU.mult), reads=[bxt, bsm], writes=[bxh])
            pt, bpt = rPST.next()
            for kc in range(8):
                P.op("pe", lambda e: e.transpose(out=pt[:, kc * 128:(kc + 1) * 128], in_=xh[:, kc * 128:(kc + 1) * 128], identity=identb[:]),
                     reads=[bxh, bidb], writes=[bpt])
            for kc in range(8):
                P.op("dve", lambda e: e.tensor_scalar(out=xn3[:, kc, tt * 128:(tt + 1) * 128], in0=pt[:, kc * 128:(kc + 1) * 128],
                                                      scalar1=A_p[:, kc:kc + 1], scalar2=B_p[:, kc:kc + 1], op0=ALU.mult, op1=ALU.add),
                     reads=[bpt, bAp, bBp], writes=[bxn])

        if STOP == 'p1':
            continue
        def p2_mm(blk, tt, w3, bw):
            ncol = 72 if blk == TM_KW else 512
            ps, bps = rPS.next()
            for kc in range(8):
                P.op("pe", lambda e: e.matmul(ps[:, 0:ncol], lhsT=xn3[:, kc, tt * 128:(tt + 1) * 128], rhs=w3[:, kc, 0:ncol],
                                              start=(kc == 0), stop=(kc == 7)), reads=[bxn, bw], writes=[bps])
            return ps, bps

        def p2_post(blk, tt, ps, bps):
            i = c * 4 + tt
            r0 = t0 + tt * 128
            t1, bt1 = rT1.next(); t2, bt2 = rT2.next()
            if blk in (TM_Q0, TM_Q1):
                qr, bqr = rQr.next()
                rope("pool", qr[:].rearrange("p (h d) -> p h d", h=4), ps[:, :].rearrange("p (h d) -> p h d", h=4),
                     rq3[:, tt, 0:128], rq3[:, tt, 128:256], 4, 128,
                     t1[:].rearrange("p (h d) -> p h d", h=4), t2[:].rearrange("p (h d) -> p h d", h=4),
                     [bps, brq], [bqr], bt1, bt2)
                pt, bpt = rPST.next()
                for h in range(4):
                    P.op("pe", lambda e: e.transpose(out=pt[:, h * 128:(h + 1) * 128], in_=qr[:, h * 128:(h + 1) * 128], identity=identb[:]),
                         reads=[bqr, bidb], writes=[bpt])
                h0 = 4 * (blk - TM_Q0)
                P.op("act", lambda e: e.copy(out=qT4[:, tt, h0:h0 + 4, :], in_=pt[:, 0:512].rearrange("p (h q) -> p h q", h=4)),
                     reads=[bpt], writes=[bqT[tt]])
            elif blk == TM_KV:
                kr, bkr = rKr.next(); vf, bvf = rVf.next(); kb, bkb = rKb.next()
                rope("pool", kr[:].rearrange("p (h d) -> p h d", h=2), ps[:, 0:256].rearrange("p (h d) -> p h d", h=2),
                     rq3[:, tt, 0:128], rq3[:, tt, 128:256], 2, 128,
                     t1[:, 0:256].rearrange("p (h d) -> p h d", h=2), t2[:, 0:256].rearrange("p (h d) -> p h d", h=2),
                     [bps, brq], [bkr], bt1, bt2)
                P.dma("act", lambda e: e.dma_start(out=nk_p[r0:r0 + 128, :], in_=kr[:]), reads=[bkr], is_out=True)
                P.op("act", lambda e: e.copy(out=vf[:], in_=ps[:, 256:512]), reads=[bps], writes=[bvf])
                P.dma("act", lambda e: e.dma_start(out=nv_p[r0:r0 + 128, :], in_=vf[:]), reads=[bvf], is_out=True)
                P.op("act", lambda e: e.copy(out=V4[:, i, :, :], in_=ps[:, 256:512].rearrange("p (g d) -> p g d", g=2)), reads=[bps], writes=[bKV[i]])
                P.op("pool", lambda e: e.tensor_copy(out=kb[:], in_=kr[:]), reads=[bkr], writes=[bkb])
                pt, bpt = rPST.next()
                for g in range(2):
                    P.op("pe", lambda e: e.transpose(out=pt[:, g * 128:(g + 1) * 128], in_=kb[:, g * 128:(g + 1) * 128], identity=identb[:]),
                         reads=[bkb, bidb], writes=[bpt])
                P.op("act", lambda e: e.copy(out=KT3[:, :, r0:r0 + 128], in_=pt[:, 0:256].rearrange("p (g q) -> p g q", g=2)),
                     reads=[bpt], writes=[bKV[i]])
            elif blk == TM_QI:
                qr, bqr = rQr.next()
                rope("pool", qr[:].rearrange("p (h d) -> p h d", h=8), ps[:, :].rearrange("p (h d) -> p h d", h=8),
                     ri3[:, tt, 0:64], ri3[:, tt, 64:128], 8, 64,
                     t1[:].rearrange("p (h d) -> p h d", h=8), t2[:].rearrange("p (h d) -> p h d", h=8),
                     [bps, bri], [bqr], bt1, bt2)
                pt, bpt = rPST.next()
                for hp in range(4):
                    P.op("pe", lambda e: e.transpose(out=pt[:, hp * 128:(hp + 1) * 128], in_=qr[:, hp * 128:(hp + 1) * 128], identity=identb[:]),
                         reads=[bqr, bidb], writes=[bpt])
                P.op("act", lambda e: e.copy(out=qiT4[:, tt, :, :], in_=pt[:, 0:512].rearrange("p (h q) -> p h q", h=4)),
                     reads=[bpt], writes=[bqiT[tt]])
            else:
                kio, bkio = rKi.next(); ki2, bki2 = rKi2.next()
                P.op("dve", lambda e: e.tensor_scalar(out=wS3[:, tt, :], in0=ps[:, 64:72], scalar1=IDX_W_SCALE, scalar2=None, op0=ALU.mult),
                     reads=[bps], writes=[bwS[tt]])
                P.op("dve", lambda e: e.tensor_scalar(out=sgS3[:, tt, :], in0=wS3[:, tt, :], scalar1=0.0, scalar2=0.5, op0=ALU.is_ge, op1=ALU.subtract),
                     reads=[bwS[tt]], writes=[bwS[tt]])
                P.op("dve", lambda e: e.scalar_tensor_tensor(out=awS3[:, tt, :], in0=wS3[:, tt, :], scalar=4.0, in1=sgS3[:, tt, :], op0=ALU.mult, op1=ALU.mult),
                     reads=[bwS[tt]], writes=[bwS[tt]])
                P.op("dve", lambda e: e.tensor_reduce(out=sm[:, 4:5], in_=ps[:, 0:64], axis=AX.X, op=ALU.add), reads=[bps], writes=[bsm])
                P.op("dve", lambda e: e.tensor_scalar(out=sm[:, 5:6], in0=sm[:, 4:5], scalar1=-1.0 / 64, scalar2=None, op0=ALU.mult), reads=[bsm], writes=[bsm])
                P.op("dve", lambda e: e.tensor_scalar(out=t1[:, 0:64], in0=ps[:, 0:64], scalar1=sm[:, 5:6], scalar2=None, op0=ALU.add),
                     reads=[bps, bsm], writes=[bt1])
                P.op("act", lambda e: e.activation(out=t2[:, 0:64], in_=t1[:, 0:64], func=AF.Square, accum_out=sm[:, 6:7]), reads=[bt1], writes=[bt2, bsm])
                P.op("act", lambda e: e.activation(out=sm[:, 7:8], in_=sm[:, 6:7], func=AF.Sqrt, scale=1.0 / 64, bias=EPS), reads=[bsm], writes=[bsm])
                P.op("dve", lambda e: e.reciprocal(out=sm[:, 8:9], in_=sm[:, 7:8]), reads=[bsm], writes=[bsm])
                P.op("dve", lambda e: e.scalar_tensor_tensor(out=t1[:, 64:128], in0=t1[:, 0:64], scalar=sm[:, 8:9], in1=idxgb_bc[:, 0:64], op0=ALU.mult, op1=ALU.mult),
                     reads=[bt1, bsm, bidxgb], writes=[bt1])
                P.op("dve", lambda e: e.tensor_tensor(out=t1[:, 128:192], in0=t1[:, 64:128], in1=idxgb_bc[:, 64:128], op=ALU.add),
                     reads=[bt1, bidxgb], writes=[bt1])
                rope("pool", kio[:].unsqueeze(1), t1[:, 128:192].unsqueeze(1), ri3[:, tt, 0:64], ri3[:, tt, 64:128], 1, 64,
                     t2[:, 64:128].unsqueeze(1), t2[:, 128:192].unsqueeze(1), [bt1, bri], [bkio], bt2, bt2)
                P.dma("act", lambda e: e.dma_start(out=nki_p[r0:r0 + 128, :], in_=kio[:]), reads=[bkio], is_out=True)
                P.op("pool", lambda e: e.tensor_copy(out=ki2[:].rearrange("p (a d) -> p a d", a=2), in_=kio[:].unsqueeze(1).to_broadcast([128, 2, 64])),
                     reads=[bkio], writes=[bki2])
                pt, bpt = rPST.next()
                P.op("pe", lambda e: e.transpose(out=pt[:, 0:128], in_=ki2[:], identity=identb[:]), reads=[bki2, bidb], writes=[bpt])
                P.op("act", lambda e: e.copy(out=kiT[:, r0:r0 + 128], in_=pt[:, 0:128]), reads=[bpt], writes=[bKV[i]])


        pend = None
        for blk in (TM_Q0, TM_Q1, TM_KV, TM_QI, TM_KW):
            w3, bw = load_tm(blk)
            for tt in range(4):
                ps, bps = p2_mm(blk, tt, w3, bw)
                if pend is not None:
                    p2_post(*pend)
                pend = (blk, tt, ps, bps)
        p2_post(*pend)
        if STOP == 'p2':
            continue
        for n in range(4):
            for c2 in range(2):
                cc = 2 * n + c2
                w3, bw = load_fm(FM_XA + cc)
                ps, bps = rP3.next()
                for kc in range(8):
                    P.op("pe", lambda e: e.matmul(ps[:, :], lhsT=w3[:, kc, :], rhs=xn3[:, kc, :], start=(kc == 0), stop=(kc == 7)),
                         reads=[bw, bxn], writes=[bps])
                P.op("pool", lambda e: e.tensor_copy(out=xa3[:, c2, 0:3], in_=hist3[:, cc, :]), reads=[bhist], writes=[bxa[c2]])
                P.op("act", lambda e: e.copy(out=xa3[:, c2, 3:515], in_=ps[:, :]), reads=[bps], writes=[bxa[c2]])
                P.op("pool", lambda e: e.tensor_copy(out=hist3[:, cc, :], in_=xa3[:, c2, 512:515]), reads=[bxa[c2]], writes=[bhist])
                wc = lambda j: chpt[:, CP_WCONV + j * 8 + cc:CP_WCONV + j * 8 + cc + 1]
                P.op("dve", lambda e: e.tensor_scalar(out=xc3[:, c2, :], in0=xa3[:, c2, 3:515], scalar1=wc(3), scalar2=chpt[:, CP_BCONV + cc:CP_BCONV + cc + 1],
                                                      op0=ALU.mult, op1=ALU.add), reads=[bxa[c2], bchp], writes=[bxc[c2]])
                for j in range(3):
                    P.op("dve", lambda e: e.scalar_tensor_tensor(out=xc3[:, c2, :], in0=xa3[:, c2, j:j + 512], scalar=wc(j), in1=xc3[:, c2, :],
                                                                 op0=ALU.mult, op1=ALU.add), reads=[bxa[c2], bxc[c2], bchp], writes=[bxc[c2]])
                P.op("pool", lambda e: e.tensor_copy(out=xcb3[:, c2, :], in_=xc3[:, c2, :]), reads=[bxc[c2]], writes=[bxcb[c2]])
            pri = {}
            for c2 in range(2):
                for which in range(2):
                    ps, bps = rP3.next()
                    for k2 in range(2):
                        P.op("pe", lambda e: e.matmul(ps[:, :], lhsT=wrr5[:, which, n, k2, c2 * 128:(c2 + 1) * 128], rhs=xcb3[:, k2, :],
                                                      start=(k2 == 0), stop=(k2 == 1)), reads=[bwrr, bxcb[k2]], writes=[bps])
                    pri[(c2, which)] = (ps, bps)
            for c2 in range(2):
                cc = 2 * n + c2
                for which, dst, bdst, bias0 in ((0, g_r[c2], b_r[c2], CP_BRA), (1, g_i[c2], b_i[c2], CP_BRX)):
                    ps, bps = pri[(c2, which)]
                    P.op("act", lambda e: e.activation(out=dst[:], in_=ps[:, :], func=AF.Sigmoid, bias=chpt[:, bias0 + cc:bias0 + cc + 1]),
                         reads=[bps, bchp], writes=[bdst])
            pga = {}
            for c2 in range(2):
                cc = 2 * n + c2
                w3, bw = load_fm(FM_GA + cc)
                ps, bps = rP3.next()
                for kc in range(8):
                    P.op("pe", lambda e: e.matmul(ps[:, :], lhsT=w3[:, kc, :], rhs=xn3[:, kc, :], start=(kc == 0), stop=(kc == 7)),
                         reads=[bw, bxn], writes=[bps])
                pga[c2] = (ps, bps)
            for c2 in range(2):
                P.op("dve", lambda e: e.tensor_tensor(out=g_i[c2][:], in0=g_i[c2][:], in1=xc3[:, c2, :], op=ALU.mult), reads=[b_i[c2], bxc[c2]], writes=[b_i[c2]])
            for c2 in range(2):
                cc = 2 * n + c2
                P.op("act", lambda e: e.activation(out=g_a[c2][:], in_=g_r[c2][:], func=AF.Exp, scale=clam[:, cc:cc + 1]), reads=[b_r[c2], bclam], writes=[b_a[c2]])
            for c2 in range(2):
                P.op("dve", lambda e: e.scalar_tensor_tensor(out=g_t[c2][:], in0=g_a[c2][:], scalar=-1.0, in1=g_a[c2][:], op0=ALU.mult, op1=ALU.mult),
                     reads=[b_a[c2]], writes=[b_t[c2]])
            for c2 in range(2):
                P.op("act", lambda e: e.activation(out=g_t[c2][:], in_=g_t[c2][:], func=AF.Sqrt, bias=1.0), reads=[b_t[c2]], writes=[b_t[c2]])
            for c2 in range(2):
                cc = 2 * n + c2
                P.op("dve", lambda e: e.tensor_tensor(out=g_i[c2][:], in0=g_i[c2][:], in1=g_t[c2][:], op=ALU.mult), reads=[b_i[c2], b_t[c2]], writes=[b_i[c2]])
                P.op("dve", lambda e: e.tensor_tensor_scan(out=g_h[c2][:], data0=g_a[c2][:], data1=g_i[c2][:], initial=hprev[:, cc:cc + 1], op0=ALU.mult, op1=ALU.add),
                     reads=[b_a[c2], b_i[c2], bhprev], writes=[b_h[c2]])
                P.op("dve", lambda e: e.tensor_copy(out=hprev[:, cc:cc + 1], in_=g_h[c2][:, CH - 1:CH]), reads=[b_h[c2]], writes=[bhprev])
            for c2 in range(2):
                ps, bps = pga[c2]
                P.op("act", lambda e: e.activation(out=g_r[c2][:], in_=ps[:, :], func=AF.Silu), reads=[bps], writes=[b_r[c2]])
            for c2 in range(2):
                cc = 2 * n + c2
                P.op("dve", lambda e: e.tensor_tensor(out=act3[:, cc, :], in0=g_h[c2][:], in1=g_r[c2][:], op=ALU.mult), reads=[b_h[c2], b_r[c2]], writes=[bact[cc]])
        if c == NCHUNK_RUN - 1:
            for src3, nrow, dst in ((hist3, 3, ncv_p), (hprev[:].unsqueeze(2), 1, nlr_p)):
                for hb in range(2):
                    ps, bps = rPS.next()
                    for c4 in range(4):
                        cc = hb * 4 + c4
                        P.op("pe", lambda e: e.transpose(out=ps[0:nrow, c4 * 128:(c4 + 1) * 128], in_=src3[:, cc, :], identity=identf),
                             reads=[bhist, bhprev, bcst], writes=[bps])
                    P.op("act", lambda e: e.copy(out=otr[0:nrow, hb * 512:(hb + 1) * 512], in_=ps[0:nrow, :]), reads=[bps], writes=[botr])
                P.dma("act", lambda e: e.dma_start(out=dst[:, :], in_=otr[0:nrow, :]), reads=[botr], is_out=True)
        if STOP == 'p3':
            continue
        for cc in range(8):
            w3, bw = load_fm(FM_MA + cc)
            ps, bps = rPS.next()
            for kc in range(8):
                P.op("pe", lambda e: e.matmul(ps[:, :], lhsT=w3[:, kc, :], rhs=xn3[:, kc, :], start=(kc == 0), stop=(kc == 7)),
                     reads=[bw, bxn], writes=[bps])
            sg, bsg = rSg.next()
            P.op("act", lambda e: e.activation(out=sg[:], in_=ps[:, :], func=AF.Sigmoid), reads=[bps], writes=[bsg])
            w3, bw = load_fm(FM_PA + cc)
            ps, bps = rPS.next()
            for kc in range(8):
                P.op("pe", lambda e: e.matmul(ps[:, :], lhsT=w3[:, kc, :], rhs=act3[:, kc, :], start=(kc == 0), stop=(kc == 7)),
                     reads=[bw, bact[kc]], writes=[bps])
            P.op("dve", lambda e: e.tensor_tensor(out=m3[:, cc, :], in0=ps[:, :], in1=sg[:], op=ALU.mult), reads=[bps, bsg], writes=[bmm[cc]])

        if STOP == 'p4':
            continue
        def stage_A(tt):
            i = c * 4 + tt
            nk = (i + 1) * 128
            dg, bdg = rDg.next()
            dg3 = dg[:].rearrange("p (h q) -> p h q", h=8)
            P.op("pool", lambda e: e.tensor_tensor(out=dg3, in0=identb[:].unsqueeze(1).to_broadcast([128, 8, 128]),
                                                   in1=sgS3[:, tt, :].unsqueeze(2).to_broadcast([128, 8, 128]), op=ALU.mult),
                 reads=[bidb, bwS[tt]], writes=[bdg])
            pendA = None

            def accA(kb, h, k0, cols, R, bR, pacc, bpacc):
                P.op("pe", lambda e: e.matmul(pacc[:, 0:cols], lhsT=dg3[:, h, :], rhs=R[:, 0:cols], start=(h == 0), stop=(h == 7)),
                     reads=[bdg, bR], writes=[bpacc])
                if h == 7:
                    P.op("act", lambda e: e.copy(out=sc[:, k0:k0 + cols], in_=pacc[:, 0:cols]), reads=[bpacc], writes=[bsc])

            for kb in range((nk + 511) // 512):
                k0 = kb * 512
                cols = min(512, nk - k0)
                pacc, bpacc = rPACC.next()
                for h in range(8):
                    hp, h2 = h // 2, h % 2
                    ps, bps = rPS.next()
                    P.op("pe", lambda e: e.matmul(ps[:, 0:cols], lhsT=qiT4[64 * h2:64 * h2 + 64, tt, hp, :], rhs=kiT[64 * h2:64 * h2 + 64, k0:k0 + cols],
                                                  start=True, stop=True), reads=[bqiT[tt]] + bKV[k0 // 128:(k0 + cols) // 128], writes=[bps])
                    R, bR = rR.next()
                    P.op("act", lambda e: e.activation(out=R[:, 0:cols], in_=ps[:, 0:cols], func=AF.Relu, scale=awS3[:, tt, h:h + 1]),
                         reads=[bps, bwS[tt]], writes=[bR])
                    if pendA is not None:
                        accA(*pendA)
                    pendA = (kb, h, k0, cols, R, bR, pacc, bpacc)
            accA(*pendA)
            P.op("dve", lambda e: e.tensor_tensor(out=sc[:, i * 128:nk], in0=sc[:, i * 128:nk], in1=caus, op=ALU.add), reads=[bsc, bcst], writes=[bsc])

        def stage_B(tt):
            i = c * 4 + tt
            nk = (i + 1) * 128
            MB, bMB = rMB.next()
            NB = N_BISECT
            if i >= 2:
                P.op("dve", lambda e: e.tensor_reduce(out=mid[:], in_=sc[:, 0:nk], axis=AX.X, op=ALU.max), reads=[bsc], writes=[bmid])
                P.op("dve", lambda e: e.tensor_reduce(out=lo[:], in_=sc[:, 0:i * 128], axis=AX.X, op=ALU.min), reads=[bsc], writes=[blo])
                P.op("dve", lambda e: e.tensor_tensor(out=wd[:], in0=mid[:], in1=lo[:], op=ALU.subtract), reads=[bmid, blo], writes=[bwd])
                P.op("dve", lambda e: e.tensor_scalar(out=Wt[:, 0:NB], in0=pow2, scalar1=wd[:], scalar2=None, op0=ALU.mult), reads=[bcst, bwd], writes=[bWt])
                P.op("dve", lambda e: e.tensor_scalar(out=W2t[:, 0:NB], in0=pow2, scalar1=wd[:], scalar2=2.0, op0=ALU.mult, op1=ALU.mult), reads=[bcst, bwd], writes=[bW2t])
                P.op("dve", lambda e: e.tensor_tensor(out=mid[:], in0=lo[:], in1=Wt[:, 0:1], op=ALU.add), reads=[blo, bWt], writes=[bmid])
                for k in range(NB):
                    P.op("dve", lambda e: e.tensor_scalar(out=MB[:, 0:nk], in0=sc[:, 0:nk], scalar1=mid[:], scalar2=None, op0=ALU.is_ge, op1=ALU.add,
                                                          accum_out=cnt[:], saturate=False), reads=[bsc, bmid], writes=[bMB, bcnt])
                    if k < NB - 1:
                        P.op("dve", lambda e: e.scalar_tensor_tensor(out=cond[:], in0=cnt[:], scalar=float(TOPK) - 0.5, in1=W2t[:, k + 1:k + 2], op0=ALU.is_ge, op1=ALU.mult),
                             reads=[bcnt, bW2t], writes=[bcond])
                        P.op("dve", lambda e: e.scalar_tensor_tensor(out=mid[:], in0=cond[:], scalar=Wt[:, k + 1:k + 2], in1=mid[:], op0=ALU.subtract, op1=ALU.add),
                             reads=[bcond, bWt, bmid], writes=[bmid])
                    else:
                        P.op("dve", lambda e: e.scalar_tensor_tensor(out=cond[:], in0=cnt[:], scalar=float(TOPK) - 0.5, in1=Wt[:, k:k + 1], op0=ALU.is_ge, op1=ALU.mult),
                             reads=[bcnt, bWt], writes=[bcond])
                        P.op("dve", lambda e: e.scalar_tensor_tensor(out=lo[:], in0=cond[:], scalar=Wt[:, k:k + 1], in1=mid[:], op0=ALU.subtract, op1=ALU.add),
                             reads=[bcond, bWt, bmid], writes=[blo])
                P.op("dve", lambda e: e.tensor_scalar(out=MB[:, 0:nk], in0=sc[:, 0:nk], scalar1=lo[:], scalar2=NEG_MB, op0=ALU.is_lt, op1=ALU.mult, saturate=False),
                     reads=[bsc, blo], writes=[bMB])
            else:
                P.op("dve", lambda e: e.tensor_scalar(out=MB[:, 0:nk], in0=sc[:, 0:nk], scalar1=-1e29, scalar2=NEG_MB, op0=ALU.is_lt, op1=ALU.mult, saturate=False),
                     reads=[bsc], writes=[bMB])
            return MB, bMB

        def stage_C(tt, MB, bMB):
            i = c * 4 + tt
            for g in range(2):
                def S_(j):
                    ps, bps = rPSS.next()
                    P.op("pe", lambda e: e.matmul(ps[:, :], lhsT=KT3[:, g, j * 128:(j + 1) * 128], rhs=qT4[:, tt, 4 * g:4 * g + 4, :],
                                                  start=True, stop=False), reads=[bKV[j], bqT[tt]], writes=[bps])
                    P.op("pe", lambda e: e.matmul(ps[:, :], lhsT=MB[:, j * 128:(j + 1) * 128], rhs=ident4[:], start=False, stop=True),
                         reads=[bMB, bid4], writes=[bps])
                    return ps, bps
                nxt = S_(0)
                for j in range(i + 1):
                    ps, bps = nxt
                    if j + 1 <= i:
                        nxt = S_(j + 1)
                    PT, bPT = rPT.next()
                    P.op("act", lambda e: e.activation(out=PT[:], in_=ps[:, :], func=AF.Exp, scale=ATT_SCALE), reads=[bps], writes=[bPT])
                    P.op("pe", lambda e: e.matmul(psO[:, :], lhsT=V4[:, j, g, :], rhs=PT[:], start=(j == 0), stop=(j == i)), reads=[bKV[j], bPT], writes=[bpsO])
                    P.op("pe", lambda e: e.matmul(psL[:, :], lhsT=onesb[:], rhs=PT[:], start=(j == 0), stop=(j == i)), reads=[bonesb, bPT], writes=[bpsL])
                bo = [bact[4 * g + hh] for hh in range(4)]
                P.op("act", lambda e: e.activation(out=rl[:], in_=psL[:, :], func=AF.Ln), reads=[bpsL], writes=[brl])
                P.op("act", lambda e: e.activation(out=rl[:], in_=rl[:], func=AF.Exp, scale=-1.0), reads=[brl], writes=[brl])
                P.op("act", lambda e: e.copy(out=act3[:, 4 * g:4 * g + 4, tt * 128:(tt + 1) * 128], in_=psO[:, :].rearrange("p (h q) -> p h q", h=4)),
                     reads=[bpsO], writes=bo)
                P.op("pool", lambda e: e.tensor_tensor(out=act3[:, 4 * g:4 * g + 4, tt * 128:(tt + 1) * 128], in0=act3[:, 4 * g:4 * g + 4, tt * 128:(tt + 1) * 128],
                                                       in1=rl[:].rearrange("p (h q) -> p h q", h=4), op=ALU.mult),
                     reads=bo + [brl], writes=bo)

        pend = None
        for tt in range(4):
            stage_A(tt)
            mb = stage_B(tt)
            if pend is not None:
                stage_C(*pend)
            pend = (tt, mb[0], mb[1])
        stage_C(*pend)
        for cc in range(8):
            w3, bw = load_fm(FM_GB + cc)
            ps, bps = rPS.next()
            for kc in range(8):
                P.op("pe", lambda e: e.matmul(ps[:, :], lhsT=w3[:, kc, :], rhs=xn3[:, kc, :], start=(kc == 0), stop=(kc == 7)),
                     reads=[bw, bxn], writes=[bps])
            sg, bsg = rSg.next()
            P.op("act", lambda e: e.activation(out=sg[:], in_=ps[:, :], func=AF.Silu), reads=[bps], writes=[bsg])
            P.op("pool", lambda e: e.tensor_tensor(out=act3[:, cc, :], in0=act3[:, cc, :], in1=sg[:], op=ALU.mult), reads=[bact[cc], bsg], writes=[bact[cc]])
        if STOP == 'p5':
            continue
        for cc in range(8):
            w3, bw = load_fm(FM_MB + cc)
            ps, bps = rPS.next()
            for kc in range(8):
                P.op("pe", lambda e: e.matmul(ps[:, :], lhsT=w3[:, kc, :], rhs=xn3[:, kc, :], start=(kc == 0), stop=(kc == 7)),
                     reads=[bw, bxn], writes=[bps])
            sg, bsg = rSg.next()
            P.op("act", lambda e: e.activation(out=sg[:], in_=ps[:, :], func=AF.Sigmoid), reads=[bps], writes=[bsg])
            w3, bw = load_fm(FM_PB + cc)
            ps, bps = rPS.next()
            for kc in range(8):
                P.op("pe", lambda e: e.matmul(ps[:, :], lhsT=w3[:, kc, :], rhs=act3[:, kc, :], start=(kc == 0), stop=(kc == 7)),
                     reads=[bw, bact[kc]], writes=[bps])
            P.op("dve", lambda e: e.tensor_tensor(out=sg[:], in0=ps[:, :], in1=sg[:], op=ALU.mult), reads=[bps, bsg], writes=[bsg])
            P.op("pool", lambda e: e.tensor_tensor(out=m3[:, cc, :], in0=m3[:, cc, :], in1=sg[:], op=ALU.add), reads=[bmm[cc], bsg], writes=[bmm[cc]])
        if STOP == 'p6':
            continue
        wo0, bwo0 = load_tm(TM_O0)
        wo1, bwo1 = load_tm(TM_O1)
        for tt in range(4):
            r0 = t0 + tt * 128
            xt, bxt = rX.next()
            P.dma("sp", lambda e: e.dma_start(out=xt[:], in_=x_p[r0:r0 + 128, :]), writes=[bxt])
            for hb, (wo, bwo) in enumerate(((wo0, bwo0), (wo1, bwo1))):
                ps, bps = rPS.next()
                for kc in range(8):
                    P.op("pe", lambda e: e.matmul(ps[:, :], lhsT=m3[:, kc, tt * 128:(tt + 1) * 128], rhs=wo[:, kc, :], start=(kc == 0), stop=(kc == 7)),
                         reads=[bmm[kc], bwo], writes=[bps])
                P.op("dve", lambda e: e.tensor_tensor(out=hres[:, hb * 512:(hb + 1) * 512], in0=ps[:, :], in1=gate_bc[:, hb * 512:(hb + 1) * 512], op=ALU.mult),
                     reads=[bps, bgate], writes=[bhres])
            P.op("pool", lambda e: e.tensor_tensor(out=hres[:], in0=hres[:], in1=xt[:], op=ALU.add), reads=[bhres, bxt], writes=[bhres])
            P.op("act", lambda e: e.activation(out=junkb[:], in_=hres[:], func=AF.Square, accum_out=sm[:, 10:11]), reads=[bhres], writes=[bxcb[0], bxcb[1], bsm])
            P.op("act", lambda e: e.activation(out=sm[:, 11:12], in_=sm[:, 10:11], func=AF.Sqrt, scale=1.0 / D, bias=EPS), reads=[bsm], writes=[bsm])
            P.op("dve", lambda e: e.reciprocal(out=sm[:, 12:13], in_=sm[:, 11:12]), reads=[bsm], writes=[bsm])
            P.op("dve", lambda e: e.scalar_tensor_tensor(out=yo[:], in0=hres[:], scalar=sm[:, 12:13], in1=gfin_bc[:], op0=ALU.mult, op1=ALU.mult),
                 reads=[bhres, bsm, bgfin], writes=[byo])
            P.dma("act", lambda e: e.dma_start(out=y_p[r0:r0 + 128, :], in_=yo[:]), reads=[byo], is_out=True)


def sample_phase(nc, P, G):
    sS = G["sS"]
    sb = lambda n, shp, d=F32: G["sb"](n, shp, d, sS)
    x_s = G["x_s"]; st_conv = G["st_conv"]; st_lru = G["st_lru"]; ptab = G["ptab"]
    cache_k = G["cache_k"]; cache_v = G["cache_v"]; cache_i = G["cache_i"]; ropes = G["ropes"]
    y_s = G["y_s"]; nk_s = G["nk_s"]; nv_s = G["nv_s"]; nki_s = G["nki_s"]; ncv_s = G["ncv_s"]; nlr_s = G["nlr_s"]
    rPS = G["rPS"]; rPST = G["rPST"]
    psA = G["psA"]; bpsA = G["bpsA"]; psB = G["psB"]; bpsB = G["bpsB"]
    psS0 = G["psS0"]; bpsS0 = G["bpsS0"]; psS1 = G["psS1"]; bpsS1 = G["bpsS1"]
    psO = G["psO"]; bpsO = G["bpsO"]; psL = G["psL"]; bpsL = G["bpsL"]
    cstt = G["cstt"]; bcst = G["bcst"]; chpt = G["chpt"]; bchp = G["bchp"]
    identb = G["identb"]; bidb = G["bidb"]; identf = G["identf"]
    onesb = G["onesb"]; bonesb = G["bonesb"]; onesf = G["onesf"]; bonesf = G["bonesf"]
    clam = G["clam"]; bclam = G["bclam"]; gfin_bc = G["gfin_bc"]; bgfin = G["bgfin"]
    idxgb_bc = G["idxgb_bc"]; bidxgb = G["bidxgb"]; wrr = G["wrr"]; bwrr = G["bwrr"]
    mT = G["mT"]; bmT = G["bmT"]; gate_s = G["gate_s"]; bgs = G["bgs"]
    load_fm = G["load_fm"]; load_tm = G["load_tm"]; rope = G["rope"]
    T = NS
    I4 = identf[0:T, 0:T]
    mT3 = mT[:].rearrange("p (c t) -> p c t", t=5)
    wrr5 = wrr[:].rearrange("p (a n k c) -> p a n k c", a=2, n=4, k=2)
    Jf = cstt[:, CS_JF:CS_JF + 256]; iota_r = cstt[:, CS_IR:CS_IR + 128]; Tstrict = cstt[:, CS_TS:CS_TS + 128]

    def bc84(col0):
        return chpt[:, col0:col0 + 8].unsqueeze(2).to_broadcast([128, 8, T])

    def tt(ek, out, a, b, op, reads, writes):
        return P.op(ek, lambda e: e.tensor_tensor(out=out, in0=a, in1=b, op=op), reads=reads, writes=writes)

    def fm_to_tm(src3, ncc, nrow_in_free, dst_tile, bdst, bsrc):
        for hb in range(2):
            ps, bps = rPS.next()
            for c4 in range(4):
                cc = hb * 4 + c4
                P.op("pe", lambda e: e.transpose(out=ps[0:nrow_in_free, c4 * 128:(c4 + 1) * 128], in_=src3[:, cc, :], identity=identf),
                     reads=[bsrc, bcst], writes=[bps])
            P.op("act", lambda e: e.copy(out=dst_tile[0:nrow_in_free, hb * 512:(hb + 1) * 512], in_=ps[0:nrow_in_free, :]), reads=[bps], writes=[bdst])

    def tm_to_fm(src_tile, nrow, dst_ps, bps, bsrc, col_of):
        for kc in range(8):
            c0 = col_of(kc)
            P.op("pe", lambda e: e.transpose(out=dst_ps[:, c0:c0 + nrow], in_=src_tile[0:nrow, kc * 128:(kc + 1) * 128], identity=identf[0:nrow, 0:nrow]),
                 reads=[bsrc, bcst], writes=[bps])

    xs = sb("xs", [T, D]); bxs = Buf()
    P.dma("sp", lambda e: e.dma_start(out=xs[:], in_=x_s[:, :]), writes=[bxs])
    tm_to_fm(xs, T, psA, bpsA, bxs, lambda kc: kc * T)
    xsT = sb("xsT", [128, 8 * T]); bxsT = Buf(); xsT3 = xsT[:].rearrange("p (k t) -> p k t", t=T)
    P.op("dve", lambda e: e.tensor_copy(out=xsT[:], in_=psA[:, 0:8 * T]), reads=[bpsA], writes=[bxsT])
    tmpA = sb("tmpA", [128, 8 * T]); btA = Buf(); tmpA3 = tmpA[:].rearrange("p (k t) -> p k t", t=T)
    tmpB = sb("tmpB", [128, 8 * T]); btB = Buf(); tmpB3 = tmpB[:].rearrange("p (k t) -> p k t", t=T)
    tt("dve", tmpA[:], xsT[:], xsT[:], ALU.mult, [bxsT], [btA])
    for kc in range(8):
        P.op("pe", lambda e: e.matmul(psB[:, 0:T], lhsT=onesf[:], rhs=tmpA[:, kc * T:(kc + 1) * T], start=(kc == 0), stop=(kc == 7)),
             reads=[bonesf, btA], writes=[bpsB])
    rstd = sb("rstd_s", [128, T]); brstd = Buf()
    P.op("act", lambda e: e.activation(out=rstd[:], in_=psB[:, 0:T], func=AF.Sqrt, scale=1.0 / D, bias=EPS), reads=[bpsB], writes=[brstd])
    P.op("dve", lambda e: e.reciprocal(out=rstd[:], in_=rstd[:]), reads=[brstd], writes=[brstd])
    P.op("dve", lambda e: e.scalar_tensor_tensor(out=tmpB3, in0=mT3[:, 8:16, 1:5], scalar=1.0, in1=bc84(CP_GN), op0=ALU.add, op1=ALU.mult),
         reads=[bmT, bchp], writes=[btB])
    tt("dve", tmpA3, xsT3, rstd[:].unsqueeze(1).to_broadcast([128, 8, T]), ALU.mult, [bxsT, brstd], [btA])
    tt("dve", tmpA3, tmpA3, tmpB3, ALU.mult, [btA, btB], [btA])
    xnsT = sb("xnsT", [128, 8 * T], BF16); bxns = Buf(); xns3 = xnsT[:].rearrange("p (k t) -> p k t", t=T)
    tt("dve", xns3, tmpA3, mT3[:, 0:8, 1:5], ALU.add, [btA, bmT], [bxns])

    ztm = sb("ztm", [T, 2120]); bztm = Buf()
    off = 0
    for blk in (TM_Q0, TM_Q1, TM_KV, TM_QI, TM_KW):
        w3, bw = load_tm(blk)
        ncol = 72 if blk == TM_KW else 512
        ps, bps = rPS.next()
        for kc in range(8):
            P.op("pe", lambda e: e.matmul(ps[0:T, 0:ncol], lhsT=xns3[:, kc, :], rhs=w3[:, kc, 0:ncol], start=(kc == 0), stop=(kc == 7)),
                 reads=[bxns, bw], writes=[bps])
        P.op("act", lambda e: e.copy(out=ztm[:, off:off + ncol], in_=ps[0:T, 0:ncol]), reads=[bps], writes=[bztm])
        off += ncol
    Q0, K0, V0, QI0, KI0, WI0 = 0, 1024, 1280, 1536, 2048, 2112
    for idx in range(40):
        w3, bw = load_fm(idx)
        for kc in range(8):
            P.op("pe", lambda e: e.matmul(psO[:, idx * T:(idx + 1) * T], lhsT=w3[:, kc, :], rhs=xns3[:, kc, :], start=(kc == 0), stop=(kc == 7)),
                 reads=[bw, bxns], writes=[bpsO])
    zfm = sb("zfm", [128, 40 * T]); bzfm = Buf(); zfm3 = zfm[:].rearrange("p (c t) -> p c t", t=T)
    P.op("dve", lambda e: e.tensor_copy(out=zfm[:], in_=psO[:, 0:40 * T]), reads=[bpsO], writes=[bzfm])

    stc = sb("stc", [T * 3, D]); bstc = Buf()
    P.dma("sp", lambda e: e.dma_start(out=stc[:], in_=st_conv[:, :]), writes=[bstc])
    for t in range(T):
        P.dma("act", lambda e: e.dma_start(out=ncv_s[t, 0:2, :], in_=stc[t * 3 + 1:t * 3 + 3, :]), reads=[bstc], is_out=True)
    tm_to_fm(stc, T * 3, psL, bpsL, bstc, lambda kc: kc * 12)
    stT = sb("stT", [128, 96]); bstT = Buf(); stT4 = stT[:].rearrange("p (c t j) -> p c t j", c=8, t=T)
    P.op("dve", lambda e: e.tensor_copy(out=stT[:], in_=psL[:, 0:96]), reads=[bpsL], writes=[bstT])
    rowt = sb("rowt", [T, D]); browt = Buf()
    fm_to_tm(zfm3[:, 0:8, :], 8, T, rowt, browt, bzfm)
    P.dma("act", lambda e: e.dma_start(out=ncv_s[:, 2, :], in_=rowt[:]), reads=[browt], is_out=True)
    xcs = sb("xcs", [128, 8 * T]); bxcs = Buf(); xcs3 = xcs[:].rearrange("p (k t) -> p k t", t=T)
    tt("dve", xcs3, zfm3[:, 0:8, :], bc84(CP_WCONV + 24), ALU.mult, [bzfm, bchp], [bxcs])
    tt("dve", xcs3, xcs3, bc84(CP_BCONV), ALU.add, [bxcs, bchp], [bxcs])
    for j in range(3):
        tt("dve", tmpA3, stT4[:, :, :, j], bc84(CP_WCONV + 8 * j), ALU.mult, [bstT, bchp], [btA])
        tt("dve", xcs3, xcs3, tmpA3, ALU.add, [bxcs, btA], [bxcs])
    xcsb = sb("xcsb", [128, 8 * T], BF16); bxcsb = Buf(); xcsb3 = xcsb[:].rearrange("p (k t) -> p k t", t=T)
    P.op("dve", lambda e: e.tensor_copy(out=xcsb[:], in_=xcs[:]), reads=[bxcs], writes=[bxcsb])
    for which in range(2):
        for cc in range(8):
            n, c2 = cc // 2, cc % 2
            c0 = (which * 8 + cc) * T
            for k2 in range(2):
                P.op("pe", lambda e: e.matmul(psB[:, c0:c0 + T], lhsT=wrr5[:, which, n, k2, c2 * 128:(c2 + 1) * 128], rhs=xcsb3[:, 2 * n + k2, :],
                                              start=(k2 == 0), stop=(k2 == 1)), reads=[bwrr, bxcsb], writes=[bpsB])
    r_s = sb("r_s", [128, 8 * T]); br_s = Buf(); r_s3 = r_s[:].rearrange("p (k t) -> p k t", t=T)
    i_s = sb("i_s", [128, 8 * T]); bi_s = Buf(); i_s3 = i_s[:].rearrange("p (k t) -> p k t", t=T)
    a_s = sb("a_s", [128, 8 * T]); ba_s = Buf(); a_s3 = a_s[:].rearrange("p (k t) -> p k t", t=T)
    tt("dve", r_s3, psB[:, 0:8 * T].rearrange("p (k t) -> p k t", t=T), bc84(CP_BRA), ALU.add, [bpsB, bchp], [br_s])
    tt("dve", i_s3, psB[:, 8 * T:16 * T].rearrange("p (k t) -> p k t", t=T), bc84(CP_BRX), ALU.add, [bpsB, bchp], [bi_s])
    P.op("act", lambda e: e.activation(out=r_s[:], in_=r_s[:], func=AF.Sigmoid), reads=[br_s], writes=[br_s])
    P.op("act", lambda e: e.activation(out=i_s[:], in_=i_s[:], func=AF.Sigmoid), reads=[bi_s], writes=[bi_s])
    tt("dve", a_s3, r_s3, clam[:].unsqueeze(2).to_broadcast([128, 8, T]), ALU.mult, [br_s, bclam], [ba_s])
    P.op("act", lambda e: e.activation(out=a_s[:], in_=a_s[:], func=AF.Exp), reads=[ba_s], writes=[ba_s])
    tt("dve", tmpA[:], a_s[:], a_s[:], ALU.mult, [ba_s], [btA])
    P.op("dve", lambda e: e.tensor_scalar(out=tmpA[:], in0=tmpA[:], scalar1=-1.0, scalar2=1.0, op0=ALU.mult, op1=ALU.add), reads=[btA], writes=[btA])
    P.op("act", lambda e: e.activation(out=tmpA[:], in_=tmpA[:], func=AF.Sqrt), reads=[btA], writes=[btA])
    tt("dve", tmpB[:], i_s[:], xcs[:], ALU.mult, [bi_s, bxcs], [btB])
    tt("dve", tmpB[:], tmpB[:], tmpA[:], ALU.mult, [btB, btA], [btB])
    hst = sb("hst", [T, D]); bhst = Buf()
    P.dma("sp", lambda e: e.dma_start(out=hst[:], in_=st_lru[:, :]), writes=[bhst])
    tm_to_fm(hst, T, psA, bpsA, bhst, lambda kc: kc * T)
    h_s = sb("h_s", [128, 8 * T]); bh_s = Buf(); h_s3 = h_s[:].rearrange("p (k t) -> p k t", t=T)
    tt("dve", h_s[:], psA[:, 0:8 * T], a_s[:], ALU.mult, [bpsA, ba_s], [bh_s])
    tt("dve", h_s[:], h_s[:], tmpB[:], ALU.add, [bh_s, btB], [bh_s])
    fm_to_tm(h_s3, 8, T, rowt, browt, bh_s)
    P.dma("act", lambda e: e.dma_start(out=nlr_s[:, :], in_=rowt[:]), reads=[browt], is_out=True)
    acta = sb("acta_s", [128, 8 * T], BF16); bacta = Buf(); acta3 = acta[:].rearrange("p (k t) -> p k t", t=T)
    P.op("act", lambda e: e.activation(out=tmpA3, in_=zfm3[:, 8:16, :], func=AF.Silu), reads=[bzfm], writes=[btA])
    tt("dve", acta[:], h_s[:], tmpA[:], ALU.mult, [bh_s, btA], [bacta])
    for cc in range(8):
        w3, bw = load_fm(FM_PA + cc)
        for kc in range(8):
            P.op("pe", lambda e: e.matmul(psB[:, cc * T:(cc + 1) * T], lhsT=w3[:, kc, :], rhs=acta3[:, kc, :], start=(kc == 0), stop=(kc == 7)),
                 reads=[bw, bacta], writes=[bpsB])
    m_s = sb("m_s", [128, 8 * T]); bm_s = Buf(); m_s3 = m_s[:].rearrange("p (k t) -> p k t", t=T)
    P.op("act", lambda e: e.activation(out=tmpA3, in_=zfm3[:, 24:32, :], func=AF.Sigmoid), reads=[bzfm], writes=[btA])
    tt("dve", m_s[:], psB[:, 0:8 * T], tmpA[:], ALU.mult, [bpsB, btA], [bm_s])

    rs = sb("ropes_t", [T, 384]); brs = Buf()
    P.dma("sp", lambda e: e.dma_start(out=rs[:], in_=ropes[:, :]), writes=[brs])
    t1 = sb("s_t1", [T, 1024]); bt1 = Buf()
    t2 = sb("s_t2", [T, 1024]); bt2 = Buf()
    q_r = sb("q_r", [T, 1024]); bq_r = Buf()
    k_r = sb("k_r", [T, 256]); bk_r = Buf()
    qi_r = sb("qi_r", [T, 512]); bqi_r = Buf()
    ki_r = sb("ki_r", [T, 64]); bki_r = Buf()
    sm = sb("s_sm", [T, 16]); bsm = Buf()
    v3 = lambda ap, h: ap.rearrange("p (h d) -> p h d", h=h)
    rope("dve", v3(q_r[:], 8), v3(ztm[:, Q0:Q0 + 1024], 8), rs[:, 0:128], rs[:, 128:256], 8, 128, v3(t1[:], 8), v3(t2[:], 8), [bztm, brs], [bq_r], bt1, bt2)
    rope("dve", v3(k_r[:], 2), v3(ztm[:, K0:K0 + 256], 2), rs[:, 0:128], rs[:, 128:256], 2, 128, v3(t1[:, 0:256], 2), v3(t2[:, 0:256], 2), [bztm, brs], [bk_r], bt1, bt2)
    P.dma("act", lambda e: e.dma_start(out=nk_s[:, :], in_=k_r[:]), reads=[bk_r], is_out=True)
    P.dma("act", lambda e: e.dma_start(out=nv_s[:, :], in_=ztm[:, V0:V0 + 256]), reads=[bztm], is_out=True)
    rope("dve", v3(qi_r[:], 8), v3(ztm[:, QI0:QI0 + 512], 8), rs[:, 256:320], rs[:, 320:384], 8, 64, v3(t1[:, 0:512], 8), v3(t2[:, 0:512], 8), [bztm, brs], [bqi_r], bt1, bt2)
    P.op("dve", lambda e: e.tensor_reduce(out=sm[:, 0:1], in_=ztm[:, KI0:KI0 + 64], axis=AX.X, op=ALU.add), reads=[bztm], writes=[bsm])
    P.op("dve", lambda e: e.tensor_scalar(out=sm[:, 1:2], in0=sm[:, 0:1], scalar1=-1.0 / 64, scalar2=None, op0=ALU.mult), reads=[bsm], writes=[bsm])
    P.op("dve", lambda e: e.tensor_scalar(out=t1[:, 0:64], in0=ztm[:, KI0:KI0 + 64], scalar1=sm[:, 1:2], scalar2=None, op0=ALU.add), reads=[bztm, bsm], writes=[bt1])
    P.op("act", lambda e: e.activation(out=t2[:, 0:64], in_=t1[:, 0:64], func=AF.Square, accum_out=sm[:, 2:3]), reads=[bt1], writes=[bt2, bsm])
    P.op("act", lambda e: e.activation(out=sm[:, 3:4], in_=sm[:, 2:3], func=AF.Sqrt, scale=1.0 / 64, bias=EPS), reads=[bsm], writes=[bsm])
    P.op("dve", lambda e: e.reciprocal(out=sm[:, 4:5], in_=sm[:, 3:4]), reads=[bsm], writes=[bsm])
    P.op("dve", lambda e: e.scalar_tensor_tensor(out=t1[:, 64:128], in0=t1[:, 0:64], scalar=sm[:, 4:5], in1=idxgb_bc[0:T, 0:64], op0=ALU.mult, op1=ALU.mult),
         reads=[bt1, bsm, bidxgb], writes=[bt1])
    tt("dve", t1[:, 128:192], t1[:, 64:128], idxgb_bc[0:T, 64:128], ALU.add, [bt1, bidxgb], [bt1])
    rope("dve", ki_r[:].unsqueeze(1), t1[:, 128:192].unsqueeze(1), rs[:, 256:320], rs[:, 320:384], 1, 64,
         t2[:, 64:128].unsqueeze(1), t2[:, 128:192].unsqueeze(1), [bt1, brs], [bki_r], bt2, bt2)
    P.dma("act", lambda e: e.dma_start(out=nki_s[:, :], in_=ki_r[:]), reads=[bki_r], is_out=True)
    w_s = sb("w_s", [T, 8]); bw_s = Buf()
    P.op("dve", lambda e: e.tensor_scalar(out=w_s[:], in0=ztm[:, WI0:WI0 + 8], scalar1=IDX_W_SCALE, scalar2=None, op0=ALU.mult), reads=[bztm], writes=[bw_s])
    s8 = sb("s8", [T, 8]); bs8 = Buf()
    selfsc = sb("selfsc", [T, 1]); bselfsc = Buf()
    sl = sb("sl", [T, 8]); bsl = Buf()
    tt("dve", v3(t1[:, 0:512], 8), v3(qi_r[:], 8), ki_r[:].unsqueeze(1).to_broadcast([T, 8, 64]), ALU.mult, [bqi_r, bki_r], [bt1])
    P.op("dve", lambda e: e.tensor_reduce(out=s8[:], in_=v3(t1[:, 0:512], 8), axis=AX.X, op=ALU.add), reads=[bt1], writes=[bs8])
    P.op("dve", lambda e: e.scalar_tensor_tensor(out=s8[:], in0=s8[:], scalar=0.0, in1=w_s[:], op0=ALU.max, op1=ALU.mult), reads=[bs8, bw_s], writes=[bs8])
    P.op("dve", lambda e: e.tensor_reduce(out=selfsc[:], in_=s8[:], axis=AX.X, op=ALU.add), reads=[bs8], writes=[bselfsc])
    tt("dve", t1[:].rearrange("p (g l d) -> p g l d", g=2, l=4), q_r[:].rearrange("p (g l d) -> p g l d", g=2, l=4),
       k_r[:].rearrange("p (g d) -> p g d", g=2).unsqueeze(2).to_broadcast([T, 2, 4, 128]), ALU.mult, [bq_r, bk_r], [bt1])
    P.op("dve", lambda e: e.tensor_reduce(out=sl[:], in_=v3(t1[:], 8), axis=AX.X, op=ALU.add), reads=[bt1], writes=[bsl])
    q_b = sb("q_b", [T, 1024], BF16); bq_b = Buf()
    P.op("dve", lambda e: e.tensor_copy(out=q_b[:], in_=q_r[:]), reads=[bq_r], writes=[bq_b])
    pt, bpt = rPST.next()
    for h in range(8):
        P.op("pe", lambda e: e.transpose(out=pt[:, h * T:(h + 1) * T], in_=q_b[:, h * 128:(h + 1) * 128], identity=identb[0:T, 0:T]),
             reads=[bq_b, bidb], writes=[bpt])
    qsT = sb("qsT", [128, 8 * T], BF16); bqsT = Buf(); qsT3 = qsT[:].rearrange("p (h t) -> p h t", t=T)
    P.op("dve", lambda e: e.tensor_copy(out=qsT[:], in_=pt[:, 0:8 * T]), reads=[bpt], writes=[bqsT])
    qpad = sb("qpad", [T, 2 * 8 * 128], BF16); bqpad = Buf(); qpad4 = qpad[:].rearrange("p (a h d) -> p a h d", a=2, h=8)
    P.op("pool", lambda e: e.memset(qpad[:], 0.0), writes=[bqpad])
    P.op("dve", lambda e: e.tensor_copy(out=qpad4[:, 0, :, 0:64], in_=v3(qi_r[:], 8)), reads=[bqi_r], writes=[bqpad])
    P.op("dve", lambda e: e.tensor_copy(out=qpad4[:, 1, :, 64:128], in_=v3(qi_r[:], 8)), reads=[bqi_r], writes=[bqpad])
    pt, bpt = rPST.next()
    for a in range(2):
        for h in range(8):
            c0 = (a * 8 + h) * T
            P.op("pe", lambda e: e.transpose(out=pt[:, c0:c0 + T], in_=qpad4[:, a, h, :], identity=identb[0:T, 0:T]), reads=[bqpad, bidb], writes=[bpt])
    qiz = sb("qiz", [128, T * 16], BF16); bqiz = Buf(); qiz3 = qiz[:].rearrange("p (t a) -> p t a", t=T)
    P.op("dve", lambda e: e.tensor_copy(out=qiz3, in_=pt[:, 0:16 * T].rearrange("p (a t) -> p t a", t=T)), reads=[bpt], writes=[bqiz])
    W4 = sb("W4", [T, T * 8 + T]); bW4 = Buf()
    tt("dve", W4[:, 0:T * 8].rearrange("p (t h) -> p t h", t=T), w_s[:].unsqueeze(1).to_broadcast([T, T, 8]), I4.unsqueeze(2).to_broadcast([T, T, 8]), ALU.mult,
       [bw_s, bcst], [bW4])
    P.op("dve", lambda e: e.tensor_scalar(out=W4[:, T * 8:T * 9], in0=I4, scalar1=selfsc[:], scalar2=None, op0=ALU.mult), reads=[bcst, bselfsc], writes=[bW4])
    P.op("pe", lambda e: e.matmul(psB[:, 0:T * 9], lhsT=onesf[0:T, :], rhs=W4[:], start=True, stop=True), reads=[bonesf, bW4], writes=[bpsB])
    wbcS = sb("wbcS", [128, T * 9]); bwbcS = Buf()
    P.op("dve", lambda e: e.tensor_copy(out=wbcS[:], in_=psB[:, 0:T * 9]), reads=[bpsB], writes=[bwbcS])
    selfb = wbcS[:, T * 8:T * 9]
    pti = sb("pti", [128, T], I32); bpti = Buf()
    ptf = sb("ptf", [128, T]); bptf = Buf()
    P.dma("sp", lambda e: e.dma_start(out=pti[:], in_=ptab[:, :]), writes=[bpti])
    P.op("dve", lambda e: e.tensor_copy(out=ptf[:], in_=pti[:]), reads=[bpti], writes=[bptf])

    Gt = sb("Gt", [128, 8192]); bGt = Buf()
    kTs = sb("kTs", [128, 64 * 128], BF16); bkTs = Buf(); kTs3 = kTs[:].rearrange("p (r q) -> p r q", r=64)
    scs = sb("scs", [128, T * 128]); bscs = Buf(); scs3 = scs[:].rearrange("p (t r) -> p t r", t=T)
    tmpS = sb("tmpS", [128, 512]); btS = Buf()
    for t in range(T):
        P.dma("pool", lambda e: e.indirect_dma_start(out=Gt[:], out_offset=None, in_=cache_i[:, :],
                                                     in_offset=bass.IndirectOffsetOnAxis(ap=pti[:, t:t + 1], axis=0)), reads=[bpti], writes=[bGt])
        for r4 in range(16):
            ps, bps = rPS.next()
            for q in range(4):
                rp = r4 * 4 + q
                P.op("pe", lambda e: e.transpose(out=ps[:, q * 128:(q + 1) * 128], in_=Gt[:, rp * 128:(rp + 1) * 128], identity=identf), reads=[bGt, bcst], writes=[bps])
            if r4 % 2 == 0:
                P.op("act", lambda e: e.copy(out=kTs[:, r4 * 512:(r4 + 1) * 512], in_=ps[:, :]), reads=[bps], writes=[bkTs])
            else:
                P.op("dve", lambda e: e.tensor_copy(out=kTs[:, r4 * 512:(r4 + 1) * 512], in_=ps[:, :]), reads=[bps], writes=[bkTs])
        for half, (psX, bpsX) in enumerate(((psS0, bpsS0), (psS1, bpsS1))):
            for rr in range(64):
                r = half * 64 + rr
                rp, r2 = r // 2, r % 2
                P.op("pe", lambda e: e.matmul(psX[:, rr * 8:(rr + 1) * 8], lhsT=kTs3[:, rp, :], rhs=qiz3[:, t, r2 * 8:(r2 + 1) * 8], start=True, stop=True),
                     reads=[bkTs, bqiz], writes=[bpsX])
            P.op("dve", lambda e: e.scalar_tensor_tensor(out=tmpS[:].rearrange("p (r h) -> p r h", h=8), in0=psX[:, :].rearrange("p (r h) -> p r h", h=8), scalar=0.0,
                                                         in1=wbcS[:, t * 8:(t + 1) * 8].unsqueeze(1).to_broadcast([128, 64, 8]), op0=ALU.max, op1=ALU.mult),
                 reads=[bpsX, bwbcS], writes=[btS])
            P.op("dve", lambda e: e.tensor_reduce(out=scs3[:, t, half * 64:(half + 1) * 64], in_=tmpS[:].rearrange("p (r h) -> p r h", h=8), axis=AX.X, op=ALU.add),
                 reads=[btS], writes=[bscs])

    mx = sb("b_mx", [128, 2 * T]); bmx = Buf()
    P.op("dve", lambda e: e.tensor_reduce(out=mx[:, 0:T], in_=scs3, axis=AX.X, op=ALU.max), reads=[bscs], writes=[bmx])
    P.op("dve", lambda e: e.tensor_reduce(out=mx[:, T:2 * T], in_=scs3, axis=AX.X, op=ALU.min), reads=[bscs], writes=[bmx])
    ps, bps = rPS.next()
    P.op("pe", lambda e: e.transpose(out=ps[0:2 * T, 0:128], in_=mx[:], identity=identf), reads=[bmx, bcst], writes=[bps])
    hl = sb("b_hl", [2 * T, 2]); bhl = Buf()
    P.op("dve", lambda e: e.tensor_reduce(out=hl[:, 0:1], in_=ps[0:2 * T, 0:128], axis=AX.X, op=ALU.max), reads=[bps], writes=[bhl])
    P.op("dve", lambda e: e.tensor_reduce(out=hl[:, 1:2], in_=ps[0:2 * T, 0:128], axis=AX.X, op=ALU.min), reads=[bps], writes=[bhl])
    HL = sb("b_HL", [2 * T, 2 * T]); bHL = Buf()
    P.op("pool", lambda e: e.memset(HL[:], 0.0), writes=[bHL])
    P.op("dve", lambda e: e.tensor_scalar(out=HL[0:T, 0:T], in0=I4, scalar1=hl[0:T, 0:1], scalar2=None, op0=ALU.mult), reads=[bcst, bhl], writes=[bHL])
    ps2, bps2 = rPS.next()
    P.op("pe", lambda e: e.matmul(ps2[:, 0:T], lhsT=onesf[0:T, :], rhs=HL[0:T, 0:T], start=True, stop=True), reads=[bonesf, bHL], writes=[bps2])
    hib = sb("b_hib", [128, T]); bhib = Buf()
    tt("dve", hib[:], ps2[:, 0:T], selfb, ALU.max, [bps2, bwbcS], [bhib])
    ps, bps = rPS.next()
    P.op("pe", lambda e: e.transpose(out=ps[0:T, 0:128], in_=mx[:, T:2 * T], identity=identf), reads=[bmx, bcst], writes=[bps])
    P.op("dve", lambda e: e.tensor_reduce(out=hl[0:T, 1:2], in_=ps[0:T, 0:128], axis=AX.X, op=ALU.min), reads=[bps], writes=[bhl])
    P.op("dve", lambda e: e.tensor_scalar(out=HL[0:T, T:2 * T], in0=I4, scalar1=hl[0:T, 1:2], scalar2=None, op0=ALU.mult), reads=[bcst, bhl], writes=[bHL])
    ps2, bps2 = rPS.next()
    P.op("pe", lambda e: e.matmul(ps2[:, 0:T], lhsT=onesf[0:T, :], rhs=HL[0:T, T:2 * T], start=True, stop=True), reads=[bonesf, bHL], writes=[bps2])
    lob = sb("b_lob", [128, T]); blob = Buf()
    tt("dve", lob[:], ps2[:, 0:T], selfb, ALU.min, [bps2, bwbcS], [blob])
    wb = sb("b_wb", [128, T]); bwb = Buf()
    tt("dve", wb[:], hib[:], lob[:], ALU.subtract, [bhib, blob], [bwb])
    midb = sb("b_mid", [128, T]); bmidb = Buf()
    cmpj = sb("b_cmpj", [128, T * 128]); bcmpj = Buf(); cmpj3 = cmpj[:].rearrange("p (t r) -> p t r", t=T)
    cntp = sb("b_cntp", [128, T]); bcntp = Buf()
    tot = sb("b_tot", [128, T]); btot = Buf()
    sge = sb("b_sge", [128, T]); bsge = Buf()
    for it in range(N_BISECT_S):
        P.op("dve", lambda e: e.tensor_scalar(out=wb[:], in0=wb[:], scalar1=0.5, scalar2=None, op0=ALU.mult), reads=[bwb], writes=[bwb])
        tt("dve", midb[:], lob[:], wb[:], ALU.add, [blob, bwb], [bmidb])
        tt("dve", cmpj3, scs3, midb[:].unsqueeze(2).to_broadcast([128, T, 128]), ALU.is_ge, [bscs, bmidb], [bcmpj])
        P.op("dve", lambda e: e.tensor_reduce(out=cntp[:], in_=cmpj3, axis=AX.X, op=ALU.add), reads=[bcmpj], writes=[bcntp])
        ps, bps = rPS.next()
        P.op("pe", lambda e: e.matmul(ps[:, 0:T], lhsT=onesf[:], rhs=cntp[:], start=True, stop=True), reads=[bonesf, bcntp], writes=[bps])
        tt("dve", sge[:], selfb, midb[:], ALU.is_ge, [bwbcS, bmidb], [bsge])
        tt("dve", tot[:], ps[:, 0:T], sge[:], ALU.add, [bps, bsge], [btot])
        P.op("dve", lambda e: e.tensor_scalar(out=tot[:], in0=tot[:], scalar1=float(TOPK) - 0.5, scalar2=None, op0=ALU.is_ge), reads=[btot], writes=[btot])
        tt("dve", tot[:], tot[:], wb[:], ALU.mult, [btot, bwb], [btot])
        tt("dve", lob[:], lob[:], tot[:], ALU.add, [blob, btot], [blob])
    Msel = sb("Msel", [128, T * 128]); bMsel = Buf(); Msel3 = Msel[:].rearrange("p (t r) -> p t r", t=T)
    tt("dve", Msel3, scs3, lob[:].unsqueeze(2).to_broadcast([128, T, 128]), ALU.is_ge, [bscs, blob], [bMsel])
    thr_tm = sb("thr_tm", [T, 4]); bthr = Buf()
    tt("dve", t1[:, 0:T], lob[0:T, :], I4, ALU.mult, [blob, bcst], [bt1])
    P.op("dve", lambda e: e.tensor_reduce(out=thr_tm[:, 0:1], in_=t1[:, 0:T], axis=AX.X, op=ALU.add), reads=[bt1], writes=[bthr])
    tt("dve", thr_tm[:, 1:2], selfsc[:], thr_tm[:, 0:1], ALU.is_ge, [bselfsc, bthr], [bthr])
    P.op("dve", lambda e: e.tensor_scalar(out=thr_tm[:, 2:3], in0=thr_tm[:, 1:2], scalar1=-NEG, scalar2=NEG, op0=ALU.mult, op1=ALU.add), reads=[bthr], writes=[bthr])
    pself = sb("pself", [T, 8]); bpself = Buf()
    P.op("act", lambda e: e.activation(out=pself[:], in_=sl[:], func=AF.Exp, scale=ATT_SCALE, bias=thr_tm[:, 2:3]), reads=[bsl, bthr], writes=[bpself])

    csel = sb("csel", [128, T]); bcsel = Buf()
    P.op("dve", lambda e: e.tensor_reduce(out=csel[:], in_=Msel3, axis=AX.X, op=ALU.add), reads=[bMsel], writes=[bcsel])
    ps, bps = rPS.next()
    P.op("pe", lambda e: e.matmul(ps[:, 0:T], lhsT=Tstrict, rhs=csel[:], start=True, stop=True), reads=[bcst, bcsel], writes=[bps])
    osel = sb("osel", [128, T]); bosel = Buf()
    esel = sb("esel", [128, T]); besel = Buf()
    P.op("dve", lambda e: e.tensor_copy(out=osel[:], in_=ps[:, 0:T]), reads=[bps], writes=[bosel])
    tt("dve", esel[:], osel[:], csel[:], ALU.add, [bosel, bcsel], [besel])
    rhsT = sb("rhsT", [128, T * 130]); brhsT = Buf(); rhsT3 = rhsT[:].rearrange("p (t c) -> p t c", t=T)
    for t in range(T):
        P.op("dve", lambda e: e.tensor_tensor_scan(out=rhsT3[:, t, 0:128], data0=onesf[:], data1=Msel3[:, t, :], initial=0.0, op0=ALU.mult, op1=ALU.add),
             reads=[bonesf, bMsel], writes=[brhsT])
    tt("dve", rhsT3[:, :, 0:128], rhsT3[:, :, 0:128], Msel3, ALU.mult, [brhsT, bMsel], [brhsT])
    P.op("dve", lambda e: e.tensor_copy(out=rhsT3[:, :, 128], in_=osel[:]), reads=[bosel], writes=[brhsT])
    P.op("dve", lambda e: e.tensor_copy(out=rhsT3[:, :, 129], in_=ptf[:]), reads=[bptf], writes=[brhsT])
    Asel = sb("Asel", [128, 256]); bAsel = Buf()
    A2 = sb("A2", [128, 256]); bA2 = Buf()
    idxT = sb("idxT", [128, 2 * T], I32); bidxT = Buf()
    vbias = sb("vbias", [128, 2 * T]); bvbias = Buf()
    c4 = sb("c4", [128, 8]); bc4 = Buf()
    eqt = sb("eqt", [128, 128]); beqt = Buf()
    for t in range(T):
        P.op("dve", lambda e: e.tensor_scalar(out=Asel[:], in0=Jf, scalar1=osel[:, t:t + 1], scalar2=None, op0=ALU.is_ge), reads=[bcst, bosel], writes=[bAsel])
        P.op("dve", lambda e: e.tensor_scalar(out=A2[:], in0=Jf, scalar1=esel[:, t:t + 1], scalar2=None, op0=ALU.is_lt), reads=[bcst, besel], writes=[bA2])
        tt("dve", Asel[:], Asel[:], A2[:], ALU.mult, [bAsel, bA2], [bAsel])
        for jc in range(2):
            col = t * 2 + jc
            ps, bps = rPS.next()
            P.op("pe", lambda e: e.matmul(ps[:, 0:130], lhsT=Asel[:, jc * 128:(jc + 1) * 128], rhs=rhsT3[:, t, :], start=True, stop=True),
                 reads=[bAsel, brhsT], writes=[bps])
            tt("dve", c4[:, 0:1], cstt[:, CS_J1 + jc:CS_J1 + jc + 1], ps[:, 128:129], ALU.subtract, [bcst, bps], [bc4])
            P.op("dve", lambda e: e.tensor_scalar(out=eqt[:], in0=ps[:, 0:128], scalar1=c4[:, 0:1], scalar2=None, op0=ALU.is_equal), reads=[bps, bc4], writes=[beqt])
            P.op("dve", lambda e: e.tensor_reduce(out=c4[:, 1:2], in_=eqt[:], axis=AX.X, op=ALU.add), reads=[beqt], writes=[bc4])
            tt("dve", eqt[:], eqt[:], iota_r, ALU.mult, [beqt, bcst], [beqt])
            P.op("dve", lambda e: e.tensor_reduce(out=c4[:, 2:3], in_=eqt[:], axis=AX.X, op=ALU.add), reads=[beqt], writes=[bc4])
            P.op("dve", lambda e: e.scalar_tensor_tensor(out=c4[:, 3:4], in0=ps[:, 129:130], scalar=128.0, in1=c4[:, 2:3], op0=ALU.mult, op1=ALU.add),
                 reads=[bps, bc4], writes=[bc4])
            P.op("dve", lambda e: e.tensor_copy(out=idxT[:, col:col + 1], in_=c4[:, 3:4]), reads=[bc4], writes=[bidxT])
            P.op("dve", lambda e: e.tensor_scalar(out=vbias[:, col:col + 1], in0=c4[:, 1:2], scalar1=-NEG, scalar2=NEG, op0=ALU.mult, op1=ALU.add),
                 reads=[bc4], writes=[bvbias])

    Ksel = sb("Ksel", [128, 512]); bKsel = Buf()
    Vsel = sb("Vsel", [128, 512]); bVsel = Buf()
    Kb = sb("Kb_s", [128, 512], BF16); bKb = Buf()
    Vx = sb("Vx", [128, 4 * 129], BF16); bVx = Buf(); Vx4 = Vx[:].rearrange("p (j g d) -> p j g d", j=2, g=2)
    KselT = sb("KselT", [128, 512], BF16); bKselT = Buf(); KselT3 = KselT[:].rearrange("p (g j) -> p g j", g=2)
    PTs = sb("PTs", [128, 16], BF16); bPTs = Buf()
    vself = sb("vself", [T, 2 * 129], BF16); bvself = Buf(); vself3 = vself[:].rearrange("p (g d) -> p g d", g=2)
    pselfm = sb("pselfm", [T, 8], BF16); bpselfm = Buf()
    osb = sb("osb", [4, T * 2 * 128]); bosb = Buf(); osb4 = osb[:].rearrange("p (t g d) -> p t g d", t=T, g=2)
    rcp = sb("rcp", [4, 2]); brcp = Buf()
    P.op("pool", lambda e: e.memset(Vx[:], 1.0), writes=[bVx])
    P.op("pool", lambda e: e.memset(vself[:], 1.0), writes=[bvself])
    P.op("dve", lambda e: e.tensor_copy(out=vself3[:, :, 0:128], in_=ztm[:, V0:V0 + 256].rearrange("p (g d) -> p g d", g=2)), reads=[bztm], writes=[bvself])
    for t in range(T):
        for jc in range(2):
            col = t * 2 + jc
            P.dma("pool", lambda e: e.indirect_dma_start(out=Ksel[:, jc * 256:(jc + 1) * 256], out_offset=None, in_=cache_k[:, :],
                                                         in_offset=bass.IndirectOffsetOnAxis(ap=idxT[:, col:col + 1], axis=0)), reads=[bidxT], writes=[bKsel])
            P.dma("pool", lambda e: e.indirect_dma_start(out=Vsel[:, jc * 256:(jc + 1) * 256], out_offset=None, in_=cache_v[:, :],
                                                         in_offset=bass.IndirectOffsetOnAxis(ap=idxT[:, col:col + 1], axis=0)), reads=[bidxT], writes=[bVsel])
        P.op("dve", lambda e: e.tensor_copy(out=Kb[:], in_=Ksel[:]), reads=[bKsel], writes=[bKb])
        P.op("dve", lambda e: e.tensor_copy(out=Vx4[:, :, :, 0:128], in_=Vsel[:].rearrange("p (j g d) -> p j g d", j=2, g=2)), reads=[bVsel], writes=[bVx])
        pt, bpt = rPST.next()
        for g in range(2):
            for jc in range(2):
                c0 = (g * 2 + jc) * 128
                P.op("pe", lambda e: e.transpose(out=pt[:, c0:c0 + 128], in_=Kb[:, jc * 256 + g * 128:jc * 256 + (g + 1) * 128], identity=identb[:]),
                     reads=[bKb, bidb], writes=[bpt])
        P.op("act", lambda e: e.copy(out=KselT[:], in_=pt[:, 0:512]), reads=[bpt], writes=[bKselT])
        ps, bps = rPS.next()
        for jc in range(2):
            for g in range(2):
                c0 = (jc * 2 + g) * 4
                P.op("pe", lambda e: e.matmul(ps[:, c0:c0 + 4], lhsT=KselT3[:, g, jc * 128:(jc + 1) * 128], rhs=qsT3[:, 4 * g:4 * g + 4, t], start=True, stop=True),
                     reads=[bKselT, bqsT], writes=[bps])
        for jc in range(2):
            col = t * 2 + jc
            P.op("act", lambda e: e.activation(out=PTs[:, jc * 8:(jc + 1) * 8], in_=ps[:, jc * 8:(jc + 1) * 8], func=AF.Exp, scale=ATT_SCALE, bias=vbias[:, col:col + 1]),
                 reads=[bps, bvbias], writes=[bPTs])
        P.op("dve", lambda e: e.tensor_scalar(out=pselfm[:], in0=pself[:], scalar1=I4[:, t:t + 1], scalar2=None, op0=ALU.mult), reads=[bpself, bcst], writes=[bpselfm])
        for g in range(2):
            c0 = g * 129
            for jc in range(2):
                P.op("pe", lambda e: e.matmul(psO[0:4, c0:c0 + 129], lhsT=PTs[:, jc * 8 + 4 * g:jc * 8 + 4 * g + 4], rhs=Vx4[:, jc, g, :], start=(jc == 0), stop=False),
                     reads=[bPTs, bVx], writes=[bpsO])
            P.op("pe", lambda e: e.matmul(psO[0:4, c0:c0 + 129], lhsT=pselfm[:, 4 * g:4 * g + 4], rhs=vself3[:, g, :], start=False, stop=True),
                 reads=[bpselfm, bvself], writes=[bpsO])
        for g in range(2):
            c0 = g * 129
            P.op("dve", lambda e: e.reciprocal(out=rcp[:, g:g + 1], in_=psO[0:4, c0 + 128:c0 + 129]), reads=[bpsO], writes=[brcp])
            P.op("dve", lambda e: e.tensor_scalar(out=osb4[:, t, g, :], in0=psO[0:4, c0:c0 + 128], scalar1=rcp[:, g:g + 1], scalar2=None, op0=ALU.mult),
                 reads=[bpsO, brcp], writes=[bosb])
    ps, bps = rPS.next()
    for t in range(T):
        for g in range(2):
            c0 = (t * 2 + g) * 4
            P.op("pe", lambda e: e.transpose(out=ps[:, c0:c0 + 4], in_=osb4[:, t, g, :], identity=identf[0:4, 0:4]), reads=[bosb, bcst], writes=[bps])
    oT = sb("oT_s", [128, 8 * T]); boT = Buf(); oT3 = oT[:].rearrange("p (h t) -> p h t", t=T)
    P.op("dve", lambda e: e.tensor_copy(out=oT3, in_=ps[:, 0:32].rearrange("p (t h) -> p h t", t=T)), reads=[bps], writes=[boT])
    actb = sb("actb_s", [128, 8 * T], BF16); bactb = Buf(); actb3 = actb[:].rearrange("p (k t) -> p k t", t=T)
    P.op("act", lambda e: e.activation(out=tmpA3, in_=zfm3[:, 16:24, :], func=AF.Silu), reads=[bzfm], writes=[btA])
    tt("dve", actb[:], oT[:], tmpA[:], ALU.mult, [boT, btA], [bactb])
    for cc in range(8):
        w3, bw = load_fm(FM_PB + cc)
        for kc in range(8):
            P.op("pe", lambda e: e.matmul(psB[:, cc * T:(cc + 1) * T], lhsT=w3[:, kc, :], rhs=actb3[:, kc, :], start=(kc == 0), stop=(kc == 7)),
                 reads=[bw, bactb], writes=[bpsB])
    P.op("act", lambda e: e.activation(out=tmpA3, in_=zfm3[:, 32:40, :], func=AF.Sigmoid), reads=[bzfm], writes=[btA])
    tt("dve", tmpA[:], psB[:, 0:8 * T], tmpA[:], ALU.mult, [bpsB, btA], [btA])
    msb = sb("msb", [128, 8 * T], BF16); bmsb = Buf(); msb3 = msb[:].rearrange("p (k t) -> p k t", t=T)
    tt("dve", msb[:], m_s[:], tmpA[:], ALU.add, [bm_s, btA], [bmsb])
    hres = sb("hres_s", [T, D]); bhres = Buf()
    for hb, blk in enumerate((TM_O0, TM_O1)):
        wo, bwo = load_tm(blk)
        ps, bps = rPS.next()
        for kc in range(8):
            P.op("pe", lambda e: e.matmul(ps[0:T, :], lhsT=msb3[:, kc, :], rhs=wo[:, kc, :], start=(kc == 0), stop=(kc == 7)), reads=[bmsb, bwo], writes=[bps])
        tt("dve", hres[:, hb * 512:(hb + 1) * 512], ps[0:T, :], gate_s[:, hb * 512:(hb + 1) * 512], ALU.mult, [bps, bgs], [bhres])
    tt("dve", hres[:], hres[:], xs[:], ALU.add, [bhres, bxs], [bhres])
    P.op("act", lambda e: e.activation(out=t1[:], in_=hres[:], func=AF.Square, accum_out=sm[:, 8:9]), reads=[bhres], writes=[bt1, bsm])
    P.op("act", lambda e: e.activation(out=sm[:, 9:10], in_=sm[:, 8:9], func=AF.Sqrt, scale=1.0 / D, bias=EPS), reads=[bsm], writes=[bsm])
    P.op("dve", lambda e: e.reciprocal(out=sm[:, 10:11], in_=sm[:, 9:10]), reads=[bsm], writes=[bsm])
    P.op("dve", lambda e: e.scalar_tensor_tensor(out=hres[:], in0=hres[:], scalar=sm[:, 10:11], in1=gfin_bc[0:T, :], op0=ALU.mult, op1=ALU.mult),
         reads=[bhres, bsm, bgfin], writes=[bhres])
    P.dma("act", lambda e: e.dma_start(out=y_s[:, :], in_=hres[:]), reads=[bhres], is_out=True)


def _fm(W, c0):
    return np.ascontiguousarray(W[:, c0:c0 + 128].reshape(8, 128, 128).transpose(1, 0, 2).reshape(128, 1024))


def _tm(W, c0, n=512):
    blk = np.zeros((1024, 512), np.float32)
    blk[:, :n] = W[:, c0:c0 + n]
    return np.ascontiguousarray(blk.reshape(8, 128, 512).transpose(1, 0, 2).reshape(128, 4096))


def _vec_fm(v):
    return np.ascontiguousarray(np.asarray(v, np.float32).reshape(-1, 128).T)


def _rope_tab(pos, half):
    inv = np.float32(10000.0) ** (-(np.arange(half, dtype=np.float32)) / np.float32(half))
    ang = (pos.astype(np.float32)[:, None] * inv[None, :]).astype(np.float32)
    c, s_ = np.cos(ang).astype(np.float32), np.sin(ang).astype(np.float32)
    return np.concatenate([c, c, -s_, s_], axis=1).astype(np.float32)


def _host_shared(inp):
    f32 = np.float32
    w_in = np.asarray(inp["w_in"][0], f32)
    w_pa = np.asarray(inp["w_pa"][0], f32); w_pb = np.asarray(inp["w_pb"][0], f32); w_o = np.asarray(inp["w_o"][0], f32)
    fm = []
    for base in FM_COLS:
        for cc in range(8):
            fm.append(_fm(w_in, base + cc * 128))
    for W in (w_pa, w_pb):
        for cc in range(8):
            fm.append(_fm(W, cc * 128))
    tm = [_tm(w_in, 2048), _tm(w_in, 2560), _tm(w_in, 3072), _tm(w_in, 4608), _tm(w_in, 5120, 72), _tm(w_o, 0), _tm(w_o, 512)]
    w_ada = np.asarray(inp["w_ada"][0], f32)
    wada = np.stack([_tm(w_ada, b * 512) for b in range(6)])
    wr = []
    for W in (inp["w_ra"][0], inp["w_rx"][0]):
        W = np.asarray(W, f32).reshape(4, 2, 128, 256).transpose(2, 0, 1, 3)
        wr.append(W.reshape(128, 2048))
    w_rr = np.ascontiguousarray(np.concatenate(wr, axis=1))
    chp = np.zeros((128, NCP), f32)
    chp[:, CP_GN:CP_GN + 8] = _vec_fm(inp["g_norm"][0])
    chp[:, CP_BADA:CP_BADA + 24] = _vec_fm(inp["b_ada"][0])
    chp[:, CP_WCONV:CP_WCONV + 32] = np.asarray(inp["w_conv"][0], f32).reshape(4, 8, 128).transpose(2, 0, 1).reshape(128, 32)
    chp[:, CP_BCONV:CP_BCONV + 8] = _vec_fm(inp["b_conv"][0])
    chp[:, CP_BRA:CP_BRA + 8] = _vec_fm(inp["b_ra"][0])
    chp[:, CP_BRX:CP_BRX + 8] = _vec_fm(inp["b_rx"][0])
    chp[:, CP_LAM:CP_LAM + 8] = _vec_fm(inp["lru_lambda"][0])
    cst = np.zeros((128, CS_END), f32)
    ar = np.arange(128)
    cst[:, CS_ID:CS_ID + 128] = np.eye(128, dtype=f32)
    cst[:, CS_TS:CS_TS + 128] = (ar[:, None] < ar[None, :]).astype(f32)
    cst[:, CS_JF:CS_JF + 256] = np.arange(256, dtype=f32)[None, :]
    cst[:, CS_IR:CS_IR + 128] = ar.astype(f32)[None, :]
    cst[:, CS_CAUS:CS_CAUS + 128] = np.where(ar[None, :] <= ar[:, None], 0.0, -1e30).astype(f32)
    cst[:, CS_J1] = ar + 1
    cst[:, CS_J1 + 1] = ar + 129
    cst[:, CS_P2:CS_P2 + 32] = (0.5 ** np.arange(1, 33, dtype=np.float64)).astype(f32)[None, :]
    pos = np.arange(SEQ)
    ropeq = _rope_tab(pos, 64)
    ropei = _rope_tab(pos, 32)
    ps_ = np.full((NS,), PAST)
    ropes = np.concatenate([_rope_tab(ps_, 64), _rope_tab(ps_, 32)], axis=1)
    sh = {
        "cache_k": np.asarray(inp["cache_k"], f32).reshape(NPOOL * 128, 256),
        "cache_v": np.asarray(inp["cache_v"], f32).reshape(NPOOL * 128, 256),
        "cache_i": np.asarray(inp["cache_idx_k"], f32).reshape(NPOOL, 128 * 64),
        "w_ada": wada, "b_ada": np.asarray(inp["b_ada"], f32).reshape(1, 3 * D),
        "w_fm": np.stack(fm), "w_tm": np.stack(tm), "w_rr": w_rr, "chp": chp, "cst": cst,
        "ropeq": ropeq, "ropei": ropei, "ropes": np.ascontiguousarray(ropes),
        "g_fin": np.asarray(inp["g_final"], f32).reshape(1, D),
        "idx_gb": np.concatenate([np.asarray(inp["idx_k_norm_g"][0], f32), np.asarray(inp["idx_k_norm_b"][0], f32)]).reshape(1, 128),
    }
    return sh


_NC_CACHE = {}


def kernel(**inputs):
    f32 = np.float32
    sh = _host_shared(inputs)
    in_maps = []
    for c in range(NCORE):
        m = dict(sh)
        s0, s1 = c * NS, (c + 1) * NS
        m["x_p"] = np.ascontiguousarray(np.asarray(inputs["x_prompt"][c], f32))
        m["x_s"] = np.ascontiguousarray(np.asarray(inputs["x_sample"][s0:s1, 0], f32))
        m["c5"] = np.ascontiguousarray(np.concatenate([np.asarray(inputs["c_prompt"][c:c + 1], f32), np.asarray(inputs["c_sample"][s0:s1], f32)], axis=0))
        m["st_conv"] = np.ascontiguousarray(np.asarray(inputs["state_conv"][0, s0:s1], f32).reshape(NS * 3, D))
        m["st_lru"] = np.ascontiguousarray(np.asarray(inputs["state_rglru"][0, s0:s1], f32))
        m["ptab"] = np.ascontiguousarray(np.asarray(inputs["page_table"][s0:s1], np.int32).T)
        in_maps.append(m)
    if "nc" not in _NC_CACHE:
        _NC_CACHE["nc"] = build_program()
    nc = _NC_CACHE["nc"]
    res = run_bass_kernel_spmd(nc, in_maps, core_ids=list(range(NCORE)))
    R = res.results
    cat = lambda k: np.stack([np.asarray(R[c][k], f32) for c in range(NCORE)])
    y_prompt = cat("y_p")
    y_sample = cat("y_s").reshape(NCORE * NS, 1, D)
    nk_p = cat("nk_p").reshape(1, NCORE, SEQ, 2, 128)
    nv_p = cat("nv_p").reshape(1, NCORE, SEQ, 2, 128)
    nki_p = cat("nki_p").reshape(1, NCORE, SEQ, 64)
    ncv_p = cat("ncv_p").reshape(1, NCORE, 3, D)
    nlr_p = cat("nlr_p").reshape(1, NCORE, D)
    nk_s = cat("nk_s").reshape(1, NCORE * NS, 1, 2, 128)
    nv_s = cat("nv_s").reshape(1, NCORE * NS, 1, 2, 128)
    nki_s = cat("nki_s").reshape(1, NCORE * NS, 1, 64)
    ncv_s = cat("ncv_s").reshape(1, NCORE * NS, 3, D)
    nlr_s = cat("nlr_s").reshape(1, NCORE * NS, D)
    return (y_prompt, y_sample, nk_p, nv_p, nki_p, ncv_p, nlr_p, nk_s, nv_s, nki_s, ncv_s, nlr_s)
```
